# Optimizing a Trainium2 kernel written in Bass

```python
import jax
import jax.numpy as jnp
from jax import lax
import numpy as np

D_MODEL = 1024
BATCH = 4
SEQ = 4096
DEPTH = 4

GRID_W = 64
CTX_LEN = 256
D_MIX = 2 * D_MODEL
D_SSD = D_MIX // 2
SSD_HEADDIM = 64
SSD_HEADS = D_SSD // SSD_HEADDIM
SSD_GROUPS = 2
SSD_HEADS_PER_GROUP = SSD_HEADS // SSD_GROUPS
SSD_STATE = 128
SSD_CHUNK = 128
CONV_K = 3
D_CONV_CH = D_SSD + 2 * SSD_GROUPS * SSD_STATE
D_S5 = D_MIX // 4
S5_GROUP = 16
S5_GROUPS = D_S5 // S5_GROUP
S5_STATE = 64
D_FNET = D_MIX // 4
FNET_HEADS = 4
FNET_HEAD_DIM = D_FNET // FNET_HEADS
PROJ_SIZES = (D_SSD, D_CONV_CH, 2 * SSD_HEADS, D_S5, D_S5, D_FNET, D_FNET)
SPLIT_POINTS = tuple(int(v) for v in np.cumsum(PROJ_SIZES)[:-1])
D_IN_PROJ = sum(PROJ_SIZES)
EPS = 1e-6

kernel_name = 'hybrid_ssd_s5_fnet_prefix_block'


def rmsnorm(v, g):
    vf = v.astype(jnp.float32)
    vf = vf * lax.rsqrt(jnp.mean(vf * vf, axis=-1, keepdims=True) + EPS)
    return vf.astype(v.dtype) * g


def maybe_flip(t, rev):
    return jnp.flip(t, axis=1) if rev else t


def conv_grid(v, w, bias):
    b, l, ch = v.shape
    rows = l // GRID_W
    img = v.reshape(b, rows, GRID_W, ch)
    out = lax.conv_general_dilated(img, w[:, :, None, :], (1, 1), 'SAME',
                                   dimension_numbers=('NHWC', 'HWIO', 'NHWC'), feature_group_count=ch)
    return out.reshape(b, l, ch) + bias


def conv_seq(v, w_row, bias):
    ch = v.shape[-1]
    out = lax.conv_general_dilated(v, w_row[:, None, :], (1,), 'SAME',
                                   dimension_numbers=('NWC', 'WIO', 'NWC'), feature_group_count=ch)
    return out + bias


def ssd_scan(x, a, bm, cm, h0):
    b, l = x.shape[:2]
    q = SSD_CHUNK
    nc = l // q
    x = x.reshape(b, nc, q, SSD_GROUPS, SSD_HEADS_PER_GROUP, SSD_HEADDIM)
    a = a.reshape(b, nc, q, SSD_GROUPS, SSD_HEADS_PER_GROUP)
    bm = bm.reshape(b, nc, q, SSD_GROUPS, SSD_STATE)
    cm = cm.reshape(b, nc, q, SSD_GROUPS, SSD_STATE)
    a_cum = jnp.cumsum(a, axis=2)
    seg = a_cum[:, :, :, None] - a_cum[:, :, None, :]
    mask = jnp.tril(jnp.ones((q, q), bool))[:, :, None, None]
    decay = jnp.where(mask, jnp.exp(jnp.where(mask, seg, 0.0)), 0.0)
    cb = jnp.einsum('bcign,bcjgn->bcijg', cm, bm)
    y_diag = jnp.einsum('bcijgh,bcjghp->bcighp', cb[..., None] * decay, x)
    decay_to_end = jnp.exp(a_cum[:, :, -1:] - a_cum)
    states = jnp.einsum('bcjgn,bcjgh,bcjghp->bcghpn', bm, decay_to_end, x)
    chunk_decay = jnp.exp(a_cum[:, :, -1])

    def step(h, inp):
        st, dec = inp
        return h * dec[..., None, None] + st, h

    h_final, h_in = lax.scan(step, h0, (jnp.moveaxis(states, 1, 0), jnp.moveaxis(chunk_decay, 1, 0)))
    h_in = jnp.moveaxis(h_in, 0, 1)
    y_off = jnp.einsum('bcign,bcghpn->bcighp', cm, h_in) * jnp.exp(a_cum)[..., None]
    y = (y_diag + y_off).reshape(b, l, SSD_HEADS, SSD_HEADDIM)
    return y, h_final


def ssd_branch(zl, xbcl, dtl, zc, xbcc, dtc, conv_w, conv_b, dt_bias, a_log, d_skip, g_norm, need_ctx):
    f32 = jnp.float32
    b = zl.shape[0]
    xbcl = jax.nn.silu(conv_grid(xbcl, conv_w, conv_b))
    xbcc = jax.nn.silu(conv_seq(xbcc, conv_w[CONV_K // 2], conv_b))
    a_neg = -jnp.exp(a_log.astype(f32))

    def prep(xbc, dt_raw):
        l = xbc.shape[1]
        xs, bm, cm = jnp.split(xbc.astype(f32), [D_SSD, D_SSD + SSD_GROUPS * SSD_STATE], axis=-1)
        dt = jax.nn.softplus(dt_raw.astype(f32).reshape(b, l, 2, SSD_HEADS) + dt_bias.astype(f32))
        return (xs.reshape(b, l, SSD_HEADS, SSD_HEADDIM), bm.reshape(b, l, SSD_GROUPS, SSD_STATE),
                cm.reshape(b, l, SSD_GROUPS, SSD_STATE), dt)

    xl_, bl_, cl_, dtl_ = prep(xbcl, dtl)
    xc_, bc_, cc_, dtc_ = prep(xbcc, dtc)
    dsk = d_skip.astype(f32)[:, None]
    yl = dsk * xl_
    yc = dsk * xc_ if need_ctx else None
    h0 = jnp.zeros((b, SSD_GROUPS, SSD_HEADS_PER_GROUP, SSD_HEADDIM, SSD_STATE), f32)
    for d in range(2):
        rev = d == 1
        y_c, h_c = ssd_scan(maybe_flip(xc_ * dtc_[:, :, d, :, None], rev),
                            maybe_flip(dtc_[:, :, d] * a_neg[d], rev),
                            maybe_flip(bc_, rev), maybe_flip(cc_, rev), h0)
        y_l, _ = ssd_scan(maybe_flip(xl_ * dtl_[:, :, d, :, None], rev),
                          maybe_flip(dtl_[:, :, d] * a_neg[d], rev),
                          maybe_flip(bl_, rev), maybe_flip(cl_, rev), h_c)
        yl = yl + maybe_flip(y_l, rev)
        if need_ctx:
            yc = yc + maybe_flip(y_c, rev)

    def gated_norm(y, z):
        l = y.shape[1]
        v = (y.reshape(b, l, D_SSD) * jax.nn.silu(z.astype(f32))).reshape(b, l, SSD_GROUPS, D_SSD // SSD_GROUPS)
        v = v * lax.rsqrt(jnp.mean(v * v, axis=-1, keepdims=True) + EPS)
        return (v.reshape(b, l, D_SSD) * g_norm.astype(f32)).astype(z.dtype)

    out_l = gated_norm(yl, zl)
    out_c = gated_norm(yc, zc) if need_ctx else None
    return out_l, out_c


def s5_discretise(lam_re, lam_im, log_step, b_re, b_im):
    f32 = jnp.float32
    lr = lam_re.astype(f32)
    li = lam_im.astype(f32)
    step = jnp.exp(log_step.astype(f32))[:, None]
    mag = jnp.exp(lr * step)
    ar = mag * jnp.cos(li * step)
    ai = mag * jnp.sin(li * step)
    den = lr * lr + li * li
    fr = ((ar - 1.0) * lr + ai * li) / den
    fi = (ai * lr - (ar - 1.0) * li) / den
    br = b_re.astype(f32)
    bi = b_im.astype(f32)
    bbr = fr[..., None] * br - fi[..., None] * bi
    bbi = fr[..., None] * bi + fi[..., None] * br
    return ar, ai, bbr, bbi


def s5_scan(u, ar, ai, bbr, bbi, h0r, h0i):
    l = u.shape[1]
    xr = jnp.einsum('blgk,gpk->blgp', u, bbr)
    xi = jnp.einsum('blgk,gpk->blgp', u, bbi)
    xr = xr.at[:, 0].add(ar * h0r - ai * h0i)
    xi = xi.at[:, 0].add(ar * h0i + ai * h0r)
    a_r = jnp.broadcast_to(ar, (1, l) + ar.shape)
    a_i = jnp.broadcast_to(ai, (1, l) + ai.shape)

    def combine(e1, e2):
        a1r, a1i, b1r, b1i = e1
        a2r, a2i, b2r, b2i = e2
        return (a2r * a1r - a2i * a1i, a2r * a1i + a2i * a1r,
                a2r * b1r - a2i * b1i + b2r, a2r * b1i + a2i * b1r + b2i)

    _, _, hr, hi = lax.associative_scan(combine, (a_r, a_i, xr, xi), axis=1)
    return hr, hi


def s5_readout(hr, hi, c_re, c_im):
    b, l = hr.shape[:2]
    y = (jnp.einsum('blgp,gkp->blgk', hr, c_re.astype(jnp.float32))
         - jnp.einsum('blgp,gkp->blgk', hi, c_im.astype(jnp.float32)))
    return y.reshape(b, l, D_S5)


def s5_branch(ul, gl, uc, gc, lam_re, lam_im, log_step, b_re, b_im, c_re, c_im, d_skip,
              w_glu, b_glu, need_ctx):
    f32 = jnp.float32
    b = ul.shape[0]

    def grouped(u):
        return u.astype(f32).reshape(b, u.shape[1], S5_GROUPS, S5_GROUP)

    ulg = grouped(ul)
    ucg = grouped(uc)
    d32 = d_skip.astype(f32)
    yl = d32 * ul.astype(f32)
    yc = d32 * uc.astype(f32) if need_ctx else None
    h0 = jnp.zeros((b, S5_GROUPS, S5_STATE), f32)
    for d in range(2):
        rev = d == 1
        ar, ai, bbr, bbi = s5_discretise(lam_re[d], lam_im[d], log_step[d], b_re[d], b_im[d])
        hcr, hci = s5_scan(maybe_flip(ucg, rev), ar, ai, bbr, bbi, h0, h0)
        hlr, hli = s5_scan(maybe_flip(ulg, rev), ar, ai, bbr, bbi, hcr[:, -1], hci[:, -1])
        yl = yl + maybe_flip(s5_readout(hlr, hli, c_re[d], c_im[d]), rev)
        if need_ctx:
            yc = yc + maybe_flip(s5_readout(hcr, hci, c_re[d], c_im[d]), rev)

    def glu(y, g, dtype):
        v = jax.nn.gelu(y)
        out = v * jax.nn.sigmoid(v @ w_glu.astype(f32) + b_glu.astype(f32))
        return (out * jax.nn.silu(g.astype(f32))).astype(dtype)

    out_l = glu(yl, gl, ul.dtype)
    out_c = glu(yc, gc, uc.dtype) if need_ctx else None
    return out_l, out_c


def fnet_branch(u, g, w, bias):
    b, l, _ = u.shape
    spec = jnp.fft.fft2(u.astype(jnp.float32).reshape(b, l, FNET_HEADS, FNET_HEAD_DIM),
                        axes=(1, 3), norm='ortho').real
    mixed = spec.reshape(b, l, D_FNET).astype(u.dtype) @ w + bias
    return mixed * jax.nn.silu(g)


def hybrid_mixer(hl, hc, w_in, conv_w, conv_b, dt_bias, a_log, d_ssd, g_ssd_norm,
                 lam_re, lam_im, log_step, b_re, b_im, c_re, c_im, s5_d, w_glu, b_glu,
                 fnet_w, fnet_b, w_out, need_ctx):
    zl, xbcl, dtl, usl, gsl, ufl, gfl = jnp.split(hl @ w_in, SPLIT_POINTS, axis=-1)
    zc, xbcc, dtc, usc, gsc, ufc, gfc = jnp.split(hc @ w_in, SPLIT_POINTS, axis=-1)
    ssd_l, ssd_c = ssd_branch(zl, xbcl, dtl, zc, xbcc, dtc, conv_w, conv_b, dt_bias, a_log,
                              d_ssd, g_ssd_norm, need_ctx)
    s5_l, s5_c = s5_branch(usl, gsl, usc, gsc, lam_re, lam_im, log_step, b_re, b_im, c_re, c_im,
                           s5_d, w_glu, b_glu, need_ctx)
    out_l = jnp.concatenate([ssd_l, s5_l, fnet_branch(ufl, gfl, fnet_w, fnet_b)], axis=-1) @ w_out
    if not need_ctx:
        return out_l, None
    out_c = jnp.concatenate([ssd_c, s5_c, fnet_branch(ufc, gfc, fnet_w, fnet_b)], axis=-1) @ w_out
    return out_l, out_c


def setup_inputs(seed: int = 0) -> dict:
    key = jax.random.key(seed)
    ks = jax.random.split(key, 32)
    f32 = jnp.float32
    L = DEPTH

    def nrm(k, shape, scale):
        return jax.random.normal(k, shape, f32) * scale

    def log_uniform(k, shape, lo, hi):
        return jax.random.uniform(k, shape, f32, np.log(lo), np.log(hi))

    dt0 = jnp.exp(log_uniform(ks[11], (L, 2, SSD_HEADS), 1e-3, 1e-1))
    lam_im = jnp.broadcast_to(jnp.pi * jnp.arange(S5_STATE, dtype=f32), (L, 2, S5_GROUPS, S5_STATE))
    return {
        'x': nrm(ks[0], (BATCH, SEQ, D_MODEL), 1.0),
        'c': nrm(ks[1], (BATCH, D_MODEL), 1.0),
        'ctx': nrm(ks[2], (BATCH, CTX_LEN, D_MODEL), 1.0),
        'c_ctx': nrm(ks[3], (D_MODEL,), 1.0),
        'w_mod': nrm(ks[4], (L, D_MODEL, 3 * D_MODEL), 0.5 * D_MODEL ** -0.5),
        'b_mod': nrm(ks[5], (L, 3 * D_MODEL), 0.02),
        'g_pre': 1.0 + nrm(ks[6], (L, D_MODEL), 0.02),
        'g_post': 1.0 + nrm(ks[7], (L, D_MODEL), 0.02),
        'w_in': nrm(ks[8], (L, D_MODEL, D_IN_PROJ), D_MODEL ** -0.5),
        'conv_w': nrm(ks[9], (L, CONV_K, CONV_K, D_CONV_CH), 1.0 / CONV_K),
        'conv_b': nrm(ks[10], (L, D_CONV_CH), 0.02),
        'dt_bias': dt0 + jnp.log(-jnp.expm1(-dt0)),
        'a_log': jnp.log(jax.random.uniform(ks[12], (L, 2, SSD_HEADS), f32, 1.0, 16.0)),
        'd_ssd': 1.0 + nrm(ks[13], (L, SSD_HEADS), 0.02),
        'g_ssd_norm': 1.0 + nrm(ks[14], (L, D_SSD), 0.02),
        's5_lambda_re': -0.5 + nrm(ks[15], (L, 2, S5_GROUPS, S5_STATE), 0.01),
        's5_lambda_im': lam_im,
        's5_log_step': log_uniform(ks[16], (L, 2, S5_GROUPS), 1e-3, 1e-1),
        's5_b_re': nrm(ks[17], (L, 2, S5_GROUPS, S5_STATE, S5_GROUP), (2.0 * S5_GROUP) ** -0.5),
        's5_b_im': nrm(ks[18], (L, 2, S5_GROUPS, S5_STATE, S5_GROUP), (2.0 * S5_GROUP) ** -0.5),
        's5_c_re': nrm(ks[19], (L, 2, S5_GROUPS, S5_GROUP, S5_STATE), (2.0 * S5_STATE) ** -0.5),
        's5_c_im': nrm(ks[20], (L, 2, S5_GROUPS, S5_GROUP, S5_STATE), (2.0 * S5_STATE) ** -0.5),
        's5_d': nrm(ks[21], (L, D_S5), 1.0),
        's5_w_glu': nrm(ks[22], (L, D_S5, D_S5), D_S5 ** -0.5),
        's5_b_glu': nrm(ks[23], (L, D_S5), 0.02),
        'fnet_w': nrm(ks[24], (L, D_FNET, D_FNET), D_FNET ** -0.5),
        'fnet_b': nrm(ks[25], (L, D_FNET), 0.02),
        'w_out': nrm(ks[26], (L, D_MIX, D_MODEL), D_MIX ** -0.5),
    }


def reference(x, c, ctx, c_ctx, w_mod, b_mod, g_pre, g_post, w_in, conv_w, conv_b, dt_bias, a_log,
              d_ssd, g_ssd_norm, s5_lambda_re, s5_lambda_im, s5_log_step, s5_b_re, s5_b_im,
              s5_c_re, s5_c_im, s5_d, s5_w_glu, s5_b_glu, fnet_w, fnet_b, w_out):
    xl, xc = x, ctx
    for i in range(DEPTH):
        need_ctx = i < DEPTH - 1
        mod_l = jax.nn.silu(c) @ w_mod[i] + b_mod[i]
        mod_c = jax.nn.silu(c_ctx) @ w_mod[i] + b_mod[i]
        shift_l, scale_l, gate_l = jnp.split(mod_l[:, None, :], 3, axis=-1)
        shift_c, scale_c, gate_c = jnp.split(mod_c, 3, axis=-1)
        hl = rmsnorm(xl, g_pre[i]) * (1.0 + scale_l) + shift_l
        hc = rmsnorm(xc, g_pre[i]) * (1.0 + scale_c) + shift_c
        yl, yc = hybrid_mixer(hl, hc, w_in[i], conv_w[i], conv_b[i], dt_bias[i], a_log[i], d_ssd[i],
                              g_ssd_norm[i], s5_lambda_re[i], s5_lambda_im[i], s5_log_step[i],
                              s5_b_re[i], s5_b_im[i], s5_c_re[i], s5_c_im[i], s5_d[i], s5_w_glu[i],
                              s5_b_glu[i], fnet_w[i], fnet_b[i], w_out[i], need_ctx)
        xl = xl + gate_l * rmsnorm(yl, g_post[i])
        if need_ctx:
            xc = xc + gate_c * rmsnorm(yc, g_post[i])
    return xl
```

```python
import math
from contextlib import ExitStack
import numpy as np
import concourse.bass as bass
import concourse.mybir as mybir
from concourse.bass_utils import run_bass_kernel_spmd

F32 = mybir.dt.float32
BF16 = mybir.dt.bfloat16
AF = mybir.ActivationFunctionType
ALU = mybir.AluOpType
AX = mybir.AxisListType


class Buf:
    __slots__ = ("w", "r", "name")

    def __init__(self, name=""):
        self.w = None
        self.r = {}
        self.name = name


class Sched:
    NR = 8

    def __init__(self, nc):
        self.nc = nc
        self.eng = {"pe": nc.tensor, "act": nc.scalar, "dve": nc.vector,
                    "pool": nc.gpsimd, "sp": nc.sync}
        self.csem = {e: nc.alloc_semaphore("c_" + e) for e in ("pe", "act", "dve", "pool")}
        self.ccnt = {e: 0 for e in self.csem}
        self.dq = {}
        for q in ("sp", "act", "pool"):
            self.dq[q] = dict(sems=[nc.alloc_semaphore(f"d_{q}{i}") for i in range(self.NR)],
                              n=0, tk=[None] * self.NR)
        self.waited = {}
        self.ninstr = 0

    def _wait(self, e, tk):
        if tk is None:
            return
        key, sem, val = tk
        if key == "pe" and e == "pe":
            return
        if self.waited.get((e, key), 0) >= val:
            return
        self.eng[e].wait_ge(sem, val)
        self.waited[(e, key)] = val

    def _deps(self, e, r, w):
        for b in r:
            self._wait(e, b.w)
        for b in w:
            self._wait(e, b.w)
            for t in list(b.r.values()):
                self._wait(e, t)

    def _record(self, tk, r, w):
        for b in r:
            b.r[tk[0]] = tk
        for b in w:
            b.w = tk
            b.r = {}

    def op(self, e, fn, r=(), w=()):
        self._deps(e, r, w)
        ins = fn()
        self.ccnt[e] += 1
        ins.then_inc(self.csem[e], 1)
        tk = (e, self.csem[e], self.ccnt[e])
        self._record(tk, r, w)
        self.ninstr += 1
        return tk

    def dma(self, q, out, in_, r=(), w=(), **kw):
        d = self.dq[q]
        slot = d["n"] % self.NR
        self._wait(q, d["tk"][slot])
        self._deps(q, r, w)
        ins = self.eng[q].dma_start(out=out, in_=in_, **kw)
        ins.then_inc(d["sems"][slot], 16)
        val = 16 * (d["n"] // self.NR + 1)
        tk = (("d", q, slot), d["sems"][slot], val)
        d["tk"][slot] = tk
        d["n"] += 1
        self._record(tk, r, w)
        self.ninstr += 1
        return tk

    def all_tickets(self):
        tks = []
        for e in self.csem:
            if self.ccnt[e]:
                tks.append((e, self.csem[e], self.ccnt[e]))
        for q, d in self.dq.items():
            for t in d["tk"]:
                if t is not None:
                    tks.append(t)
        return tks

    def barrier(self, engines=("pe", "act", "dve", "pool", "sp")):
        tks = self.all_tickets()
        for e in engines:
            for t in tks:
                self._wait(e, t)

    def drain(self, e="sp"):
        for t in self.all_tickets():
            if t[0] == "pe" and e == "pe":
                continue
            self._wait(e, t)

T = 4352

def phase_conv(k, l):
    nc, s, I, B = k.nc, k.s, k.I, k.B
    with ExitStack() as st:
        T_ = lambda name, shape, dt: st.enter_context(nc.sbuf_tensor(name + k.sfx, shape, dt))
        P_ = lambda name, shape, dt: st.enter_context(nc.psum_tensor(name + k.sfx, shape, dt))
        cw9 = T_("cw9", [9, 1536], F32)
        cwT = T_("cwT", [128, 108], F32)
        cb = T_("cb", [128, 12], F32)
        dg = T_("dg", [128, 108, 128], BF16)
        xp = [T_(f"xp{i}", [128, 258 + 66 * 66], BF16) for i in range(2)]
        ev = [T_(f"cev{i}", [128, 512], BF16) for i in range(2)]
        psW = P_("psW", [128, 108], F32)
        psC = [P_(f"psC{i}", [128, 512], F32) for i in range(2)]
        s.dma("sp", cw9[:], I["conv_w"][l], w=[B("cw9")])
        s.dma("sp", cb[:], I["conv_b"][l].rearrange("(t p) -> p t", p=128), w=[B("cb")], allow_slow_non_contiguous=True)
        for t in range(12):
            s.op("pe", lambda t=t: nc.tensor.transpose(out=psW[:, t * 9:(t + 1) * 9], in_=cw9[:, t * 128:(t + 1) * 128], identity=k.cst[0:9, 0:9]),
                 r=[B("cw9"), B("cst")], w=[B("psW")])
        s.op("dve", lambda: nc.vector.tensor_copy(out=cwT[:], in_=psW[:]), r=[B("psW")], w=[B("cwT")])
        for j in range(108):
            e = "dve" if j % 2 == 0 else "pool"
            eng = nc.vector if e == "dve" else nc.gpsimd
            s.op(e, lambda j=j, eng=eng: eng.tensor_scalar(out=dg[:, j, :], in0=k.identb[:], scalar1=cwT[:, j:j + 1], scalar2=None, op0=ALU.mult),
                 r=[B("cwT"), B("identb")], w=[B(f"dg{j}")])
        for i in range(2):
            s.op("pool", lambda i=i: nc.gpsimd.memset(xp[i][:], 0.0), w=[B(f"xp{i}")])
        n = 0
        for t in range(12):
            x_ = xp[t % 2]; xb = B(f"xp{t % 2}")
            rows = k.PB[t * 128:(t + 1) * 128, :]
            s.dma("sp", x_[:, 1:257], rows[:, 0:256], r=[B("PB")], w=[xb])
            grid = x_[:, 258:258 + 4356].rearrange("p (r c) -> p r c", c=66)
            for hh in range(2):
                s.dma("sp", grid[:, 1 + hh * 32:33 + hh * 32, 1:65], rows[:, 256 + hh * 2048:256 + (hh + 1) * 2048].rearrange("p (r c) -> p r c", c=64), r=[B("PB")], w=[xb])
            for rg in range(9):
                pc = psC[n % 2]; pcb = B(f"psC{n % 2}")
                if rg < 8:
                    for tap in range(9):
                        ky, kx = tap // 3, tap % 3
                        rhs = grid[:, rg * 8 + ky:rg * 8 + ky + 8, kx:kx + 64]
                        s.op("pe", lambda pc=pc, t=t, tap=tap, rhs=rhs: nc.tensor.matmul(pc[:].rearrange("p (r c) -> p r c", c=64), lhsT=dg[:, t * 9 + tap, :], rhs=rhs,
                                                                              start=(tap == 0), stop=(tap == 8)), r=[xb, B(f"dg{t * 9 + tap}")], w=[pcb])
                    ntok = 512; t0 = 256 + rg * 512
                else:
                    for kx in range(3):
                        s.op("pe", lambda pc=pc, t=t, kx=kx: nc.tensor.matmul(pc[:, 0:256], lhsT=dg[:, t * 9 + 3 + kx, :], rhs=x_[:, kx:kx + 256],
                                                                    start=(kx == 0), stop=(kx == 2)), r=[xb, B(f"dg{t * 9 + 3 + kx}")], w=[pcb])
                    ntok = 256; t0 = 0
                e_ = ev[n % 2]; eb = B(f"cev{n % 2}")
                s.op("act", lambda e_=e_, pc=pc, t=t, ntok=ntok: nc.scalar.activation(out=e_[:, 0:ntok], in_=pc[:, 0:ntok], func=AF.Silu, bias=cb[:, t:t + 1]),
                     r=[pcb, B("cb")], w=[eb])
                s.dma("pool", k.XC[t * 128:(t + 1) * 128, t0:t0 + ntok], e_[:, 0:ntok], r=[eb], w=[B("XC")])
                n += 1
        s.barrier()

T = 4352; NCH = 34; D = 1024; EPS = 1e-6
C_MF = 137; C_MB = 138; C_ONES = 140; C_NEGF = 268; C_NEGB = 396; C_BM8 = 524; C_IOTA = 1024

def phase_ssd(k, l, need_ctx=True):
    nc, s, I, B = k.nc, k.s, k.I, k.B
    cst = k.cst
    with ExitStack() as st:
        T_ = lambda name, shape, dt: st.enter_context(nc.sbuf_tensor(name + k.sfx, shape, dt))
        P_ = lambda name, shape, dt: st.enter_context(nc.psum_tensor(name + k.sfx, shape, dt))
        pst = ExitStack()
        TP_ = lambda name, shape, dt: pst.enter_context(nc.sbuf_tensor(name + k.sfx, shape, dt))
        acs = T_("s_acs", [128, T], F32)
        nacs = T_("s_nacs", [128, T], F32)
        Q = T_("s_Q", [128, T], F32)
        colp = T_("s_colp", [128, 4], F32)
        tot = T_("s_tot", [128, NCH], F32)
        DT = T_("s_DT", [32, NCH, 32], F32)
        cdall = T_("s_cd", [128, NCH, 32], F32)
        Esel = T_("s_Esel", [32, 32, 128], F32)
        negm = T_("s_negm", [128, 2, 512], BF16)
        dskc = T_("s_dskc", [128, 8], F32)
        dgd = T_("s_dgd", [128, 8, 128], BF16)
        gnb = T_("s_gnb", [128, D], F32)
        dt_ = TP_("s_dt", [128, T], F32)
        a_ = TP_("s_a", [128, T], F32)
        cum = TP_("s_cum", [128, T], F32)
        psCB = P_("psCB", [128, 256], F32)
        psX = P_("psX", [128, 1024], BF16)
        psB = P_("psB", [128, 256], BF16)
        psE = P_("psE", [128, 512], F32)
        psY = [P_(f"psY{i}", [128, 512], F32) for i in range(2)]
        psOS = [P_(f"psOS{i}", [128, 512], F32) for i in range(2)]

        V = lambda fn, r=(), w=(): s.op("dve", fn, r=r, w=w)
        A = lambda fn, r=(), w=(): s.op("act", fn, r=r, w=w)
        G = lambda fn, r=(), w=(): s.op("pool", fn, r=r, w=w)
        PE = lambda fn, r=(), w=(): s.op("pe", fn, r=r, w=w)
        for q in range(4):
            s.dma("sp", dt_[32 * q:32 * q + 32, :], k.PF[0:32, :], r=[B("PF")], w=[B("s_dt")])
            s.dma("sp", colp[32 * q:32 * q + 32, 0:1], I["dt_bias"][l].rearrange("(p o) -> p o", o=1), w=[B("s_colp")])
            s.dma("sp", colp[32 * q:32 * q + 32, 1:2], I["a_log"][l].rearrange("(p o) -> p o", o=1), w=[B("s_colp")])
        for t in range(8):
            for hh in range(2):
                s.dma("sp", dskc[64 * hh:64 * hh + 64, t:t + 1], I["d_ssd"][l, 2 * t + hh:2 * t + hh + 1].partition_broadcast(64), w=[B("s_dskc")])
        s.dma("sp", gnb[:], I["g_ssd_norm"][l].partition_broadcast(128), w=[B("s_gnb")])
        A(lambda: nc.scalar.activation(out=colp[:, 2:3], in_=colp[:, 1:2], func=AF.Exp), w=[B("s_colp")])
        V(lambda: nc.vector.tensor_scalar(out=colp[:, 3:4], in0=colp[:, 2:3], scalar1=-1.0, scalar2=None, op0=ALU.mult), w=[B("s_colp")])
        A(lambda: nc.scalar.activation(out=dt_[:], in_=dt_[:], func=AF.Exp, bias=colp[:, 0:1]), r=[B("s_colp")], w=[B("s_dt")])
        A(lambda: nc.scalar.activation(out=dt_[:], in_=dt_[:], func=AF.Ln, bias=1.0), w=[B("s_dt")])
        V(lambda: nc.vector.tensor_scalar(out=a_[:], in0=dt_[:], scalar1=colp[:, 3:4], scalar2=None, op0=ALU.mult), r=[B("s_dt"), B("s_colp")], w=[B("s_a")])
        for c in range(NCH):
            V(lambda c=c: nc.vector.tensor_tensor_scan(out=cum[:, c * 128:(c + 1) * 128], data0=cst[:, C_ONES:C_ONES + 128], data1=a_[:, c * 128:(c + 1) * 128],
                                                   initial=0.0, op0=ALU.mult, op1=ALU.add), r=[B("s_a"), B("cst")], w=[B("s_cum")])
        cum3 = cum[:].rearrange("p (c i) -> p c i", i=128)
        V(lambda: nc.vector.tensor_copy(out=tot[:], in_=cum3[:, :, 127]), r=[B("s_cum")], w=[B("s_tot")])
        totb = tot[:].unsqueeze(2).to_broadcast([128, NCH, 128])
        V(lambda: nc.vector.tensor_tensor(out=nacs[:], in0=a_[:], in1=cum[:], op=ALU.subtract), r=[B("s_a"), B("s_cum")], w=[B("s_nacs")])
        V(lambda: nc.vector.tensor_tensor(out=nacs[:].rearrange("p (c i) -> p c i", i=128), in0=nacs[:].rearrange("p (c i) -> p c i", i=128), in1=totb, op=ALU.add),
          r=[B("s_tot")], w=[B("s_nacs")])
        V(lambda: nc.vector.tensor_scalar(out=acs[:], in0=cum[:], scalar1=cst[:, C_MF:C_MF + 1], scalar2=None, op0=ALU.mult), r=[B("s_cum"), B("cst")], w=[B("s_acs")])
        V(lambda: nc.vector.scalar_tensor_tensor(out=acs[:], in0=nacs[:], scalar=cst[:, C_MB:C_MB + 1], in1=acs[:], op0=ALU.mult, op1=ALU.add), r=[B("s_nacs")], w=[B("s_acs")])
        V(lambda: nc.vector.tensor_scalar(out=nacs[:], in0=acs[:], scalar1=-1.0, scalar2=None, op0=ALU.mult), r=[B("s_acs")], w=[B("s_nacs")])
        V(lambda: nc.vector.tensor_tensor(out=a_[:].rearrange("p (c i) -> p c i", i=128), in0=nacs[:].rearrange("p (c i) -> p c i", i=128), in1=totb, op=ALU.add),
          r=[B("s_nacs"), B("s_tot")], w=[B("s_a")])
        A(lambda: nc.scalar.activation(out=a_[:], in_=a_[:], func=AF.Exp), w=[B("s_a")])
        V(lambda: nc.vector.tensor_tensor(out=a_[:], in0=a_[:], in1=dt_[:], op=ALU.mult), r=[B("s_dt")], w=[B("s_a")])
        A(lambda: nc.scalar.activation(out=cum[:], in_=acs[:], func=AF.Exp), r=[B("s_acs")], w=[B("s_cum")])
        G(lambda: nc.gpsimd.tensor_copy(out=Q[0:32, :], in_=dt_[0:32, :]), r=[B("s_dt")], w=[B("s_Q")])
        G(lambda: nc.gpsimd.tensor_copy(out=Q[32:64, :], in_=a_[32:64, :]), r=[B("s_a")], w=[B("s_Q")])
        G(lambda: nc.gpsimd.tensor_copy(out=Q[64:96, :], in_=cum[64:96, :]), r=[B("s_cum")], w=[B("s_Q")])
        G(lambda: nc.gpsimd.tensor_copy(out=Q[96:128, :], in_=nacs[96:128, :]), r=[B("s_nacs")], w=[B("s_Q")])
        V(lambda: nc.vector.tensor_copy(out=Esel[:], in_=cst[0:32, 0:32].unsqueeze(2).to_broadcast([32, 32, 128])), r=[B("cst")], w=[B("s_Esel")])
        V(lambda: nc.vector.tensor_copy(out=negm[:, 0, :].rearrange("p (a i) -> p a i", i=128), in_=cst[:, C_NEGF:C_NEGF + 128].unsqueeze(1).to_broadcast([128, 4, 128])), r=[B("cst")], w=[B("s_negm")])
        V(lambda: nc.vector.tensor_copy(out=negm[:, 1, :].rearrange("p (a i) -> p a i", i=128), in_=cst[:, C_NEGB:C_NEGB + 128].unsqueeze(1).to_broadcast([128, 4, 128])), r=[B("cst")], w=[B("s_negm")])
        V(lambda: nc.vector.tensor_tensor(out=DT[:], in0=cst[0:32, 0:32].unsqueeze(1).to_broadcast([32, NCH, 32]), in1=tot[0:32, :].unsqueeze(2).to_broadcast([32, NCH, 32]), op=ALU.mult),
          r=[B("s_tot"), B("cst")], w=[B("s_DT")])
        for c0 in range(0, NCH, 16):
            n = min(16, NCH - c0)
            PE(lambda c0=c0, n=n: nc.tensor.matmul(psE[:, 0:n * 32], lhsT=cst[0:32, C_ONES:C_ONES + 128], rhs=DT[:, c0:c0 + n, :].rearrange("p c k -> p (c k)"), start=True, stop=True),
               r=[B("s_DT"), B("cst")], w=[B("psE")])
            A(lambda c0=c0, n=n: nc.scalar.activation(out=cdall[:, c0:c0 + n, :].rearrange("p c k -> p (c k)"), in_=psE[:, 0:n * 32], func=AF.Exp), r=[], w=[B("psE"), B("s_cd")])
        for t in range(8):
            V(lambda t=t: nc.vector.tensor_scalar(out=dgd[:, t, :], in0=k.identb[:], scalar1=dskc[:, t:t + 1], scalar2=None, op0=ALU.mult), r=[B("s_dskc"), B("identb")], w=[B("s_dgd")])
        s.barrier()
        pst.close()
        hst = [T_(f"s_h{d}", [128, D], F32) for d in range(2)]
        hbf = [T_(f"s_hb{d}", [128, D], BF16) for d in range(2)]
        xT = [T_(f"s_xT{i}", [128, 8, 128], BF16) for i in range(2)]
        BC = [T_(f"s_BC{i}", [128, 4, 128], BF16) for i in range(2)]
        tmq = T_("s_tmq", [128, 128], F32)
        xdt = T_("s_xdt", [128, D], BF16)
        xw = T_("s_xw", [128, D], BF16)
        Btok = T_("s_Btok", [128, 256], BF16)
        dec = [T_(f"s_dec{i}", [128, 512], F32) for i in range(2)]
        MTt = T_("s_MT", [128, 16, 128], BF16)
        tmp = T_("s_tmp", [128, D], F32)
        yt = T_("s_y", [128, D], F32)
        yf = [T_(f"s_yf{i}", [128, D], F32) for i in range(2)]
        zt = [T_(f"s_z{i}", [128, D], F32) for i in range(2)]
        sz = T_("s_sz", [128, D], F32)
        junk = T_("s_junk", [128, 512], BF16)
        st2 = T_("s_st2", [128, 4], F32)
        mo = T_("s_mo", [128, D], BF16)
        mT = [T_(f"s_mT{i}", [128, 8, 128], BF16) for i in range(2)]
        for d in range(2):
            hb_ = B(f"s_h{d}"); hbb = B(f"s_hb{d}")
            V(lambda d=d: nc.vector.memset(hst[d][:], 0.0), w=[hb_])
            V(lambda d=d: nc.vector.memset(hbf[d][:], 0.0), w=[hbb])
            order = list(range(NCH)) if d == 0 else [1, 0] + list(range(NCH - 1, 1, -1))
            for n_, c in enumerate(order):
                t0 = c * 128
                x_ = xT[n_ % 2]; xb = B(f"s_xT{n_ % 2}"); bc_ = BC[n_ % 2]; bcb = B(f"s_BC{n_ % 2}")
                s.dma("sp", x_[:], k.XC[0:1024, t0:t0 + 128].rearrange("(t p) j -> p t j", p=128), r=[B("XC")], w=[xb])
                s.dma("sp", bc_[:], k.XC[1024:1536, t0:t0 + 128].rearrange("(t p) j -> p t j", p=128), r=[B("XC")], w=[bcb])
                if d == 1:
                    z_ = zt[n_ % 2]; zb = B(f"s_z{n_ % 2}"); yf_ = yf[n_ % 2]; yfb = B(f"s_yf{n_ % 2}")
                    s.dma("sp", z_[:], k.ZT[t0:t0 + 128, :], r=[B("ZT")], w=[zb])
                    s.dma("sp", yf_[:], k.YF[t0:t0 + 128, :], r=[B("YF")], w=[yfb])
                PE(lambda t0=t0: nc.tensor.transpose(out=psE[:, 0:128], in_=Q[:, t0:t0 + 128], identity=cst[:, 0:128]), r=[B("s_Q"), B("cst")], w=[B("psE")])
                A(lambda: nc.scalar.copy(out=tmq[:], in_=psE[:, 0:128]), w=[B("psE"), B("s_tmq")])
                for t in range(8):
                    PE(lambda t=t, x_=x_: nc.tensor.transpose(out=psX[:, t * 128:(t + 1) * 128], in_=x_[:, t, :], identity=k.identb[:]), r=[xb, B("identb")], w=[B("psX")])
                for g in range(2):
                    PE(lambda g=g, bc_=bc_: nc.tensor.transpose(out=psB[:, g * 128:(g + 1) * 128], in_=bc_[:, g, :], identity=k.identb[:]), r=[bcb, B("identb")], w=[B("psB")])
                    PE(lambda g=g, bc_=bc_: nc.tensor.matmul(psCB[:, g * 128:(g + 1) * 128], lhsT=bc_[:, g, :], rhs=bc_[:, 2 + g, :], start=True, stop=True), r=[bcb], w=[B("psCB")])
                A(lambda: nc.scalar.copy(out=Btok[:], in_=psB[:]), r=[], w=[B("psB"), B("s_Btok")])
                psX3 = psX[:].rearrange("p (h e) -> p h e", e=64)
                V(lambda d=d: nc.vector.tensor_tensor(out=xdt[:].rearrange("p (h e) -> p h e", e=64), in0=psX3, in1=tmq[:, d * 16:d * 16 + 16].unsqueeze(2).to_broadcast([128, 16, 64]), op=ALU.mult),
                  r=[B("s_tmq")], w=[B("psX"), B("s_xdt")])
                V(lambda d=d: nc.vector.tensor_tensor(out=xw[:].rearrange("p (h e) -> p h e", e=64), in0=psX3, in1=tmq[:, 32 + d * 16:48 + d * 16].unsqueeze(2).to_broadcast([128, 16, 64]), op=ALU.mult),
                  r=[B("s_tmq")], w=[B("psX"), B("s_xw")])
                for hq in range(4):
                    g = hq // 2
                    PE(lambda d=d: nc.tensor.matmul(psE[:], lhsT=k.identb[:], rhs=negm[:, d, :], start=True, stop=False), r=[B("s_negm"), B("identb")], w=[B("psE")])
                    for hh in range(4):
                        dh = d * 16 + hq * 4 + hh
                        o_ = psE[:, hh * 128:(hh + 1) * 128]
                        PE(lambda o_=o_, dh=dh, t0=t0: nc.tensor.matmul(o_, lhsT=Esel[:, dh, :], rhs=acs[0:32, t0:t0 + 128], start=False, stop=False, skip_group_check=True), r=[B("s_Esel"), B("s_acs")], w=[B("psE")])
                        PE(lambda o_=o_, dh=dh, t0=t0, hh=hh: nc.tensor.matmul(o_, lhsT=nacs[0:32, t0:t0 + 128], rhs=Esel[:, dh, :], start=False, stop=(hh == 3), skip_group_check=True), r=[B("s_Esel"), B("s_nacs")], w=[B("psE")])
                    dc = dec[hq % 2]; dcb = B(f"s_dec{hq % 2}")
                    A(lambda dc=dc: nc.scalar.activation(out=dc[:], in_=psE[:], func=AF.Exp), r=[], w=[B("psE"), dcb])
                    V(lambda dc=dc, hq=hq, g=g: nc.vector.tensor_tensor(out=MTt[:, hq * 4:hq * 4 + 4, :], in0=dc[:].rearrange("p (a i) -> p a i", i=128),
                                                                   in1=psCB[:, g * 128:(g + 1) * 128].unsqueeze(1).to_broadcast([128, 4, 128]), op=ALU.mult),
                      r=[dcb], w=[B("psCB"), B(f"s_MT{hq}")])
                if d == 0:
                    for t in range(8):
                        py = psY[t // 4]
                        PE(lambda t=t, py=py, x_=x_: nc.tensor.matmul(py[:, (t % 4) * 128:(t % 4) * 128 + 128], lhsT=x_[:, t, :], rhs=dgd[:, t, :], start=(t % 4 == 0), stop=False, skip_group_check=True),
                           r=[xb, B("s_dgd")], w=[B(f"psY{t // 4}")])
                for h in range(16):
                    py = psY[h // 8]
                    PE(lambda h=h, py=py: nc.tensor.matmul(py[:, (h % 8) * 64:(h % 8) * 64 + 64], lhsT=MTt[:, h, :], rhs=xdt[:, h * 64:(h + 1) * 64], start=(d == 1 and h % 8 == 0), stop=(h % 8 == 7), skip_group_check=True),
                       r=[B(f"s_MT{h // 4}"), B("s_xdt")], w=[B(f"psY{h // 8}")])
                for g in range(2):
                    PE(lambda g=g, bc_=bc_, d=d: nc.tensor.matmul(psOS[g][:], lhsT=bc_[:, 2 + g, :], rhs=hbf[d][:, g * 512:(g + 1) * 512], start=True, stop=True),
                       r=[bcb, hbb], w=[B(f"psOS{g}")])
                for g in range(2):
                    V(lambda g=g, d=d: nc.vector.tensor_tensor(out=tmp[:, g * 512:(g + 1) * 512].rearrange("p (h e) -> p h e", e=64), in0=psOS[g][:].rearrange("p (h e) -> p h e", e=64),
                                                          in1=tmq[:, 64 + d * 16 + g * 8:64 + d * 16 + g * 8 + 8].unsqueeze(2).to_broadcast([128, 8, 64]), op=ALU.mult),
                      r=[B("s_tmq")], w=[B(f"psOS{g}"), B("s_tmp")])
                if d == 0:
                    y_ = yf[n_ % 2]; yb = B(f"s_yf{n_ % 2}")
                    for g in range(2):
                        V(lambda g=g, y_=y_: nc.vector.tensor_tensor(out=y_[:, g * 512:(g + 1) * 512], in0=tmp[:, g * 512:(g + 1) * 512], in1=psY[g][:], op=ALU.add),
                          r=[B("s_tmp")], w=[B(f"psY{g}"), yb])
                    s.dma("pool", k.YF[t0:t0 + 128, :], y_[:], r=[yb], w=[B("YF")])
                else:
                    for g in range(2):
                        V(lambda g=g: nc.vector.tensor_tensor(out=yt[:, g * 512:(g + 1) * 512], in0=tmp[:, g * 512:(g + 1) * 512], in1=psY[g][:], op=ALU.add),
                          r=[B("s_tmp")], w=[B(f"psY{g}"), B("s_y")])
                    G(lambda yf_=yf_: nc.gpsimd.tensor_tensor(out=yt[:], in0=yt[:], in1=yf_[:], op=ALU.add), r=[yfb], w=[B("s_y")])
                for g in range(2):
                    PE(lambda g=g: nc.tensor.matmul(psOS[g][:], lhsT=Btok[:, g * 128:(g + 1) * 128], rhs=xw[:, g * 512:(g + 1) * 512], start=True, stop=True),
                       r=[B("s_Btok"), B("s_xw")], w=[B(f"psOS{g}")])
                G(lambda d=d, c=c: nc.gpsimd.tensor_tensor(out=hst[d][:].rearrange("p (h e) -> p h e", e=64), in0=hst[d][:].rearrange("p (h e) -> p h e", e=64),
                                                       in1=cdall[:, c, d * 16:d * 16 + 16].unsqueeze(2).to_broadcast([128, 16, 64]), op=ALU.mult), r=[B("s_cd")], w=[hb_])
                for g in range(2):
                    V(lambda g=g, d=d: nc.vector.tensor_tensor(out=hst[d][:, g * 512:(g + 1) * 512], in0=hst[d][:, g * 512:(g + 1) * 512], in1=psOS[g][:], op=ALU.add),
                      r=[], w=[B(f"psOS{g}"), hb_])
                A(lambda d=d: nc.scalar.copy(out=hbf[d][:], in_=hst[d][:]), r=[hb_], w=[hbb])
                if d == 1 and (need_ctx or c >= 2):
                    A(lambda z_=z_: nc.scalar.activation(out=sz[:], in_=z_[:], func=AF.Silu), r=[zb], w=[B("s_sz")])
                    V(lambda: nc.vector.tensor_tensor(out=yt[:], in0=yt[:], in1=sz[:], op=ALU.mult), r=[B("s_sz")], w=[B("s_y")])
                    for g in range(2):
                        A(lambda g=g: nc.scalar.activation(out=junk[:], in_=yt[:, g * 512:(g + 1) * 512], func=AF.Square, accum_out=st2[:, g:g + 1]), r=[B("s_y")], w=[B("s_junk"), B("s_st2")])
                    V(lambda: nc.vector.tensor_scalar(out=st2[:, 0:2], in0=st2[:, 0:2], scalar1=1.0 / 512, scalar2=EPS, op0=ALU.mult, op1=ALU.add), w=[B("s_st2")])
                    A(lambda: nc.scalar.activation(out=st2[:, 0:2], in_=st2[:, 0:2], func=AF.Sqrt), w=[B("s_st2")])
                    V(lambda: nc.vector.reciprocal(out=st2[:, 2:4], in_=st2[:, 0:2]), w=[B("s_st2")])
                    for g in range(2):
                        V(lambda g=g: nc.vector.scalar_tensor_tensor(out=mo[:, g * 512:(g + 1) * 512], in0=yt[:, g * 512:(g + 1) * 512], scalar=st2[:, 2 + g:3 + g], in1=gnb[:, g * 512:(g + 1) * 512],
                                                                   op0=ALU.mult, op1=ALU.mult), r=[B("s_y"), B("s_st2"), B("s_gnb")], w=[B("s_mo")])
                    for t in range(8):
                        PE(lambda t=t: nc.tensor.transpose(out=psX[:, t * 128:(t + 1) * 128], in_=mo[:, t * 128:(t + 1) * 128], identity=k.identb[:]), r=[B("s_mo"), B("identb")], w=[B("psX")])
                    m_ = mT[n_ % 2]; mb_ = B(f"s_mT{n_ % 2}")
                    A(lambda m_=m_: nc.scalar.copy(out=m_[:], in_=psX[:].rearrange("p (t j) -> p t j", j=128)), r=[], w=[B("psX"), mb_])
                    s.dma("pool", k.MT[0:1024, t0:t0 + 128].rearrange("(t p) j -> p t j", p=128), m_[:], r=[mb_], w=[B("MT")])
        s.barrier()

T = 4352; D = 1024; EPS = 1e-6; DEPTH = 4
C_PIDX = 139
I32 = mybir.dt.int32

def gen_dft(k):
    nc, s, B = k.nc, k.s, k.B
    cst = k.cst
    with ExitStack() as st:
        T_ = lambda name, shape, dt: st.enter_context(nc.sbuf_tensor(name + k.sfx, shape, dt))
        kio_i = T_("kio_i", [128, 4096], I32)
        kio = T_("kio", [128, 4096], F32)
        lcol = T_("lcol", [128, 32], F32)
        pi_ = [T_(f"pi{i}", [128, 4096], I32) for i in range(2)]
        tb = [T_(f"tb{i}", [128, 4096], BF16) for i in range(4)]
        s.op("pool", lambda: nc.gpsimd.iota(kio_i[:], pattern=[[1, 4096]], base=0, channel_multiplier=0), w=[B("kio_i")])
        s.op("dve", lambda: nc.vector.tensor_copy(out=kio[:], in_=kio_i[:]), r=[B("kio_i")], w=[B("kio")])
        for lt in range(32):
            s.op("dve", lambda lt=lt: nc.vector.tensor_scalar(out=lcol[:, lt:lt + 1], in0=cst[:, C_PIDX:C_PIDX + 1], scalar1=float(lt * 128), scalar2=None, op0=ALU.add), r=[B("cst")], w=[B("lcol")])
        sc = 2.0 * math.pi / 4096.0
        for lt in range(32):
            for j, (off, dst) in enumerate([(0.0, k.SLN), (3072.0, k.CL)]):
                e = "dve" if j == 0 else "pool"
                eng = nc.vector if j == 0 else nc.gpsimd
                p_ = pi_[j]; pb = B(f"pi{j}")
                s.op(e, lambda eng=eng, p_=p_, lt=lt, off=off: eng.tensor_scalar(out=p_[:], in0=kio[:], scalar1=lcol[:, lt:lt + 1], scalar2=off, op0=ALU.mult, op1=ALU.add), r=[B("kio"), B("lcol")], w=[pb])
                s.op("dve", lambda p_=p_: nc.vector.tensor_single_scalar(out=p_[:], in_=p_[:], scalar=4095, op=ALU.bitwise_and), w=[pb])
                t_ = tb[(lt % 2) * 2 + j]; tbb = B(f"tb{(lt % 2) * 2 + j}")
                s.op("act", lambda t_=t_, p_=p_: nc.scalar.activation(out=t_[:], in_=p_[:], func=AF.Sin, scale=sc, bias=k.negpi[:, 0:1]), r=[pb, B("negpi")], w=[tbb])
                s.dma("sp", dst[lt * 128:(lt + 1) * 128, :], t_[:], r=[tbb], w=[B("DFT")])
        s.barrier()


def phase_fnet(k, l, need_ctx=True):
    nc, s, I, B = k.nc, k.s, k.I, k.B
    with ExitStack() as st:
        T_ = lambda name, shape, dt: st.enter_context(nc.sbuf_tensor(name + k.sfx, shape, dt))
        P_ = lambda name, shape, dt: st.enter_context(nc.psum_tensor(name + k.sfx, shape, dt))
        cs = T_("f_cs", [128, 256], BF16)
        fw32 = T_("f_fw32", [128, 4, 512], F32)
        fw = T_("f_fw", [128, 4, 512], BF16)
        fb = T_("f_fb", [128, 4], F32)
        fuT = [T_(f"f_fuT{i}", [128, 4, 128], BF16) for i in range(2)]
        PQ = T_("f_PQ", [128, 34, 4, 2, 128], BF16)
        tabs = [T_(f"f_tab{i}", [128, 2, 512], BF16) for i in range(4)]
        specT = [T_(f"f_spec{i}", [128, 4, 512], BF16) for i in range(2)]
        gt = [T_(f"f_g{i}", [128, 512], F32) for i in range(2)]
        ob = [T_(f"f_ob{i}", [128, 512], BF16) for i in range(2)]
        psPQ = [P_(f"psPQ{i}", [128, 512], F32) for i in range(2)]
        psS = [P_(f"psS{i}", [128, 512], F32) for i in range(4)]
        psM = [P_(f"psMx{i}", [128, 512], F32) for i in range(2)]
        rows32 = lambda tsr: bass.AP(tsr.tensor, tsr.offset, [[32 * 4096, 128], [1, 128]])
        s.dma("sp", cs[:, 0:128], rows32(k.CL), r=[B("DFT")], w=[B("f_cs")])
        s.dma("sp", cs[:, 128:256], rows32(k.SLN), r=[B("DFT")], w=[B("f_cs")])
        s.dma("sp", fw32[:], I["fnet_w"][l].rearrange("(t p) c -> p t c", p=128), w=[B("f_fw32")])
        s.dma("sp", fb[:], I["fnet_b"][l].rearrange("(t p) -> p t", p=128), w=[B("f_fb")], allow_slow_non_contiguous=True)
        s.op("dve", lambda: nc.vector.tensor_copy(out=fw[:], in_=fw32[:]), r=[B("f_fw32")], w=[B("f_fw")])
        for tt in range(34):
            if tt < 2 and not need_ctx:
                continue
            f_ = fuT[tt % 2]; fb_ = B(f"f_fuT{tt % 2}")
            s.dma("sp", f_[:], k.PB[2048:2560, tt * 128:(tt + 1) * 128].rearrange("(h p) j -> p h j", p=128), r=[B("PB")], w=[fb_])
            for hd in range(4):
                pp = psPQ[hd // 2]; ppb = B(f"psPQ{hd // 2}")
                s.op("pe", lambda pp=pp, hd=hd, f_=f_: nc.tensor.matmul(pp[:, (hd % 2) * 256:(hd % 2) * 256 + 256], lhsT=f_[:, hd, :], rhs=cs[:], start=True, stop=True), r=[fb_, B("f_cs")], w=[ppb])
            for hf in range(2):
                pp = psPQ[hf]; ppb = B(f"psPQ{hf}")
                src = pp[:].rearrange("p (h q m) -> p h q m", h=2, q=2)
                s.op("act", lambda tt=tt, hf=hf, src=src: nc.scalar.copy(out=PQ[:, tt, hf * 2:hf * 2 + 2, 0, :], in_=src[:, :, 0, :]), w=[ppb, B(f"f_PQ{tt}")])
                s.op("dve", lambda tt=tt, hf=hf, src=src: nc.vector.tensor_scalar(out=PQ[:, tt, hf * 2:hf * 2 + 2, 1, :], in0=src[:, :, 1, :], scalar1=-1.0, scalar2=None, op0=ALU.mult), w=[ppb, B(f"f_PQ{tt}")])
        nload = 0
        jobs = []
        if need_ctx:
            jobs.append(("c", 0))
        jobs += [("l", kt) for kt in range(8)]
        for ji, (kind, kt) in enumerate(jobs):
            if kind == "c":
                nlt = 2; ncol = 256; tt0 = 0; tok0 = 0; nrm = 1.0 / math.sqrt(256.0 * 128.0)
            else:
                nlt = 32; ncol = 512; tt0 = 2; tok0 = 256 + kt * 512; nrm = 1.0 / math.sqrt(4096.0 * 128.0)
            for lt in range(nlt):
                tab = tabs[nload % 4]; tabb = B(f"f_tab{nload % 4}")
                for j, tsr in enumerate([k.CL, k.SLN]):
                    if kind == "c":
                        src = bass.AP(tsr.tensor, tsr.offset + (lt * 128) * 16 * 4096, [[16 * 4096, 128], [1, 256]])
                        s.dma("sp", tab[:, j, 0:256], src, r=[B("DFT")], w=[tabb], allow_slow_non_contiguous=True)
                    else:
                        s.dma("sp", tab[:, j, :], tsr[lt * 128:(lt + 1) * 128, kt * 512:(kt + 1) * 512], r=[B("DFT")], w=[tabb])
                nload += 1
                for hd in range(4):
                    s.op("pe", lambda hd=hd, lt=lt, tab=tab, ncol=ncol, tt0=tt0: nc.tensor.matmul(psS[hd][:, 0:ncol], lhsT=PQ[:, tt0 + lt, hd, 0, :], rhs=tab[:, 0, 0:ncol], start=(lt == 0), stop=False),
                         r=[B(f"f_PQ{tt0 + lt}"), tabb], w=[B(f"psS{hd}")])
                    s.op("pe", lambda hd=hd, lt=lt, tab=tab, ncol=ncol, tt0=tt0, nlt=nlt: nc.tensor.matmul(psS[hd][:, 0:ncol], lhsT=PQ[:, tt0 + lt, hd, 1, :], rhs=tab[:, 1, 0:ncol], start=False, stop=(lt == nlt - 1)),
                         r=[B(f"f_PQ{tt0 + lt}"), tabb], w=[B(f"psS{hd}")])
            sp_ = specT[ji % 2]; spb = B(f"f_spec{ji % 2}")
            for hd in range(4):
                if hd % 2 == 0:
                    s.op("act", lambda hd=hd, sp_=sp_, ncol=ncol, nrm=nrm: nc.scalar.mul(out=sp_[:, hd, 0:ncol], in_=psS[hd][:, 0:ncol], mul=nrm), w=[B(f"psS{hd}"), spb])
                else:
                    s.op("dve", lambda hd=hd, sp_=sp_, ncol=ncol, nrm=nrm: nc.vector.tensor_scalar(out=sp_[:, hd, 0:ncol], in0=psS[hd][:, 0:ncol], scalar1=nrm, scalar2=None, op0=ALU.mult), w=[B(f"psS{hd}"), spb])
            for ct in range(4):
                pm = psM[ct % 2]; pmb = B(f"psMx{ct % 2}")
                g_ = gt[ct % 2]; gb = B(f"f_g{ct % 2}")
                s.dma("sp", g_[:, 0:ncol], k.PF[544 + ct * 128:544 + (ct + 1) * 128, tok0:tok0 + ncol], r=[B("PF")], w=[gb])
                for hd in range(4):
                    s.op("pe", lambda pm=pm, hd=hd, ct=ct, sp_=sp_, ncol=ncol: nc.tensor.matmul(pm[:, 0:ncol], lhsT=fw[:, hd, ct * 128:(ct + 1) * 128], rhs=sp_[:, hd, 0:ncol], start=(hd == 0), stop=(hd == 3)),
                         r=[B("f_fw"), spb], w=[pmb])
                s.op("act", lambda g_=g_, ncol=ncol: nc.scalar.activation(out=g_[:, 0:ncol], in_=g_[:, 0:ncol], func=AF.Silu), w=[gb])
                o_ = ob[ct % 2]; obb = B(f"f_ob{ct % 2}")
                s.op("dve", lambda o_=o_, pm=pm, ct=ct, g_=g_, ncol=ncol: nc.vector.scalar_tensor_tensor(out=o_[:, 0:ncol], in0=pm[:, 0:ncol], scalar=fb[:, ct:ct + 1], in1=g_[:, 0:ncol], op0=ALU.add, op1=ALU.mult),
                     r=[gb, B("f_fb")], w=[pmb, obb])
                s.dma("pool", k.MT[1536 + ct * 128:1536 + (ct + 1) * 128, tok0:tok0 + ncol], o_[:, 0:ncol], r=[obb], w=[B("MT")])
        s.barrier()


def phase_out(k, l, last=False):
    nc, s, I, B = k.nc, k.s, k.I, k.B
    with ExitStack() as st:
        T_ = lambda name, shape, dt: st.enter_context(nc.sbuf_tensor(name + k.sfx, shape, dt))
        P_ = lambda name, shape, dt: st.enter_context(nc.psum_tensor(name + k.sfx, shape, dt))
        wo = T_("o_wo", [128, 16, D], BF16)
        wst = [T_(f"o_wst{i}", [128, 2, D], F32) for i in range(2)]
        bc = T_("o_bc", [128, 2, D], F32)
        mT = [T_(f"o_mT{i}", [128, 16, 128], BF16) for i in range(2)]
        xt = [T_(f"o_x{i}", [128, D], F32) for i in range(2)]
        ot = [T_(f"o_o{i}", [128, D], F32) for i in range(2)]
        junk = T_("o_junk", [128, 512], BF16)
        st4 = T_("o_st", [128, 4], F32)
        psO = [P_(f"psO{i}", [128, 512], F32) for i in range(4)]
        for c in range(8):
            w_ = wst[c % 2]; wb = B(f"o_wst{c % 2}")
            s.dma("sp", w_[:], I["w_out"][l, c * 256:(c + 1) * 256, :].rearrange("(t p) c -> p t c", p=128), w=[wb])
            if c % 2 == 0:
                s.op("act", lambda w_=w_, c=c: nc.scalar.copy(out=wo[:, 2 * c:2 * c + 2, :], in_=w_[:]), r=[wb], w=[B("o_wo")])
            else:
                s.op("pool", lambda w_=w_, c=c: nc.gpsimd.tensor_copy(out=wo[:, 2 * c:2 * c + 2, :], in_=w_[:]), r=[wb], w=[B("o_wo")])
        for j in range(2):
            s.dma("pool", bc[:, j, :], k.MODS[l, 1 - j, 2, :].partition_broadcast(128), r=[B("MODS")], w=[B("o_bc")])
        for n_, tt in enumerate(range(2 if last else 0, 34)):
            t0 = tt * 128
            m_ = mT[n_ % 2]; mb = B(f"o_mT{n_ % 2}")
            s.dma("sp", m_[:], k.MT[:, t0:t0 + 128].rearrange("(t p) j -> p t j", p=128), r=[B("MT")], w=[mb])
            x_ = xt[n_ % 2]; xb = B(f"o_x{n_ % 2}")
            if l == 0:
                src = I["ctx"][t0:t0 + 128, :] if tt < 2 else I["x"][t0 - 256:t0 - 128, :]
                s.dma("sp", x_[:], src, w=[xb])
            else:
                s.dma("sp", x_[:], k.XS[t0:t0 + 128, :], r=[B("XS")], w=[xb])
            for hf in range(2):
                po = psO[(n_ % 2) * 2 + hf]; pob = B(f"psO{(n_ % 2) * 2 + hf}")
                for ct in range(16):
                    s.op("pe", lambda po=po, m_=m_, ct=ct, hf=hf: nc.tensor.matmul(po[:], lhsT=m_[:, ct, :], rhs=wo[:, ct, hf * 512:(hf + 1) * 512], start=(ct == 0), stop=(ct == 15)),
                         r=[mb, B("o_wo")], w=[pob])
                s.op("act", lambda po=po, hf=hf: nc.scalar.activation(out=junk[:], in_=po[:], func=AF.Square, accum_out=st4[:, hf:hf + 1]), w=[pob, B("o_junk"), B("o_st")])
            s.op("dve", lambda: nc.vector.tensor_tensor(out=st4[:, 2:3], in0=st4[:, 0:1], in1=st4[:, 1:2], op=ALU.add), w=[B("o_st")])
            s.op("dve", lambda: nc.vector.tensor_scalar(out=st4[:, 2:3], in0=st4[:, 2:3], scalar1=1.0 / D, scalar2=EPS, op0=ALU.mult, op1=ALU.add), w=[B("o_st")])
            s.op("act", lambda: nc.scalar.activation(out=st4[:, 2:3], in_=st4[:, 2:3], func=AF.Sqrt), w=[B("o_st")])
            s.op("dve", lambda: nc.vector.reciprocal(out=st4[:, 3:4], in_=st4[:, 2:3]), w=[B("o_st")])
            o_ = ot[n_ % 2]; ob = B(f"o_o{n_ % 2}")
            jb = 0 if tt < 2 else 1
            for hf in range(2):
                po = psO[(n_ % 2) * 2 + hf]; pob = B(f"psO{(n_ % 2) * 2 + hf}")
                s.op("dve", lambda po=po, hf=hf, o_=o_, jb=jb: nc.vector.scalar_tensor_tensor(out=o_[:, hf * 512:(hf + 1) * 512], in0=po[:], scalar=st4[:, 3:4], in1=bc[:, jb, hf * 512:(hf + 1) * 512],
                                                                                      op0=ALU.mult, op1=ALU.mult), r=[B("o_st"), B("o_bc")], w=[pob, ob])
            s.op("pool", lambda o_=o_, x_=x_: nc.gpsimd.tensor_tensor(out=o_[:], in0=o_[:], in1=x_[:], op=ALU.add), r=[xb], w=[ob])
            if last:
                s.dma("pool", k.out[t0 - 256:t0 - 128, :], o_[:], r=[ob], w=[B("OUT")])
            else:
                s.dma("pool", k.XS[t0:t0 + 128, :], o_[:], r=[ob], w=[B("XS")])
        s.barrier()

T = 4352; D = 1024; NB = 544
C_MVEC = 128; C_BM8 = 524; C_IDXF = 1024; C_IDXB = 1600
I32 = mybir.dt.int32
TWO_PI = 2.0 * math.pi

def phase_s5(k, l, need_ctx=True):
    nc, s, I, B = k.nc, k.s, k.I, k.B
    cst = k.cst
    V = lambda fn, r=(), w=(): s.op("dve", fn, r=r, w=w)
    A = lambda fn, r=(), w=(): s.op("act", fn, r=r, w=w)
    G = lambda fn, r=(), w=(): s.op("pool", fn, r=r, w=w)
    PE = lambda fn, r=(), w=(): s.op("pe", fn, r=r, w=w)
    with ExitStack() as st:
        T_ = lambda name, shape, dt: st.enter_context(nc.sbuf_tensor(name + k.sfx, shape, dt))
        WS = T_("q_WS", [128, 2, 4, 8, 2, 128], BF16)
        WR = T_("q_WR", [128, 2, 16, 8, 2, 32], BF16)
        KD = T_("q_KD", [128, 2, 4, 8, 128], BF16)
        rho = T_("q_rho", [128, 2, 16], F32)
        th = T_("q_th", [128, 2, 16], F32)
        with ExitStack() as ps:
            TP = lambda name, shape, dt: ps.enter_context(nc.sbuf_tensor(name + k.sfx, shape, dt))
            PP = lambda name, shape, dt: ps.enter_context(nc.psum_tensor(name + k.sfx, shape, dt))
            lam16 = TP("p_lam16", [16, 2, 128], F32)
            lr = TP("p_lr", [128, 16], F32); li = TP("p_li", [128, 16], F32)
            stp = TP("p_stp", [128, 16], F32)
            lrs = TP("p_lrs", [128, 16], F32); lis = TP("p_lis", [128, 16], F32)
            a9 = TP("p_a9", [128, 16, 9], F32); a9b = TP("p_a9b", [128, 16, 9], F32)
            ki = TP("p_ki", [128, 16, 9], I32)
            mag9 = TP("p_mag9", [128, 16, 9], F32)
            Ar = TP("p_Ar", [128, 16, 9], F32); Ai = TP("p_Ai", [128, 16, 9], F32)
            t16 = [TP(f"p_t16_{i}", [128, 16], F32) for i in range(6)]
            Br = TP("p_Br", [128, 16, 16], F32); Bi = TP("p_Bi", [128, 16, 16], F32)
            Bbr = TP("p_Bbr", [128, 16, 16], F32); Bbi = TP("p_Bbi", [128, 16, 16], F32)
            tB = TP("p_tB", [128, 16, 16], F32)
            Cn = [TP(f"p_Cn{i}", [128, 128], F32) for i in range(2)]
            Cr = TP("p_Cr", [128, 16, 16], F32); Ci = TP("p_Ci", [128, 16, 16], F32)
            Zr = TP("p_Zr", [128, 16, 9, 16], F32); Zi = TP("p_Zi", [128, 16, 9, 16], F32)
            tZ = TP("p_tZ", [128, 16, 9, 16], F32)
            Pr = TP("p_Pr", [128, 16, 8, 16], F32); Pi = TP("p_Pi", [128, 16, 8, 16], F32)
            BPr = TP("p_BPr", [128, 16, 128], F32); BPi = TP("p_BPi", [128, 16, 128], F32)
            PPd = [TP(f"p_PPd{i}", [128, 16, 128], BF16) for i in range(2)]
            kdc = TP("p_kdc", [128, 8, 16], F32)
            psT1 = PP("psT1", [128, 128], F32)
            psK = PP("psK", [128, 128], F32)
            psW = [PP(f"psWs{i}", [128, 128], F32) for i in range(2)]
            G(lambda: nc.gpsimd.memset(WR[:], 0.0), w=[B("q_WR")])
            for d in range(2):
                s.dma("sp", lam16[:, 0, :], I["s5_lambda_re"][l, d].rearrange("(a b) n -> a (b n)", b=2), w=[B("p_lam16")])
                s.dma("sp", lam16[:, 1, :], I["s5_lambda_im"][l, d].rearrange("(a b) n -> a (b n)", b=2), w=[B("p_lam16")])
                for j, dst in enumerate([lr, li]):
                    PE(lambda j=j: nc.tensor.transpose(out=psT1[:, 0:16], in_=lam16[:, j, :], identity=cst[0:16, 0:16]), r=[B("p_lam16"), B("cst")], w=[B("psT1")])
                    V(lambda dst=dst: nc.vector.tensor_copy(out=dst[:], in_=psT1[:, 0:16]), w=[B("psT1"), B("p_l")])
                ls = I["s5_log_step"]
                for g2 in range(2):
                    src = bass.AP(ls.tensor, ls.offset + (l * 2 + d) * 32 + g2, [[0, 64], [2, 16]])
                    s.dma("sp", stp[64 * g2:64 * g2 + 64, :], src, w=[B("p_stp")], allow_slow_non_contiguous=True)
                A(lambda: nc.scalar.activation(out=stp[:], in_=stp[:], func=AF.Exp), w=[B("p_stp")])
                V(lambda: nc.vector.tensor_tensor(out=lrs[:], in0=lr[:], in1=stp[:], op=ALU.mult), r=[B("p_l"), B("p_stp")], w=[B("p_ls")])
                V(lambda: nc.vector.tensor_tensor(out=lis[:], in0=li[:], in1=stp[:], op=ALU.mult), r=[B("p_l"), B("p_stp")], w=[B("p_ls")])
                mv = cst[:, C_MVEC:C_MVEC + 9].unsqueeze(1).to_broadcast([128, 16, 9])
                V(lambda: nc.vector.tensor_tensor(out=a9[:], in0=lrs[:].unsqueeze(2).to_broadcast([128, 16, 9]), in1=mv, op=ALU.mult), r=[B("p_ls"), B("cst")], w=[B("p_a9")])
                A(lambda: nc.scalar.activation(out=mag9[:], in_=a9[:], func=AF.Exp), r=[B("p_a9")], w=[B("p_mag9")])
                V(lambda: nc.vector.tensor_tensor(out=a9[:], in0=lis[:].unsqueeze(2).to_broadcast([128, 16, 9]), in1=mv, op=ALU.mult), r=[B("p_ls"), B("cst")], w=[B("p_a9")])
                def reduce_sin(dst, src_ap, shift, shape3):
                    V(lambda: nc.vector.tensor_scalar(out=a9b[:], in0=src_ap, scalar1=shift, scalar2=None, op0=ALU.add), r=[B("p_a9")], w=[B("p_a9b")])
                    V(lambda: nc.vector.tensor_scalar(out=ki[:], in0=a9b[:], scalar1=1.0 / TWO_PI, scalar2=None, op0=ALU.mult), r=[B("p_a9b")], w=[B("p_ki")])
                    V(lambda: nc.vector.scalar_tensor_tensor(out=a9b[:], in0=ki[:], scalar=-TWO_PI, in1=a9b[:], op0=ALU.mult, op1=ALU.add), r=[B("p_ki")], w=[B("p_a9b")])
                    A(lambda: nc.scalar.activation(out=dst[:], in_=a9b[:], func=AF.Sin), r=[B("p_a9b")], w=[B("p_sc")])
                reduce_sin(Ai, a9[:], 0.0, None)
                V(lambda d=d: nc.vector.tensor_copy(out=th[:, d, :], in_=a9b[:, :, 8]), r=[B("p_a9b")], w=[B("q_th")])
                reduce_sin(Ar, a9[:], math.pi / 2.0, None)
                V(lambda: nc.vector.tensor_tensor(out=Ar[:], in0=Ar[:], in1=mag9[:], op=ALU.mult), r=[B("p_mag9")], w=[B("p_sc")])
                V(lambda: nc.vector.tensor_tensor(out=Ai[:], in0=Ai[:], in1=mag9[:], op=ALU.mult), r=[B("p_mag9")], w=[B("p_sc")])
                V(lambda d=d: nc.vector.tensor_copy(out=rho[:, d, :], in_=mag9[:, :, 8]), r=[B("p_mag9")], w=[B("q_rho")])
                am1, den, fr, fi, u1, u2 = t16
                V(lambda: nc.vector.tensor_scalar(out=am1[:], in0=Ar[:, :, 1], scalar1=-1.0, scalar2=None, op0=ALU.add), r=[B("p_sc")], w=[B("p_t16")])
                V(lambda: nc.vector.tensor_tensor(out=den[:], in0=lr[:], in1=lr[:], op=ALU.mult), r=[B("p_l")], w=[B("p_t16")])
                V(lambda: nc.vector.tensor_tensor(out=u1[:], in0=li[:], in1=li[:], op=ALU.mult), r=[B("p_l")], w=[B("p_t16")])
                V(lambda: nc.vector.tensor_tensor(out=den[:], in0=den[:], in1=u1[:], op=ALU.add), w=[B("p_t16")])
                V(lambda: nc.vector.reciprocal(out=den[:], in_=den[:]), w=[B("p_t16")])
                V(lambda: nc.vector.tensor_tensor(out=u1[:], in0=am1[:], in1=lr[:], op=ALU.mult), w=[B("p_t16")])
                V(lambda: nc.vector.tensor_tensor(out=u2[:], in0=Ai[:, :, 1], in1=li[:], op=ALU.mult), w=[B("p_t16")])
                V(lambda: nc.vector.tensor_tensor(out=fr[:], in0=u1[:], in1=u2[:], op=ALU.add), w=[B("p_t16")])
                V(lambda: nc.vector.tensor_tensor(out=fr[:], in0=fr[:], in1=den[:], op=ALU.mult), w=[B("p_t16")])
                V(lambda: nc.vector.tensor_tensor(out=u1[:], in0=Ai[:, :, 1], in1=lr[:], op=ALU.mult), w=[B("p_t16")])
                V(lambda: nc.vector.tensor_tensor(out=u2[:], in0=am1[:], in1=li[:], op=ALU.mult), w=[B("p_t16")])
                V(lambda: nc.vector.tensor_tensor(out=fi[:], in0=u1[:], in1=u2[:], op=ALU.subtract), w=[B("p_t16")])
                V(lambda: nc.vector.tensor_tensor(out=fi[:], in0=fi[:], in1=den[:], op=ALU.mult), w=[B("p_t16")])
                for j, (dst, nm) in enumerate([(Br, "s5_b_re"), (Bi, "s5_b_im")]):
                    bt = I[nm]
                    for g2 in range(2):
                        src = bass.AP(bt.tensor, bt.offset + ((l * 2 + d) * 32 + g2) * 1024, [[16, 64], [2048, 16], [1, 16]])
                        s.dma("sp", dst[64 * g2:64 * g2 + 64, :, :], src, w=[B("p_B")])
                frb = fr[:].unsqueeze(2).to_broadcast([128, 16, 16]); fib = fi[:].unsqueeze(2).to_broadcast([128, 16, 16])
                V(lambda: nc.vector.tensor_tensor(out=Bbr[:], in0=Br[:], in1=frb, op=ALU.mult), r=[B("p_B"), B("p_t16")], w=[B("p_Bb")])
                V(lambda: nc.vector.tensor_tensor(out=tB[:], in0=Bi[:], in1=fib, op=ALU.mult), r=[B("p_B"), B("p_t16")], w=[B("p_tB")])
                V(lambda: nc.vector.tensor_tensor(out=Bbr[:], in0=Bbr[:], in1=tB[:], op=ALU.subtract), r=[B("p_tB")], w=[B("p_Bb")])
                V(lambda: nc.vector.tensor_tensor(out=Bbi[:], in0=Bi[:], in1=frb, op=ALU.mult), r=[B("p_B"), B("p_t16")], w=[B("p_Bb")])
                V(lambda: nc.vector.tensor_tensor(out=tB[:], in0=Br[:], in1=fib, op=ALU.mult), r=[B("p_B"), B("p_t16")], w=[B("p_tB")])
                V(lambda: nc.vector.tensor_tensor(out=Bbi[:], in0=Bbi[:], in1=tB[:], op=ALU.add), r=[B("p_tB")], w=[B("p_Bb")])
                for j, (dst, nm) in enumerate([(Cr, "s5_c_re"), (Ci, "s5_c_im")]):
                    ct_ = I[nm]
                    for pset in range(2):
                        cn = Cn[pset]; cnb = B(f"p_Cn{pset}")
                        for pr in range(8):
                            g0 = 2 * (8 * pset + pr)
                            src = bass.AP(ct_.tensor, ct_.offset + ((l * 2 + d) * 32 + g0) * 1024, [[64, 16], [1024, 2], [1, 64]])
                            s.dma("sp", cn[16 * pr:16 * pr + 16, :].rearrange("p (a n) -> p a n", a=2), src, w=[cnb])
                        PE(lambda cn=cn: nc.tensor.transpose(out=psT1[:], in_=cn[:], identity=cst[:, 0:128]), r=[cnb, B("cst")], w=[B("psT1")])
                        V(lambda dst=dst, pset=pset: nc.vector.tensor_copy(out=dst[:, 8 * pset:8 * pset + 8, :], in_=psT1[:].rearrange("p (a k) -> p a k", k=16)), w=[B("psT1"), B("p_C")])
                Crb = lambda X: X[:].unsqueeze(2).to_broadcast([128, 16, 9, 16])
                Ab = lambda X: X[:].unsqueeze(3).to_broadcast([128, 16, 9, 16])
                V(lambda: nc.vector.tensor_tensor(out=Zr[:], in0=Crb(Cr), in1=Ab(Ar), op=ALU.mult), r=[B("p_C"), B("p_sc")], w=[B("p_Z")])
                G(lambda: nc.gpsimd.tensor_tensor(out=tZ[:], in0=Crb(Ci), in1=Ab(Ai), op=ALU.mult), r=[B("p_C"), B("p_sc")], w=[B("p_tZ")])
                V(lambda: nc.vector.tensor_tensor(out=Zr[:], in0=Zr[:], in1=tZ[:], op=ALU.subtract), r=[B("p_tZ")], w=[B("p_Z")])
                V(lambda: nc.vector.tensor_tensor(out=Zi[:], in0=Crb(Cr), in1=Ab(Ai), op=ALU.mult), r=[B("p_C"), B("p_sc")], w=[B("p_Z")])
                G(lambda: nc.gpsimd.tensor_tensor(out=tZ[:], in0=Crb(Ci), in1=Ab(Ar), op=ALU.mult), r=[B("p_C"), B("p_sc")], w=[B("p_tZ")])
                V(lambda: nc.vector.tensor_tensor(out=Zi[:], in0=Zi[:], in1=tZ[:], op=ALU.add), r=[B("p_tZ")], w=[B("p_Z")])
                for g2 in range(2):
                    sl = slice(64 * g2, 64 * g2 + 64)
                    V(lambda sl=sl, g2=g2, d=d: nc.vector.tensor_copy(out=WR[sl, d, :, :, 0, 16 * g2:16 * g2 + 16], in_=Zr[sl, :, 1:9, :]), r=[B("p_Z")], w=[B("q_WR")])
                    V(lambda sl=sl, g2=g2, d=d: nc.vector.tensor_scalar(out=WR[sl, d, :, :, 1, 16 * g2:16 * g2 + 16], in0=Zi[sl, :, 1:9, :], scalar1=-1.0, scalar2=None, op0=ALU.mult), r=[B("p_Z")], w=[B("q_WR")])
                Bb8 = lambda X: X[:].unsqueeze(2).to_broadcast([128, 16, 8, 16])
                A8 = lambda X: X[:, :, 0:8].unsqueeze(3).to_broadcast([128, 16, 8, 16])
                tP = tZ[:, :, 0:8, :]
                V(lambda: nc.vector.tensor_tensor(out=Pr[:], in0=Bb8(Bbr), in1=A8(Ar), op=ALU.mult), r=[B("p_Bb"), B("p_sc")], w=[B("p_P")])
                G(lambda: nc.gpsimd.tensor_tensor(out=tP, in0=Bb8(Bbi), in1=A8(Ai), op=ALU.mult), r=[B("p_Bb"), B("p_sc")], w=[B("p_tZ")])
                V(lambda: nc.vector.tensor_tensor(out=Pr[:], in0=Pr[:], in1=tP, op=ALU.subtract), r=[B("p_tZ")], w=[B("p_P")])
                V(lambda: nc.vector.tensor_tensor(out=Pi[:], in0=Bb8(Bbi), in1=A8(Ar), op=ALU.mult), r=[B("p_Bb"), B("p_sc")], w=[B("p_P")])
                G(lambda: nc.gpsimd.tensor_tensor(out=tP, in0=Bb8(Bbr), in1=A8(Ai), op=ALU.mult), r=[B("p_Bb"), B("p_sc")], w=[B("p_tZ")])
                V(lambda: nc.vector.tensor_tensor(out=Pi[:], in0=Pi[:], in1=tP, op=ALU.add), r=[B("p_tZ")], w=[B("p_P")])
                G(lambda: nc.gpsimd.memset(BPr[:], 0.0), w=[B("p_BP")])
                G(lambda: nc.gpsimd.memset(BPi[:], 0.0), w=[B("p_BP")])
                for g2 in range(2):
                    sl = slice(64 * g2, 64 * g2 + 64)
                    for q in range(4):
                        c0 = 32 * q + 16 * g2
                        V(lambda sl=sl, q=q, c0=c0: nc.vector.tensor_copy(out=BPr[sl, q::4, c0:c0 + 16], in_=Bbr[sl, q::4, :]), r=[B("p_Bb")], w=[B("p_BP")])
                        V(lambda sl=sl, q=q, c0=c0: nc.vector.tensor_scalar(out=BPi[sl, q::4, c0:c0 + 16], in0=Bbi[sl, q::4, :], scalar1=-1.0, scalar2=None, op0=ALU.mult), r=[B("p_Bb")], w=[B("p_BP")])
                for t in range(4):
                    n_ = 0
                    for q in range(4):
                        pr = 4 * t + q
                        for (BP_, Z_) in ((BPr, Zr), (BPi, Zi)):
                            PE(lambda pr=pr, BP_=BP_, Z_=Z_, n_=n_: nc.tensor.matmul(psK[:], lhsT=BP_[:, pr, :], rhs=Z_[:, pr, 0:8, :].rearrange("p a k -> p (a k)"), start=(n_ == 0), stop=(n_ == 7)),
                               r=[B("p_BP"), B("p_Z")], w=[B("psK")])
                            n_ += 1
                    V(lambda: nc.vector.tensor_copy(out=kdc[:], in_=psK[:].rearrange("p (a k) -> p a k", k=16)), w=[B("psK"), B("p_kdc")])
                    V(lambda t=t, d=d: nc.vector.tensor_tensor(out=KD[:, d, t, :, :].rearrange("p a (g k) -> p a g k", k=16), in0=kdc[:].unsqueeze(2).to_broadcast([128, 8, 8, 16]),
                                                          in1=cst[:, C_BM8:C_BM8 + 8].unsqueeze(1).unsqueeze(3).to_broadcast([128, 8, 8, 16]), op=ALU.mult), r=[B("p_kdc"), B("cst")], w=[B("q_KD")])
                n_w = 0
                for m in range(8):
                    for ri, P_ in enumerate((Pr, Pi)):
                        pd = PPd[n_w % 2]; pdb = B(f"p_PPd{n_w % 2}")
                        G(lambda pd=pd: nc.gpsimd.memset(pd[:], 0.0), w=[pdb])
                        for g2 in range(2):
                            sl = slice(64 * g2, 64 * g2 + 64)
                            for q in range(4):
                                c0 = 32 * q + 16 * g2
                                e = V if (q % 2 == 0) else G
                                eng = nc.vector if (q % 2 == 0) else nc.gpsimd
                                e(lambda sl=sl, q=q, c0=c0, pd=pd, P_=P_, m=m, eng=eng: eng.tensor_copy(out=pd[sl, q::4, c0:c0 + 16], in_=P_[sl, q::4, m, :]), r=[B("p_P")], w=[pdb])
                        for t in range(4):
                            pw = psW[t % 2]; pwb = B(f"psWs{t % 2}")
                            for q in range(4):
                                PE(lambda pw=pw, pd=pd, t=t, q=q: nc.tensor.matmul(pw[:], lhsT=pd[:, 4 * t + q, :], rhs=k.identb[:], start=(q == 0), stop=(q == 3)), r=[pdb, B("identb")], w=[pwb])
                            if t % 2 == 0:
                                A(lambda pw=pw, t=t, m=m, ri=ri, d=d: nc.scalar.copy(out=WS[:, d, t, m, ri, :], in_=pw[:]), w=[pwb, B("q_WS")])
                            else:
                                V(lambda pw=pw, t=t, m=m, ri=ri, d=d: nc.vector.tensor_copy(out=WS[:, d, t, m, ri, :], in_=pw[:]), w=[pwb, B("q_WS")])
                        n_w += 1
            s.barrier()
        if getattr(k, 'stop', None) == 's5prep':
            return
        phase_s5_main(k, l, need_ctx, WS, WR, KD, rho, th, st)
        s.barrier()


def phase_s5_main(k, l, need_ctx, WS, WR, KD, rho, th, st):
    nc, s, I, B = k.nc, k.s, k.I, k.B
    cst = k.cst
    V = lambda fn, r=(), w=(): s.op("dve", fn, r=r, w=w)
    A = lambda fn, r=(), w=(): s.op("act", fn, r=r, w=w)
    G = lambda fn, r=(), w=(): s.op("pool", fn, r=r, w=w)
    PE = lambda fn, r=(), w=(): s.op("pe", fn, r=r, w=w)
    T_ = lambda name, shape, dt: st.enter_context(nc.sbuf_tensor(name + k.sfx, shape, dt))
    P_ = lambda name, shape, dt: st.enter_context(nc.psum_tensor(name + k.sfx, shape, dt))
    uT = st.enter_context(nc.sbuf_tensor("q_uT" + k.sfx, [128, 4, T], BF16))
    mst = ExitStack()
    T_ = lambda name, shape, dt: mst.enter_context(nc.sbuf_tensor(name + k.sfx, shape, dt))
    hst = [T_(f"q_hst{i}", [128, 2, NB], BF16) for i in range(2)]
    Sr = T_("q_Sr", [128, NB], F32); Si = T_("q_Si", [128, NB], F32)
    cosT = T_("q_cos", [128, NB], F32); sinT = T_("q_sin", [128, NB], F32)
    ang = T_("q_ang", [128, NB], F32); kiT = T_("q_ki", [128, NB], I32)
    xr = T_("q_xr", [128, NB], F32); xi = T_("q_xi", [128, NB], F32)
    t1 = T_("q_t1", [128, NB], F32); t2 = T_("q_t2", [128, NB], F32)
    Gr = T_("q_Gr", [128, NB], F32); Gi = T_("q_Gi", [128, NB], F32)
    psS = [P_(f"psSq{i}", [128, 1024], F32) for i in range(2)]
    for t in range(4):
        s.dma("sp", uT[:, t, :], k.PB[1536 + t * 128:1536 + (t + 1) * 128, :], r=[B("PB")], w=[B("q_uT")])
    pieces = [(32, 288, 0), (288, 544, 256), (0, 32, 512)]
    for d in range(2):
        for pr in range(16):
            t = pr // 4; q = pr % 4
            rows = slice(32 * q, 32 * q + 32)
            for ri in range(2):
                for (b0, b1, pc) in pieces:
                    for pos in range(8):
                        m = 7 - pos if d == 0 else pos
                        rhs = uT[rows, t, 8 * b0 + pos:8 * b1:8]
                        PE(lambda ri=ri, pc=pc, b0=b0, b1=b1, m=m, rhs=rhs, pos=pos: nc.tensor.matmul(psS[ri][:, pc:pc + (b1 - b0)], lhsT=WS[rows, d, t, m, ri, :], rhs=rhs, start=(pos == 0), stop=(pos == 7),
                                                                                            tile_position=(32 * q, 0), skip_group_check=True),
                           r=[B("q_uT"), B("q_WS")], w=[B(f"psSq{ri}")])
            for ri, dst in enumerate((Sr, Si)):
                A(lambda ri=ri, dst=dst: nc.scalar.copy(out=dst[:, 32:544], in_=psS[ri][:, 0:512]), w=[B(f"psSq{ri}"), B("q_S")])
                A(lambda ri=ri, dst=dst: nc.scalar.copy(out=dst[:, 0:32], in_=psS[ri][:, 512:544]), w=[B(f"psSq{ri}"), B("q_S")])
            idx = cst[:, C_IDXF:C_IDXF + NB] if d == 0 else cst[:, C_IDXB:C_IDXB + NB]
            for (dst, shift) in ((sinT, 0.0), (cosT, math.pi / 2.0)):
                V(lambda shift=shift: nc.vector.tensor_scalar(out=ang[:], in0=idx, scalar1=th[:, d, pr:pr + 1], scalar2=shift, op0=ALU.mult, op1=ALU.add), r=[B("q_th"), B("cst")], w=[B("q_ang")])
                V(lambda: nc.vector.tensor_scalar(out=kiT[:], in0=ang[:], scalar1=1.0 / TWO_PI, scalar2=None, op0=ALU.mult), r=[B("q_ang")], w=[B("q_ki")])
                V(lambda: nc.vector.scalar_tensor_tensor(out=ang[:], in0=kiT[:], scalar=-TWO_PI, in1=ang[:], op0=ALU.mult, op1=ALU.add), r=[B("q_ki")], w=[B("q_ang")])
                A(lambda dst=dst: nc.scalar.activation(out=dst[:], in_=ang[:], func=AF.Sin), r=[B("q_ang")], w=[B("q_tw")])
            V(lambda: nc.vector.tensor_tensor(out=xr[:], in0=Sr[:], in1=cosT[:], op=ALU.mult), r=[B("q_S"), B("q_tw")], w=[B("q_xr")])
            G(lambda: nc.gpsimd.tensor_tensor(out=t1[:], in0=Si[:], in1=sinT[:], op=ALU.mult), r=[B("q_S"), B("q_tw")], w=[B("q_t1")])
            V(lambda: nc.vector.tensor_tensor(out=xr[:], in0=xr[:], in1=t1[:], op=ALU.add), r=[B("q_t1")], w=[B("q_xr")])
            G(lambda: nc.gpsimd.tensor_tensor(out=xi[:], in0=Si[:], in1=cosT[:], op=ALU.mult), r=[B("q_S"), B("q_tw")], w=[B("q_xi")])
            V(lambda: nc.vector.tensor_tensor(out=t2[:], in0=Sr[:], in1=sinT[:], op=ALU.mult), r=[B("q_S"), B("q_tw")], w=[B("q_t2")])
            G(lambda: nc.gpsimd.tensor_tensor(out=xi[:], in0=xi[:], in1=t2[:], op=ALU.subtract), r=[B("q_t2")], w=[B("q_xi")])
            rcol = rho[:, d, pr:pr + 1]
            for (src, dst, nm) in ((xr, Gr, "q_xr"), (xi, Gi, "q_xi")):
                if d == 0:
                    V(lambda src=src, dst=dst: nc.vector.tensor_tensor_scan(out=dst[:], data0=rcol.to_broadcast([128, NB]), data1=src[:], initial=0.0, op0=ALU.mult, op1=ALU.add),
                      r=[B(nm), B("q_rho")], w=[B("q_G")])
                else:
                    rv = lambda X, a, b: bass.AP(X[:].tensor, X[:, b - 1:b].offset, [list(X[:].ap[0]), [-1, b - a]])
                    V(lambda src=src, dst=dst: nc.vector.tensor_tensor_scan(out=rv(dst, 0, 32), data0=rcol.to_broadcast([128, 32]), data1=rv(src, 0, 32), initial=0.0, op0=ALU.mult, op1=ALU.add),
                      r=[B(nm), B("q_rho")], w=[B("q_G")])
                    V(lambda src=src, dst=dst: nc.vector.tensor_tensor_scan(out=rv(dst, 32, 544), data0=rcol.to_broadcast([128, 512]), data1=rv(src, 32, 544), initial=dst[:, 0:1], op0=ALU.mult, op1=ALU.add),
                      r=[B(nm), B("q_rho")], w=[B("q_G")])
            hs_ = hst[pr % 2]; hsb = B(f"q_hst{pr % 2}")
            if d == 0:
                G(lambda hs_=hs_: nc.gpsimd.memset(hs_[:, :, 0:1], 0.0), w=[hsb])
            if d == 0:
                so, si_ = slice(1, 544), slice(0, 543)
            else:
                so, si_ = slice(0, 543), slice(1, 544)
            V(lambda: nc.vector.tensor_tensor(out=t1[:], in0=Gr[:], in1=cosT[:], op=ALU.mult), r=[B("q_G"), B("q_tw")], w=[B("q_t1")])
            G(lambda: nc.gpsimd.tensor_tensor(out=t2[:], in0=Gi[:], in1=sinT[:], op=ALU.mult), r=[B("q_G"), B("q_tw")], w=[B("q_t2")])
            V(lambda: nc.vector.tensor_tensor(out=hs_[:, 0, so], in0=t1[:, si_], in1=t2[:, si_], op=ALU.subtract), r=[B("q_t1"), B("q_t2")], w=[hsb])
            if d == 1:
                V(lambda: nc.vector.tensor_tensor(out=hs_[:, 0, 543:544], in0=t1[:, 0:1], in1=t2[:, 0:1], op=ALU.subtract), r=[B("q_t1"), B("q_t2")], w=[hsb])
            G(lambda: nc.gpsimd.tensor_tensor(out=xr[:], in0=Gr[:], in1=sinT[:], op=ALU.mult), r=[B("q_G"), B("q_tw")], w=[B("q_xr")])
            V(lambda: nc.vector.tensor_tensor(out=xi[:], in0=Gi[:], in1=cosT[:], op=ALU.mult), r=[B("q_G"), B("q_tw")], w=[B("q_xi")])
            G(lambda: nc.gpsimd.tensor_tensor(out=hs_[:, 1, so], in0=xr[:, si_], in1=xi[:, si_], op=ALU.add), r=[B("q_xr"), B("q_xi")], w=[hsb])
            if d == 1:
                G(lambda: nc.gpsimd.tensor_tensor(out=hs_[:, 1, 543:544], in0=xr[:, 0:1], in1=xi[:, 0:1], op=ALU.add), r=[B("q_xr"), B("q_xi")], w=[hsb])
                G(lambda: nc.gpsimd.memset(hs_[:, :, 31:32], 0.0), w=[hsb])
            s.dma("sp", k.HD[d, pr].rearrange("r p c -> p r c"), hs_[:], r=[hsb], w=[B("HD")])
    s.barrier()
    mst.close()
    if getattr(k, 'stop', None) == 's5main':
        return
    s5_glu(k, l, need_ctx, WR, KD, uT, None, st)


def s5_glu(k, l, need_ctx, WR, KD, uT, Hin, st):
    nc, s, I, B = k.nc, k.s, k.I, k.B
    V = lambda fn, r=(), w=(): s.op("dve", fn, r=r, w=w)
    A = lambda fn, r=(), w=(): s.op("act", fn, r=r, w=w)
    G = lambda fn, r=(), w=(): s.op("pool", fn, r=r, w=w)
    PE = lambda fn, r=(), w=(): s.op("pe", fn, r=r, w=w)
    T_ = lambda name, shape, dt: st.enter_context(nc.sbuf_tensor(name + k.sfx, shape, dt))
    P_ = lambda name, shape, dt: st.enter_context(nc.psum_tensor(name + k.sfx, shape, dt))
    wg32 = T_("g_wg32", [128, 4, 512], F32); wg = T_("g_wg", [128, 4, 512], BF16)
    bg = T_("g_bg", [128, 4], F32); dsk = T_("g_dsk", [128, 4], F32)
    dgs = T_("g_dgs", [128, 4, 128], BF16)
    y32 = T_("g_y", [128, 512], F32); y2 = T_("g_y2", [128, 512], F32)
    vT = [T_(f"g_v{i}", [128, 4, 512], BF16) for i in range(2)]
    v32 = T_("g_v32", [128, 4, 512], F32)
    gs = [T_(f"g_gs{i}", [128, 512], F32) for i in range(2)]
    o1 = T_("g_o1", [128, 512], F32)
    ob = [T_(f"g_ob{i}", [128, 512], BF16) for i in range(2)]
    Hc = [T_(f"g_Hc{i}", [128, 64, 64], BF16) for i in range(2)]
    psY = [P_(f"psYq{i}", [128, 512], F32) for i in range(2)]
    psG = [P_(f"psGq{i}", [128, 512], F32) for i in range(2)]
    s.dma("sp", wg32[:], I["s5_w_glu"][l].rearrange("(t p) c -> p t c", p=128), w=[B("g_wg32")])
    s.dma("sp", bg[:], I["s5_b_glu"][l].rearrange("(t p) -> p t", p=128), w=[B("g_bg")], allow_slow_non_contiguous=True)
    s.dma("sp", dsk[:], I["s5_d"][l].rearrange("(t p) -> p t", p=128), w=[B("g_dsk")], allow_slow_non_contiguous=True)
    V(lambda: nc.vector.tensor_copy(out=wg[:], in_=wg32[:]), r=[B("g_wg32")], w=[B("g_wg")])
    for t in range(4):
        V(lambda t=t: nc.vector.tensor_scalar(out=dgs[:, t, :], in0=k.identb[:], scalar1=dsk[:, t:t + 1], scalar2=None, op0=ALU.mult), r=[B("g_dsk"), B("identb")], w=[B("g_dgs")])
    chunks = [(0, 256)] if need_ctx else []
    chunks += [(256 + i * 512, 512) for i in range(8)]
    for ci, (t0, ntok) in enumerate(chunks):
        nb = ntok // 8; b0 = t0 // 8
        v_ = vT[ci % 2]; vb = B(f"g_v{ci % 2}")
        hc_ = Hc[ci % 2]; hcb = B(f"g_Hc{ci % 2}")
        for dd in range(2):
            for qq in range(4):
                s.dma("sp", hc_[:, dd * 32 + qq * 8:dd * 32 + qq * 8 + 8, 0:nb], k.HD[dd, qq * 4:qq * 4 + 4, :, :, b0:b0 + nb].rearrange("q r p c -> p (q r) c"), r=[B("HD")], w=[hcb])
        for t in range(4):
            py = psY[t % 2]; pyb = B(f"psYq{t % 2}")
            PE(lambda py=py, t=t, t0=t0, ntok=ntok: nc.tensor.matmul(py[:, 0:ntok], lhsT=dgs[:, t, :], rhs=uT[:, t, t0:t0 + ntok], start=True, stop=False, skip_group_check=True),
               r=[B("g_dgs"), B("q_uT")], w=[pyb])
            for d in range(2):
                for o in range(8):
                    srcs = range(0, o + 1) if d == 0 else range(o, 8)
                    for o2 in srcs:
                        lag = abs(o - o2)
                        PE(lambda py=py, t=t, d=d, lag=lag, o=o, o2=o2, t0=t0, ntok=ntok: nc.tensor.matmul(py[:, o:ntok:8], lhsT=KD[:, d, t, lag, :], rhs=uT[:, t, t0 + o2:t0 + ntok:8], start=False, stop=False, skip_group_check=True),
                           r=[B("q_KD"), B("q_uT")], w=[pyb])
                for q in range(4):
                    pr = 4 * t + q
                    for o in range(8):
                        i_ = o if d == 0 else 7 - o
                        for ri in range(2):
                            last = (d == 1 and q == 3 and o == 7 and ri == 1)
                            PE(lambda py=py, d=d, pr=pr, i_=i_, ri=ri, o=o, q=q, b0=b0, nb=nb, ntok=ntok, last=last, hc_=hc_: nc.tensor.matmul(py[32 * q:32 * q + 32, o:ntok:8], lhsT=WR[:, d, pr, i_, ri, :], rhs=hc_[:, (d * 16 + pr) * 2 + ri, 0:nb],
                                                                                                              start=False, stop=last, tile_position=(0, 32 * q), skip_group_check=True),
                               r=[B("q_WR"), hcb], w=[pyb])
            A(lambda py=py, ntok=ntok: nc.scalar.copy(out=y32[:, 0:ntok], in_=py[:, 0:ntok]), w=[pyb, B("g_y")])
            G(lambda ntok=ntok: nc.gpsimd.tensor_tensor(out=y2[:, 0:ntok], in0=y32[:, 0:ntok], in1=y32[:, 0:ntok], op=ALU.mult), r=[B("g_y")], w=[B("g_y2")])
            V(lambda ntok=ntok: nc.vector.tensor_scalar(out=y2[:, 0:ntok], in0=y2[:, 0:ntok], scalar1=0.044715, scalar2=1.0, op0=ALU.mult, op1=ALU.add), w=[B("g_y2")])
            V(lambda ntok=ntok: nc.vector.tensor_tensor(out=y2[:, 0:ntok], in0=y2[:, 0:ntok], in1=y32[:, 0:ntok], op=ALU.mult), r=[B("g_y")], w=[B("g_y2")])
            A(lambda ntok=ntok: nc.scalar.activation(out=y2[:, 0:ntok], in_=y2[:, 0:ntok], func=AF.Sigmoid, scale=1.5957691216), w=[B("g_y2")])
            V(lambda ntok=ntok, t=t: nc.vector.tensor_tensor(out=v32[:, t, 0:ntok], in0=y2[:, 0:ntok], in1=y32[:, 0:ntok], op=ALU.mult), r=[B("g_y"), B("g_y2")], w=[B("g_v32")])
            G(lambda ntok=ntok, t=t, v_=v_: nc.gpsimd.tensor_copy(out=v_[:, t, 0:ntok], in_=v32[:, t, 0:ntok]), r=[B("g_v32")], w=[vb])
        for ct in range(4):
            pg = psG[ct % 2]; pgb = B(f"psGq{ct % 2}")
            g_ = gs[ct % 2]; gb = B(f"g_gs{ct % 2}")
            s.dma("sp", g_[:, 0:ntok], k.PF[32 + ct * 128:32 + (ct + 1) * 128, t0:t0 + ntok], r=[B("PF")], w=[gb])
            for t in range(4):
                PE(lambda pg=pg, t=t, ct=ct, v_=v_, ntok=ntok: nc.tensor.matmul(pg[:, 0:ntok], lhsT=wg[:, t, ct * 128:(ct + 1) * 128], rhs=v_[:, t, 0:ntok], start=(t == 0), stop=(t == 3)),
                   r=[B("g_wg"), vb], w=[pgb])
            A(lambda pg=pg, ct=ct, ntok=ntok: nc.scalar.activation(out=o1[:, 0:ntok], in_=pg[:, 0:ntok], func=AF.Sigmoid, bias=bg[:, ct:ct + 1]), r=[B("g_bg")], w=[pgb, B("g_o1")])
            V(lambda ct=ct, ntok=ntok: nc.vector.tensor_tensor(out=o1[:, 0:ntok], in0=o1[:, 0:ntok], in1=v32[:, ct, 0:ntok], op=ALU.mult), r=[B("g_v32")], w=[B("g_o1")])
            A(lambda g_=g_, ntok=ntok: nc.scalar.activation(out=g_[:, 0:ntok], in_=g_[:, 0:ntok], func=AF.Silu), w=[gb])
            o_ = ob[ct % 2]; obb = B(f"g_ob{ct % 2}")
            V(lambda o_=o_, g_=g_, ntok=ntok: nc.vector.tensor_tensor(out=o_[:, 0:ntok], in0=o1[:, 0:ntok], in1=g_[:, 0:ntok], op=ALU.mult), r=[gb, B("g_o1")], w=[obb])
            s.dma("pool", k.MT[1024 + ct * 128:1024 + (ct + 1) * 128, t0:t0 + ntok], o_[:, 0:ntok], r=[obb], w=[B("MT")])


D = 1024; T = 4352; TC = 256; TL = 4096; NT = 34; DEPTH = 4
DIN = 4640; EPS = 1e-6

class K:
    pass

def dram(nc, name, shape, dt, kind="Internal"):
    return nc.dram_tensor(name, list(shape), dt, kind=kind).ap()

def build(nlayers=DEPTH, stop=None, dbg=()):
    nc = bass.Bass("TRN2", target_bir_lowering=False)
    s = Sched(nc)
    k = K(); k.nc = nc; k.s = s; k.dbg = dbg; k.stop = stop; k.sfx = ''
    I = {}
    def inp(name, shape):
        I[name] = dram(nc, name, shape, F32, "ExternalInput")
    inp("x", [TL, D]); inp("c", [D]); inp("ctx", [TC, D]); inp("c_ctx", [D])
    inp("w_mod", [DEPTH, D, 3 * D]); inp("b_mod", [DEPTH, 3 * D]); inp("g_pre", [DEPTH, D]); inp("g_post", [DEPTH, D])
    inp("w_in", [DEPTH, D, DIN]); inp("conv_w", [DEPTH, 9, 1536]); inp("conv_b", [DEPTH, 1536])
    inp("dt_bias", [DEPTH, 32]); inp("a_log", [DEPTH, 32]); inp("d_ssd", [DEPTH, 16]); inp("g_ssd_norm", [DEPTH, D])
    inp("s5_lambda_re", [DEPTH, 2, 32, 64]); inp("s5_lambda_im", [DEPTH, 2, 32, 64]); inp("s5_log_step", [DEPTH, 2, 32])
    inp("s5_b_re", [DEPTH, 2, 32, 64, 16]); inp("s5_b_im", [DEPTH, 2, 32, 64, 16])
    inp("s5_c_re", [DEPTH, 2, 32, 16, 64]); inp("s5_c_im", [DEPTH, 2, 32, 16, 64])
    inp("s5_d", [DEPTH, 512]); inp("s5_w_glu", [DEPTH, 512, 512]); inp("s5_b_glu", [DEPTH, 512])
    inp("fnet_w", [DEPTH, 512, 512]); inp("fnet_b", [DEPTH, 512]); inp("w_out", [DEPTH, 2048, D])
    inp("cst", [128, 2304])
    k.I = I
    k.out = dram(nc, "out", [TL, D], F32, "ExternalOutput")
    def scr(name, shape, dt):
        return dram(nc, name, shape, dt, "ExternalOutput" if name in dbg else "Internal")
    k.XS = scr("XS", [T, D], F32)
    k.ZT = scr("ZT", [T, D], F32)
    k.PF = scr("PF", [1056, T], F32)
    k.PB = scr("PB", [2560, T], BF16)
    k.XC = scr("XC", [1536, T], BF16)
    k.MT = scr("MT", [2048, T], BF16)
    k.YF = scr("YF", [T, D], F32)
    k.MODS = scr("MODS", [DEPTH, 2, 3, D], F32)
    k.CL = scr("CL", [4096, 4096], BF16)
    k.HD = scr("HD", [2, 16, 2, 128, 544], BF16)
    k.SLN = scr("SLN", [4096, 4096], BF16)
    k.bufs = {}
    def B(name):
        if name not in k.bufs:
            k.bufs[name] = Buf(name)
        return k.bufs[name]
    k.B = B
    k.cst = nc.alloc_sbuf_tensor("cst_sb", [128, 2304], F32)
    k.identb = nc.alloc_sbuf_tensor("identb", [128, 128], BF16)
    s.dma("sp", k.cst[:], I["cst"][:, :], w=[B("cst")])
    s.op("dve", lambda: nc.vector.tensor_copy(out=k.identb[:], in_=k.cst[:, 0:128]), r=[B("cst")], w=[B("identb")])
    k.negpi = nc.alloc_sbuf_tensor("negpi", [128, 1], F32)
    s.op("dve", lambda: nc.vector.memset(k.negpi[:], -3.14159265), w=[B("negpi")])
    prep_mods(k)
    gen_dft(k)
    for l in range(nlayers):
        k.sfx = f'_L{l}'
        phase_ab(k, l)
        if stop == "ab":
            break
        phase_conv(k, l)
        if stop == "conv":
            break
        phase_ssd(k, l, need_ctx=(l < DEPTH - 1))
        if stop == "ssd":
            break
        phase_s5(k, l, need_ctx=(l < DEPTH - 1))
        if stop in ("s5", "s5prep", "s5main"):
            break
        phase_fnet(k, l, need_ctx=(l < DEPTH - 1))
        if stop == "fnet":
            break
        if stop != "outonly":
            pass
        phase_out(k, l, last=(l == nlayers - 1 and nlayers == DEPTH))
        if stop == "out":
            break
    s.drain("sp")
    return nc


def prep_mods(k):
    nc, s, I, B = k.nc, k.s, k.I, k.B
    with ExitStack() as st:
        T_ = lambda name, shape, dt: st.enter_context(nc.sbuf_tensor(name + k.sfx, shape, dt))
        craw = T_("craw", [128, 8, 2], F32)
        sc = T_("sc", [128, 8, 2], F32)
        wm = [T_(f"wm{i}", [128, 3 * D], F32) for i in range(2)]
        rows = T_("mrows", [2, 3 * D], F32)
        gp = T_("gp", [2, 2, D], F32)
        res = T_("mres", [2, 3, D], F32)
        psM = [st.enter_context(nc.psum_tensor(f"psM{i}", [128, 512], F32)) for i in range(6)]
        s.dma("sp", craw[:, :, 0], I["c"].rearrange("(k p) -> p k", p=128), w=[B("craw")], allow_slow_non_contiguous=True)
        s.dma("sp", craw[:, :, 1], I["c_ctx"].rearrange("(k p) -> p k", p=128), w=[B("craw")], allow_slow_non_contiguous=True)
        s.op("act", lambda: nc.scalar.activation(out=sc[:], in_=craw[:], func=AF.Silu), r=[B("craw")], w=[B("sc")])
        for l in range(DEPTH):
            for kk in range(8):
                w_ = wm[kk % 2]; wb = B(f"wm{kk % 2}")
                s.dma("sp", w_[:], I["w_mod"][l, kk * 128:(kk + 1) * 128, :], w=[wb])
                for n in range(6):
                    s.op("pe", lambda n=n, w_=w_, kk=kk: nc.tensor.matmul(psM[n][0:2, :], lhsT=sc[:, kk, :], rhs=w_[:, n * 512:(n + 1) * 512],
                                                                start=(kk == 0), stop=(kk == 7)), r=[B("sc"), wb], w=[B(f"psM{n}")])
            s.dma("sp", rows[:], I["b_mod"][l:l + 1, :].partition_broadcast(2) if False else I["b_mod"][l, :].partition_broadcast(2), w=[B("mrows")])
            s.dma("sp", gp[:, 0, :], I["g_pre"][l, :].partition_broadcast(2), w=[B("gp")])
            s.dma("sp", gp[:, 1, :], I["g_post"][l, :].partition_broadcast(2), w=[B("gp")])
            for n in range(6):
                s.op("dve", lambda n=n: nc.vector.tensor_tensor(out=rows[:, n * 512:(n + 1) * 512], in0=rows[:, n * 512:(n + 1) * 512],
                                                            in1=psM[n][0:2, :], op=ALU.add), r=[B(f"psM{n}")], w=[B("mrows")])
            s.op("dve", lambda: nc.vector.tensor_copy(out=res[:, 0, :], in_=rows[:, 0:D]), r=[B("mrows")], w=[B("mres")])
            s.op("dve", lambda: nc.vector.scalar_tensor_tensor(out=res[:, 1, :], in0=rows[:, D:2 * D], scalar=1.0, in1=gp[:, 0, :],
                                                             op0=ALU.add, op1=ALU.mult), r=[B("mrows"), B("gp")], w=[B("mres")])
            s.op("dve", lambda: nc.vector.tensor_tensor(out=res[:, 2, :], in0=rows[:, 2 * D:3 * D], in1=gp[:, 1, :], op=ALU.mult),
                 r=[B("mrows"), B("gp")], w=[B("mres")])
            s.dma("sp", k.MODS[l], res[:], r=[B("mres")], w=[B("MODS")])
        s.barrier()


def fm_cols():
    lst = []
    for i in range(12):
        lst.append((1024 + i * 128, 128, "PB", i * 128, BF16))
    lst.append((2560, 32, "PF", 0, F32))
    for i in range(4):
        lst.append((2592 + i * 128, 128, "PB", 1536 + i * 128, BF16))
    for i in range(4):
        lst.append((3104 + i * 128, 128, "PF", 32 + i * 128, F32))
    for i in range(4):
        lst.append((3616 + i * 128, 128, "PB", 2048 + i * 128, BF16))
    for i in range(4):
        lst.append((4128 + i * 128, 128, "PF", 544 + i * 128, F32))
    return lst


def phase_ab(k, l):
    nc, s, I, B = k.nc, k.s, k.I, k.B
    with ExitStack() as st:
        T_ = lambda name, shape, dt: st.enter_context(nc.sbuf_tensor(name + k.sfx, shape, dt))
        P_ = lambda name, shape, dt: st.enter_context(nc.psum_tensor(name + k.sfx, shape, dt))
        wsb = T_("wsb", [128, 8, DIN], BF16)
        wst = [T_(f"wst{i}", [128, DIN], F32) for i in range(2)]
        bc = T_("bcab", [128, 4, D], F32)
        xt = [T_(f"xt{i}", [128, D], F32) for i in range(2)]
        junk = T_("junk", [128, D], BF16)
        h1 = T_("h1", [128, D], F32)
        hl = [T_(f"hl{i}", [128, D], BF16) for i in range(2)]
        hlT = [T_(f"hlT{i}", [128, 8, 512], BF16) for i in range(2)]
        st4 = T_("st4", [128, 8], F32)
        zt = [T_(f"zt{i}", [128, D], F32) for i in range(2)]
        ev32 = [T_(f"ev32_{i}", [128, 512], F32) for i in range(2)]
        ev16 = [T_(f"ev16_{i}", [128, 512], BF16) for i in range(2)]
        psT = P_("psT", [128, 1024], BF16)
        psZ = [P_(f"psZ{i}", [128, 512], F32) for i in range(2)]
        psF = [P_(f"psF{i}", [128, 512], F32) for i in range(3)]
        for kk in range(8):
            w_ = wst[kk % 2]; wb = B(f"wst{kk % 2}")
            s.dma("sp", w_[:], I["w_in"][l, kk * 128:(kk + 1) * 128, :], w=[wb])
            e = "act" if kk % 2 == 0 else "pool"
            if e == "act":
                s.op("act", lambda w_=w_, kk=kk: nc.scalar.copy(out=wsb[:, kk, :], in_=w_[:]), r=[wb], w=[B(f"wsb{kk}")])
            else:
                s.op("pool", lambda w_=w_, kk=kk: nc.gpsimd.tensor_copy(out=wsb[:, kk, :], in_=w_[:]), r=[wb], w=[B(f"wsb{kk}")])
        wsb_bufs = [B(f"wsb{kk}") for kk in range(8)]
        for j, (which, comp) in enumerate([(1, 0), (1, 1), (0, 0), (0, 1)]):
            s.dma("pool", bc[:, j, :], k.MODS[l, which, comp, :].partition_broadcast(128), r=[B("MODS")], w=[B("bcab")])
        cols = fm_cols()
        ngroups = 9
        ev_i = 0
        for g in range(ngroups):
            ntok = 512 if g < 8 else 256
            nt = ntok // 128
            hT = hlT[g % 2]; hTb = B(f"hlT{g % 2}")
            for tt in range(nt):
                ti = g * 4 + tt
                x_ = xt[ti % 2]; xb = B(f"xt{ti % 2}")
                if l == 0:
                    src = I["ctx"][ti * 128:(ti + 1) * 128, :] if ti < 2 else I["x"][(ti - 2) * 128:(ti - 1) * 128, :]
                    s.dma("sp", x_[:], src, w=[xb])
                else:
                    s.dma("sp", x_[:], k.XS[ti * 128:(ti + 1) * 128, :], r=[B("XS")], w=[xb])
                jb = 0 if ti < 2 else 2
                c0 = (ti % 4) * 2
                s.op("act", lambda x_=x_, c0=c0: nc.scalar.activation(out=junk[:], in_=x_[:], func=AF.Square, accum_out=st4[:, c0:c0 + 1]),
                     r=[xb], w=[B("junk"), B(f"st4_{ti % 4}")])
                s.op("dve", lambda c0=c0: nc.vector.tensor_scalar(out=st4[:, c0:c0 + 1], in0=st4[:, c0:c0 + 1], scalar1=1.0 / D, scalar2=EPS,
                                                              op0=ALU.mult, op1=ALU.add), w=[B(f"st4_{ti % 4}")])
                s.op("act", lambda c0=c0: nc.scalar.activation(out=st4[:, c0:c0 + 1], in_=st4[:, c0:c0 + 1], func=AF.Sqrt), w=[B(f"st4_{ti % 4}")])
                s.op("dve", lambda c0=c0: nc.vector.reciprocal(out=st4[:, c0 + 1:c0 + 2], in_=st4[:, c0:c0 + 1]), w=[B(f"st4_{ti % 4}")])
                s.op("dve", lambda x_=x_, c0=c0, jb=jb: nc.vector.scalar_tensor_tensor(out=h1[:], in0=x_[:], scalar=st4[:, c0 + 1:c0 + 2], in1=bc[:, jb + 1, :],
                                                                                 op0=ALU.mult, op1=ALU.mult), r=[xb, B(f"st4_{ti % 4}"), B("bcab")], w=[B("h1")])
                h_ = hl[ti % 2]; hb = B(f"hl{ti % 2}")
                s.op("dve", lambda h_=h_, jb=jb: nc.vector.tensor_tensor(out=h_[:], in0=h1[:], in1=bc[:, jb, :], op=ALU.add), r=[B("h1"), B("bcab")], w=[hb])
                for kk in range(8):
                    s.op("pe", lambda h_=h_, kk=kk: nc.tensor.transpose(out=psT[:, kk * 128:(kk + 1) * 128], in_=h_[:, kk * 128:(kk + 1) * 128], identity=k.identb[:]),
                         r=[hb, B("identb")], w=[B("psT")])
                s.op("act", lambda hT=hT, tt=tt: nc.scalar.copy(out=hT[:, :, tt * 128:(tt + 1) * 128], in_=psT[:].rearrange("p (k t) -> p k t", t=128)),
                     r=[B("psT")], w=[hTb])
                z_ = zt[ti % 2]; zb = B(f"zt{ti % 2}")
                for hh in range(2):
                    pz = psZ[hh]; pzb = B(f"psZ{hh}")
                    for kk in range(8):
                        s.op("pe", lambda pz=pz, hT=hT, kk=kk, tt=tt, hh=hh: nc.tensor.matmul(pz[:], lhsT=hT[:, kk, tt * 128:(tt + 1) * 128], rhs=wsb[:, kk, hh * 512:(hh + 1) * 512],
                                                                                        start=(kk == 0), stop=(kk == 7)), r=[hTb, wsb_bufs[kk]], w=[pzb])
                    if hh == 0:
                        s.op("act", lambda z_=z_, pz=pz: nc.scalar.copy(out=z_[:, 0:512], in_=pz[:]), r=[pzb], w=[zb])
                    else:
                        s.op("dve", lambda z_=z_, pz=pz: nc.vector.tensor_copy(out=z_[:, 512:1024], in_=pz[:]), r=[pzb], w=[zb])
                s.dma("pool", k.ZT[ti * 128:(ti + 1) * 128, :], z_[:], r=[zb], w=[B("ZT")])
            t0 = g * 512
            for ci, (c0, wd, dest, r0, dt_) in enumerate(cols):
                pf = psF[ci % 3]; pfb = B(f"psF{ci % 3}")
                for kk in range(8):
                    s.op("pe", lambda pf=pf, hT=hT, kk=kk, c0=c0, wd=wd, ntok=ntok: nc.tensor.matmul(pf[0:wd, 0:ntok], lhsT=wsb[:, kk, c0:c0 + wd], rhs=hT[:, kk, 0:ntok],
                                                                                             start=(kk == 0), stop=(kk == 7)), r=[hTb, wsb_bufs[kk]], w=[pfb])
                ev = (ev32 if dt_ == F32 else ev16)[ev_i % 2]
                evb = B(("ev32_" if dt_ == F32 else "ev16_") + str(ev_i % 2))
                if ev_i % 2 == 0:
                    s.op("act", lambda ev=ev, pf=pf, wd=wd, ntok=ntok: nc.scalar.copy(out=ev[0:wd, 0:ntok], in_=pf[0:wd, 0:ntok]), r=[pfb], w=[evb])
                else:
                    s.op("dve", lambda ev=ev, pf=pf, wd=wd, ntok=ntok: nc.vector.tensor_copy(out=ev[0:wd, 0:ntok], in_=pf[0:wd, 0:ntok]), r=[pfb], w=[evb])
                dst = getattr(k, dest)
                s.dma("pool", dst[r0:r0 + wd, t0:t0 + ntok], ev[0:wd, 0:ntok], r=[evb], w=[B(dest)])
                ev_i += 1
        s.barrier()


def _consts():
    c = np.zeros((128, 2304), np.float32)
    c[:, 0:128] = np.eye(128)
    c[:, 128:137] = np.arange(9)
    p = np.arange(128)
    c[:, 137] = (p % 32) < 16
    c[:, 138] = (p % 32) >= 16
    c[:, 139] = p
    c[:, 140:268] = 1.0
    jj, ii = np.meshgrid(p, p, indexing="ij")
    c[:, 268:396] = np.where(jj > ii, -30000.0, 0.0)
    c[:, 396:524] = np.where(jj < ii, -30000.0, 0.0)
    c[:, 524:532] = (p[:, None] // 16) == np.arange(8)[None]
    c[:, 1024:1568] = np.arange(544)[None]
    cn = np.arange(544)
    c[:, 1600:2144] = np.where(cn < 32, 31 - cn, 575 - cn)[None]
    return c


def kernel(**inputs):
    inp = {k_: np.asarray(v) for k_, v in inputs.items()}
    cst = _consts()
    shared = {}
    for name, v in inp.items():
        if name in ("x", "c", "ctx"):
            continue
        v = np.ascontiguousarray(v, dtype=np.float32)
        if name == "conv_w":
            v = np.ascontiguousarray(v.reshape(4, 9, 1536))
        elif name in ("dt_bias", "a_log"):
            v = np.ascontiguousarray(v.reshape(4, 32))
        shared[name] = v
    shared["cst"] = cst
    in_maps = []
    for core in range(8):
        b = core % 4
        m = dict(shared)
        m["x"] = np.ascontiguousarray(inp["x"][b], dtype=np.float32)
        m["c"] = np.ascontiguousarray(inp["c"][b], dtype=np.float32)
        m["ctx"] = np.ascontiguousarray(inp["ctx"][b], dtype=np.float32)
        in_maps.append(m)
    nc = build()
    res = run_bass_kernel_spmd(nc, in_maps, core_ids=list(range(8)))
    out = np.stack([np.asarray(res.results[b]["out"], dtype=np.float32) for b in range(4)], axis=0)
    return out
```

```python
import math
from contextlib import ExitStack
import numpy as np
import concourse.bass as bass
import concourse.mybir as mybir
from concourse.bass_utils import run_bass_kernel_spmd

F32 = mybir.dt.float32
BF16 = mybir.dt.bfloat16
AF = mybir.ActivationFunctionType
ALU = mybir.AluOpType
AX = mybir.AxisListType


class Buf:
    __slots__ = ("w", "r", "name")

    def __init__(self, name=""):
        self.w = None
        self.r = {}
        self.name = name


class Sched:
    NR = 8

    def __init__(self, nc):
        self.nc = nc
        self.eng = {"pe": nc.tensor, "act": nc.scalar, "dve": nc.vector,
                    "pool": nc.gpsimd, "sp": nc.sync}
        self.csem = {e: nc.alloc_semaphore("c_" + e) for e in ("pe", "act", "dve", "pool")}
        self.ccnt = {e: 0 for e in self.csem}
        self.dq = {}
        for q in ("sp", "act", "pool"):
            self.dq[q] = dict(sems=[nc.alloc_semaphore(f"d_{q}{i}") for i in range(self.NR)],
                              n=0, tk=[None] * self.NR)
        self.waited = {}
        self.ninstr = 0

    def _wait(self, e, tk):
        if tk is None:
            return
        key, sem, val = tk
        if key == "pe" and e == "pe":
            return
        if self.waited.get((e, key), 0) >= val:
            return
        self.eng[e].wait_ge(sem, val)
        self.waited[(e, key)] = val

    def _deps(self, e, r, w):
        for b in r:
            self._wait(e, b.w)
        for b in w:
            self._wait(e, b.w)
            for t in list(b.r.values()):
                self._wait(e, t)

    def _record(self, tk, r, w):
        for b in r:
            b.r[tk[0]] = tk
        for b in w:
            b.w = tk
            b.r = {}

    def op(self, e, fn, r=(), w=()):
        self._deps(e, r, w)
        ins = fn()
        self.ccnt[e] += 1
        ins.then_inc(self.csem[e], 1)
        tk = (e, self.csem[e], self.ccnt[e])
        self._record(tk, r, w)
        self.ninstr += 1
        return tk

    def dma(self, q, out, in_, r=(), w=(), **kw):
        d = self.dq[q]
        slot = d["n"] % self.NR
        self._wait(q, d["tk"][slot])
        self._deps(q, r, w)
        ins = self.eng[q].dma_start(out=out, in_=in_, **kw)
        ins.then_inc(d["sems"][slot], 16)
        val = 16 * (d["n"] // self.NR + 1)
        tk = (("d", q, slot), d["sems"][slot], val)
        d["tk"][slot] = tk
        d["n"] += 1
        self._record(tk, r, w)
        self.ninstr += 1
        return tk

    def all_tickets(self):
        tks = []
        for e in self.csem:
            if self.ccnt[e]:
                tks.append((e, self.csem[e], self.ccnt[e]))
        for q, d in self.dq.items():
            for t in d["tk"]:
                if t is not None:
                    tks.append(t)
        return tks

    def barrier(self, engines=("pe", "act", "dve", "pool", "sp")):
        tks = self.all_tickets()
        for e in engines:
            for t in tks:
                self._wait(e, t)

    def drain(self, e="sp"):
        for t in self.all_tickets():
            if t[0] == "pe" and e == "pe":
                continue
            self._wait(e, t)

T = 4352

def phase_conv(k, l):
    nc, s, I, B = k.nc, k.s, k.I, k.B
    with ExitStack() as st:
        T_ = lambda name, shape, dt: st.enter_context(nc.sbuf_tensor(name + k.sfx, shape, dt))
        P_ = lambda name, shape, dt: st.enter_context(nc.psum_tensor(name + k.sfx, shape, dt))
        cw9 = T_("cw9", [9, 1536], F32)
        cwT = T_("cwT", [128, 108], F32)
        cb = T_("cb", [128, 12], F32)
        dg = T_("dg", [128, 108, 128], BF16)
        xp = [T_(f"xp{i}", [128, 258 + 66 * 66], BF16) for i in range(2)]
        ev = [T_(f"cev{i}", [128, 512], BF16) for i in range(2)]
        psW = P_("psW", [128, 108], F32)
        psC = [P_(f"psC{i}", [128, 512], F32) for i in range(2)]
        s.dma("sp", cw9[:], I["conv_w"][l], w=[B("cw9")])
        s.dma("sp", cb[:], I["conv_b"][l].rearrange("(t p) -> p t", p=128), w=[B("cb")], allow_slow_non_contiguous=True)
        for t in range(12):
            s.op("pe", lambda t=t: nc.tensor.transpose(out=psW[:, t * 9:(t + 1) * 9], in_=cw9[:, t * 128:(t + 1) * 128], identity=k.cst[0:9, 0:9]),
                 r=[B("cw9"), B("cst")], w=[B("psW")])
        s.op("dve", lambda: nc.vector.tensor_copy(out=cwT[:], in_=psW[:]), r=[B("psW")], w=[B("cwT")])
        for j in range(108):
            e = "dve" if j % 2 == 0 else "pool"
            eng = nc.vector if e == "dve" else nc.gpsimd
            s.op(e, lambda j=j, eng=eng: eng.tensor_scalar(out=dg[:, j, :], in0=k.identb[:], scalar1=cwT[:, j:j + 1], scalar2=None, op0=ALU.mult),
                 r=[B("cwT"), B("identb")], w=[B(f"dg{j}")])
        for i in range(2):
            s.op("pool", lambda i=i: nc.gpsimd.memset(xp[i][:], 0.0), w=[B(f"xp{i}")])
        n = 0
        for t in range(12):
            x_ = xp[t % 2]; xb = B(f"xp{t % 2}")
            rows = k.PB[t * 128:(t + 1) * 128, :]
            s.dma("sp", x_[:, 1:257], rows[:, 0:256], r=[B("PB")], w=[xb])
            grid = x_[:, 258:258 + 4356].rearrange("p (r c) -> p r c", c=66)
            for hh in range(2):
                s.dma("sp", grid[:, 1 + hh * 32:33 + hh * 32, 1:65], rows[:, 256 + hh * 2048:256 + (hh + 1) * 2048].rearrange("p (r c) -> p r c", c=64), r=[B("PB")], w=[xb])
            for rg in range(9):
                pc = psC[n % 2]; pcb = B(f"psC{n % 2}")
                if rg < 8:
                    for tap in range(9):
                        ky, kx = tap // 3, tap % 3
                        rhs = grid[:, rg * 8 + ky:rg * 8 + ky + 8, kx:kx + 64]
                        s.op("pe", lambda pc=pc, t=t, tap=tap, rhs=rhs: nc.tensor.matmul(pc[:].rearrange("p (r c) -> p r c", c=64), lhsT=dg[:, t * 9 + tap, :], rhs=rhs,
                                                                              start=(tap == 0), stop=(tap == 8)), r=[xb, B(f"dg{t * 9 + tap}")], w=[pcb])
                    ntok = 512; t0 = 256 + rg * 512
                else:
                    for kx in range(3):
                        s.op("pe", lambda pc=pc, t=t, kx=kx: nc.tensor.matmul(pc[:, 0:256], lhsT=dg[:, t * 9 + 3 + kx, :], rhs=x_[:, kx:kx + 256],
                                                                    start=(kx == 0), stop=(kx == 2)), r=[xb, B(f"dg{t * 9 + 3 + kx}")], w=[pcb])
                    ntok = 256; t0 = 0
                e_ = ev[n % 2]; eb = B(f"cev{n % 2}")
                s.op("act", lambda e_=e_, pc=pc, t=t, ntok=ntok: nc.scalar.activation(out=e_[:, 0:ntok], in_=pc[:, 0:ntok], func=AF.Silu, bias=cb[:, t:t + 1]),
                     r=[pcb, B("cb")], w=[eb])
                s.dma("pool", k.XC[t * 128:(t + 1) * 128, t0:t0 + ntok], e_[:, 0:ntok], r=[eb], w=[B("XC")])
                n += 1
        s.barrier()

T = 4352; NCH = 34; D = 1024; EPS = 1e-6
C_MF = 137; C_MB = 138; C_ONES = 140; C_NEGF = 268; C_NEGB = 396; C_BM8 = 524; C_IOTA = 1024

def phase_ssd(k, l, need_ctx=True):
    nc, s, I, B = k.nc, k.s, k.I, k.B
    cst = k.cst
    with ExitStack() as st:
        T_ = lambda name, shape, dt: st.enter_context(nc.sbuf_tensor(name + k.sfx, shape, dt))
        P_ = lambda name, shape, dt: st.enter_context(nc.psum_tensor(name + k.sfx, shape, dt))
        pst = ExitStack()
        TP_ = lambda name, shape, dt: pst.enter_context(nc.sbuf_tensor(name + k.sfx, shape, dt))
        acs = T_("s_acs", [128, T], F32)
        nacs = T_("s_nacs", [128, T], F32)
        Q = T_("s_Q", [128, T], F32)
        colp = T_("s_colp", [128, 4], F32)
        tot = T_("s_tot", [128, NCH], F32)
        DT = T_("s_DT", [32, NCH, 32], F32)
        cdall = T_("s_cd", [128, NCH, 32], F32)
        Esel = T_("s_Esel", [32, 32, 128], F32)
        negm = T_("s_negm", [128, 2, 512], BF16)
        dskc = T_("s_dskc", [128, 8], F32)
        dgd = T_("s_dgd", [128, 8, 128], BF16)
        gnb = T_("s_gnb", [128, D], F32)
        dt_ = TP_("s_dt", [128, T], F32)
        a_ = TP_("s_a", [128, T], F32)
        cum = TP_("s_cum", [128, T], F32)
        psCB = P_("psCB", [128, 256], F32)
        psX = P_("psX", [128, 1024], BF16)
        psB = P_("psB", [128, 256], BF16)
        psE = P_("psE", [128, 512], F32)
        psY = [P_(f"psY{i}", [128, 512], F32) for i in range(2)]
        psS1 = P_("psSst", [128, 512], F32)
        psO1 = P_("psOst", [128, 512], F32)

        V = lambda fn, r=(), w=(): s.op("dve", fn, r=r, w=w)
        A = lambda fn, r=(), w=(): s.op("act", fn, r=r, w=w)
        G = lambda fn, r=(), w=(): s.op("pool", fn, r=r, w=w)
        PE = lambda fn, r=(), w=(): s.op("pe", fn, r=r, w=w)
        for q in range(4):
            s.dma("sp", dt_[32 * q:32 * q + 32, :], k.PF[0:32, :], r=[B("PF")], w=[B("s_dt")])
            s.dma("sp", colp[32 * q:32 * q + 32, 0:1], I["dt_bias"][l].rearrange("(p o) -> p o", o=1), w=[B("s_colp")])
            s.dma("sp", colp[32 * q:32 * q + 32, 1:2], I["a_log"][l].rearrange("(p o) -> p o", o=1), w=[B("s_colp")])
        for t in range(8):
            for hh in range(2):
                s.dma("sp", dskc[64 * hh:64 * hh + 64, t:t + 1], I["d_ssd"][l, 2 * t + hh:2 * t + hh + 1].partition_broadcast(64), w=[B("s_dskc")])
        s.dma("sp", gnb[:], I["g_ssd_norm"][l].partition_broadcast(128), w=[B("s_gnb")])
        A(lambda: nc.scalar.activation(out=colp[:, 2:3], in_=colp[:, 1:2], func=AF.Exp), w=[B("s_colp")])
        V(lambda: nc.vector.tensor_scalar(out=colp[:, 3:4], in0=colp[:, 2:3], scalar1=-1.0, scalar2=None, op0=ALU.mult), w=[B("s_colp")])
        A(lambda: nc.scalar.activation(out=dt_[:], in_=dt_[:], func=AF.Exp, bias=colp[:, 0:1]), r=[B("s_colp")], w=[B("s_dt")])
        A(lambda: nc.scalar.activation(out=dt_[:], in_=dt_[:], func=AF.Ln, bias=1.0), w=[B("s_dt")])
        V(lambda: nc.vector.tensor_scalar(out=a_[:], in0=dt_[:], scalar1=colp[:, 3:4], scalar2=None, op0=ALU.mult), r=[B("s_dt"), B("s_colp")], w=[B("s_a")])
        for c in range(NCH):
            V(lambda c=c: nc.vector.tensor_tensor_scan(out=cum[:, c * 128:(c + 1) * 128], data0=cst[:, C_ONES:C_ONES + 128], data1=a_[:, c * 128:(c + 1) * 128],
                                                   initial=0.0, op0=ALU.mult, op1=ALU.add), r=[B("s_a"), B("cst")], w=[B("s_cum")])
        cum3 = cum[:].rearrange("p (c i) -> p c i", i=128)
        V(lambda: nc.vector.tensor_copy(out=tot[:], in_=cum3[:, :, 127]), r=[B("s_cum")], w=[B("s_tot")])
        totb = tot[:].unsqueeze(2).to_broadcast([128, NCH, 128])
        V(lambda: nc.vector.tensor_tensor(out=nacs[:], in0=a_[:], in1=cum[:], op=ALU.subtract), r=[B("s_a"), B("s_cum")], w=[B("s_nacs")])
        V(lambda: nc.vector.tensor_tensor(out=nacs[:].rearrange("p (c i) -> p c i", i=128), in0=nacs[:].rearrange("p (c i) -> p c i", i=128), in1=totb, op=ALU.add),
          r=[B("s_tot")], w=[B("s_nacs")])
        V(lambda: nc.vector.tensor_scalar(out=acs[:], in0=cum[:], scalar1=cst[:, C_MF:C_MF + 1], scalar2=None, op0=ALU.mult), r=[B("s_cum"), B("cst")], w=[B("s_acs")])
        V(lambda: nc.vector.scalar_tensor_tensor(out=acs[:], in0=nacs[:], scalar=cst[:, C_MB:C_MB + 1], in1=acs[:], op0=ALU.mult, op1=ALU.add), r=[B("s_nacs")], w=[B("s_acs")])
        V(lambda: nc.vector.tensor_scalar(out=nacs[:], in0=acs[:], scalar1=-1.0, scalar2=None, op0=ALU.mult), r=[B("s_acs")], w=[B("s_nacs")])
        V(lambda: nc.vector.tensor_tensor(out=a_[:].rearrange("p (c i) -> p c i", i=128), in0=nacs[:].rearrange("p (c i) -> p c i", i=128), in1=totb, op=ALU.add),
          r=[B("s_nacs"), B("s_tot")], w=[B("s_a")])
        A(lambda: nc.scalar.activation(out=a_[:], in_=a_[:], func=AF.Exp), w=[B("s_a")])
        V(lambda: nc.vector.tensor_tensor(out=a_[:], in0=a_[:], in1=dt_[:], op=ALU.mult), r=[B("s_dt")], w=[B("s_a")])
        A(lambda: nc.scalar.activation(out=cum[:], in_=acs[:], func=AF.Exp), r=[B("s_acs")], w=[B("s_cum")])
        G(lambda: nc.gpsimd.tensor_copy(out=Q[0:32, :], in_=dt_[0:32, :]), r=[B("s_dt")], w=[B("s_Q")])
        G(lambda: nc.gpsimd.tensor_copy(out=Q[32:64, :], in_=a_[32:64, :]), r=[B("s_a")], w=[B("s_Q")])
        G(lambda: nc.gpsimd.tensor_copy(out=Q[64:96, :], in_=cum[64:96, :]), r=[B("s_cum")], w=[B("s_Q")])
        G(lambda: nc.gpsimd.tensor_copy(out=Q[96:128, :], in_=nacs[96:128, :]), r=[B("s_nacs")], w=[B("s_Q")])
        V(lambda: nc.vector.tensor_copy(out=Esel[:], in_=cst[0:32, 0:32].unsqueeze(2).to_broadcast([32, 32, 128])), r=[B("cst")], w=[B("s_Esel")])
        V(lambda: nc.vector.tensor_copy(out=negm[:, 0, :].rearrange("p (a i) -> p a i", i=128), in_=cst[:, C_NEGF:C_NEGF + 128].unsqueeze(1).to_broadcast([128, 4, 128])), r=[B("cst")], w=[B("s_negm")])
        V(lambda: nc.vector.tensor_copy(out=negm[:, 1, :].rearrange("p (a i) -> p a i", i=128), in_=cst[:, C_NEGB:C_NEGB + 128].unsqueeze(1).to_broadcast([128, 4, 128])), r=[B("cst")], w=[B("s_negm")])
        V(lambda: nc.vector.tensor_tensor(out=DT[:], in0=cst[0:32, 0:32].unsqueeze(1).to_broadcast([32, NCH, 32]), in1=tot[0:32, :].unsqueeze(2).to_broadcast([32, NCH, 32]), op=ALU.mult),
          r=[B("s_tot"), B("cst")], w=[B("s_DT")])
        for c0 in range(0, NCH, 16):
            n = min(16, NCH - c0)
            PE(lambda c0=c0, n=n: nc.tensor.matmul(psE[:, 0:n * 32], lhsT=cst[0:32, C_ONES:C_ONES + 128], rhs=DT[:, c0:c0 + n, :].rearrange("p c k -> p (c k)"), start=True, stop=True),
               r=[B("s_DT"), B("cst")], w=[B("psE")])
            A(lambda c0=c0, n=n: nc.scalar.activation(out=cdall[:, c0:c0 + n, :].rearrange("p c k -> p (c k)"), in_=psE[:, 0:n * 32], func=AF.Exp), r=[], w=[B("psE"), B("s_cd")])
        for t in range(8):
            V(lambda t=t: nc.vector.tensor_scalar(out=dgd[:, t, :], in0=k.identb[:], scalar1=dskc[:, t:t + 1], scalar2=None, op0=ALU.mult), r=[B("s_dskc"), B("identb")], w=[B("s_dgd")])
        s.barrier()
        pst.close()
        hst = [T_(f"s_h{d}", [128, D], F32) for d in range(2)]
        hbf = [T_(f"s_hb{d}", [128, D], BF16) for d in range(2)]
        xT = [T_(f"s_xT{i}", [128, 8, 128], BF16) for i in range(2)]
        BC = [T_(f"s_BC{i}", [128, 4, 128], BF16) for i in range(2)]
        tmq = [T_(f"s_tmq{i}", [128, 128], F32) for i in range(2)]
        xdt = [T_(f"s_xdt{i}", [128, D], BF16) for i in range(2)]
        xw = [T_(f"s_xw{i}", [128, D], BF16) for i in range(2)]
        Btok = [T_(f"s_Btok{i}", [128, 256], BF16) for i in range(2)]
        dec = [T_(f"s_dec{i}", [128, 512], F32) for i in range(2)]
        MTt = [T_(f"s_MT{i}", [128, 16, 128], BF16) for i in range(2)]
        ydg = [T_(f"s_ydg{i}", [128, D], F32) for i in range(2)]
        Sc = [T_(f"s_Sc{i}", [128, D], F32) for i in range(2)]
        tmp = T_("s_tmp", [128, 512], F32)
        yt = [T_(f"s_y{i}", [128, D], F32) for i in range(2)]
        yf = [T_(f"s_yf{i}", [128, D], F32) for i in range(2)]
        zt = [T_(f"s_z{i}", [128, D], F32) for i in range(2)]
        sz = T_("s_sz", [128, D], F32)
        junk = T_("s_junk", [128, 512], BF16)
        st2 = T_("s_st2", [128, 4], F32)
        mo = T_("s_mo", [128, D], BF16)
        mT = [T_(f"s_mT{i}", [128, 8, 128], BF16) for i in range(2)]
        def stageA(d, n_, c):
            sl = n_ % 2
            t0 = c * 128
            x_ = xT[sl]; xb = B(f"s_xT{sl}"); bc_ = BC[sl]; bcb = B(f"s_BC{sl}")
            tq = tmq[sl]; tqb = B(f"s_tmq{sl}")
            s.dma("sp", x_[:], k.XC[0:1024, t0:t0 + 128].rearrange("(t p) j -> p t j", p=128), r=[B("XC")], w=[xb])
            s.dma("sp", bc_[:], k.XC[1024:1536, t0:t0 + 128].rearrange("(t p) j -> p t j", p=128), r=[B("XC")], w=[bcb])
            if d == 1:
                s.dma("sp", zt[sl][:], k.ZT[t0:t0 + 128, :], r=[B("ZT")], w=[B(f"s_z{sl}")])
                s.dma("sp", yf[sl][:], k.YF[t0:t0 + 128, :], r=[B("YF")], w=[B(f"s_yf{sl}")])
            PE(lambda: nc.tensor.transpose(out=psE[:, 0:128], in_=Q[:, t0:t0 + 128], identity=cst[:, 0:128]), r=[B("s_Q"), B("cst")], w=[B("psE")])
            A(lambda: nc.scalar.copy(out=tq[:], in_=psE[:, 0:128]), w=[B("psE"), tqb])
            for t in range(8):
                PE(lambda t=t: nc.tensor.transpose(out=psX[:, t * 128:(t + 1) * 128], in_=x_[:, t, :], identity=k.identb[:]), r=[xb, B("identb")], w=[B("psX")])
            for g in range(2):
                PE(lambda g=g: nc.tensor.transpose(out=psB[:, g * 128:(g + 1) * 128], in_=bc_[:, g, :], identity=k.identb[:]), r=[bcb, B("identb")], w=[B("psB")])
                PE(lambda g=g: nc.tensor.matmul(psCB[:, g * 128:(g + 1) * 128], lhsT=bc_[:, g, :], rhs=bc_[:, 2 + g, :], start=True, stop=True), r=[bcb], w=[B("psCB")])
            A(lambda: nc.scalar.copy(out=Btok[sl][:], in_=psB[:]), w=[B("psB"), B(f"s_Btok{sl}")])
            yield
            psX3 = psX[:].rearrange("p (h e) -> p h e", e=64)
            V(lambda: nc.vector.tensor_tensor(out=xdt[sl][:].rearrange("p (h e) -> p h e", e=64), in0=psX3, in1=tq[:, d * 16:d * 16 + 16].unsqueeze(2).to_broadcast([128, 16, 64]), op=ALU.mult),
              r=[tqb], w=[B("psX"), B(f"s_xdt{sl}")])
            V(lambda: nc.vector.tensor_tensor(out=xw[sl][:].rearrange("p (h e) -> p h e", e=64), in0=psX3, in1=tq[:, 32 + d * 16:48 + d * 16].unsqueeze(2).to_broadcast([128, 16, 64]), op=ALU.mult),
              r=[tqb], w=[B("psX"), B(f"s_xw{sl}")])
            yield
            for hq in range(4):
                g = hq // 2
                PE(lambda: nc.tensor.matmul(psE[:], lhsT=k.identb[:], rhs=negm[:, d, :], start=True, stop=False), r=[B("s_negm"), B("identb")], w=[B("psE")])
                for hh in range(4):
                    dh = d * 16 + hq * 4 + hh
                    o_ = psE[:, hh * 128:(hh + 1) * 128]
                    PE(lambda o_=o_, dh=dh: nc.tensor.matmul(o_, lhsT=Esel[:, dh, :], rhs=acs[0:32, t0:t0 + 128], start=False, stop=False, skip_group_check=True), r=[B("s_Esel"), B("s_acs")], w=[B("psE")])
                    PE(lambda o_=o_, dh=dh, hh=hh: nc.tensor.matmul(o_, lhsT=nacs[0:32, t0:t0 + 128], rhs=Esel[:, dh, :], start=False, stop=(hh == 3), skip_group_check=True), r=[B("s_Esel"), B("s_nacs")], w=[B("psE")])
                dc = dec[hq % 2]; dcb = B(f"s_dec{hq % 2}")
                A(lambda dc=dc: nc.scalar.activation(out=dc[:], in_=psE[:], func=AF.Exp), w=[B("psE"), dcb])
                V(lambda dc=dc, hq=hq, g=g: nc.vector.tensor_tensor(out=MTt[sl][:, hq * 4:hq * 4 + 4, :], in0=dc[:].rearrange("p (a i) -> p a i", i=128),
                                                               in1=psCB[:, g * 128:(g + 1) * 128].unsqueeze(1).to_broadcast([128, 4, 128]), op=ALU.mult),
                  r=[dcb], w=[B("psCB"), B(f"s_MT{sl}_{hq}")])
                yield
            if d == 0:
                for t in range(8):
                    py = psY[t // 4]
                    PE(lambda t=t, py=py: nc.tensor.matmul(py[:, (t % 4) * 128:(t % 4) * 128 + 128], lhsT=x_[:, t, :], rhs=dgd[:, t, :], start=(t % 4 == 0), stop=False, skip_group_check=True),
                       r=[xb, B("s_dgd")], w=[B(f"psY{t // 4}")])
            for h in range(16):
                py = psY[h // 8]
                PE(lambda h=h, py=py: nc.tensor.matmul(py[:, (h % 8) * 64:(h % 8) * 64 + 64], lhsT=MTt[sl][:, h, :], rhs=xdt[sl][:, h * 64:(h + 1) * 64], start=(d == 1 and h % 8 == 0), stop=(h % 8 == 7), skip_group_check=True),
                   r=[B(f"s_MT{sl}_{h // 4}"), B(f"s_xdt{sl}")], w=[B(f"psY{h // 8}")])
            yield
            A(lambda: nc.scalar.copy(out=ydg[sl][:, 0:512], in_=psY[0][:]), w=[B("psY0"), B(f"s_ydg{sl}")])
            V(lambda: nc.vector.tensor_copy(out=ydg[sl][:, 512:1024], in_=psY[1][:]), w=[B("psY1"), B(f"s_ydg{sl}")])
            if d == 1:
                G(lambda: nc.gpsimd.tensor_tensor(out=ydg[sl][:], in0=ydg[sl][:], in1=yf[sl][:], op=ALU.add), r=[B(f"s_yf{sl}")], w=[B(f"s_ydg{sl}")])
            yield
            for g in range(2):
                PE(lambda g=g: nc.tensor.matmul(psS1[:], lhsT=Btok[sl][:, g * 128:(g + 1) * 128], rhs=xw[sl][:, g * 512:(g + 1) * 512], start=True, stop=True),
                   r=[B(f"s_Btok{sl}"), B(f"s_xw{sl}")], w=[B("psSst")])
                if g == 0:
                    A(lambda: nc.scalar.copy(out=Sc[sl][:, 0:512], in_=psS1[:]), w=[B("psSst"), B(f"s_Sc{sl}")])
                else:
                    V(lambda: nc.vector.tensor_copy(out=Sc[sl][:, 512:1024], in_=psS1[:]), w=[B("psSst"), B(f"s_Sc{sl}")])
                yield

        def stageB(d, n_, c):
            sl = n_ % 2
            t0 = c * 128
            hb_ = B(f"s_h{d}"); hbb = B(f"s_hb{d}")
            bc_ = BC[sl]; bcb = B(f"s_BC{sl}")
            tq = tmq[sl]; tqb = B(f"s_tmq{sl}")
            y_ = yt[sl]; yb = B(f"s_y{sl}")
            for g in range(2):
                PE(lambda g=g: nc.tensor.matmul(psO1[:], lhsT=bc_[:, 2 + g, :], rhs=hbf[d][:, g * 512:(g + 1) * 512], start=True, stop=True), r=[bcb, hbb], w=[B("psOst")])
                V(lambda g=g: nc.vector.tensor_tensor(out=tmp[:].rearrange("p (h e) -> p h e", e=64), in0=psO1[:].rearrange("p (h e) -> p h e", e=64),
                                                   in1=tq[:, 64 + d * 16 + g * 8:64 + d * 16 + g * 8 + 8].unsqueeze(2).to_broadcast([128, 8, 64]), op=ALU.mult),
                  r=[tqb], w=[B("psOst"), B("s_tmp")])
                G(lambda g=g: nc.gpsimd.tensor_tensor(out=y_[:, g * 512:(g + 1) * 512], in0=tmp[:], in1=ydg[sl][:, g * 512:(g + 1) * 512], op=ALU.add), r=[B("s_tmp"), B(f"s_ydg{sl}")], w=[yb])
                yield
            G(lambda: nc.gpsimd.tensor_tensor(out=hst[d][:].rearrange("p (h e) -> p h e", e=64), in0=hst[d][:].rearrange("p (h e) -> p h e", e=64),
                                               in1=cdall[:, c, d * 16:d * 16 + 16].unsqueeze(2).to_broadcast([128, 16, 64]), op=ALU.mult), r=[B("s_cd")], w=[hb_])
            G(lambda: nc.gpsimd.tensor_tensor(out=hst[d][:], in0=hst[d][:], in1=Sc[sl][:], op=ALU.add), r=[B(f"s_Sc{sl}")], w=[hb_])
            A(lambda: nc.scalar.copy(out=hbf[d][:], in_=hst[d][:]), r=[hb_], w=[hbb])
            yield
            if d == 0:
                s.dma("pool", k.YF[t0:t0 + 128, :], y_[:], r=[yb], w=[B("YF")])
            elif need_ctx or c >= 2:
                z_ = zt[sl]; zb = B(f"s_z{sl}")
                A(lambda: nc.scalar.activation(out=sz[:], in_=z_[:], func=AF.Silu), r=[zb], w=[B("s_sz")])
                V(lambda: nc.vector.tensor_tensor(out=y_[:], in0=y_[:], in1=sz[:], op=ALU.mult), r=[B("s_sz")], w=[yb])
                for g in range(2):
                    A(lambda g=g: nc.scalar.activation(out=junk[:], in_=y_[:, g * 512:(g + 1) * 512], func=AF.Square, accum_out=st2[:, g:g + 1]), r=[yb], w=[B("s_junk"), B("s_st2")])
                yield
                V(lambda: nc.vector.tensor_scalar(out=st2[:, 0:2], in0=st2[:, 0:2], scalar1=1.0 / 512, scalar2=EPS, op0=ALU.mult, op1=ALU.add), w=[B("s_st2")])
                A(lambda: nc.scalar.activation(out=st2[:, 0:2], in_=st2[:, 0:2], func=AF.Sqrt), w=[B("s_st2")])
                V(lambda: nc.vector.reciprocal(out=st2[:, 2:4], in_=st2[:, 0:2]), w=[B("s_st2")])
                for g in range(2):
                    V(lambda g=g: nc.vector.scalar_tensor_tensor(out=mo[:, g * 512:(g + 1) * 512], in0=y_[:, g * 512:(g + 1) * 512], scalar=st2[:, 2 + g:3 + g], in1=gnb[:, g * 512:(g + 1) * 512],
                                                               op0=ALU.mult, op1=ALU.mult), r=[yb, B("s_st2"), B("s_gnb")], w=[B("s_mo")])
                yield
                for t in range(8):
                    PE(lambda t=t: nc.tensor.transpose(out=psX[:, t * 128:(t + 1) * 128], in_=mo[:, t * 128:(t + 1) * 128], identity=k.identb[:]), r=[B("s_mo"), B("identb")], w=[B("psX")])
                m_ = mT[sl]; mb_ = B(f"s_mT{sl}")
                A(lambda: nc.scalar.copy(out=m_[:], in_=psX[:].rearrange("p (t j) -> p t j", j=128)), w=[B("psX"), mb_])
                s.dma("pool", k.MT[0:1024, t0:t0 + 128].rearrange("(t p) j -> p t j", p=128), m_[:], r=[mb_], w=[B("MT")])
            yield

        def interleave(gens):
            gens = [g for g in gens if g is not None]
            while gens:
                for g in list(gens):
                    try:
                        next(g)
                    except StopIteration:
                        gens.remove(g)

        for d in range(2):
            hb_ = B(f"s_h{d}"); hbb = B(f"s_hb{d}")
            V(lambda d=d: nc.vector.memset(hst[d][:], 0.0), w=[hb_])
            V(lambda d=d: nc.vector.memset(hbf[d][:], 0.0), w=[hbb])
            order = list(range(NCH)) if d == 0 else [1, 0] + list(range(NCH - 1, 1, -1))
            interleave([stageA(d, 0, order[0])])
            for n_ in range(len(order)):
                ga = stageA(d, n_ + 1, order[n_ + 1]) if n_ + 1 < len(order) else None
                interleave([ga, stageB(d, n_, order[n_])])
        s.barrier()

T = 4352; D = 1024; EPS = 1e-6; DEPTH = 4
C_PIDX = 139
I32 = mybir.dt.int32

def gen_dft(k):
    nc, s, B = k.nc, k.s, k.B
    cst = k.cst
    with ExitStack() as st:
        T_ = lambda name, shape, dt: st.enter_context(nc.sbuf_tensor(name + k.sfx, shape, dt))
        kio_i = T_("kio_i", [128, 4096], I32)
        kio = T_("kio", [128, 4096], F32)
        lcol = T_("lcol", [128, 32], F32)
        pi_ = [T_(f"pi{i}", [128, 4096], I32) for i in range(2)]
        tb = [T_(f"tb{i}", [128, 4096], BF16) for i in range(4)]
        s.op("pool", lambda: nc.gpsimd.iota(kio_i[:], pattern=[[1, 4096]], base=0, channel_multiplier=0), w=[B("kio_i")])
        s.op("dve", lambda: nc.vector.tensor_copy(out=kio[:], in_=kio_i[:]), r=[B("kio_i")], w=[B("kio")])
        for lt in range(32):
            s.op("dve", lambda lt=lt: nc.vector.tensor_scalar(out=lcol[:, lt:lt + 1], in0=cst[:, C_PIDX:C_PIDX + 1], scalar1=float(lt * 128), scalar2=None, op0=ALU.add), r=[B("cst")], w=[B("lcol")])
        sc = 2.0 * math.pi / 4096.0
        for lt in range(32):
            for j, (off, dst) in enumerate([(0.0, k.SLN), (3072.0, k.CL)]):
                e = "dve" if j == 0 else "pool"
                eng = nc.vector if j == 0 else nc.gpsimd
                p_ = pi_[j]; pb = B(f"pi{j}")
                s.op(e, lambda eng=eng, p_=p_, lt=lt, off=off: eng.tensor_scalar(out=p_[:], in0=kio[:], scalar1=lcol[:, lt:lt + 1], scalar2=off, op0=ALU.mult, op1=ALU.add), r=[B("kio"), B("lcol")], w=[pb])
                s.op("dve", lambda p_=p_: nc.vector.tensor_single_scalar(out=p_[:], in_=p_[:], scalar=4095, op=ALU.bitwise_and), w=[pb])
                t_ = tb[(lt % 2) * 2 + j]; tbb = B(f"tb{(lt % 2) * 2 + j}")
                s.op("act", lambda t_=t_, p_=p_: nc.scalar.activation(out=t_[:], in_=p_[:], func=AF.Sin, scale=sc, bias=k.negpi[:, 0:1]), r=[pb, B("negpi")], w=[tbb])
                s.dma("sp", dst[lt * 128:(lt + 1) * 128, :], t_[:], r=[tbb], w=[B("DFT")])
        s.barrier()


def phase_fnet(k, l, need_ctx=True):
    nc, s, I, B = k.nc, k.s, k.I, k.B
    with ExitStack() as st:
        T_ = lambda name, shape, dt: st.enter_context(nc.sbuf_tensor(name + k.sfx, shape, dt))
        P_ = lambda name, shape, dt: st.enter_context(nc.psum_tensor(name + k.sfx, shape, dt))
        cs = T_("f_cs", [128, 256], BF16)
        fw32 = T_("f_fw32", [128, 4, 512], F32)
        fw = T_("f_fw", [128, 4, 512], BF16)
        fb = T_("f_fb", [128, 4], F32)
        fuT = [T_(f"f_fuT{i}", [128, 4, 128], BF16) for i in range(2)]
        PQ = T_("f_PQ", [128, 34, 4, 2, 128], BF16)
        tabs = [T_(f"f_tab{i}", [128, 2, 512], BF16) for i in range(4)]
        specT = [T_(f"f_spec{i}", [128, 4, 512], BF16) for i in range(2)]
        gt = [T_(f"f_g{i}", [128, 512], F32) for i in range(2)]
        ob = [T_(f"f_ob{i}", [128, 512], BF16) for i in range(2)]
        psPQ = [P_(f"psPQ{i}", [128, 512], F32) for i in range(2)]
        psS = [P_(f"psS{i}", [128, 512], F32) for i in range(4)]
        psM = [P_(f"psMx{i}", [128, 512], F32) for i in range(2)]
        rows32 = lambda tsr: bass.AP(tsr.tensor, tsr.offset, [[32 * 4096, 128], [1, 128]])
        s.dma("sp", cs[:, 0:128], rows32(k.CL), r=[B("DFT")], w=[B("f_cs")])
        s.dma("sp", cs[:, 128:256], rows32(k.SLN), r=[B("DFT")], w=[B("f_cs")])
        s.dma("sp", fw32[:], I["fnet_w"][l].rearrange("(t p) c -> p t c", p=128), w=[B("f_fw32")])
        s.dma("sp", fb[:], I["fnet_b"][l].rearrange("(t p) -> p t", p=128), w=[B("f_fb")], allow_slow_non_contiguous=True)
        s.op("dve", lambda: nc.vector.tensor_copy(out=fw[:], in_=fw32[:]), r=[B("f_fw32")], w=[B("f_fw")])
        for tt in range(34):
            if tt < 2 and not need_ctx:
                continue
            f_ = fuT[tt % 2]; fb_ = B(f"f_fuT{tt % 2}")
            s.dma("sp", f_[:], k.PB[2048:2560, tt * 128:(tt + 1) * 128].rearrange("(h p) j -> p h j", p=128), r=[B("PB")], w=[fb_])
            for hd in range(4):
                pp = psPQ[hd // 2]; ppb = B(f"psPQ{hd // 2}")
                s.op("pe", lambda pp=pp, hd=hd, f_=f_: nc.tensor.matmul(pp[:, (hd % 2) * 256:(hd % 2) * 256 + 256], lhsT=f_[:, hd, :], rhs=cs[:], start=True, stop=True), r=[fb_, B("f_cs")], w=[ppb])
            for hf in range(2):
                pp = psPQ[hf]; ppb = B(f"psPQ{hf}")
                src = pp[:].rearrange("p (h q m) -> p h q m", h=2, q=2)
                s.op("act", lambda tt=tt, hf=hf, src=src: nc.scalar.copy(out=PQ[:, tt, hf * 2:hf * 2 + 2, 0, :], in_=src[:, :, 0, :]), w=[ppb, B(f"f_PQ{tt}")])
                s.op("dve", lambda tt=tt, hf=hf, src=src: nc.vector.tensor_scalar(out=PQ[:, tt, hf * 2:hf * 2 + 2, 1, :], in0=src[:, :, 1, :], scalar1=-1.0, scalar2=None, op0=ALU.mult), w=[ppb, B(f"f_PQ{tt}")])
        nload = 0
        jobs = []
        if need_ctx:
            jobs.append(("c", 0))
        jobs += [("l", kt) for kt in range(8)]
        for ji, (kind, kt) in enumerate(jobs):
            if kind == "c":
                nlt = 2; ncol = 256; tt0 = 0; tok0 = 0; nrm = 1.0 / math.sqrt(256.0 * 128.0)
            else:
                nlt = 32; ncol = 512; tt0 = 2; tok0 = 256 + kt * 512; nrm = 1.0 / math.sqrt(4096.0 * 128.0)
            for lt in range(nlt):
                tab = tabs[nload % 4]; tabb = B(f"f_tab{nload % 4}")
                for j, tsr in enumerate([k.CL, k.SLN]):
                    if kind == "c":
                        src = bass.AP(tsr.tensor, tsr.offset + (lt * 128) * 16 * 4096, [[16 * 4096, 128], [1, 256]])
                        s.dma("sp", tab[:, j, 0:256], src, r=[B("DFT")], w=[tabb], allow_slow_non_contiguous=True)
                    else:
                        s.dma("sp", tab[:, j, :], tsr[lt * 128:(lt + 1) * 128, kt * 512:(kt + 1) * 512], r=[B("DFT")], w=[tabb])
                nload += 1
                for hd in range(4):
                    s.op("pe", lambda hd=hd, lt=lt, tab=tab, ncol=ncol, tt0=tt0: nc.tensor.matmul(psS[hd][:, 0:ncol], lhsT=PQ[:, tt0 + lt, hd, 0, :], rhs=tab[:, 0, 0:ncol], start=(lt == 0), stop=False),
                         r=[B(f"f_PQ{tt0 + lt}"), tabb], w=[B(f"psS{hd}")])
                    s.op("pe", lambda hd=hd, lt=lt, tab=tab, ncol=ncol, tt0=tt0, nlt=nlt: nc.tensor.matmul(psS[hd][:, 0:ncol], lhsT=PQ[:, tt0 + lt, hd, 1, :], rhs=tab[:, 1, 0:ncol], start=False, stop=(lt == nlt - 1)),
                         r=[B(f"f_PQ{tt0 + lt}"), tabb], w=[B(f"psS{hd}")])
            sp_ = specT[ji % 2]; spb = B(f"f_spec{ji % 2}")
            for hd in range(4):
                if hd % 2 == 0:
                    s.op("act", lambda hd=hd, sp_=sp_, ncol=ncol, nrm=nrm: nc.scalar.mul(out=sp_[:, hd, 0:ncol], in_=psS[hd][:, 0:ncol], mul=nrm), w=[B(f"psS{hd}"), spb])
                else:
                    s.op("dve", lambda hd=hd, sp_=sp_, ncol=ncol, nrm=nrm: nc.vector.tensor_scalar(out=sp_[:, hd, 0:ncol], in0=psS[hd][:, 0:ncol], scalar1=nrm, scalar2=None, op0=ALU.mult), w=[B(f"psS{hd}"), spb])
            for ct in range(4):
                pm = psM[ct % 2]; pmb = B(f"psMx{ct % 2}")
                g_ = gt[ct % 2]; gb = B(f"f_g{ct % 2}")
                s.dma("sp", g_[:, 0:ncol], k.PF[544 + ct * 128:544 + (ct + 1) * 128, tok0:tok0 + ncol], r=[B("PF")], w=[gb])
                for hd in range(4):
                    s.op("pe", lambda pm=pm, hd=hd, ct=ct, sp_=sp_, ncol=ncol: nc.tensor.matmul(pm[:, 0:ncol], lhsT=fw[:, hd, ct * 128:(ct + 1) * 128], rhs=sp_[:, hd, 0:ncol], start=(hd == 0), stop=(hd == 3)),
                         r=[B("f_fw"), spb], w=[pmb])
                s.op("act", lambda g_=g_, ncol=ncol: nc.scalar.activation(out=g_[:, 0:ncol], in_=g_[:, 0:ncol], func=AF.Silu), w=[gb])
                o_ = ob[ct % 2]; obb = B(f"f_ob{ct % 2}")
                s.op("dve", lambda o_=o_, pm=pm, ct=ct, g_=g_, ncol=ncol: nc.vector.scalar_tensor_tensor(out=o_[:, 0:ncol], in0=pm[:, 0:ncol], scalar=fb[:, ct:ct + 1], in1=g_[:, 0:ncol], op0=ALU.add, op1=ALU.mult),
                     r=[gb, B("f_fb")], w=[pmb, obb])
                s.dma("pool", k.MT[1536 + ct * 128:1536 + (ct + 1) * 128, tok0:tok0 + ncol], o_[:, 0:ncol], r=[obb], w=[B("MT")])
        s.barrier()


def phase_out(k, l, last=False):
    nc, s, I, B = k.nc, k.s, k.I, k.B
    with ExitStack() as st:
        T_ = lambda name, shape, dt: st.enter_context(nc.sbuf_tensor(name + k.sfx, shape, dt))
        P_ = lambda name, shape, dt: st.enter_context(nc.psum_tensor(name + k.sfx, shape, dt))
        wo = T_("o_wo", [128, 16, D], BF16)
        wst = [T_(f"o_wst{i}", [128, 2, D], F32) for i in range(2)]
        bc = T_("o_bc", [128, 2, D], F32)
        mT = [T_(f"o_mT{i}", [128, 16, 128], BF16) for i in range(2)]
        xt = [T_(f"o_x{i}", [128, D], F32) for i in range(2)]
        ot = [T_(f"o_o{i}", [128, D], F32) for i in range(2)]
        junk = T_("o_junk", [128, 512], BF16)
        st4 = T_("o_st", [128, 4], F32)
        psO = [P_(f"psO{i}", [128, 512], F32) for i in range(4)]
        for c in range(8):
            w_ = wst[c % 2]; wb = B(f"o_wst{c % 2}")
            s.dma("sp", w_[:], I["w_out"][l, c * 256:(c + 1) * 256, :].rearrange("(t p) c -> p t c", p=128), w=[wb])
            if c % 2 == 0:
                s.op("act", lambda w_=w_, c=c: nc.scalar.copy(out=wo[:, 2 * c:2 * c + 2, :], in_=w_[:]), r=[wb], w=[B("o_wo")])
            else:
                s.op("pool", lambda w_=w_, c=c: nc.gpsimd.tensor_copy(out=wo[:, 2 * c:2 * c + 2, :], in_=w_[:]), r=[wb], w=[B("o_wo")])
        for j in range(2):
            s.dma("pool", bc[:, j, :], k.MODS[l, 1 - j, 2, :].partition_broadcast(128), r=[B("MODS")], w=[B("o_bc")])
        for n_, tt in enumerate(range(2 if last else 0, 34)):
            t0 = tt * 128
            m_ = mT[n_ % 2]; mb = B(f"o_mT{n_ % 2}")
            s.dma("sp", m_[:], k.MT[:, t0:t0 + 128].rearrange("(t p) j -> p t j", p=128), r=[B("MT")], w=[mb])
            x_ = xt[n_ % 2]; xb = B(f"o_x{n_ % 2}")
            if l == 0:
                src = I["ctx"][t0:t0 + 128, :] if tt < 2 else I["x"][t0 - 256:t0 - 128, :]
                s.dma("sp", x_[:], src, w=[xb])
            else:
                s.dma("sp", x_[:], k.XS[t0:t0 + 128, :], r=[B("XS")], w=[xb])
            for hf in range(2):
                po = psO[(n_ % 2) * 2 + hf]; pob = B(f"psO{(n_ % 2) * 2 + hf}")
                for ct in range(16):
                    s.op("pe", lambda po=po, m_=m_, ct=ct, hf=hf: nc.tensor.matmul(po[:], lhsT=m_[:, ct, :], rhs=wo[:, ct, hf * 512:(hf + 1) * 512], start=(ct == 0), stop=(ct == 15)),
                         r=[mb, B("o_wo")], w=[pob])
                s.op("act", lambda po=po, hf=hf: nc.scalar.activation(out=junk[:], in_=po[:], func=AF.Square, accum_out=st4[:, hf:hf + 1]), w=[pob, B("o_junk"), B("o_st")])
            s.op("dve", lambda: nc.vector.tensor_tensor(out=st4[:, 2:3], in0=st4[:, 0:1], in1=st4[:, 1:2], op=ALU.add), w=[B("o_st")])
            s.op("dve", lambda: nc.vector.tensor_scalar(out=st4[:, 2:3], in0=st4[:, 2:3], scalar1=1.0 / D, scalar2=EPS, op0=ALU.mult, op1=ALU.add), w=[B("o_st")])
            s.op("act", lambda: nc.scalar.activation(out=st4[:, 2:3], in_=st4[:, 2:3], func=AF.Sqrt), w=[B("o_st")])
            s.op("dve", lambda: nc.vector.reciprocal(out=st4[:, 3:4], in_=st4[:, 2:3]), w=[B("o_st")])
            o_ = ot[n_ % 2]; ob = B(f"o_o{n_ % 2}")
            jb = 0 if tt < 2 else 1
            for hf in range(2):
                po = psO[(n_ % 2) * 2 + hf]; pob = B(f"psO{(n_ % 2) * 2 + hf}")
                s.op("dve", lambda po=po, hf=hf, o_=o_, jb=jb: nc.vector.scalar_tensor_tensor(out=o_[:, hf * 512:(hf + 1) * 512], in0=po[:], scalar=st4[:, 3:4], in1=bc[:, jb, hf * 512:(hf + 1) * 512],
                                                                                      op0=ALU.mult, op1=ALU.mult), r=[B("o_st"), B("o_bc")], w=[pob, ob])
            s.op("pool", lambda o_=o_, x_=x_: nc.gpsimd.tensor_tensor(out=o_[:], in0=o_[:], in1=x_[:], op=ALU.add), r=[xb], w=[ob])
            if last:
                s.dma("pool", k.out[t0 - 256:t0 - 128, :], o_[:], r=[ob], w=[B("OUT")])
            else:
                s.dma("pool", k.XS[t0:t0 + 128, :], o_[:], r=[ob], w=[B("XS")])
        s.barrier()

T = 4352; D = 1024; NB = 544
C_MVEC = 128; C_BM8 = 524; C_IDXF = 1024; C_IDXB = 1600
I32 = mybir.dt.int32
TWO_PI = 2.0 * math.pi

def phase_s5(k, l, need_ctx=True):
    nc, s, I, B = k.nc, k.s, k.I, k.B
    cst = k.cst
    V = lambda fn, r=(), w=(): s.op("dve", fn, r=r, w=w)
    A = lambda fn, r=(), w=(): s.op("act", fn, r=r, w=w)
    G = lambda fn, r=(), w=(): s.op("pool", fn, r=r, w=w)
    PE = lambda fn, r=(), w=(): s.op("pe", fn, r=r, w=w)
    with ExitStack() as st:
        T_ = lambda name, shape, dt: st.enter_context(nc.sbuf_tensor(name + k.sfx, shape, dt))
        WS = T_("q_WS", [128, 2, 4, 8, 2, 128], BF16)
        WR = T_("q_WR", [128, 2, 16, 8, 2, 32], BF16)
        KD = T_("q_KD", [128, 2, 4, 8, 128], BF16)
        rho = T_("q_rho", [128, 2, 16], F32)
        th = T_("q_th", [128, 2, 16], F32)
        with ExitStack() as ps:
            TP = lambda name, shape, dt: ps.enter_context(nc.sbuf_tensor(name + k.sfx, shape, dt))
            PP = lambda name, shape, dt: ps.enter_context(nc.psum_tensor(name + k.sfx, shape, dt))
            lam16 = TP("p_lam16", [16, 2, 128], F32)
            lr = TP("p_lr", [128, 16], F32); li = TP("p_li", [128, 16], F32)
            stp = TP("p_stp", [128, 16], F32)
            lrs = TP("p_lrs", [128, 16], F32); lis = TP("p_lis", [128, 16], F32)
            a9 = TP("p_a9", [128, 16, 9], F32); a9b = TP("p_a9b", [128, 16, 9], F32)
            ki = TP("p_ki", [128, 16, 9], I32)
            mag9 = TP("p_mag9", [128, 16, 9], F32)
            Ar = TP("p_Ar", [128, 16, 9], F32); Ai = TP("p_Ai", [128, 16, 9], F32)
            t16 = [TP(f"p_t16_{i}", [128, 16], F32) for i in range(6)]
            Br = TP("p_Br", [128, 16, 16], F32); Bi = TP("p_Bi", [128, 16, 16], F32)
            Bbr = TP("p_Bbr", [128, 16, 16], F32); Bbi = TP("p_Bbi", [128, 16, 16], F32)
            tB = TP("p_tB", [128, 16, 16], F32)
            Cn = [TP(f"p_Cn{i}", [128, 128], F32) for i in range(2)]
            Cr = TP("p_Cr", [128, 16, 16], F32); Ci = TP("p_Ci", [128, 16, 16], F32)
            Zr = TP("p_Zr", [128, 16, 9, 16], F32); Zi = TP("p_Zi", [128, 16, 9, 16], F32)
            tZ = TP("p_tZ", [128, 16, 9, 16], F32)
            Pr = TP("p_Pr", [128, 16, 8, 16], F32); Pi = TP("p_Pi", [128, 16, 8, 16], F32)
            BPr = TP("p_BPr", [128, 16, 128], F32); BPi = TP("p_BPi", [128, 16, 128], F32)
            PPd = [TP(f"p_PPd{i}", [128, 16, 128], BF16) for i in range(2)]
            kdc = TP("p_kdc", [128, 8, 16], F32)
            psT1 = PP("psT1", [128, 128], F32)
            psK = PP("psK", [128, 128], F32)
            psW = [PP(f"psWs{i}", [128, 128], F32) for i in range(2)]
            G(lambda: nc.gpsimd.memset(WR[:], 0.0), w=[B("q_WR")])
            for d in range(2):
                s.dma("sp", lam16[:, 0, :], I["s5_lambda_re"][l, d].rearrange("(a b) n -> a (b n)", b=2), w=[B("p_lam16")])
                s.dma("sp", lam16[:, 1, :], I["s5_lambda_im"][l, d].rearrange("(a b) n -> a (b n)", b=2), w=[B("p_lam16")])
                for j, dst in enumerate([lr, li]):
                    PE(lambda j=j: nc.tensor.transpose(out=psT1[:, 0:16], in_=lam16[:, j, :], identity=cst[0:16, 0:16]), r=[B("p_lam16"), B("cst")], w=[B("psT1")])
                    V(lambda dst=dst: nc.vector.tensor_copy(out=dst[:], in_=psT1[:, 0:16]), w=[B("psT1"), B("p_l")])
                ls = I["s5_log_step"]
                for g2 in range(2):
                    src = bass.AP(ls.tensor, ls.offset + (l * 2 + d) * 32 + g2, [[0, 64], [2, 16]])
                    s.dma("sp", stp[64 * g2:64 * g2 + 64, :], src, w=[B("p_stp")], allow_slow_non_contiguous=True)
                A(lambda: nc.scalar.activation(out=stp[:], in_=stp[:], func=AF.Exp), w=[B("p_stp")])
                V(lambda: nc.vector.tensor_tensor(out=lrs[:], in0=lr[:], in1=stp[:], op=ALU.mult), r=[B("p_l"), B("p_stp")], w=[B("p_ls")])
                V(lambda: nc.vector.tensor_tensor(out=lis[:], in0=li[:], in1=stp[:], op=ALU.mult), r=[B("p_l"), B("p_stp")], w=[B("p_ls")])
                mv = cst[:, C_MVEC:C_MVEC + 9].unsqueeze(1).to_broadcast([128, 16, 9])
                V(lambda: nc.vector.tensor_tensor(out=a9[:], in0=lrs[:].unsqueeze(2).to_broadcast([128, 16, 9]), in1=mv, op=ALU.mult), r=[B("p_ls"), B("cst")], w=[B("p_a9")])
                A(lambda: nc.scalar.activation(out=mag9[:], in_=a9[:], func=AF.Exp), r=[B("p_a9")], w=[B("p_mag9")])
                V(lambda: nc.vector.tensor_tensor(out=a9[:], in0=lis[:].unsqueeze(2).to_broadcast([128, 16, 9]), in1=mv, op=ALU.mult), r=[B("p_ls"), B("cst")], w=[B("p_a9")])
                def reduce_sin(dst, src_ap, shift, shape3):
                    V(lambda: nc.vector.tensor_scalar(out=a9b[:], in0=src_ap, scalar1=shift, scalar2=None, op0=ALU.add), r=[B("p_a9")], w=[B("p_a9b")])
                    V(lambda: nc.vector.tensor_scalar(out=ki[:], in0=a9b[:], scalar1=1.0 / TWO_PI, scalar2=None, op0=ALU.mult), r=[B("p_a9b")], w=[B("p_ki")])
                    V(lambda: nc.vector.scalar_tensor_tensor(out=a9b[:], in0=ki[:], scalar=-TWO_PI, in1=a9b[:], op0=ALU.mult, op1=ALU.add), r=[B("p_ki")], w=[B("p_a9b")])
                    A(lambda: nc.scalar.activation(out=dst[:], in_=a9b[:], func=AF.Sin), r=[B("p_a9b")], w=[B("p_sc")])
                reduce_sin(Ai, a9[:], 0.0, None)
                V(lambda d=d: nc.vector.tensor_copy(out=th[:, d, :], in_=a9b[:, :, 8]), r=[B("p_a9b")], w=[B("q_th")])
                reduce_sin(Ar, a9[:], math.pi / 2.0, None)
                V(lambda: nc.vector.tensor_tensor(out=Ar[:], in0=Ar[:], in1=mag9[:], op=ALU.mult), r=[B("p_mag9")], w=[B("p_sc")])
                V(lambda: nc.vector.tensor_tensor(out=Ai[:], in0=Ai[:], in1=mag9[:], op=ALU.mult), r=[B("p_mag9")], w=[B("p_sc")])
                V(lambda d=d: nc.vector.tensor_copy(out=rho[:, d, :], in_=mag9[:, :, 8]), r=[B("p_mag9")], w=[B("q_rho")])
                am1, den, fr, fi, u1, u2 = t16
                V(lambda: nc.vector.tensor_scalar(out=am1[:], in0=Ar[:, :, 1], scalar1=-1.0, scalar2=None, op0=ALU.add), r=[B("p_sc")], w=[B("p_t16")])
                V(lambda: nc.vector.tensor_tensor(out=den[:], in0=lr[:], in1=lr[:], op=ALU.mult), r=[B("p_l")], w=[B("p_t16")])
                V(lambda: nc.vector.tensor_tensor(out=u1[:], in0=li[:], in1=li[:], op=ALU.mult), r=[B("p_l")], w=[B("p_t16")])
                V(lambda: nc.vector.tensor_tensor(out=den[:], in0=den[:], in1=u1[:], op=ALU.add), w=[B("p_t16")])
                V(lambda: nc.vector.reciprocal(out=den[:], in_=den[:]), w=[B("p_t16")])
                V(lambda: nc.vector.tensor_tensor(out=u1[:], in0=am1[:], in1=lr[:], op=ALU.mult), w=[B("p_t16")])
                V(lambda: nc.vector.tensor_tensor(out=u2[:], in0=Ai[:, :, 1], in1=li[:], op=ALU.mult), w=[B("p_t16")])
                V(lambda: nc.vector.tensor_tensor(out=fr[:], in0=u1[:], in1=u2[:], op=ALU.add), w=[B("p_t16")])
                V(lambda: nc.vector.tensor_tensor(out=fr[:], in0=fr[:], in1=den[:], op=ALU.mult), w=[B("p_t16")])
                V(lambda: nc.vector.tensor_tensor(out=u1[:], in0=Ai[:, :, 1], in1=lr[:], op=ALU.mult), w=[B("p_t16")])
                V(lambda: nc.vector.tensor_tensor(out=u2[:], in0=am1[:], in1=li[:], op=ALU.mult), w=[B("p_t16")])
                V(lambda: nc.vector.tensor_tensor(out=fi[:], in0=u1[:], in1=u2[:], op=ALU.subtract), w=[B("p_t16")])
                V(lambda: nc.vector.tensor_tensor(out=fi[:], in0=fi[:], in1=den[:], op=ALU.mult), w=[B("p_t16")])
                for j, (dst, nm) in enumerate([(Br, "s5_b_re"), (Bi, "s5_b_im")]):
                    bt = I[nm]
                    for g2 in range(2):
                        src = bass.AP(bt.tensor, bt.offset + ((l * 2 + d) * 32 + g2) * 1024, [[16, 64], [2048, 16], [1, 16]])
                        s.dma("sp", dst[64 * g2:64 * g2 + 64, :, :], src, w=[B("p_B")])
                frb = fr[:].unsqueeze(2).to_broadcast([128, 16, 16]); fib = fi[:].unsqueeze(2).to_broadcast([128, 16, 16])
                V(lambda: nc.vector.tensor_tensor(out=Bbr[:], in0=Br[:], in1=frb, op=ALU.mult), r=[B("p_B"), B("p_t16")], w=[B("p_Bb")])
                V(lambda: nc.vector.tensor_tensor(out=tB[:], in0=Bi[:], in1=fib, op=ALU.mult), r=[B("p_B"), B("p_t16")], w=[B("p_tB")])
                V(lambda: nc.vector.tensor_tensor(out=Bbr[:], in0=Bbr[:], in1=tB[:], op=ALU.subtract), r=[B("p_tB")], w=[B("p_Bb")])
                V(lambda: nc.vector.tensor_tensor(out=Bbi[:], in0=Bi[:], in1=frb, op=ALU.mult), r=[B("p_B"), B("p_t16")], w=[B("p_Bb")])
                V(lambda: nc.vector.tensor_tensor(out=tB[:], in0=Br[:], in1=fib, op=ALU.mult), r=[B("p_B"), B("p_t16")], w=[B("p_tB")])
                V(lambda: nc.vector.tensor_tensor(out=Bbi[:], in0=Bbi[:], in1=tB[:], op=ALU.add), r=[B("p_tB")], w=[B("p_Bb")])
                for j, (dst, nm) in enumerate([(Cr, "s5_c_re"), (Ci, "s5_c_im")]):
                    ct_ = I[nm]
                    for pset in range(2):
                        cn = Cn[pset]; cnb = B(f"p_Cn{pset}")
                        for pr in range(8):
                            g0 = 2 * (8 * pset + pr)
                            src = bass.AP(ct_.tensor, ct_.offset + ((l * 2 + d) * 32 + g0) * 1024, [[64, 16], [1024, 2], [1, 64]])
                            s.dma("sp", cn[16 * pr:16 * pr + 16, :].rearrange("p (a n) -> p a n", a=2), src, w=[cnb])
                        PE(lambda cn=cn: nc.tensor.transpose(out=psT1[:], in_=cn[:], identity=cst[:, 0:128]), r=[cnb, B("cst")], w=[B("psT1")])
                        V(lambda dst=dst, pset=pset: nc.vector.tensor_copy(out=dst[:, 8 * pset:8 * pset + 8, :], in_=psT1[:].rearrange("p (a k) -> p a k", k=16)), w=[B("psT1"), B("p_C")])
                Crb = lambda X: X[:].unsqueeze(2).to_broadcast([128, 16, 9, 16])
                Ab = lambda X: X[:].unsqueeze(3).to_broadcast([128, 16, 9, 16])
                V(lambda: nc.vector.tensor_tensor(out=Zr[:], in0=Crb(Cr), in1=Ab(Ar), op=ALU.mult), r=[B("p_C"), B("p_sc")], w=[B("p_Z")])
                G(lambda: nc.gpsimd.tensor_tensor(out=tZ[:], in0=Crb(Ci), in1=Ab(Ai), op=ALU.mult), r=[B("p_C"), B("p_sc")], w=[B("p_tZ")])
                V(lambda: nc.vector.tensor_tensor(out=Zr[:], in0=Zr[:], in1=tZ[:], op=ALU.subtract), r=[B("p_tZ")], w=[B("p_Z")])
                V(lambda: nc.vector.tensor_tensor(out=Zi[:], in0=Crb(Cr), in1=Ab(Ai), op=ALU.mult), r=[B("p_C"), B("p_sc")], w=[B("p_Z")])
                G(lambda: nc.gpsimd.tensor_tensor(out=tZ[:], in0=Crb(Ci), in1=Ab(Ar), op=ALU.mult), r=[B("p_C"), B("p_sc")], w=[B("p_tZ")])
                V(lambda: nc.vector.tensor_tensor(out=Zi[:], in0=Zi[:], in1=tZ[:], op=ALU.add), r=[B("p_tZ")], w=[B("p_Z")])
                for g2 in range(2):
                    sl = slice(64 * g2, 64 * g2 + 64)
                    V(lambda sl=sl, g2=g2, d=d: nc.vector.tensor_copy(out=WR[sl, d, :, :, 0, 16 * g2:16 * g2 + 16], in_=Zr[sl, :, 1:9, :]), r=[B("p_Z")], w=[B("q_WR")])
                    V(lambda sl=sl, g2=g2, d=d: nc.vector.tensor_scalar(out=WR[sl, d, :, :, 1, 16 * g2:16 * g2 + 16], in0=Zi[sl, :, 1:9, :], scalar1=-1.0, scalar2=None, op0=ALU.mult), r=[B("p_Z")], w=[B("q_WR")])
                Bb8 = lambda X: X[:].unsqueeze(2).to_broadcast([128, 16, 8, 16])
                A8 = lambda X: X[:, :, 0:8].unsqueeze(3).to_broadcast([128, 16, 8, 16])
                tP = tZ[:, :, 0:8, :]
                V(lambda: nc.vector.tensor_tensor(out=Pr[:], in0=Bb8(Bbr), in1=A8(Ar), op=ALU.mult), r=[B("p_Bb"), B("p_sc")], w=[B("p_P")])
                G(lambda: nc.gpsimd.tensor_tensor(out=tP, in0=Bb8(Bbi), in1=A8(Ai), op=ALU.mult), r=[B("p_Bb"), B("p_sc")], w=[B("p_tZ")])
                V(lambda: nc.vector.tensor_tensor(out=Pr[:], in0=Pr[:], in1=tP, op=ALU.subtract), r=[B("p_tZ")], w=[B("p_P")])
                V(lambda: nc.vector.tensor_tensor(out=Pi[:], in0=Bb8(Bbi), in1=A8(Ar), op=ALU.mult), r=[B("p_Bb"), B("p_sc")], w=[B("p_P")])
                G(lambda: nc.gpsimd.tensor_tensor(out=tP, in0=Bb8(Bbr), in1=A8(Ai), op=ALU.mult), r=[B("p_Bb"), B("p_sc")], w=[B("p_tZ")])
                V(lambda: nc.vector.tensor_tensor(out=Pi[:], in0=Pi[:], in1=tP, op=ALU.add), r=[B("p_tZ")], w=[B("p_P")])
                G(lambda: nc.gpsimd.memset(BPr[:], 0.0), w=[B("p_BP")])
                G(lambda: nc.gpsimd.memset(BPi[:], 0.0), w=[B("p_BP")])
                for g2 in range(2):
                    sl = slice(64 * g2, 64 * g2 + 64)
                    for q in range(4):
                        c0 = 32 * q + 16 * g2
                        V(lambda sl=sl, q=q, c0=c0: nc.vector.tensor_copy(out=BPr[sl, q::4, c0:c0 + 16], in_=Bbr[sl, q::4, :]), r=[B("p_Bb")], w=[B("p_BP")])
                        V(lambda sl=sl, q=q, c0=c0: nc.vector.tensor_scalar(out=BPi[sl, q::4, c0:c0 + 16], in0=Bbi[sl, q::4, :], scalar1=-1.0, scalar2=None, op0=ALU.mult), r=[B("p_Bb")], w=[B("p_BP")])
                for t in range(4):
                    n_ = 0
                    for q in range(4):
                        pr = 4 * t + q
                        for (BP_, Z_) in ((BPr, Zr), (BPi, Zi)):
                            PE(lambda pr=pr, BP_=BP_, Z_=Z_, n_=n_: nc.tensor.matmul(psK[:], lhsT=BP_[:, pr, :], rhs=Z_[:, pr, 0:8, :].rearrange("p a k -> p (a k)"), start=(n_ == 0), stop=(n_ == 7)),
                               r=[B("p_BP"), B("p_Z")], w=[B("psK")])
                            n_ += 1
                    V(lambda: nc.vector.tensor_copy(out=kdc[:], in_=psK[:].rearrange("p (a k) -> p a k", k=16)), w=[B("psK"), B("p_kdc")])
                    V(lambda t=t, d=d: nc.vector.tensor_tensor(out=KD[:, d, t, :, :].rearrange("p a (g k) -> p a g k", k=16), in0=kdc[:].unsqueeze(2).to_broadcast([128, 8, 8, 16]),
                                                          in1=cst[:, C_BM8:C_BM8 + 8].unsqueeze(1).unsqueeze(3).to_broadcast([128, 8, 8, 16]), op=ALU.mult), r=[B("p_kdc"), B("cst")], w=[B("q_KD")])
                n_w = 0
                for m in range(8):
                    for ri, P_ in enumerate((Pr, Pi)):
                        pd = PPd[n_w % 2]; pdb = B(f"p_PPd{n_w % 2}")
                        G(lambda pd=pd: nc.gpsimd.memset(pd[:], 0.0), w=[pdb])
                        for g2 in range(2):
                            sl = slice(64 * g2, 64 * g2 + 64)
                            for q in range(4):
                                c0 = 32 * q + 16 * g2
                                e = V if (q % 2 == 0) else G
                                eng = nc.vector if (q % 2 == 0) else nc.gpsimd
                                e(lambda sl=sl, q=q, c0=c0, pd=pd, P_=P_, m=m, eng=eng: eng.tensor_copy(out=pd[sl, q::4, c0:c0 + 16], in_=P_[sl, q::4, m, :]), r=[B("p_P")], w=[pdb])
                        for t in range(4):
                            pw = psW[t % 2]; pwb = B(f"psWs{t % 2}")
                            for q in range(4):
                                PE(lambda pw=pw, pd=pd, t=t, q=q: nc.tensor.matmul(pw[:], lhsT=pd[:, 4 * t + q, :], rhs=k.identb[:], start=(q == 0), stop=(q == 3)), r=[pdb, B("identb")], w=[pwb])
                            if t % 2 == 0:
                                A(lambda pw=pw, t=t, m=m, ri=ri, d=d: nc.scalar.copy(out=WS[:, d, t, m, ri, :], in_=pw[:]), w=[pwb, B("q_WS")])
                            else:
                                V(lambda pw=pw, t=t, m=m, ri=ri, d=d: nc.vector.tensor_copy(out=WS[:, d, t, m, ri, :], in_=pw[:]), w=[pwb, B("q_WS")])
                        n_w += 1
            s.barrier()
        if getattr(k, 'stop', None) == 's5prep':
            return
        phase_s5_main(k, l, need_ctx, WS, WR, KD, rho, th, st)
        s.barrier()


def phase_s5_main(k, l, need_ctx, WS, WR, KD, rho, th, st):
    nc, s, I, B = k.nc, k.s, k.I, k.B
    cst = k.cst
    V = lambda fn, r=(), w=(): s.op("dve", fn, r=r, w=w)
    A = lambda fn, r=(), w=(): s.op("act", fn, r=r, w=w)
    G = lambda fn, r=(), w=(): s.op("pool", fn, r=r, w=w)
    PE = lambda fn, r=(), w=(): s.op("pe", fn, r=r, w=w)
    T_ = lambda name, shape, dt: st.enter_context(nc.sbuf_tensor(name + k.sfx, shape, dt))
    P_ = lambda name, shape, dt: st.enter_context(nc.psum_tensor(name + k.sfx, shape, dt))
    uT = st.enter_context(nc.sbuf_tensor("q_uT" + k.sfx, [128, 4, T], BF16))
    mst = ExitStack()
    T_ = lambda name, shape, dt: mst.enter_context(nc.sbuf_tensor(name + k.sfx, shape, dt))
    hst = [T_(f"q_hst{i}", [128, 2, NB], BF16) for i in range(2)]
    SL = []
    for i_ in range(2):
        o = {}
        for nm in ("Sr", "Si", "cosT", "sinT", "ang", "xr", "xi", "t1", "t2", "Gr", "Gi"):
            o[nm] = T_(f"q_{nm}{i_}", [128, NB], F32)
        o["kiT"] = T_(f"q_ki{i_}", [128, NB], I32)
        o["psS"] = [mst.enter_context(nc.psum_tensor(f"psSq{i_}_{j}" + k.sfx, [128, 1024], F32)) for j in range(2)]
        SL.append(o)
    for t in range(4):
        s.dma("sp", uT[:, t, :], k.PB[1536 + t * 128:1536 + (t + 1) * 128, :], r=[B("PB")], w=[B("q_uT")])
    pieces = [(32, 288, 0), (288, 544, 256), (0, 32, 512)]
    def it_gen(d, pr, si):
        o = SL[si]
        Sr, Si, cosT, sinT, ang, xr, xi, t1, t2, Gr, Gi, kiT, psS = (o[n] for n in ('Sr','Si','cosT','sinT','ang','xr','xi','t1','t2','Gr','Gi','kiT','psS'))
        sfx_ = str(si)
        t = pr // 4; q = pr % 4
        rows = slice(32 * q, 32 * q + 32)
        for ri in range(2):
            for (b0, b1, pc) in pieces:
                for pos in range(8):
                    m = 7 - pos if d == 0 else pos
                    rhs = uT[rows, t, 8 * b0 + pos:8 * b1:8]
                    PE(lambda ri=ri, pc=pc, b0=b0, b1=b1, m=m, rhs=rhs, pos=pos: nc.tensor.matmul(psS[ri][:, pc:pc + (b1 - b0)], lhsT=WS[rows, d, t, m, ri, :], rhs=rhs, start=(pos == 0), stop=(pos == 7),
                                                                                        tile_position=(32 * q, 0), skip_group_check=True),
                       r=[B("q_uT"), B("q_WS")], w=[B(f"psSq{ri}_" + sfx_)])
        yield
        for ri, dst in enumerate((Sr, Si)):
            A(lambda ri=ri, dst=dst: nc.scalar.copy(out=dst[:, 32:544], in_=psS[ri][:, 0:512]), w=[B(f"psSq{ri}_" + sfx_), B("q_S" + sfx_)])
            A(lambda ri=ri, dst=dst: nc.scalar.copy(out=dst[:, 0:32], in_=psS[ri][:, 512:544]), w=[B(f"psSq{ri}_" + sfx_), B("q_S" + sfx_)])
        yield
        idx = cst[:, C_IDXF:C_IDXF + NB] if d == 0 else cst[:, C_IDXB:C_IDXB + NB]
        for (dst, shift) in ((sinT, 0.0), (cosT, math.pi / 2.0)):
            V(lambda shift=shift: nc.vector.tensor_scalar(out=ang[:], in0=idx, scalar1=th[:, d, pr:pr + 1], scalar2=shift, op0=ALU.mult, op1=ALU.add), r=[B("q_th"), B("cst")], w=[B("q_ang" + sfx_)])
            V(lambda: nc.vector.tensor_scalar(out=kiT[:], in0=ang[:], scalar1=1.0 / TWO_PI, scalar2=None, op0=ALU.mult), r=[B("q_ang" + sfx_)], w=[B("q_ki" + sfx_)])
            V(lambda: nc.vector.scalar_tensor_tensor(out=ang[:], in0=kiT[:], scalar=-TWO_PI, in1=ang[:], op0=ALU.mult, op1=ALU.add), r=[B("q_ki" + sfx_)], w=[B("q_ang" + sfx_)])
            A(lambda dst=dst: nc.scalar.activation(out=dst[:], in_=ang[:], func=AF.Sin), r=[B("q_ang" + sfx_)], w=[B("q_tw" + sfx_)])
        yield
        V(lambda: nc.vector.tensor_tensor(out=xr[:], in0=Sr[:], in1=cosT[:], op=ALU.mult), r=[B("q_S" + sfx_), B("q_tw" + sfx_)], w=[B("q_xr" + sfx_)])
        G(lambda: nc.gpsimd.tensor_tensor(out=t1[:], in0=Si[:], in1=sinT[:], op=ALU.mult), r=[B("q_S" + sfx_), B("q_tw" + sfx_)], w=[B("q_t1" + sfx_)])
        V(lambda: nc.vector.tensor_tensor(out=xr[:], in0=xr[:], in1=t1[:], op=ALU.add), r=[B("q_t1" + sfx_)], w=[B("q_xr" + sfx_)])
        G(lambda: nc.gpsimd.tensor_tensor(out=xi[:], in0=Si[:], in1=cosT[:], op=ALU.mult), r=[B("q_S" + sfx_), B("q_tw" + sfx_)], w=[B("q_xi" + sfx_)])
        V(lambda: nc.vector.tensor_tensor(out=t2[:], in0=Sr[:], in1=sinT[:], op=ALU.mult), r=[B("q_S" + sfx_), B("q_tw" + sfx_)], w=[B("q_t2" + sfx_)])
        G(lambda: nc.gpsimd.tensor_tensor(out=xi[:], in0=xi[:], in1=t2[:], op=ALU.subtract), r=[B("q_t2" + sfx_)], w=[B("q_xi" + sfx_)])
        yield
        rcol = rho[:, d, pr:pr + 1]
        for (src, dst, nm) in ((xr, Gr, "q_xr"), (xi, Gi, "q_xi")):
            if d == 0:
                V(lambda src=src, dst=dst: nc.vector.tensor_tensor_scan(out=dst[:], data0=rcol.to_broadcast([128, NB]), data1=src[:], initial=0.0, op0=ALU.mult, op1=ALU.add),
                  r=[B(nm), B("q_rho")], w=[B("q_G" + sfx_)])
            else:
                rv = lambda X, a, b: bass.AP(X[:].tensor, X[:, b - 1:b].offset, [list(X[:].ap[0]), [-1, b - a]])
                V(lambda src=src, dst=dst: nc.vector.tensor_tensor_scan(out=rv(dst, 0, 32), data0=rcol.to_broadcast([128, 32]), data1=rv(src, 0, 32), initial=0.0, op0=ALU.mult, op1=ALU.add),
                  r=[B(nm), B("q_rho")], w=[B("q_G" + sfx_)])
                V(lambda src=src, dst=dst: nc.vector.tensor_tensor_scan(out=rv(dst, 32, 544), data0=rcol.to_broadcast([128, 512]), data1=rv(src, 32, 544), initial=dst[:, 0:1], op0=ALU.mult, op1=ALU.add),
                  r=[B(nm), B("q_rho")], w=[B("q_G" + sfx_)])
        yield
        hs_ = hst[si]; hsb = B(f"q_hst{si}")
        if d == 0:
            G(lambda hs_=hs_: nc.gpsimd.memset(hs_[:, :, 0:1], 0.0), w=[hsb])
        if d == 0:
            so, si_ = slice(1, 544), slice(0, 543)
        else:
            so, si_ = slice(0, 543), slice(1, 544)
        V(lambda: nc.vector.tensor_tensor(out=t1[:], in0=Gr[:], in1=cosT[:], op=ALU.mult), r=[B("q_G" + sfx_), B("q_tw" + sfx_)], w=[B("q_t1" + sfx_)])
        G(lambda: nc.gpsimd.tensor_tensor(out=t2[:], in0=Gi[:], in1=sinT[:], op=ALU.mult), r=[B("q_G" + sfx_), B("q_tw" + sfx_)], w=[B("q_t2" + sfx_)])
        V(lambda: nc.vector.tensor_tensor(out=hs_[:, 0, so], in0=t1[:, si_], in1=t2[:, si_], op=ALU.subtract), r=[B("q_t1" + sfx_), B("q_t2" + sfx_)], w=[hsb])
        if d == 1:
            V(lambda: nc.vector.tensor_tensor(out=hs_[:, 0, 543:544], in0=t1[:, 0:1], in1=t2[:, 0:1], op=ALU.subtract), r=[B("q_t1" + sfx_), B("q_t2" + sfx_)], w=[hsb])
        G(lambda: nc.gpsimd.tensor_tensor(out=xr[:], in0=Gr[:], in1=sinT[:], op=ALU.mult), r=[B("q_G" + sfx_), B("q_tw" + sfx_)], w=[B("q_xr" + sfx_)])
        V(lambda: nc.vector.tensor_tensor(out=xi[:], in0=Gi[:], in1=cosT[:], op=ALU.mult), r=[B("q_G" + sfx_), B("q_tw" + sfx_)], w=[B("q_xi" + sfx_)])
        G(lambda: nc.gpsimd.tensor_tensor(out=hs_[:, 1, so], in0=xr[:, si_], in1=xi[:, si_], op=ALU.add), r=[B("q_xr" + sfx_), B("q_xi" + sfx_)], w=[hsb])
        if d == 1:
            G(lambda: nc.gpsimd.tensor_tensor(out=hs_[:, 1, 543:544], in0=xr[:, 0:1], in1=xi[:, 0:1], op=ALU.add), r=[B("q_xr" + sfx_), B("q_xi" + sfx_)], w=[hsb])
            G(lambda: nc.gpsimd.memset(hs_[:, :, 31:32], 0.0), w=[hsb])
        s.dma("sp", k.HD[d, pr].rearrange("r p c -> p r c"), hs_[:], r=[hsb], w=[B("HD")])

    def interleave(gens):
        gens = list(gens)
        while gens:
            for g_ in list(gens):
                try:
                    next(g_)
                except StopIteration:
                    gens.remove(g_)
    its = [(d, pr) for d in range(2) for pr in range(16)]
    for i_ in range(0, len(its), 2):
        interleave([it_gen(its[i_][0], its[i_][1], 0), it_gen(its[i_ + 1][0], its[i_ + 1][1], 1)])
    s.barrier()
    mst.close()
    if getattr(k, 'stop', None) == 's5main':
        return
    s5_glu(k, l, need_ctx, WR, KD, uT, None, st)


def s5_glu(k, l, need_ctx, WR, KD, uT, Hin, st):
    nc, s, I, B = k.nc, k.s, k.I, k.B
    V = lambda fn, r=(), w=(): s.op("dve", fn, r=r, w=w)
    A = lambda fn, r=(), w=(): s.op("act", fn, r=r, w=w)
    G = lambda fn, r=(), w=(): s.op("pool", fn, r=r, w=w)
    PE = lambda fn, r=(), w=(): s.op("pe", fn, r=r, w=w)
    T_ = lambda name, shape, dt: st.enter_context(nc.sbuf_tensor(name + k.sfx, shape, dt))
    P_ = lambda name, shape, dt: st.enter_context(nc.psum_tensor(name + k.sfx, shape, dt))
    wg32 = T_("g_wg32", [128, 4, 512], F32); wg = T_("g_wg", [128, 4, 512], BF16)
    bg = T_("g_bg", [128, 4], F32); dsk = T_("g_dsk", [128, 4], F32)
    dgs = T_("g_dgs", [128, 4, 128], BF16)
    y32 = T_("g_y", [128, 512], F32); y2 = T_("g_y2", [128, 512], F32)
    vT = [T_(f"g_v{i}", [128, 4, 512], BF16) for i in range(2)]
    v32 = T_("g_v32", [128, 4, 512], F32)
    gs = [T_(f"g_gs{i}", [128, 512], F32) for i in range(2)]
    o1 = T_("g_o1", [128, 512], F32)
    ob = [T_(f"g_ob{i}", [128, 512], BF16) for i in range(2)]
    Hc = [T_(f"g_Hc{i}", [128, 64, 64], BF16) for i in range(2)]
    psY = [P_(f"psYq{i}", [128, 512], F32) for i in range(2)]
    psG = [P_(f"psGq{i}", [128, 512], F32) for i in range(2)]
    s.dma("sp", wg32[:], I["s5_w_glu"][l].rearrange("(t p) c -> p t c", p=128), w=[B("g_wg32")])
    s.dma("sp", bg[:], I["s5_b_glu"][l].rearrange("(t p) -> p t", p=128), w=[B("g_bg")], allow_slow_non_contiguous=True)
    s.dma("sp", dsk[:], I["s5_d"][l].rearrange("(t p) -> p t", p=128), w=[B("g_dsk")], allow_slow_non_contiguous=True)
    V(lambda: nc.vector.tensor_copy(out=wg[:], in_=wg32[:]), r=[B("g_wg32")], w=[B("g_wg")])
    for t in range(4):
        V(lambda t=t: nc.vector.tensor_scalar(out=dgs[:, t, :], in0=k.identb[:], scalar1=dsk[:, t:t + 1], scalar2=None, op0=ALU.mult), r=[B("g_dsk"), B("identb")], w=[B("g_dgs")])
    chunks = [(0, 256)] if need_ctx else []
    chunks += [(256 + i * 512, 512) for i in range(8)]
    for ci, (t0, ntok) in enumerate(chunks):
        nb = ntok // 8; b0 = t0 // 8
        v_ = vT[ci % 2]; vb = B(f"g_v{ci % 2}")
        hc_ = Hc[ci % 2]; hcb = B(f"g_Hc{ci % 2}")
        for dd in range(2):
            for qq in range(4):
                s.dma("sp", hc_[:, dd * 32 + qq * 8:dd * 32 + qq * 8 + 8, 0:nb], k.HD[dd, qq * 4:qq * 4 + 4, :, :, b0:b0 + nb].rearrange("q r p c -> p (q r) c"), r=[B("HD")], w=[hcb])
        for t in range(4):
            py = psY[t % 2]; pyb = B(f"psYq{t % 2}")
            PE(lambda py=py, t=t, t0=t0, ntok=ntok: nc.tensor.matmul(py[:, 0:ntok], lhsT=dgs[:, t, :], rhs=uT[:, t, t0:t0 + ntok], start=True, stop=False, skip_group_check=True),
               r=[B("g_dgs"), B("q_uT")], w=[pyb])
            for d in range(2):
                for o in range(8):
                    srcs = range(0, o + 1) if d == 0 else range(o, 8)
                    for o2 in srcs:
                        lag = abs(o - o2)
                        PE(lambda py=py, t=t, d=d, lag=lag, o=o, o2=o2, t0=t0, ntok=ntok: nc.tensor.matmul(py[:, o:ntok:8], lhsT=KD[:, d, t, lag, :], rhs=uT[:, t, t0 + o2:t0 + ntok:8], start=False, stop=False, skip_group_check=True),
                           r=[B("q_KD"), B("q_uT")], w=[pyb])
                for q in range(4):
                    pr = 4 * t + q
                    for o in range(8):
                        i_ = o if d == 0 else 7 - o
                        for ri in range(2):
                            last = (d == 1 and q == 3 and o == 7 and ri == 1)
                            PE(lambda py=py, d=d, pr=pr, i_=i_, ri=ri, o=o, q=q, b0=b0, nb=nb, ntok=ntok, last=last, hc_=hc_: nc.tensor.matmul(py[32 * q:32 * q + 32, o:ntok:8], lhsT=WR[:, d, pr, i_, ri, :], rhs=hc_[:, (d * 16 + pr) * 2 + ri, 0:nb],
                                                                                                              start=False, stop=last, tile_position=(0, 32 * q), skip_group_check=True),
                               r=[B("q_WR"), hcb], w=[pyb])
            A(lambda py=py, ntok=ntok: nc.scalar.copy(out=y32[:, 0:ntok], in_=py[:, 0:ntok]), w=[pyb, B("g_y")])
            G(lambda ntok=ntok: nc.gpsimd.tensor_tensor(out=y2[:, 0:ntok], in0=y32[:, 0:ntok], in1=y32[:, 0:ntok], op=ALU.mult), r=[B("g_y")], w=[B("g_y2")])
            V(lambda ntok=ntok: nc.vector.tensor_scalar(out=y2[:, 0:ntok], in0=y2[:, 0:ntok], scalar1=0.044715, scalar2=1.0, op0=ALU.mult, op1=ALU.add), w=[B("g_y2")])
            V(lambda ntok=ntok: nc.vector.tensor_tensor(out=y2[:, 0:ntok], in0=y2[:, 0:ntok], in1=y32[:, 0:ntok], op=ALU.mult), r=[B("g_y")], w=[B("g_y2")])
            A(lambda ntok=ntok: nc.scalar.activation(out=y2[:, 0:ntok], in_=y2[:, 0:ntok], func=AF.Sigmoid, scale=1.5957691216), w=[B("g_y2")])
            V(lambda ntok=ntok, t=t: nc.vector.tensor_tensor(out=v32[:, t, 0:ntok], in0=y2[:, 0:ntok], in1=y32[:, 0:ntok], op=ALU.mult), r=[B("g_y"), B("g_y2")], w=[B("g_v32")])
            G(lambda ntok=ntok, t=t, v_=v_: nc.gpsimd.tensor_copy(out=v_[:, t, 0:ntok], in_=v32[:, t, 0:ntok]), r=[B("g_v32")], w=[vb])
        for ct in range(4):
            pg = psG[ct % 2]; pgb = B(f"psGq{ct % 2}")
            g_ = gs[ct % 2]; gb = B(f"g_gs{ct % 2}")
            s.dma("sp", g_[:, 0:ntok], k.PF[32 + ct * 128:32 + (ct + 1) * 128, t0:t0 + ntok], r=[B("PF")], w=[gb])
            for t in range(4):
                PE(lambda pg=pg, t=t, ct=ct, v_=v_, ntok=ntok: nc.tensor.matmul(pg[:, 0:ntok], lhsT=wg[:, t, ct * 128:(ct + 1) * 128], rhs=v_[:, t, 0:ntok], start=(t == 0), stop=(t == 3)),
                   r=[B("g_wg"), vb], w=[pgb])
            A(lambda pg=pg, ct=ct, ntok=ntok: nc.scalar.activation(out=o1[:, 0:ntok], in_=pg[:, 0:ntok], func=AF.Sigmoid, bias=bg[:, ct:ct + 1]), r=[B("g_bg")], w=[pgb, B("g_o1")])
            V(lambda ct=ct, ntok=ntok: nc.vector.tensor_tensor(out=o1[:, 0:ntok], in0=o1[:, 0:ntok], in1=v32[:, ct, 0:ntok], op=ALU.mult), r=[B("g_v32")], w=[B("g_o1")])
            A(lambda g_=g_, ntok=ntok: nc.scalar.activation(out=g_[:, 0:ntok], in_=g_[:, 0:ntok], func=AF.Silu), w=[gb])
            o_ = ob[ct % 2]; obb = B(f"g_ob{ct % 2}")
            V(lambda o_=o_, g_=g_, ntok=ntok: nc.vector.tensor_tensor(out=o_[:, 0:ntok], in0=o1[:, 0:ntok], in1=g_[:, 0:ntok], op=ALU.mult), r=[gb, B("g_o1")], w=[obb])
            s.dma("pool", k.MT[1024 + ct * 128:1024 + (ct + 1) * 128, t0:t0 + ntok], o_[:, 0:ntok], r=[obb], w=[B("MT")])


D = 1024; T = 4352; TC = 256; TL = 4096; NT = 34; DEPTH = 4
DIN = 4640; EPS = 1e-6

class K:
    pass

def dram(nc, name, shape, dt, kind="Internal"):
    return nc.dram_tensor(name, list(shape), dt, kind=kind).ap()

def build(nlayers=DEPTH, stop=None, dbg=()):
    nc = bass.Bass("TRN2", target_bir_lowering=False)
    s = Sched(nc)
    k = K(); k.nc = nc; k.s = s; k.dbg = dbg; k.stop = stop; k.sfx = ''
    I = {}
    def inp(name, shape):
        I[name] = dram(nc, name, shape, F32, "ExternalInput")
    inp("x", [TL, D]); inp("c", [D]); inp("ctx", [TC, D]); inp("c_ctx", [D])
    inp("w_mod", [DEPTH, D, 3 * D]); inp("b_mod", [DEPTH, 3 * D]); inp("g_pre", [DEPTH, D]); inp("g_post", [DEPTH, D])
    inp("w_in", [DEPTH, D, DIN]); inp("conv_w", [DEPTH, 9, 1536]); inp("conv_b", [DEPTH, 1536])
    inp("dt_bias", [DEPTH, 32]); inp("a_log", [DEPTH, 32]); inp("d_ssd", [DEPTH, 16]); inp("g_ssd_norm", [DEPTH, D])
    inp("s5_lambda_re", [DEPTH, 2, 32, 64]); inp("s5_lambda_im", [DEPTH, 2, 32, 64]); inp("s5_log_step", [DEPTH, 2, 32])
    inp("s5_b_re", [DEPTH, 2, 32, 64, 16]); inp("s5_b_im", [DEPTH, 2, 32, 64, 16])
    inp("s5_c_re", [DEPTH, 2, 32, 16, 64]); inp("s5_c_im", [DEPTH, 2, 32, 16, 64])
    inp("s5_d", [DEPTH, 512]); inp("s5_w_glu", [DEPTH, 512, 512]); inp("s5_b_glu", [DEPTH, 512])
    inp("fnet_w", [DEPTH, 512, 512]); inp("fnet_b", [DEPTH, 512]); inp("w_out", [DEPTH, 2048, D])
    inp("cst", [128, 2304])
    k.I = I
    k.out = dram(nc, "out", [TL, D], F32, "ExternalOutput")
    def scr(name, shape, dt):
        return dram(nc, name, shape, dt, "ExternalOutput" if name in dbg else "Internal")
    k.XS = scr("XS", [T, D], F32)
    k.ZT = scr("ZT", [T, D], F32)
    k.PF = scr("PF", [1056, T], F32)
    k.PB = scr("PB", [2560, T], BF16)
    k.XC = scr("XC", [1536, T], BF16)
    k.MT = scr("MT", [2048, T], BF16)
    k.YF = scr("YF", [T, D], F32)
    k.MODS = scr("MODS", [DEPTH, 2, 3, D], F32)
    k.CL = scr("CL", [4096, 4096], BF16)
    k.HD = scr("HD", [2, 16, 2, 128, 544], BF16)
    k.SLN = scr("SLN", [4096, 4096], BF16)
    k.bufs = {}
    def B(name):
        if name not in k.bufs:
            k.bufs[name] = Buf(name)
        return k.bufs[name]
    k.B = B
    k.cst = nc.alloc_sbuf_tensor("cst_sb", [128, 2304], F32)
    k.identb = nc.alloc_sbuf_tensor("identb", [128, 128], BF16)
    s.dma("sp", k.cst[:], I["cst"][:, :], w=[B("cst")])
    s.op("dve", lambda: nc.vector.tensor_copy(out=k.identb[:], in_=k.cst[:, 0:128]), r=[B("cst")], w=[B("identb")])
    k.negpi = nc.alloc_sbuf_tensor("negpi", [128, 1], F32)
    s.op("dve", lambda: nc.vector.memset(k.negpi[:], -3.14159265), w=[B("negpi")])
    prep_mods(k)
    gen_dft(k)
    for l in range(nlayers):
        k.sfx = f'_L{l}'
        phase_ab(k, l)
        if stop == "ab":
            break
        phase_conv(k, l)
        if stop == "conv":
            break
        phase_ssd(k, l, need_ctx=(l < DEPTH - 1))
        if stop == "ssd":
            break
        phase_s5(k, l, need_ctx=(l < DEPTH - 1))
        if stop in ("s5", "s5prep", "s5main"):
            break
        phase_fnet(k, l, need_ctx=(l < DEPTH - 1))
        if stop == "fnet":
            break
        if stop != "outonly":
            pass
        phase_out(k, l, last=(l == nlayers - 1 and nlayers == DEPTH))
        if stop == "out":
            break
    s.drain("sp")
    return nc


def prep_mods(k):
    nc, s, I, B = k.nc, k.s, k.I, k.B
    with ExitStack() as st:
        T_ = lambda name, shape, dt: st.enter_context(nc.sbuf_tensor(name + k.sfx, shape, dt))
        craw = T_("craw", [128, 8, 2], F32)
        sc = T_("sc", [128, 8, 2], F32)
        wm = [T_(f"wm{i}", [128, 3 * D], F32) for i in range(2)]
        rows = T_("mrows", [2, 3 * D], F32)
        gp = T_("gp", [2, 2, D], F32)
        res = T_("mres", [2, 3, D], F32)
        psM = [st.enter_context(nc.psum_tensor(f"psM{i}", [128, 512], F32)) for i in range(6)]
        s.dma("sp", craw[:, :, 0], I["c"].rearrange("(k p) -> p k", p=128), w=[B("craw")], allow_slow_non_contiguous=True)
        s.dma("sp", craw[:, :, 1], I["c_ctx"].rearrange("(k p) -> p k", p=128), w=[B("craw")], allow_slow_non_contiguous=True)
        s.op("act", lambda: nc.scalar.activation(out=sc[:], in_=craw[:], func=AF.Silu), r=[B("craw")], w=[B("sc")])
        for l in range(DEPTH):
            for kk in range(8):
                w_ = wm[kk % 2]; wb = B(f"wm{kk % 2}")
                s.dma("sp", w_[:], I["w_mod"][l, kk * 128:(kk + 1) * 128, :], w=[wb])
                for n in range(6):
                    s.op("pe", lambda n=n, w_=w_, kk=kk: nc.tensor.matmul(psM[n][0:2, :], lhsT=sc[:, kk, :], rhs=w_[:, n * 512:(n + 1) * 512],
                                                                start=(kk == 0), stop=(kk == 7)), r=[B("sc"), wb], w=[B(f"psM{n}")])
            s.dma("sp", rows[:], I["b_mod"][l:l + 1, :].partition_broadcast(2) if False else I["b_mod"][l, :].partition_broadcast(2), w=[B("mrows")])
            s.dma("sp", gp[:, 0, :], I["g_pre"][l, :].partition_broadcast(2), w=[B("gp")])
            s.dma("sp", gp[:, 1, :], I["g_post"][l, :].partition_broadcast(2), w=[B("gp")])
            for n in range(6):
                s.op("dve", lambda n=n: nc.vector.tensor_tensor(out=rows[:, n * 512:(n + 1) * 512], in0=rows[:, n * 512:(n + 1) * 512],
                                                            in1=psM[n][0:2, :], op=ALU.add), r=[B(f"psM{n}")], w=[B("mrows")])
            s.op("dve", lambda: nc.vector.tensor_copy(out=res[:, 0, :], in_=rows[:, 0:D]), r=[B("mrows")], w=[B("mres")])
            s.op("dve", lambda: nc.vector.scalar_tensor_tensor(out=res[:, 1, :], in0=rows[:, D:2 * D], scalar=1.0, in1=gp[:, 0, :],
                                                             op0=ALU.add, op1=ALU.mult), r=[B("mrows"), B("gp")], w=[B("mres")])
            s.op("dve", lambda: nc.vector.tensor_tensor(out=res[:, 2, :], in0=rows[:, 2 * D:3 * D], in1=gp[:, 1, :], op=ALU.mult),
                 r=[B("mrows"), B("gp")], w=[B("mres")])
            s.dma("sp", k.MODS[l], res[:], r=[B("mres")], w=[B("MODS")])
        s.barrier()


def fm_cols():
    lst = []
    for i in range(12):
        lst.append((1024 + i * 128, 128, "PB", i * 128, BF16))
    lst.append((2560, 32, "PF", 0, F32))
    for i in range(4):
        lst.append((2592 + i * 128, 128, "PB", 1536 + i * 128, BF16))
    for i in range(4):
        lst.append((3104 + i * 128, 128, "PF", 32 + i * 128, F32))
    for i in range(4):
        lst.append((3616 + i * 128, 128, "PB", 2048 + i * 128, BF16))
    for i in range(4):
        lst.append((4128 + i * 128, 128, "PF", 544 + i * 128, F32))
    return lst


def phase_ab(k, l):
    nc, s, I, B = k.nc, k.s, k.I, k.B
    with ExitStack() as st:
        T_ = lambda name, shape, dt: st.enter_context(nc.sbuf_tensor(name + k.sfx, shape, dt))
        P_ = lambda name, shape, dt: st.enter_context(nc.psum_tensor(name + k.sfx, shape, dt))
        wsb = T_("wsb", [128, 8, DIN], BF16)
        wst = [T_(f"wst{i}", [128, DIN], F32) for i in range(2)]
        bc = T_("bcab", [128, 4, D], F32)
        xt = [T_(f"xt{i}", [128, D], F32) for i in range(2)]
        junk = T_("junk", [128, D], BF16)
        h1 = T_("h1", [128, D], F32)
        hl = [T_(f"hl{i}", [128, D], BF16) for i in range(2)]
        hlT = [T_(f"hlT{i}", [128, 8, 512], BF16) for i in range(2)]
        st4 = T_("st4", [128, 8], F32)
        zt = [T_(f"zt{i}", [128, D], F32) for i in range(2)]
        ev32 = [T_(f"ev32_{i}", [128, 512], F32) for i in range(2)]
        ev16 = [T_(f"ev16_{i}", [128, 512], BF16) for i in range(2)]
        psT = P_("psT", [128, 1024], BF16)
        psZ = [P_(f"psZ{i}", [128, 512], F32) for i in range(2)]
        psF = [P_(f"psF{i}", [128, 512], F32) for i in range(3)]
        for kk in range(8):
            w_ = wst[kk % 2]; wb = B(f"wst{kk % 2}")
            s.dma("sp", w_[:], I["w_in"][l, kk * 128:(kk + 1) * 128, :], w=[wb])
            e = "act" if kk % 2 == 0 else "pool"
            if e == "act":
                s.op("act", lambda w_=w_, kk=kk: nc.scalar.copy(out=wsb[:, kk, :], in_=w_[:]), r=[wb], w=[B(f"wsb{kk}")])
            else:
                s.op("pool", lambda w_=w_, kk=kk: nc.gpsimd.tensor_copy(out=wsb[:, kk, :], in_=w_[:]), r=[wb], w=[B(f"wsb{kk}")])
        wsb_bufs = [B(f"wsb{kk}") for kk in range(8)]
        for j, (which, comp) in enumerate([(1, 0), (1, 1), (0, 0), (0, 1)]):
            s.dma("pool", bc[:, j, :], k.MODS[l, which, comp, :].partition_broadcast(128), r=[B("MODS")], w=[B("bcab")])
        cols = fm_cols()
        ngroups = 9
        ev_i = 0
        for g in range(ngroups):
            ntok = 512 if g < 8 else 256
            nt = ntok // 128
            hT = hlT[g % 2]; hTb = B(f"hlT{g % 2}")
            for tt in range(nt):
                ti = g * 4 + tt
                x_ = xt[ti % 2]; xb = B(f"xt{ti % 2}")
                if l == 0:
                    src = I["ctx"][ti * 128:(ti + 1) * 128, :] if ti < 2 else I["x"][(ti - 2) * 128:(ti - 1) * 128, :]
                    s.dma("sp", x_[:], src, w=[xb])
                else:
                    s.dma("sp", x_[:], k.XS[ti * 128:(ti + 1) * 128, :], r=[B("XS")], w=[xb])
                jb = 0 if ti < 2 else 2
                c0 = (ti % 4) * 2
                s.op("act", lambda x_=x_, c0=c0: nc.scalar.activation(out=junk[:], in_=x_[:], func=AF.Square, accum_out=st4[:, c0:c0 + 1]),
                     r=[xb], w=[B("junk"), B(f"st4_{ti % 4}")])
                s.op("dve", lambda c0=c0: nc.vector.tensor_scalar(out=st4[:, c0:c0 + 1], in0=st4[:, c0:c0 + 1], scalar1=1.0 / D, scalar2=EPS,
                                                              op0=ALU.mult, op1=ALU.add), w=[B(f"st4_{ti % 4}")])
                s.op("act", lambda c0=c0: nc.scalar.activation(out=st4[:, c0:c0 + 1], in_=st4[:, c0:c0 + 1], func=AF.Sqrt), w=[B(f"st4_{ti % 4}")])
                s.op("dve", lambda c0=c0: nc.vector.reciprocal(out=st4[:, c0 + 1:c0 + 2], in_=st4[:, c0:c0 + 1]), w=[B(f"st4_{ti % 4}")])
                s.op("dve", lambda x_=x_, c0=c0, jb=jb: nc.vector.scalar_tensor_tensor(out=h1[:], in0=x_[:], scalar=st4[:, c0 + 1:c0 + 2], in1=bc[:, jb + 1, :],
                                                                                 op0=ALU.mult, op1=ALU.mult), r=[xb, B(f"st4_{ti % 4}"), B("bcab")], w=[B("h1")])
                h_ = hl[ti % 2]; hb = B(f"hl{ti % 2}")
                s.op("dve", lambda h_=h_, jb=jb: nc.vector.tensor_tensor(out=h_[:], in0=h1[:], in1=bc[:, jb, :], op=ALU.add), r=[B("h1"), B("bcab")], w=[hb])
                for kk in range(8):
                    s.op("pe", lambda h_=h_, kk=kk: nc.tensor.transpose(out=psT[:, kk * 128:(kk + 1) * 128], in_=h_[:, kk * 128:(kk + 1) * 128], identity=k.identb[:]),
                         r=[hb, B("identb")], w=[B("psT")])
                s.op("act", lambda hT=hT, tt=tt: nc.scalar.copy(out=hT[:, :, tt * 128:(tt + 1) * 128], in_=psT[:].rearrange("p (k t) -> p k t", t=128)),
                     r=[B("psT")], w=[hTb])
                z_ = zt[ti % 2]; zb = B(f"zt{ti % 2}")
                for hh in range(2):
                    pz = psZ[hh]; pzb = B(f"psZ{hh}")
                    for kk in range(8):
                        s.op("pe", lambda pz=pz, hT=hT, kk=kk, tt=tt, hh=hh: nc.tensor.matmul(pz[:], lhsT=hT[:, kk, tt * 128:(tt + 1) * 128], rhs=wsb[:, kk, hh * 512:(hh + 1) * 512],
                                                                                        start=(kk == 0), stop=(kk == 7)), r=[hTb, wsb_bufs[kk]], w=[pzb])
                    if hh == 0:
                        s.op("act", lambda z_=z_, pz=pz: nc.scalar.copy(out=z_[:, 0:512], in_=pz[:]), r=[pzb], w=[zb])
                    else:
                        s.op("dve", lambda z_=z_, pz=pz: nc.vector.tensor_copy(out=z_[:, 512:1024], in_=pz[:]), r=[pzb], w=[zb])
                s.dma("pool", k.ZT[ti * 128:(ti + 1) * 128, :], z_[:], r=[zb], w=[B("ZT")])
            t0 = g * 512
            for ci, (c0, wd, dest, r0, dt_) in enumerate(cols):
                pf = psF[ci % 3]; pfb = B(f"psF{ci % 3}")
                for kk in range(8):
                    s.op("pe", lambda pf=pf, hT=hT, kk=kk, c0=c0, wd=wd, ntok=ntok: nc.tensor.matmul(pf[0:wd, 0:ntok], lhsT=wsb[:, kk, c0:c0 + wd], rhs=hT[:, kk, 0:ntok],
                                                                                             start=(kk == 0), stop=(kk == 7)), r=[hTb, wsb_bufs[kk]], w=[pfb])
                ev = (ev32 if dt_ == F32 else ev16)[ev_i % 2]
                evb = B(("ev32_" if dt_ == F32 else "ev16_") + str(ev_i % 2))
                if ev_i % 2 == 0:
                    s.op("act", lambda ev=ev, pf=pf, wd=wd, ntok=ntok: nc.scalar.copy(out=ev[0:wd, 0:ntok], in_=pf[0:wd, 0:ntok]), r=[pfb], w=[evb])
                else:
                    s.op("dve", lambda ev=ev, pf=pf, wd=wd, ntok=ntok: nc.vector.tensor_copy(out=ev[0:wd, 0:ntok], in_=pf[0:wd, 0:ntok]), r=[pfb], w=[evb])
                dst = getattr(k, dest)
                s.dma("pool", dst[r0:r0 + wd, t0:t0 + ntok], ev[0:wd, 0:ntok], r=[evb], w=[B(dest)])
                ev_i += 1
        s.barrier()


def _consts():
    c = np.zeros((128, 2304), np.float32)
    c[:, 0:128] = np.eye(128)
    c[:, 128:137] = np.arange(9)
    p = np.arange(128)
    c[:, 137] = (p % 32) < 16
    c[:, 138] = (p % 32) >= 16
    c[:, 139] = p
    c[:, 140:268] = 1.0
    jj, ii = np.meshgrid(p, p, indexing="ij")
    c[:, 268:396] = np.where(jj > ii, -30000.0, 0.0)
    c[:, 396:524] = np.where(jj < ii, -30000.0, 0.0)
    c[:, 524:532] = (p[:, None] // 16) == np.arange(8)[None]
    c[:, 1024:1568] = np.arange(544)[None]
    cn = np.arange(544)
    c[:, 1600:2144] = np.where(cn < 32, 31 - cn, 575 - cn)[None]
    return c


def kernel(**inputs):
    inp = {k_: np.asarray(v) for k_, v in inputs.items()}
    cst = _consts()
    shared = {}
    for name, v in inp.items():
        if name in ("x", "c", "ctx"):
            continue
        v = np.ascontiguousarray(v, dtype=np.float32)
        if name == "conv_w":
            v = np.ascontiguousarray(v.reshape(4, 9, 1536))
        elif name in ("dt_bias", "a_log"):
            v = np.ascontiguousarray(v.reshape(4, 32))
        shared[name] = v
    shared["cst"] = cst
    in_maps = []
    for core in range(8):
        b = core % 4
        m = dict(shared)
        m["x"] = np.ascontiguousarray(inp["x"][b], dtype=np.float32)
        m["c"] = np.ascontiguousarray(inp["c"][b], dtype=np.float32)
        m["ctx"] = np.ascontiguousarray(inp["ctx"][b], dtype=np.float32)
        in_maps.append(m)
    nc = build()
    res = run_bass_kernel_spmd(nc, in_maps, core_ids=list(range(8)))
    out = np.stack([np.asarray(res.results[b]["out"], dtype=np.float32) for b in range(4)], axis=0)
    return out
```

```python
import math
from contextlib import ExitStack
import numpy as np
import concourse.bass as bass
import concourse.mybir as mybir
from concourse.bass_utils import run_bass_kernel_spmd

F32 = mybir.dt.float32
BF16 = mybir.dt.bfloat16
AF = mybir.ActivationFunctionType
ALU = mybir.AluOpType
AX = mybir.AxisListType


class Buf:
    __slots__ = ("w", "r", "name")

    def __init__(self, name=""):
        self.w = None
        self.r = {}
        self.name = name


class Sched:
    NR = 8

    def __init__(self, nc):
        self.nc = nc
        self.eng = {"pe": nc.tensor, "act": nc.scalar, "dve": nc.vector,
                    "pool": nc.gpsimd, "sp": nc.sync}
        self.csem = {e: nc.alloc_semaphore("c_" + e) for e in ("pe", "act", "dve", "pool")}
        self.ccnt = {e: 0 for e in self.csem}
        self.dq = {}
        for q in ("sp", "act", "pool"):
            self.dq[q] = dict(sems=[nc.alloc_semaphore(f"d_{q}{i}") for i in range(self.NR)],
                              n=0, tk=[None] * self.NR)
        self.waited = {}
        self.ninstr = 0

    def _wait(self, e, tk):
        if tk is None:
            return
        key, sem, val = tk
        if key == "pe" and e == "pe":
            return
        if self.waited.get((e, key), 0) >= val:
            return
        self.eng[e].wait_ge(sem, val)
        self.waited[(e, key)] = val

    def _deps(self, e, r, w):
        for b in r:
            self._wait(e, b.w)
        for b in w:
            self._wait(e, b.w)
            for t in list(b.r.values()):
                self._wait(e, t)

    def _record(self, tk, r, w):
        for b in r:
            b.r[tk[0]] = tk
        for b in w:
            b.w = tk
            b.r = {}

    def op(self, e, fn, r=(), w=()):
        self._deps(e, r, w)
        ins = fn()
        self.ccnt[e] += 1
        ins.then_inc(self.csem[e], 1)
        tk = (e, self.csem[e], self.ccnt[e])
        self._record(tk, r, w)
        self.ninstr += 1
        return tk

    def dma(self, q, out, in_, r=(), w=(), **kw):
        d = self.dq[q]
        slot = d["n"] % self.NR
        self._wait(q, d["tk"][slot])
        self._deps(q, r, w)
        ins = self.eng[q].dma_start(out=out, in_=in_, **kw)
        ins.then_inc(d["sems"][slot], 16)
        val = 16 * (d["n"] // self.NR + 1)
        tk = (("d", q, slot), d["sems"][slot], val)
        d["tk"][slot] = tk
        d["n"] += 1
        self._record(tk, r, w)
        self.ninstr += 1
        return tk

    def op_cc(self, fn, r=(), w=()):
        q = "pool"
        d = self.dq[q]
        slot = d["n"] % self.NR
        self._wait(q, d["tk"][slot])
        self._deps(q, r, w)
        ins = fn()
        ins.then_inc(d["sems"][slot], 16)
        val = 16 * (d["n"] // self.NR + 1)
        tk = (("d", q, slot), d["sems"][slot], val)
        d["tk"][slot] = tk
        d["n"] += 1
        self._record(tk, r, w)
        return tk

    def all_tickets(self):
        tks = []
        for e in self.csem:
            if self.ccnt[e]:
                tks.append((e, self.csem[e], self.ccnt[e]))
        for q, d in self.dq.items():
            for t in d["tk"]:
                if t is not None:
                    tks.append(t)
        return tks

    def barrier(self, engines=("pe", "act", "dve", "pool", "sp")):
        tks = self.all_tickets()
        for e in engines:
            for t in tks:
                self._wait(e, t)

    def drain(self, e="sp"):
        for t in self.all_tickets():
            if t[0] == "pe" and e == "pe":
                continue
            self._wait(e, t)

T = 4352

def phase_conv(k, l):
    nc, s, I, B = k.nc, k.s, k.I, k.B
    with ExitStack() as st:
        T_ = lambda name, shape, dt: st.enter_context(nc.sbuf_tensor(name + k.sfx, shape, dt))
        P_ = lambda name, shape, dt: st.enter_context(nc.psum_tensor(name + k.sfx, shape, dt))
        cw9 = T_("cw9", [9, 1536], F32)
        cwT = T_("cwT", [128, 108], F32)
        cb = T_("cb", [128, 12], F32)
        dg = T_("dg", [128, 108, 128], BF16)
        xp = [T_(f"xp{i}", [128, 258 + 66 * 66], BF16) for i in range(2)]
        ev = [T_(f"cev{i}", [128, 512], BF16) for i in range(2)]
        psW = P_("psW", [128, 108], F32)
        psC = [P_(f"psC{i}", [128, 512], F32) for i in range(2)]
        s.dma("sp", cw9[:], I["conv_w"][l], w=[B("cw9")])
        s.dma("sp", cb[:], I["conv_b"][l].rearrange("(t p) -> p t", p=128), w=[B("cb")], allow_slow_non_contiguous=True)
        for t in range(12):
            s.op("pe", lambda t=t: nc.tensor.transpose(out=psW[:, t * 9:(t + 1) * 9], in_=cw9[:, t * 128:(t + 1) * 128], identity=k.cst[0:9, 0:9]),
                 r=[B("cw9"), B("cst")], w=[B("psW")])
        s.op("dve", lambda: nc.vector.tensor_copy(out=cwT[:], in_=psW[:]), r=[B("psW")], w=[B("cwT")])
        for j in range(108):
            e = "dve" if j % 2 == 0 else "pool"
            eng = nc.vector if e == "dve" else nc.gpsimd
            s.op(e, lambda j=j, eng=eng: eng.tensor_scalar(out=dg[:, j, :], in0=k.identb[:], scalar1=cwT[:, j:j + 1], scalar2=None, op0=ALU.mult),
                 r=[B("cwT"), B("identb")], w=[B(f"dg{j}")])
        for i in range(2):
            s.op("pool", lambda i=i: nc.gpsimd.memset(xp[i][:], 0.0), w=[B(f"xp{i}")])
        n = 0
        for t in range(12):
            x_ = xp[t % 2]; xb = B(f"xp{t % 2}")
            rows = k.PB[t * 128:(t + 1) * 128, :]
            s.dma("sp", x_[:, 1:257], rows[:, 0:256], r=[B("PB")], w=[xb])
            grid = x_[:, 258:258 + 4356].rearrange("p (r c) -> p r c", c=66)
            for hh in range(2):
                s.dma("sp", grid[:, 1 + hh * 32:33 + hh * 32, 1:65], rows[:, 256 + hh * 2048:256 + (hh + 1) * 2048].rearrange("p (r c) -> p r c", c=64), r=[B("PB")], w=[xb])
            for rg in range(9):
                pc = psC[n % 2]; pcb = B(f"psC{n % 2}")
                if rg < 8:
                    for tap in range(9):
                        ky, kx = tap // 3, tap % 3
                        rhs = grid[:, rg * 8 + ky:rg * 8 + ky + 8, kx:kx + 64]
                        s.op("pe", lambda pc=pc, t=t, tap=tap, rhs=rhs: nc.tensor.matmul(pc[:].rearrange("p (r c) -> p r c", c=64), lhsT=dg[:, t * 9 + tap, :], rhs=rhs,
                                                                              start=(tap == 0), stop=(tap == 8)), r=[xb, B(f"dg{t * 9 + tap}")], w=[pcb])
                    ntok = 512; t0 = 256 + rg * 512
                else:
                    for kx in range(3):
                        s.op("pe", lambda pc=pc, t=t, kx=kx: nc.tensor.matmul(pc[:, 0:256], lhsT=dg[:, t * 9 + 3 + kx, :], rhs=x_[:, kx:kx + 256],
                                                                    start=(kx == 0), stop=(kx == 2)), r=[xb, B(f"dg{t * 9 + 3 + kx}")], w=[pcb])
                    ntok = 256; t0 = 0
                e_ = ev[n % 2]; eb = B(f"cev{n % 2}")
                s.op("act", lambda e_=e_, pc=pc, t=t, ntok=ntok: nc.scalar.activation(out=e_[:, 0:ntok], in_=pc[:, 0:ntok], func=AF.Silu, bias=cb[:, t:t + 1]),
                     r=[pcb, B("cb")], w=[eb])
                s.dma("pool", k.XC[t * 128:(t + 1) * 128, t0:t0 + ntok], e_[:, 0:ntok], r=[eb], w=[B("XC")])
                n += 1
        s.barrier()

T = 4352; NCH = 34; D = 1024; EPS = 1e-6
C_MF = 137; C_MB = 138; C_ONES = 140; C_NEGF = 268; C_NEGB = 396; C_BM8 = 524; C_IOTA = 1024

def phase_ssd(k, l, need_ctx=True):
    nc, s, I, B = k.nc, k.s, k.I, k.B
    cst = k.cst
    with ExitStack() as st:
        T_ = lambda name, shape, dt: st.enter_context(nc.sbuf_tensor(name + k.sfx, shape, dt))
        P_ = lambda name, shape, dt: st.enter_context(nc.psum_tensor(name + k.sfx, shape, dt))
        pst = ExitStack()
        TP_ = lambda name, shape, dt: pst.enter_context(nc.sbuf_tensor(name + k.sfx, shape, dt))
        acs = T_("s_acs", [128, T], F32)
        nacs = T_("s_nacs", [128, T], F32)
        Q = T_("s_Q", [128, T], F32)
        colp = T_("s_colp", [128, 4], F32)
        tot = T_("s_tot", [128, NCH], F32)
        DT = T_("s_DT", [32, NCH, 32], F32)
        cdall = T_("s_cd", [128, NCH, 32], F32)
        Esel2 = T_("s_Esel2", [64, 32, 128], BF16)
        hl = T_("s_hl", [64, T], BF16)
        nhl = T_("s_nhl", [64, T], BF16)
        negm = T_("s_negm", [128, 2, 512], BF16)
        dskc = T_("s_dskc", [128, 8], F32)
        dgd = T_("s_dgd", [128, 8, 128], BF16)
        gnb = T_("s_gnb", [128, D], F32)
        dt_ = TP_("s_dt", [128, T], F32)
        a_ = TP_("s_a", [128, T], F32)
        cum = TP_("s_cum", [128, T], F32)
        psCB = P_("psCB", [128, 256], F32)
        psX = P_("psX", [128, 1024], BF16)
        psB = P_("psB", [128, 256], BF16)
        psE = P_("psE", [128, 512], F32)
        psY = [P_(f"psY{i}", [128, 512], F32) for i in range(2)]
        psS1 = P_("psSst", [128, 512], F32)
        psO1 = P_("psOst", [128, 512], F32)

        V = lambda fn, r=(), w=(): s.op("dve", fn, r=r, w=w)
        A = lambda fn, r=(), w=(): s.op("act", fn, r=r, w=w)
        G = lambda fn, r=(), w=(): s.op("pool", fn, r=r, w=w)
        PE = lambda fn, r=(), w=(): s.op("pe", fn, r=r, w=w)
        for q in range(4):
            s.dma("sp", dt_[32 * q:32 * q + 32, :], k.PF[0:32, :], r=[B("PF")], w=[B("s_dt")])
            s.dma("sp", colp[32 * q:32 * q + 32, 0:1], I["dt_bias"][l].rearrange("(p o) -> p o", o=1), w=[B("s_colp")])
            s.dma("sp", colp[32 * q:32 * q + 32, 1:2], I["a_log"][l].rearrange("(p o) -> p o", o=1), w=[B("s_colp")])
        for t in range(8):
            for hh in range(2):
                s.dma("sp", dskc[64 * hh:64 * hh + 64, t:t + 1], I["d_ssd"][l, 2 * t + hh:2 * t + hh + 1].partition_broadcast(64), w=[B("s_dskc")])
        s.dma("sp", gnb[:], I["g_ssd_norm"][l].partition_broadcast(128), w=[B("s_gnb")])
        A(lambda: nc.scalar.activation(out=colp[:, 2:3], in_=colp[:, 1:2], func=AF.Exp), w=[B("s_colp")])
        V(lambda: nc.vector.tensor_scalar(out=colp[:, 3:4], in0=colp[:, 2:3], scalar1=-1.0, scalar2=None, op0=ALU.mult), w=[B("s_colp")])
        A(lambda: nc.scalar.activation(out=dt_[:], in_=dt_[:], func=AF.Exp, bias=colp[:, 0:1]), r=[B("s_colp")], w=[B("s_dt")])
        A(lambda: nc.scalar.activation(out=dt_[:], in_=dt_[:], func=AF.Ln, bias=1.0), w=[B("s_dt")])
        V(lambda: nc.vector.tensor_scalar(out=a_[:], in0=dt_[:], scalar1=colp[:, 3:4], scalar2=None, op0=ALU.mult), r=[B("s_dt"), B("s_colp")], w=[B("s_a")])
        for c in range(NCH):
            V(lambda c=c: nc.vector.tensor_tensor_scan(out=cum[:, c * 128:(c + 1) * 128], data0=cst[:, C_ONES:C_ONES + 128], data1=a_[:, c * 128:(c + 1) * 128],
                                                   initial=0.0, op0=ALU.mult, op1=ALU.add), r=[B("s_a"), B("cst")], w=[B("s_cum")])
        cum3 = cum[:].rearrange("p (c i) -> p c i", i=128)
        V(lambda: nc.vector.tensor_copy(out=tot[:], in_=cum3[:, :, 127]), r=[B("s_cum")], w=[B("s_tot")])
        totb = tot[:].unsqueeze(2).to_broadcast([128, NCH, 128])
        V(lambda: nc.vector.tensor_tensor(out=nacs[:], in0=a_[:], in1=cum[:], op=ALU.subtract), r=[B("s_a"), B("s_cum")], w=[B("s_nacs")])
        V(lambda: nc.vector.tensor_tensor(out=nacs[:].rearrange("p (c i) -> p c i", i=128), in0=nacs[:].rearrange("p (c i) -> p c i", i=128), in1=totb, op=ALU.add),
          r=[B("s_tot")], w=[B("s_nacs")])
        V(lambda: nc.vector.tensor_scalar(out=acs[:], in0=cum[:], scalar1=cst[:, C_MF:C_MF + 1], scalar2=None, op0=ALU.mult), r=[B("s_cum"), B("cst")], w=[B("s_acs")])
        V(lambda: nc.vector.scalar_tensor_tensor(out=acs[:], in0=nacs[:], scalar=cst[:, C_MB:C_MB + 1], in1=acs[:], op0=ALU.mult, op1=ALU.add), r=[B("s_nacs")], w=[B("s_acs")])
        V(lambda: nc.vector.tensor_scalar(out=nacs[:], in0=acs[:], scalar1=-1.0, scalar2=None, op0=ALU.mult), r=[B("s_acs")], w=[B("s_nacs")])
        V(lambda: nc.vector.tensor_tensor(out=a_[:].rearrange("p (c i) -> p c i", i=128), in0=nacs[:].rearrange("p (c i) -> p c i", i=128), in1=totb, op=ALU.add),
          r=[B("s_nacs"), B("s_tot")], w=[B("s_a")])
        A(lambda: nc.scalar.activation(out=a_[:], in_=a_[:], func=AF.Exp), w=[B("s_a")])
        V(lambda: nc.vector.tensor_tensor(out=a_[:], in0=a_[:], in1=dt_[:], op=ALU.mult), r=[B("s_dt")], w=[B("s_a")])
        A(lambda: nc.scalar.activation(out=cum[:], in_=acs[:], func=AF.Exp), r=[B("s_acs")], w=[B("s_cum")])
        G(lambda: nc.gpsimd.tensor_copy(out=Q[0:32, :], in_=dt_[0:32, :]), r=[B("s_dt")], w=[B("s_Q")])
        G(lambda: nc.gpsimd.tensor_copy(out=Q[32:64, :], in_=a_[32:64, :]), r=[B("s_a")], w=[B("s_Q")])
        G(lambda: nc.gpsimd.tensor_copy(out=Q[64:96, :], in_=cum[64:96, :]), r=[B("s_cum")], w=[B("s_Q")])
        G(lambda: nc.gpsimd.tensor_copy(out=Q[96:128, :], in_=nacs[96:128, :]), r=[B("s_nacs")], w=[B("s_Q")])
        V(lambda: nc.vector.tensor_copy(out=Esel2[0:32], in_=cst[0:32, 0:32].unsqueeze(2).to_broadcast([32, 32, 128])), r=[B("cst")], w=[B("s_Esel")])
        V(lambda: nc.vector.tensor_copy(out=Esel2[32:64], in_=cst[32:64, 32:64].unsqueeze(2).to_broadcast([32, 32, 128])), r=[B("cst")], w=[B("s_Esel")])
        V(lambda: nc.vector.tensor_copy(out=hl[0:32, :], in_=acs[0:32, :]), r=[B("s_acs")], w=[B("s_hl")])
        V(lambda: nc.vector.tensor_copy(out=nhl[32:64, :], in_=acs[32:64, :]), r=[B("s_acs")], w=[B("s_nhl")])
        V(lambda: nc.vector.tensor_tensor(out=hl[32:64, :], in0=acs[32:64, :], in1=nhl[32:64, :], op=ALU.subtract), r=[B("s_acs"), B("s_nhl")], w=[B("s_hl")])
        V(lambda: nc.vector.tensor_scalar(out=nhl[:], in0=hl[:], scalar1=-1.0, scalar2=None, op0=ALU.mult), r=[B("s_hl")], w=[B("s_nhl")])
        V(lambda: nc.vector.tensor_copy(out=negm[:, 0, :].rearrange("p (a i) -> p a i", i=128), in_=cst[:, C_NEGF:C_NEGF + 128].unsqueeze(1).to_broadcast([128, 4, 128])), r=[B("cst")], w=[B("s_negm")])
        V(lambda: nc.vector.tensor_copy(out=negm[:, 1, :].rearrange("p (a i) -> p a i", i=128), in_=cst[:, C_NEGB:C_NEGB + 128].unsqueeze(1).to_broadcast([128, 4, 128])), r=[B("cst")], w=[B("s_negm")])
        V(lambda: nc.vector.tensor_tensor(out=DT[:], in0=cst[0:32, 0:32].unsqueeze(1).to_broadcast([32, NCH, 32]), in1=tot[0:32, :].unsqueeze(2).to_broadcast([32, NCH, 32]), op=ALU.mult),
          r=[B("s_tot"), B("cst")], w=[B("s_DT")])
        for c0 in range(0, NCH, 16):
            n = min(16, NCH - c0)
            PE(lambda c0=c0, n=n: nc.tensor.matmul(psE[:, 0:n * 32], lhsT=cst[0:32, C_ONES:C_ONES + 128], rhs=DT[:, c0:c0 + n, :].rearrange("p c k -> p (c k)"), start=True, stop=True),
               r=[B("s_DT"), B("cst")], w=[B("psE")])
            A(lambda c0=c0, n=n: nc.scalar.activation(out=cdall[:, c0:c0 + n, :].rearrange("p c k -> p (c k)"), in_=psE[:, 0:n * 32], func=AF.Exp), r=[], w=[B("psE"), B("s_cd")])
        for t in range(8):
            V(lambda t=t: nc.vector.tensor_scalar(out=dgd[:, t, :], in0=k.identb[:], scalar1=dskc[:, t:t + 1], scalar2=None, op0=ALU.mult), r=[B("s_dskc"), B("identb")], w=[B("s_dgd")])
        s.barrier()
        pst.close()
        hst = [T_(f"s_h{d}", [128, D], F32) for d in range(2)]
        hbf = [T_(f"s_hb{d}", [128, D], BF16) for d in range(2)]
        xT = [T_(f"s_xT{i}", [128, 8, 128], BF16) for i in range(2)]
        BC = [T_(f"s_BC{i}", [128, 4, 128], BF16) for i in range(2)]
        tmq = [T_(f"s_tmq{i}", [128, 128], F32) for i in range(2)]
        xdt = [T_(f"s_xdt{i}", [128, D], BF16) for i in range(2)]
        xw = [T_(f"s_xw{i}", [128, D], BF16) for i in range(2)]
        Btok = [T_(f"s_Btok{i}", [128, 256], BF16) for i in range(2)]
        dec = [T_(f"s_dec{i}", [128, 512], F32) for i in range(2)]
        MTt = [T_(f"s_MT{i}", [128, 16, 128], BF16) for i in range(2)]
        ydg = [T_(f"s_ydg{i}", [128, D], F32) for i in range(2)]
        Sc = [T_(f"s_Sc{i}", [128, D], F32) for i in range(2)]
        tmp = T_("s_tmp", [128, 512], F32)
        yt = [T_(f"s_y{i}", [128, D], F32) for i in range(2)]
        yf = [T_(f"s_yf{i}", [128, D], F32) for i in range(2)]
        zt = [T_(f"s_z{i}", [128, D], F32) for i in range(2)]
        sz = T_("s_sz", [128, D], F32)
        junk = T_("s_junk", [128, 512], BF16)
        st2 = T_("s_st2", [128, 4], F32)
        mo = T_("s_mo", [128, D], BF16)
        mT = [T_(f"s_mT{i}", [128, 8, 128], BF16) for i in range(2)]
        def stageA(d, n_, c):
            sl = n_ % 2
            t0 = c * 128
            x_ = xT[sl]; xb = B(f"s_xT{sl}"); bc_ = BC[sl]; bcb = B(f"s_BC{sl}")
            tq = tmq[sl]; tqb = B(f"s_tmq{sl}")
            s.dma("sp", x_[:], k.XC[0:1024, t0:t0 + 128].rearrange("(t p) j -> p t j", p=128), r=[B("XC")], w=[xb])
            s.dma("sp", bc_[:], k.XC[1024:1536, t0:t0 + 128].rearrange("(t p) j -> p t j", p=128), r=[B("XC")], w=[bcb])
            if d == 1:
                s.dma("sp", zt[sl][:], k.ZT[t0:t0 + 128, :], r=[B("ZT")], w=[B(f"s_z{sl}")])
                s.dma("sp", yf[sl][:], k.YF[t0:t0 + 128, :], r=[B("YF")], w=[B(f"s_yf{sl}")])
            PE(lambda: nc.tensor.transpose(out=psE[:, 0:128], in_=Q[:, t0:t0 + 128], identity=cst[:, 0:128]), r=[B("s_Q"), B("cst")], w=[B("psE")])
            A(lambda: nc.scalar.copy(out=tq[:], in_=psE[:, 0:128]), w=[B("psE"), tqb])
            for t in range(8):
                PE(lambda t=t: nc.tensor.transpose(out=psX[:, t * 128:(t + 1) * 128], in_=x_[:, t, :], identity=k.identb[:]), r=[xb, B("identb")], w=[B("psX")])
            for g in range(2):
                PE(lambda g=g: nc.tensor.transpose(out=psB[:, g * 128:(g + 1) * 128], in_=bc_[:, g, :], identity=k.identb[:]), r=[bcb, B("identb")], w=[B("psB")])
                PE(lambda g=g: nc.tensor.matmul(psCB[:, g * 128:(g + 1) * 128], lhsT=bc_[:, g, :], rhs=bc_[:, 2 + g, :], start=True, stop=True), r=[bcb], w=[B("psCB")])
            A(lambda: nc.scalar.copy(out=Btok[sl][:], in_=psB[:]), w=[B("psB"), B(f"s_Btok{sl}")])
            yield
            psX3 = psX[:].rearrange("p (h e) -> p h e", e=64)
            V(lambda: nc.vector.tensor_tensor(out=xdt[sl][:].rearrange("p (h e) -> p h e", e=64), in0=psX3, in1=tq[:, d * 16:d * 16 + 16].unsqueeze(2).to_broadcast([128, 16, 64]), op=ALU.mult),
              r=[tqb], w=[B("psX"), B(f"s_xdt{sl}")])
            V(lambda: nc.vector.tensor_tensor(out=xw[sl][:].rearrange("p (h e) -> p h e", e=64), in0=psX3, in1=tq[:, 32 + d * 16:48 + d * 16].unsqueeze(2).to_broadcast([128, 16, 64]), op=ALU.mult),
              r=[tqb], w=[B("psX"), B(f"s_xw{sl}")])
            yield
            for hq in range(4):
                g = hq // 2
                PE(lambda: nc.tensor.matmul(psE[:], lhsT=k.identb[:], rhs=negm[:, d, :], start=True, stop=False, skip_group_check=True), r=[B("s_negm"), B("identb")], w=[B("psE")])
                for hh in range(4):
                    dh = d * 16 + hq * 4 + hh
                    o_ = psE[:, hh * 128:(hh + 1) * 128]
                    PE(lambda o_=o_, dh=dh: nc.tensor.matmul(o_, lhsT=Esel2[:, dh, :], rhs=hl[:, t0:t0 + 128], start=False, stop=False, skip_group_check=True), r=[B("s_Esel"), B("s_hl")], w=[B("psE")])
                    PE(lambda o_=o_, dh=dh, hh=hh: nc.tensor.matmul(o_, lhsT=nhl[:, t0:t0 + 128], rhs=Esel2[:, dh, :], start=False, stop=(hh == 3), skip_group_check=True), r=[B("s_Esel"), B("s_nhl")], w=[B("psE")])
                dc = dec[hq % 2]; dcb = B(f"s_dec{hq % 2}")
                A(lambda dc=dc: nc.scalar.activation(out=dc[:], in_=psE[:], func=AF.Exp), w=[B("psE"), dcb])
                V(lambda dc=dc, hq=hq, g=g: nc.vector.tensor_tensor(out=MTt[sl][:, hq * 4:hq * 4 + 4, :], in0=dc[:].rearrange("p (a i) -> p a i", i=128),
                                                               in1=psCB[:, g * 128:(g + 1) * 128].unsqueeze(1).to_broadcast([128, 4, 128]), op=ALU.mult),
                  r=[dcb], w=[B("psCB"), B(f"s_MT{sl}_{hq}")])
                yield
            if d == 0:
                for t in range(8):
                    py = psY[t // 4]
                    PE(lambda t=t, py=py: nc.tensor.matmul(py[:, (t % 4) * 128:(t % 4) * 128 + 128], lhsT=x_[:, t, :], rhs=dgd[:, t, :], start=(t % 4 == 0), stop=False, skip_group_check=True),
                       r=[xb, B("s_dgd")], w=[B(f"psY{t // 4}")])
            for h in range(16):
                py = psY[h // 8]
                PE(lambda h=h, py=py: nc.tensor.matmul(py[:, (h % 8) * 64:(h % 8) * 64 + 64], lhsT=MTt[sl][:, h, :], rhs=xdt[sl][:, h * 64:(h + 1) * 64], start=(d == 1 and h % 8 == 0), stop=(h % 8 == 7), skip_group_check=True),
                   r=[B(f"s_MT{sl}_{h // 4}"), B(f"s_xdt{sl}")], w=[B(f"psY{h // 8}")])
            yield
            A(lambda: nc.scalar.copy(out=ydg[sl][:, 0:512], in_=psY[0][:]), w=[B("psY0"), B(f"s_ydg{sl}")])
            V(lambda: nc.vector.tensor_copy(out=ydg[sl][:, 512:1024], in_=psY[1][:]), w=[B("psY1"), B(f"s_ydg{sl}")])
            if d == 1:
                G(lambda: nc.gpsimd.tensor_tensor(out=ydg[sl][:], in0=ydg[sl][:], in1=yf[sl][:], op=ALU.add), r=[B(f"s_yf{sl}")], w=[B(f"s_ydg{sl}")])
            yield
            for g in range(2):
                PE(lambda g=g: nc.tensor.matmul(psS1[:], lhsT=Btok[sl][:, g * 128:(g + 1) * 128], rhs=xw[sl][:, g * 512:(g + 1) * 512], start=True, stop=True),
                   r=[B(f"s_Btok{sl}"), B(f"s_xw{sl}")], w=[B("psSst")])
                if g == 0:
                    A(lambda: nc.scalar.copy(out=Sc[sl][:, 0:512], in_=psS1[:]), w=[B("psSst"), B(f"s_Sc{sl}")])
                else:
                    V(lambda: nc.vector.tensor_copy(out=Sc[sl][:, 512:1024], in_=psS1[:]), w=[B("psSst"), B(f"s_Sc{sl}")])
                yield

        def stageB(d, n_, c):
            sl = n_ % 2
            t0 = c * 128
            hb_ = B(f"s_h{d}"); hbb = B(f"s_hb{d}")
            bc_ = BC[sl]; bcb = B(f"s_BC{sl}")
            tq = tmq[sl]; tqb = B(f"s_tmq{sl}")
            y_ = yt[sl]; yb = B(f"s_y{sl}")
            for g in range(2):
                PE(lambda g=g: nc.tensor.matmul(psO1[:], lhsT=bc_[:, 2 + g, :], rhs=hbf[d][:, g * 512:(g + 1) * 512], start=True, stop=True), r=[bcb, hbb], w=[B("psOst")])
                V(lambda g=g: nc.vector.tensor_tensor(out=tmp[:].rearrange("p (h e) -> p h e", e=64), in0=psO1[:].rearrange("p (h e) -> p h e", e=64),
                                                   in1=tq[:, 64 + d * 16 + g * 8:64 + d * 16 + g * 8 + 8].unsqueeze(2).to_broadcast([128, 8, 64]), op=ALU.mult),
                  r=[tqb], w=[B("psOst"), B("s_tmp")])
                G(lambda g=g: nc.gpsimd.tensor_tensor(out=y_[:, g * 512:(g + 1) * 512], in0=tmp[:], in1=ydg[sl][:, g * 512:(g + 1) * 512], op=ALU.add), r=[B("s_tmp"), B(f"s_ydg{sl}")], w=[yb])
                yield
            G(lambda: nc.gpsimd.tensor_tensor(out=hst[d][:].rearrange("p (h e) -> p h e", e=64), in0=hst[d][:].rearrange("p (h e) -> p h e", e=64),
                                               in1=cdall[:, c, d * 16:d * 16 + 16].unsqueeze(2).to_broadcast([128, 16, 64]), op=ALU.mult), r=[B("s_cd")], w=[hb_])
            G(lambda: nc.gpsimd.tensor_tensor(out=hst[d][:], in0=hst[d][:], in1=Sc[sl][:], op=ALU.add), r=[B(f"s_Sc{sl}")], w=[hb_])
            A(lambda: nc.scalar.copy(out=hbf[d][:], in_=hst[d][:]), r=[hb_], w=[hbb])
            yield
            if d == 0:
                s.dma("pool", k.YF[t0:t0 + 128, :], y_[:], r=[yb], w=[B("YF")])
            elif need_ctx or c >= 2:
                z_ = zt[sl]; zb = B(f"s_z{sl}")
                A(lambda: nc.scalar.activation(out=sz[:], in_=z_[:], func=AF.Silu), r=[zb], w=[B("s_sz")])
                V(lambda: nc.vector.tensor_tensor(out=y_[:], in0=y_[:], in1=sz[:], op=ALU.mult), r=[B("s_sz")], w=[yb])
                for g in range(2):
                    A(lambda g=g: nc.scalar.activation(out=junk[:], in_=y_[:, g * 512:(g + 1) * 512], func=AF.Square, accum_out=st2[:, g:g + 1]), r=[yb], w=[B("s_junk"), B("s_st2")])
                yield
                V(lambda: nc.vector.tensor_scalar(out=st2[:, 0:2], in0=st2[:, 0:2], scalar1=1.0 / 512, scalar2=EPS, op0=ALU.mult, op1=ALU.add), w=[B("s_st2")])
                A(lambda: nc.scalar.activation(out=st2[:, 0:2], in_=st2[:, 0:2], func=AF.Sqrt), w=[B("s_st2")])
                V(lambda: nc.vector.reciprocal(out=st2[:, 2:4], in_=st2[:, 0:2]), w=[B("s_st2")])
                for g in range(2):
                    V(lambda g=g: nc.vector.scalar_tensor_tensor(out=mo[:, g * 512:(g + 1) * 512], in0=y_[:, g * 512:(g + 1) * 512], scalar=st2[:, 2 + g:3 + g], in1=gnb[:, g * 512:(g + 1) * 512],
                                                               op0=ALU.mult, op1=ALU.mult), r=[yb, B("s_st2"), B("s_gnb")], w=[B("s_mo")])
                yield
                for t in range(8):
                    PE(lambda t=t: nc.tensor.transpose(out=psX[:, t * 128:(t + 1) * 128], in_=mo[:, t * 128:(t + 1) * 128], identity=k.identb[:]), r=[B("s_mo"), B("identb")], w=[B("psX")])
                m_ = mT[sl]; mb_ = B(f"s_mT{sl}")
                A(lambda: nc.scalar.copy(out=m_[:], in_=psX[:].rearrange("p (t j) -> p t j", j=128)), w=[B("psX"), mb_])
                s.dma("pool", k.MT[0:1024, t0:t0 + 128].rearrange("(t p) j -> p t j", p=128), m_[:], r=[mb_], w=[B("MT")])
            yield

        def interleave(gens):
            gens = [g for g in gens if g is not None]
            while gens:
                for g in list(gens):
                    try:
                        next(g)
                    except StopIteration:
                        gens.remove(g)

        for d in range(2):
            hb_ = B(f"s_h{d}"); hbb = B(f"s_hb{d}")
            V(lambda d=d: nc.vector.memset(hst[d][:], 0.0), w=[hb_])
            V(lambda d=d: nc.vector.memset(hbf[d][:], 0.0), w=[hbb])
            order = list(range(NCH)) if d == 0 else [1, 0] + list(range(NCH - 1, 1, -1))
            interleave([stageA(d, 0, order[0])])
            for n_ in range(len(order)):
                ga = stageA(d, n_ + 1, order[n_ + 1]) if n_ + 1 < len(order) else None
                interleave([ga, stageB(d, n_, order[n_])])
        s.barrier()

T = 4352; D = 1024; EPS = 1e-6; DEPTH = 4
C_PIDX = 139
I32 = mybir.dt.int32

def gen_dft(k):
    nc, s, B = k.nc, k.s, k.B
    cst = k.cst
    with ExitStack() as st:
        T_ = lambda name, shape, dt: st.enter_context(nc.sbuf_tensor(name + k.sfx, shape, dt))
        kio_i = T_("kio_i", [128, 4096], I32)
        kio = T_("kio", [128, 4096], F32)
        lcol = T_("lcol", [128, 32], F32)
        pi_ = [T_(f"pi{i}", [128, 4096], I32) for i in range(2)]
        tb = [T_(f"tb{i}", [128, 4096], BF16) for i in range(4)]
        s.op("pool", lambda: nc.gpsimd.iota(kio_i[:], pattern=[[1, 4096]], base=0, channel_multiplier=0), w=[B("kio_i")])
        s.op("dve", lambda: nc.vector.tensor_copy(out=kio[:], in_=kio_i[:]), r=[B("kio_i")], w=[B("kio")])
        for lt in range(32):
            s.op("dve", lambda lt=lt: nc.vector.tensor_scalar(out=lcol[:, lt:lt + 1], in0=cst[:, C_PIDX:C_PIDX + 1], scalar1=float(lt * 128), scalar2=None, op0=ALU.add), r=[B("cst")], w=[B("lcol")])
        sc = 2.0 * math.pi / 4096.0
        for lt in range(32):
            for j, (off, dst) in enumerate([(0.0, k.SLN), (3072.0, k.CL)]):
                e = "dve" if j == 0 else "pool"
                eng = nc.vector if j == 0 else nc.gpsimd
                p_ = pi_[j]; pb = B(f"pi{j}")
                s.op(e, lambda eng=eng, p_=p_, lt=lt, off=off: eng.tensor_scalar(out=p_[:], in0=kio[:], scalar1=lcol[:, lt:lt + 1], scalar2=off, op0=ALU.mult, op1=ALU.add), r=[B("kio"), B("lcol")], w=[pb])
                s.op("dve", lambda p_=p_: nc.vector.tensor_single_scalar(out=p_[:], in_=p_[:], scalar=4095, op=ALU.bitwise_and), w=[pb])
                t_ = tb[(lt % 2) * 2 + j]; tbb = B(f"tb{(lt % 2) * 2 + j}")
                s.op("act", lambda t_=t_, p_=p_: nc.scalar.activation(out=t_[:], in_=p_[:], func=AF.Sin, scale=sc, bias=k.negpi[:, 0:1]), r=[pb, B("negpi")], w=[tbb])
                s.dma("sp", dst[lt * 128:(lt + 1) * 128, :], t_[:], r=[tbb], w=[B("DFT")])
        s.barrier()


def phase_fnet(k, l, need_ctx=True):
    nc, s, I, B = k.nc, k.s, k.I, k.B
    with ExitStack() as st:
        T_ = lambda name, shape, dt: st.enter_context(nc.sbuf_tensor(name + k.sfx, shape, dt))
        P_ = lambda name, shape, dt: st.enter_context(nc.psum_tensor(name + k.sfx, shape, dt))
        cs = T_("f_cs", [128, 256], BF16)
        fw32 = T_("f_fw32", [128, 4, 512], F32)
        fw = T_("f_fw", [128, 4, 512], BF16)
        fb = T_("f_fb", [128, 4], F32)
        fuT = [T_(f"f_fuT{i}", [128, 4, 128], BF16) for i in range(2)]
        PQ = T_("f_PQ", [128, 34, 4, 2, 128], BF16)
        tabs = [T_(f"f_tab{i}", [128, 2, 512], BF16) for i in range(4)]
        specT = [T_(f"f_spec{i}", [128, 4, 512], BF16) for i in range(2)]
        gt = [T_(f"f_g{i}", [128, 512], F32) for i in range(2)]
        ob = [T_(f"f_ob{i}", [128, 512], BF16) for i in range(2)]
        psPQ = [P_(f"psPQ{i}", [128, 512], F32) for i in range(2)]
        psS = [P_(f"psS{i}", [128, 512], F32) for i in range(4)]
        psM = [P_(f"psMx{i}", [128, 512], F32) for i in range(2)]
        rows32 = lambda tsr: bass.AP(tsr.tensor, tsr.offset, [[32 * 4096, 128], [1, 128]])
        s.dma("sp", cs[:, 0:128], rows32(k.CL), r=[B("DFT")], w=[B("f_cs")])
        s.dma("sp", cs[:, 128:256], rows32(k.SLN), r=[B("DFT")], w=[B("f_cs")])
        s.dma("sp", fw32[:], I["fnet_w"][l].rearrange("(t p) c -> p t c", p=128), w=[B("f_fw32")])
        s.dma("sp", fb[:], I["fnet_b"][l].rearrange("(t p) -> p t", p=128), w=[B("f_fb")], allow_slow_non_contiguous=True)
        s.op("dve", lambda: nc.vector.tensor_copy(out=fw[:], in_=fw32[:]), r=[B("f_fw32")], w=[B("f_fw")])
        for tt in range(34):
            if tt < 2 and not need_ctx:
                continue
            f_ = fuT[tt % 2]; fb_ = B(f"f_fuT{tt % 2}")
            s.dma("sp", f_[:], k.PB[2048:2560, tt * 128:(tt + 1) * 128].rearrange("(h p) j -> p h j", p=128), r=[B("PB")], w=[fb_])
            for hd in range(4):
                pp = psPQ[hd // 2]; ppb = B(f"psPQ{hd // 2}")
                s.op("pe", lambda pp=pp, hd=hd, f_=f_: nc.tensor.matmul(pp[:, (hd % 2) * 256:(hd % 2) * 256 + 256], lhsT=f_[:, hd, :], rhs=cs[:], start=True, stop=True), r=[fb_, B("f_cs")], w=[ppb])
            for hf in range(2):
                pp = psPQ[hf]; ppb = B(f"psPQ{hf}")
                src = pp[:].rearrange("p (h q m) -> p h q m", h=2, q=2)
                s.op("act", lambda tt=tt, hf=hf, src=src: nc.scalar.copy(out=PQ[:, tt, hf * 2:hf * 2 + 2, 0, :], in_=src[:, :, 0, :]), w=[ppb, B(f"f_PQ{tt}")])
                s.op("dve", lambda tt=tt, hf=hf, src=src: nc.vector.tensor_scalar(out=PQ[:, tt, hf * 2:hf * 2 + 2, 1, :], in0=src[:, :, 1, :], scalar1=-1.0, scalar2=None, op0=ALU.mult), w=[ppb, B(f"f_PQ{tt}")])
        nload = 0
        jobs = []
        if need_ctx:
            jobs.append(("c", 0))
        jobs += [("l", kt) for kt in range(8)]
        for ji, (kind, kt) in enumerate(jobs):
            if kind == "c":
                nlt = 2; ncol = 256; tt0 = 0; tok0 = 0; nrm = 1.0 / math.sqrt(256.0 * 128.0)
            else:
                nlt = 32; ncol = 512; tt0 = 2; tok0 = 256 + kt * 512; nrm = 1.0 / math.sqrt(4096.0 * 128.0)
            for lt in range(nlt):
                tab = tabs[nload % 4]; tabb = B(f"f_tab{nload % 4}")
                for j, tsr in enumerate([k.CL, k.SLN]):
                    if kind == "c":
                        src = bass.AP(tsr.tensor, tsr.offset + (lt * 128) * 16 * 4096, [[16 * 4096, 128], [1, 256]])
                        s.dma("sp", tab[:, j, 0:256], src, r=[B("DFT")], w=[tabb], allow_slow_non_contiguous=True)
                    else:
                        s.dma("sp", tab[:, j, :], tsr[lt * 128:(lt + 1) * 128, kt * 512:(kt + 1) * 512], r=[B("DFT")], w=[tabb])
                nload += 1
                for hd in range(4):
                    s.op("pe", lambda hd=hd, lt=lt, tab=tab, ncol=ncol, tt0=tt0: nc.tensor.matmul(psS[hd][:, 0:ncol], lhsT=PQ[:, tt0 + lt, hd, 0, :], rhs=tab[:, 0, 0:ncol], start=(lt == 0), stop=False),
                         r=[B(f"f_PQ{tt0 + lt}"), tabb], w=[B(f"psS{hd}")])
                    s.op("pe", lambda hd=hd, lt=lt, tab=tab, ncol=ncol, tt0=tt0, nlt=nlt: nc.tensor.matmul(psS[hd][:, 0:ncol], lhsT=PQ[:, tt0 + lt, hd, 1, :], rhs=tab[:, 1, 0:ncol], start=False, stop=(lt == nlt - 1)),
                         r=[B(f"f_PQ{tt0 + lt}"), tabb], w=[B(f"psS{hd}")])
            sp_ = specT[ji % 2]; spb = B(f"f_spec{ji % 2}")
            for hd in range(4):
                if hd % 2 == 0:
                    s.op("act", lambda hd=hd, sp_=sp_, ncol=ncol, nrm=nrm: nc.scalar.mul(out=sp_[:, hd, 0:ncol], in_=psS[hd][:, 0:ncol], mul=nrm), w=[B(f"psS{hd}"), spb])
                else:
                    s.op("dve", lambda hd=hd, sp_=sp_, ncol=ncol, nrm=nrm: nc.vector.tensor_scalar(out=sp_[:, hd, 0:ncol], in0=psS[hd][:, 0:ncol], scalar1=nrm, scalar2=None, op0=ALU.mult), w=[B(f"psS{hd}"), spb])
            for ct in range(4):
                pm = psM[ct % 2]; pmb = B(f"psMx{ct % 2}")
                g_ = gt[ct % 2]; gb = B(f"f_g{ct % 2}")
                s.dma("sp", g_[:, 0:ncol], k.PF[544 + ct * 128:544 + (ct + 1) * 128, tok0:tok0 + ncol], r=[B("PF")], w=[gb])
                for hd in range(4):
                    s.op("pe", lambda pm=pm, hd=hd, ct=ct, sp_=sp_, ncol=ncol: nc.tensor.matmul(pm[:, 0:ncol], lhsT=fw[:, hd, ct * 128:(ct + 1) * 128], rhs=sp_[:, hd, 0:ncol], start=(hd == 0), stop=(hd == 3)),
                         r=[B("f_fw"), spb], w=[pmb])
                s.op("act", lambda g_=g_, ncol=ncol: nc.scalar.activation(out=g_[:, 0:ncol], in_=g_[:, 0:ncol], func=AF.Silu), w=[gb])
                o_ = ob[ct % 2]; obb = B(f"f_ob{ct % 2}")
                s.op("dve", lambda o_=o_, pm=pm, ct=ct, g_=g_, ncol=ncol: nc.vector.scalar_tensor_tensor(out=o_[:, 0:ncol], in0=pm[:, 0:ncol], scalar=fb[:, ct:ct + 1], in1=g_[:, 0:ncol], op0=ALU.add, op1=ALU.mult),
                     r=[gb, B("f_fb")], w=[pmb, obb])
                s.dma("pool", k.MT[1536 + ct * 128:1536 + (ct + 1) * 128, tok0:tok0 + ncol], o_[:, 0:ncol], r=[obb], w=[B("MT")])
        s.barrier()


def phase_out(k, l, last=False):
    nc, s, I, B = k.nc, k.s, k.I, k.B
    with ExitStack() as st:
        T_ = lambda name, shape, dt: st.enter_context(nc.sbuf_tensor(name + k.sfx, shape, dt))
        P_ = lambda name, shape, dt: st.enter_context(nc.psum_tensor(name + k.sfx, shape, dt))
        wo = T_("o_wo", [128, 16, D], BF16)
        wst = [T_(f"o_wst{i}", [128, 2, D], F32) for i in range(2)]
        bc = T_("o_bc", [128, 2, D], F32)
        mT = [T_(f"o_mT{i}", [128, 16, 128], BF16) for i in range(2)]
        xt = [T_(f"o_x{i}", [128, D], F32) for i in range(2)]
        ot = [T_(f"o_o{i}", [128, D], F32) for i in range(2)]
        junk = T_("o_junk", [128, 512], BF16)
        st4 = T_("o_st", [128, 4], F32)
        psO = [P_(f"psO{i}", [128, 512], F32) for i in range(4)]
        for c in range(8):
            w_ = wst[c % 2]; wb = B(f"o_wst{c % 2}")
            s.dma("sp", w_[:], I["w_out"][l, c * 256:(c + 1) * 256, :].rearrange("(t p) c -> p t c", p=128), w=[wb])
            if c % 2 == 0:
                s.op("act", lambda w_=w_, c=c: nc.scalar.copy(out=wo[:, 2 * c:2 * c + 2, :], in_=w_[:]), r=[wb], w=[B("o_wo")])
            else:
                s.op("pool", lambda w_=w_, c=c: nc.gpsimd.tensor_copy(out=wo[:, 2 * c:2 * c + 2, :], in_=w_[:]), r=[wb], w=[B("o_wo")])
        for j in range(2):
            s.dma("pool", bc[:, j, :], k.MODS[l, 1 - j, 2, :].partition_broadcast(128), r=[B("MODS")], w=[B("o_bc")])
        for n_, tt in enumerate(range(2 if last else 0, 34)):
            t0 = tt * 128
            m_ = mT[n_ % 2]; mb = B(f"o_mT{n_ % 2}")
            s.dma("sp", m_[:], k.MT[:, t0:t0 + 128].rearrange("(t p) j -> p t j", p=128), r=[B("MT")], w=[mb])
            x_ = xt[n_ % 2]; xb = B(f"o_x{n_ % 2}")
            if l == 0:
                src = I["ctx"][t0:t0 + 128, :] if tt < 2 else I["x"][t0 - 256:t0 - 128, :]
                s.dma("sp", x_[:], src, w=[xb])
            else:
                s.dma("sp", x_[:], k.XS[t0:t0 + 128, :], r=[B("XS")], w=[xb])
            for hf in range(2):
                po = psO[(n_ % 2) * 2 + hf]; pob = B(f"psO{(n_ % 2) * 2 + hf}")
                for ct in range(16):
                    s.op("pe", lambda po=po, m_=m_, ct=ct, hf=hf: nc.tensor.matmul(po[:], lhsT=m_[:, ct, :], rhs=wo[:, ct, hf * 512:(hf + 1) * 512], start=(ct == 0), stop=(ct == 15)),
                         r=[mb, B("o_wo")], w=[pob])
                s.op("act", lambda po=po, hf=hf: nc.scalar.activation(out=junk[:], in_=po[:], func=AF.Square, accum_out=st4[:, hf:hf + 1]), w=[pob, B("o_junk"), B("o_st")])
            s.op("dve", lambda: nc.vector.tensor_tensor(out=st4[:, 2:3], in0=st4[:, 0:1], in1=st4[:, 1:2], op=ALU.add), w=[B("o_st")])
            s.op("dve", lambda: nc.vector.tensor_scalar(out=st4[:, 2:3], in0=st4[:, 2:3], scalar1=1.0 / D, scalar2=EPS, op0=ALU.mult, op1=ALU.add), w=[B("o_st")])
            s.op("act", lambda: nc.scalar.activation(out=st4[:, 2:3], in_=st4[:, 2:3], func=AF.Sqrt), w=[B("o_st")])
            s.op("dve", lambda: nc.vector.reciprocal(out=st4[:, 3:4], in_=st4[:, 2:3]), w=[B("o_st")])
            o_ = ot[n_ % 2]; ob = B(f"o_o{n_ % 2}")
            jb = 0 if tt < 2 else 1
            for hf in range(2):
                po = psO[(n_ % 2) * 2 + hf]; pob = B(f"psO{(n_ % 2) * 2 + hf}")
                s.op("dve", lambda po=po, hf=hf, o_=o_, jb=jb: nc.vector.scalar_tensor_tensor(out=o_[:, hf * 512:(hf + 1) * 512], in0=po[:], scalar=st4[:, 3:4], in1=bc[:, jb, hf * 512:(hf + 1) * 512],
                                                                                      op0=ALU.mult, op1=ALU.mult), r=[B("o_st"), B("o_bc")], w=[pob, ob])
            s.op("pool", lambda o_=o_, x_=x_: nc.gpsimd.tensor_tensor(out=o_[:], in0=o_[:], in1=x_[:], op=ALU.add), r=[xb], w=[ob])
            if last:
                s.dma("pool", k.out[t0 - 256:t0 - 128, :], o_[:], r=[ob], w=[B("OUT")])
            else:
                s.dma("pool", k.XS[t0:t0 + 128, :], o_[:], r=[ob], w=[B("XS")])
        s.barrier()

T = 4352; D = 1024; NB = 544
C_MVEC = 128; C_BM8 = 524; C_IDXF = 1024; C_IDXB = 1600
I32 = mybir.dt.int32
TWO_PI = 2.0 * math.pi

def phase_s5(k, l, need_ctx=True):
    nc, s, I, B = k.nc, k.s, k.I, k.B
    cst = k.cst
    V = lambda fn, r=(), w=(): s.op("dve", fn, r=r, w=w)
    A = lambda fn, r=(), w=(): s.op("act", fn, r=r, w=w)
    G = lambda fn, r=(), w=(): s.op("pool", fn, r=r, w=w)
    PE = lambda fn, r=(), w=(): s.op("pe", fn, r=r, w=w)
    with ExitStack() as st:
        T_ = lambda name, shape, dt: st.enter_context(nc.sbuf_tensor(name + k.sfx, shape, dt))
        WS = T_("q_WS", [128, 2, 4, 8, 2, 128], BF16)
        WR = T_("q_WR", [128, 2, 16, 8, 2, 32], BF16)
        KD = T_("q_KD", [128, 2, 4, 8, 128], BF16)
        rho = T_("q_rho", [128, 2, 16], F32)
        th = T_("q_th", [128, 2, 16], F32)
        with ExitStack() as ps:
            TP = lambda name, shape, dt: ps.enter_context(nc.sbuf_tensor(name + k.sfx, shape, dt))
            PP = lambda name, shape, dt: ps.enter_context(nc.psum_tensor(name + k.sfx, shape, dt))
            lam16 = TP("p_lam16", [16, 2, 128], F32)
            lr = TP("p_lr", [128, 16], F32); li = TP("p_li", [128, 16], F32)
            stp = TP("p_stp", [128, 16], F32)
            lrs = TP("p_lrs", [128, 16], F32); lis = TP("p_lis", [128, 16], F32)
            a9 = TP("p_a9", [128, 16, 9], F32); a9b = TP("p_a9b", [128, 16, 9], F32)
            ki = TP("p_ki", [128, 16, 9], I32)
            mag9 = TP("p_mag9", [128, 16, 9], F32)
            Ar = TP("p_Ar", [128, 16, 9], F32); Ai = TP("p_Ai", [128, 16, 9], F32)
            t16 = [TP(f"p_t16_{i}", [128, 16], F32) for i in range(6)]
            Br = TP("p_Br", [128, 16, 16], F32); Bi = TP("p_Bi", [128, 16, 16], F32)
            Bbr = TP("p_Bbr", [128, 16, 16], F32); Bbi = TP("p_Bbi", [128, 16, 16], F32)
            tB = TP("p_tB", [128, 16, 16], F32)
            Cn = [TP(f"p_Cn{i}", [128, 128], F32) for i in range(2)]
            Cr = TP("p_Cr", [128, 16, 16], F32); Ci = TP("p_Ci", [128, 16, 16], F32)
            Zr = TP("p_Zr", [128, 16, 9, 16], F32); Zi = TP("p_Zi", [128, 16, 9, 16], F32)
            tZ = TP("p_tZ", [128, 16, 9, 16], F32)
            Pr = TP("p_Pr", [128, 16, 8, 16], F32); Pi = TP("p_Pi", [128, 16, 8, 16], F32)
            BPr = TP("p_BPr", [128, 16, 128], F32); BPi = TP("p_BPi", [128, 16, 128], F32)
            PPd = [TP(f"p_PPd{i}", [128, 16, 128], BF16) for i in range(2)]
            kdc = TP("p_kdc", [128, 8, 16], F32)
            psT1 = PP("psT1", [128, 128], F32)
            psK = PP("psK", [128, 128], F32)
            psW = [PP(f"psWs{i}", [128, 128], F32) for i in range(2)]
            G(lambda: nc.gpsimd.memset(WR[:], 0.0), w=[B("q_WR")])
            for d in range(2):
                s.dma("sp", lam16[:, 0, :], I["s5_lambda_re"][l, d].rearrange("(a b) n -> a (b n)", b=2), w=[B("p_lam16")])
                s.dma("sp", lam16[:, 1, :], I["s5_lambda_im"][l, d].rearrange("(a b) n -> a (b n)", b=2), w=[B("p_lam16")])
                for j, dst in enumerate([lr, li]):
                    PE(lambda j=j: nc.tensor.transpose(out=psT1[:, 0:16], in_=lam16[:, j, :], identity=cst[0:16, 0:16]), r=[B("p_lam16"), B("cst")], w=[B("psT1")])
                    V(lambda dst=dst: nc.vector.tensor_copy(out=dst[:], in_=psT1[:, 0:16]), w=[B("psT1"), B("p_l")])
                ls = I["s5_log_step"]
                for g2 in range(2):
                    src = bass.AP(ls.tensor, ls.offset + (l * 2 + d) * 32 + g2, [[0, 64], [2, 16]])
                    s.dma("sp", stp[64 * g2:64 * g2 + 64, :], src, w=[B("p_stp")], allow_slow_non_contiguous=True)
                A(lambda: nc.scalar.activation(out=stp[:], in_=stp[:], func=AF.Exp), w=[B("p_stp")])
                V(lambda: nc.vector.tensor_tensor(out=lrs[:], in0=lr[:], in1=stp[:], op=ALU.mult), r=[B("p_l"), B("p_stp")], w=[B("p_ls")])
                V(lambda: nc.vector.tensor_tensor(out=lis[:], in0=li[:], in1=stp[:], op=ALU.mult), r=[B("p_l"), B("p_stp")], w=[B("p_ls")])
                mv = cst[:, C_MVEC:C_MVEC + 9].unsqueeze(1).to_broadcast([128, 16, 9])
                V(lambda: nc.vector.tensor_tensor(out=a9[:], in0=lrs[:].unsqueeze(2).to_broadcast([128, 16, 9]), in1=mv, op=ALU.mult), r=[B("p_ls"), B("cst")], w=[B("p_a9")])
                A(lambda: nc.scalar.activation(out=mag9[:], in_=a9[:], func=AF.Exp), r=[B("p_a9")], w=[B("p_mag9")])
                V(lambda: nc.vector.tensor_tensor(out=a9[:], in0=lis[:].unsqueeze(2).to_broadcast([128, 16, 9]), in1=mv, op=ALU.mult), r=[B("p_ls"), B("cst")], w=[B("p_a9")])
                def reduce_sin(dst, src_ap, shift, shape3):
                    V(lambda: nc.vector.tensor_scalar(out=a9b[:], in0=src_ap, scalar1=shift, scalar2=None, op0=ALU.add), r=[B("p_a9")], w=[B("p_a9b")])
                    V(lambda: nc.vector.tensor_scalar(out=ki[:], in0=a9b[:], scalar1=1.0 / TWO_PI, scalar2=None, op0=ALU.mult), r=[B("p_a9b")], w=[B("p_ki")])
                    V(lambda: nc.vector.scalar_tensor_tensor(out=a9b[:], in0=ki[:], scalar=-TWO_PI, in1=a9b[:], op0=ALU.mult, op1=ALU.add), r=[B("p_ki")], w=[B("p_a9b")])
                    A(lambda: nc.scalar.activation(out=dst[:], in_=a9b[:], func=AF.Sin), r=[B("p_a9b")], w=[B("p_sc")])
                reduce_sin(Ai, a9[:], 0.0, None)
                V(lambda d=d: nc.vector.tensor_copy(out=th[:, d, :], in_=a9b[:, :, 8]), r=[B("p_a9b")], w=[B("q_th")])
                reduce_sin(Ar, a9[:], math.pi / 2.0, None)
                V(lambda: nc.vector.tensor_tensor(out=Ar[:], in0=Ar[:], in1=mag9[:], op=ALU.mult), r=[B("p_mag9")], w=[B("p_sc")])
                V(lambda: nc.vector.tensor_tensor(out=Ai[:], in0=Ai[:], in1=mag9[:], op=ALU.mult), r=[B("p_mag9")], w=[B("p_sc")])
                V(lambda d=d: nc.vector.tensor_copy(out=rho[:, d, :], in_=mag9[:, :, 8]), r=[B("p_mag9")], w=[B("q_rho")])
                am1, den, fr, fi, u1, u2 = t16
                V(lambda: nc.vector.tensor_scalar(out=am1[:], in0=Ar[:, :, 1], scalar1=-1.0, scalar2=None, op0=ALU.add), r=[B("p_sc")], w=[B("p_t16")])
                V(lambda: nc.vector.tensor_tensor(out=den[:], in0=lr[:], in1=lr[:], op=ALU.mult), r=[B("p_l")], w=[B("p_t16")])
                V(lambda: nc.vector.tensor_tensor(out=u1[:], in0=li[:], in1=li[:], op=ALU.mult), r=[B("p_l")], w=[B("p_t16")])
                V(lambda: nc.vector.tensor_tensor(out=den[:], in0=den[:], in1=u1[:], op=ALU.add), w=[B("p_t16")])
                V(lambda: nc.vector.reciprocal(out=den[:], in_=den[:]), w=[B("p_t16")])
                V(lambda: nc.vector.tensor_tensor(out=u1[:], in0=am1[:], in1=lr[:], op=ALU.mult), w=[B("p_t16")])
                V(lambda: nc.vector.tensor_tensor(out=u2[:], in0=Ai[:, :, 1], in1=li[:], op=ALU.mult), w=[B("p_t16")])
                V(lambda: nc.vector.tensor_tensor(out=fr[:], in0=u1[:], in1=u2[:], op=ALU.add), w=[B("p_t16")])
                V(lambda: nc.vector.tensor_tensor(out=fr[:], in0=fr[:], in1=den[:], op=ALU.mult), w=[B("p_t16")])
                V(lambda: nc.vector.tensor_tensor(out=u1[:], in0=Ai[:, :, 1], in1=lr[:], op=ALU.mult), w=[B("p_t16")])
                V(lambda: nc.vector.tensor_tensor(out=u2[:], in0=am1[:], in1=li[:], op=ALU.mult), w=[B("p_t16")])
                V(lambda: nc.vector.tensor_tensor(out=fi[:], in0=u1[:], in1=u2[:], op=ALU.subtract), w=[B("p_t16")])
                V(lambda: nc.vector.tensor_tensor(out=fi[:], in0=fi[:], in1=den[:], op=ALU.mult), w=[B("p_t16")])
                for j, (dst, nm) in enumerate([(Br, "s5_b_re"), (Bi, "s5_b_im")]):
                    bt = I[nm]
                    for g2 in range(2):
                        src = bass.AP(bt.tensor, bt.offset + ((l * 2 + d) * 32 + g2) * 1024, [[16, 64], [2048, 16], [1, 16]])
                        s.dma("sp", dst[64 * g2:64 * g2 + 64, :, :], src, w=[B("p_B")])
                frb = fr[:].unsqueeze(2).to_broadcast([128, 16, 16]); fib = fi[:].unsqueeze(2).to_broadcast([128, 16, 16])
                V(lambda: nc.vector.tensor_tensor(out=Bbr[:], in0=Br[:], in1=frb, op=ALU.mult), r=[B("p_B"), B("p_t16")], w=[B("p_Bb")])
                V(lambda: nc.vector.tensor_tensor(out=tB[:], in0=Bi[:], in1=fib, op=ALU.mult), r=[B("p_B"), B("p_t16")], w=[B("p_tB")])
                V(lambda: nc.vector.tensor_tensor(out=Bbr[:], in0=Bbr[:], in1=tB[:], op=ALU.subtract), r=[B("p_tB")], w=[B("p_Bb")])
                V(lambda: nc.vector.tensor_tensor(out=Bbi[:], in0=Bi[:], in1=frb, op=ALU.mult), r=[B("p_B"), B("p_t16")], w=[B("p_Bb")])
                V(lambda: nc.vector.tensor_tensor(out=tB[:], in0=Br[:], in1=fib, op=ALU.mult), r=[B("p_B"), B("p_t16")], w=[B("p_tB")])
                V(lambda: nc.vector.tensor_tensor(out=Bbi[:], in0=Bbi[:], in1=tB[:], op=ALU.add), r=[B("p_tB")], w=[B("p_Bb")])
                for j, (dst, nm) in enumerate([(Cr, "s5_c_re"), (Ci, "s5_c_im")]):
                    ct_ = I[nm]
                    for pset in range(2):
                        cn = Cn[pset]; cnb = B(f"p_Cn{pset}")
                        for pr in range(8):
                            g0 = 2 * (8 * pset + pr)
                            src = bass.AP(ct_.tensor, ct_.offset + ((l * 2 + d) * 32 + g0) * 1024, [[64, 16], [1024, 2], [1, 64]])
                            s.dma("sp", cn[16 * pr:16 * pr + 16, :].rearrange("p (a n) -> p a n", a=2), src, w=[cnb])
                        PE(lambda cn=cn: nc.tensor.transpose(out=psT1[:], in_=cn[:], identity=cst[:, 0:128]), r=[cnb, B("cst")], w=[B("psT1")])
                        V(lambda dst=dst, pset=pset: nc.vector.tensor_copy(out=dst[:, 8 * pset:8 * pset + 8, :], in_=psT1[:].rearrange("p (a k) -> p a k", k=16)), w=[B("psT1"), B("p_C")])
                Crb = lambda X: X[:].unsqueeze(2).to_broadcast([128, 16, 9, 16])
                Ab = lambda X: X[:].unsqueeze(3).to_broadcast([128, 16, 9, 16])
                V(lambda: nc.vector.tensor_tensor(out=Zr[:], in0=Crb(Cr), in1=Ab(Ar), op=ALU.mult), r=[B("p_C"), B("p_sc")], w=[B("p_Z")])
                G(lambda: nc.gpsimd.tensor_tensor(out=tZ[:], in0=Crb(Ci), in1=Ab(Ai), op=ALU.mult), r=[B("p_C"), B("p_sc")], w=[B("p_tZ")])
                V(lambda: nc.vector.tensor_tensor(out=Zr[:], in0=Zr[:], in1=tZ[:], op=ALU.subtract), r=[B("p_tZ")], w=[B("p_Z")])
                V(lambda: nc.vector.tensor_tensor(out=Zi[:], in0=Crb(Cr), in1=Ab(Ai), op=ALU.mult), r=[B("p_C"), B("p_sc")], w=[B("p_Z")])
                G(lambda: nc.gpsimd.tensor_tensor(out=tZ[:], in0=Crb(Ci), in1=Ab(Ar), op=ALU.mult), r=[B("p_C"), B("p_sc")], w=[B("p_tZ")])
                V(lambda: nc.vector.tensor_tensor(out=Zi[:], in0=Zi[:], in1=tZ[:], op=ALU.add), r=[B("p_tZ")], w=[B("p_Z")])
                for g2 in range(2):
                    sl = slice(64 * g2, 64 * g2 + 64)
                    V(lambda sl=sl, g2=g2, d=d: nc.vector.tensor_copy(out=WR[sl, d, :, :, 0, 16 * g2:16 * g2 + 16], in_=Zr[sl, :, 1:9, :]), r=[B("p_Z")], w=[B("q_WR")])
                    V(lambda sl=sl, g2=g2, d=d: nc.vector.tensor_scalar(out=WR[sl, d, :, :, 1, 16 * g2:16 * g2 + 16], in0=Zi[sl, :, 1:9, :], scalar1=-1.0, scalar2=None, op0=ALU.mult), r=[B("p_Z")], w=[B("q_WR")])
                Bb8 = lambda X: X[:].unsqueeze(2).to_broadcast([128, 16, 8, 16])
                A8 = lambda X: X[:, :, 0:8].unsqueeze(3).to_broadcast([128, 16, 8, 16])
                tP = tZ[:, :, 0:8, :]
                V(lambda: nc.vector.tensor_tensor(out=Pr[:], in0=Bb8(Bbr), in1=A8(Ar), op=ALU.mult), r=[B("p_Bb"), B("p_sc")], w=[B("p_P")])
                G(lambda: nc.gpsimd.tensor_tensor(out=tP, in0=Bb8(Bbi), in1=A8(Ai), op=ALU.mult), r=[B("p_Bb"), B("p_sc")], w=[B("p_tZ")])
                V(lambda: nc.vector.tensor_tensor(out=Pr[:], in0=Pr[:], in1=tP, op=ALU.subtract), r=[B("p_tZ")], w=[B("p_P")])
                V(lambda: nc.vector.tensor_tensor(out=Pi[:], in0=Bb8(Bbi), in1=A8(Ar), op=ALU.mult), r=[B("p_Bb"), B("p_sc")], w=[B("p_P")])
                G(lambda: nc.gpsimd.tensor_tensor(out=tP, in0=Bb8(Bbr), in1=A8(Ai), op=ALU.mult), r=[B("p_Bb"), B("p_sc")], w=[B("p_tZ")])
                V(lambda: nc.vector.tensor_tensor(out=Pi[:], in0=Pi[:], in1=tP, op=ALU.add), r=[B("p_tZ")], w=[B("p_P")])
                G(lambda: nc.gpsimd.memset(BPr[:], 0.0), w=[B("p_BP")])
                G(lambda: nc.gpsimd.memset(BPi[:], 0.0), w=[B("p_BP")])
                for g2 in range(2):
                    sl = slice(64 * g2, 64 * g2 + 64)
                    for q in range(4):
                        c0 = 32 * q + 16 * g2
                        V(lambda sl=sl, q=q, c0=c0: nc.vector.tensor_copy(out=BPr[sl, q::4, c0:c0 + 16], in_=Bbr[sl, q::4, :]), r=[B("p_Bb")], w=[B("p_BP")])
                        V(lambda sl=sl, q=q, c0=c0: nc.vector.tensor_scalar(out=BPi[sl, q::4, c0:c0 + 16], in0=Bbi[sl, q::4, :], scalar1=-1.0, scalar2=None, op0=ALU.mult), r=[B("p_Bb")], w=[B("p_BP")])
                for t in range(4):
                    n_ = 0
                    for q in range(4):
                        pr = 4 * t + q
                        for (BP_, Z_) in ((BPr, Zr), (BPi, Zi)):
                            PE(lambda pr=pr, BP_=BP_, Z_=Z_, n_=n_: nc.tensor.matmul(psK[:], lhsT=BP_[:, pr, :], rhs=Z_[:, pr, 0:8, :].rearrange("p a k -> p (a k)"), start=(n_ == 0), stop=(n_ == 7)),
                               r=[B("p_BP"), B("p_Z")], w=[B("psK")])
                            n_ += 1
                    V(lambda: nc.vector.tensor_copy(out=kdc[:], in_=psK[:].rearrange("p (a k) -> p a k", k=16)), w=[B("psK"), B("p_kdc")])
                    V(lambda t=t, d=d: nc.vector.tensor_tensor(out=KD[:, d, t, :, :].rearrange("p a (g k) -> p a g k", k=16), in0=kdc[:].unsqueeze(2).to_broadcast([128, 8, 8, 16]),
                                                          in1=cst[:, C_BM8:C_BM8 + 8].unsqueeze(1).unsqueeze(3).to_broadcast([128, 8, 8, 16]), op=ALU.mult), r=[B("p_kdc"), B("cst")], w=[B("q_KD")])
                n_w = 0
                for m in range(8):
                    for ri, P_ in enumerate((Pr, Pi)):
                        pd = PPd[n_w % 2]; pdb = B(f"p_PPd{n_w % 2}")
                        G(lambda pd=pd: nc.gpsimd.memset(pd[:], 0.0), w=[pdb])
                        for g2 in range(2):
                            sl = slice(64 * g2, 64 * g2 + 64)
                            for q in range(4):
                                c0 = 32 * q + 16 * g2
                                e = V if (q % 2 == 0) else G
                                eng = nc.vector if (q % 2 == 0) else nc.gpsimd
                                e(lambda sl=sl, q=q, c0=c0, pd=pd, P_=P_, m=m, eng=eng: eng.tensor_copy(out=pd[sl, q::4, c0:c0 + 16], in_=P_[sl, q::4, m, :]), r=[B("p_P")], w=[pdb])
                        for t in range(4):
                            pw = psW[t % 2]; pwb = B(f"psWs{t % 2}")
                            for q in range(4):
                                PE(lambda pw=pw, pd=pd, t=t, q=q: nc.tensor.matmul(pw[:], lhsT=pd[:, 4 * t + q, :], rhs=k.identb[:], start=(q == 0), stop=(q == 3)), r=[pdb, B("identb")], w=[pwb])
                            if t % 2 == 0:
                                A(lambda pw=pw, t=t, m=m, ri=ri, d=d: nc.scalar.copy(out=WS[:, d, t, m, ri, :], in_=pw[:]), w=[pwb, B("q_WS")])
                            else:
                                V(lambda pw=pw, t=t, m=m, ri=ri, d=d: nc.vector.tensor_copy(out=WS[:, d, t, m, ri, :], in_=pw[:]), w=[pwb, B("q_WS")])
                        n_w += 1
            s.barrier()
        if getattr(k, 'stop', None) == 's5prep':
            return
        phase_s5_main(k, l, need_ctx, WS, WR, KD, rho, th, st)
        s.barrier()


def phase_s5_main(k, l, need_ctx, WS, WR, KD, rho, th, st):
    nc, s, I, B = k.nc, k.s, k.I, k.B
    cst = k.cst
    V = lambda fn, r=(), w=(): s.op("dve", fn, r=r, w=w)
    A = lambda fn, r=(), w=(): s.op("act", fn, r=r, w=w)
    G = lambda fn, r=(), w=(): s.op("pool", fn, r=r, w=w)
    PE = lambda fn, r=(), w=(): s.op("pe", fn, r=r, w=w)
    T_ = lambda name, shape, dt: st.enter_context(nc.sbuf_tensor(name + k.sfx, shape, dt))
    P_ = lambda name, shape, dt: st.enter_context(nc.psum_tensor(name + k.sfx, shape, dt))
    uT = st.enter_context(nc.sbuf_tensor("q_uT" + k.sfx, [128, 4, T], BF16))
    mst = ExitStack()
    T_ = lambda name, shape, dt: mst.enter_context(nc.sbuf_tensor(name + k.sfx, shape, dt))
    hst = [T_(f"q_hst{i}", [128, 2, NB], BF16) for i in range(2)]
    SL = []
    for i_ in range(2):
        o = {}
        for nm in ("Sr", "Si", "cosT", "sinT", "ang", "xr", "xi", "t1", "t2", "Gr", "Gi"):
            o[nm] = T_(f"q_{nm}{i_}", [128, NB], F32)
        o["kiT"] = T_(f"q_ki{i_}", [128, NB], I32)
        o["psS"] = [mst.enter_context(nc.psum_tensor(f"psSq{i_}_{j}" + k.sfx, [128, 1024], F32)) for j in range(2)]
        SL.append(o)
    for t in range(4):
        s.dma("sp", uT[:, t, :], k.PB[1536 + t * 128:1536 + (t + 1) * 128, :], r=[B("PB")], w=[B("q_uT")])
    pieces = [(32, 288, 0), (288, 544, 256), (0, 32, 512)]
    def it_gen(d, pr, si):
        o = SL[si]
        Sr, Si, cosT, sinT, ang, xr, xi, t1, t2, Gr, Gi, kiT, psS = (o[n] for n in ('Sr','Si','cosT','sinT','ang','xr','xi','t1','t2','Gr','Gi','kiT','psS'))
        sfx_ = str(si)
        t = pr // 4; q = pr % 4
        rows = slice(32 * q, 32 * q + 32)
        for ri in range(2):
            for (b0, b1, pc) in pieces:
                for pos in range(8):
                    m = 7 - pos if d == 0 else pos
                    rhs = uT[rows, t, 8 * b0 + pos:8 * b1:8]
                    PE(lambda ri=ri, pc=pc, b0=b0, b1=b1, m=m, rhs=rhs, pos=pos: nc.tensor.matmul(psS[ri][:, pc:pc + (b1 - b0)], lhsT=WS[rows, d, t, m, ri, :], rhs=rhs, start=(pos == 0), stop=(pos == 7),
                                                                                        tile_position=(32 * q, 0), skip_group_check=True),
                       r=[B("q_uT"), B("q_WS")], w=[B(f"psSq{ri}_" + sfx_)])
        yield
        for ri, dst in enumerate((Sr, Si)):
            A(lambda ri=ri, dst=dst: nc.scalar.copy(out=dst[:, 32:544], in_=psS[ri][:, 0:512]), w=[B(f"psSq{ri}_" + sfx_), B("q_S" + sfx_)])
            A(lambda ri=ri, dst=dst: nc.scalar.copy(out=dst[:, 0:32], in_=psS[ri][:, 512:544]), w=[B(f"psSq{ri}_" + sfx_), B("q_S" + sfx_)])
        yield
        idx = cst[:, C_IDXF:C_IDXF + NB] if d == 0 else cst[:, C_IDXB:C_IDXB + NB]
        for (dst, shift) in ((sinT, 0.0), (cosT, math.pi / 2.0)):
            V(lambda shift=shift: nc.vector.tensor_scalar(out=ang[:], in0=idx, scalar1=th[:, d, pr:pr + 1], scalar2=shift, op0=ALU.mult, op1=ALU.add), r=[B("q_th"), B("cst")], w=[B("q_ang" + sfx_)])
            V(lambda: nc.vector.tensor_scalar(out=kiT[:], in0=ang[:], scalar1=1.0 / TWO_PI, scalar2=None, op0=ALU.mult), r=[B("q_ang" + sfx_)], w=[B("q_ki" + sfx_)])
            V(lambda: nc.vector.scalar_tensor_tensor(out=ang[:], in0=kiT[:], scalar=-TWO_PI, in1=ang[:], op0=ALU.mult, op1=ALU.add), r=[B("q_ki" + sfx_)], w=[B("q_ang" + sfx_)])
            A(lambda dst=dst: nc.scalar.activation(out=dst[:], in_=ang[:], func=AF.Sin), r=[B("q_ang" + sfx_)], w=[B("q_tw" + sfx_)])
        yield
        V(lambda: nc.vector.tensor_tensor(out=xr[:], in0=Sr[:], in1=cosT[:], op=ALU.mult), r=[B("q_S" + sfx_), B("q_tw" + sfx_)], w=[B("q_xr" + sfx_)])
        G(lambda: nc.gpsimd.tensor_tensor(out=t1[:], in0=Si[:], in1=sinT[:], op=ALU.mult), r=[B("q_S" + sfx_), B("q_tw" + sfx_)], w=[B("q_t1" + sfx_)])
        V(lambda: nc.vector.tensor_tensor(out=xr[:], in0=xr[:], in1=t1[:], op=ALU.add), r=[B("q_t1" + sfx_)], w=[B("q_xr" + sfx_)])
        G(lambda: nc.gpsimd.tensor_tensor(out=xi[:], in0=Si[:], in1=cosT[:], op=ALU.mult), r=[B("q_S" + sfx_), B("q_tw" + sfx_)], w=[B("q_xi" + sfx_)])
        V(lambda: nc.vector.tensor_tensor(out=t2[:], in0=Sr[:], in1=sinT[:], op=ALU.mult), r=[B("q_S" + sfx_), B("q_tw" + sfx_)], w=[B("q_t2" + sfx_)])
        G(lambda: nc.gpsimd.tensor_tensor(out=xi[:], in0=xi[:], in1=t2[:], op=ALU.subtract), r=[B("q_t2" + sfx_)], w=[B("q_xi" + sfx_)])
        yield
        rcol = rho[:, d, pr:pr + 1]
        for (src, dst, nm) in ((xr, Gr, "q_xr"), (xi, Gi, "q_xi")):
            if d == 0:
                V(lambda src=src, dst=dst: nc.vector.tensor_tensor_scan(out=dst[:], data0=rcol.to_broadcast([128, NB]), data1=src[:], initial=0.0, op0=ALU.mult, op1=ALU.add),
                  r=[B(nm), B("q_rho")], w=[B("q_G" + sfx_)])
            else:
                rv = lambda X, a, b: bass.AP(X[:].tensor, X[:, b - 1:b].offset, [list(X[:].ap[0]), [-1, b - a]])
                V(lambda src=src, dst=dst: nc.vector.tensor_tensor_scan(out=rv(dst, 0, 32), data0=rcol.to_broadcast([128, 32]), data1=rv(src, 0, 32), initial=0.0, op0=ALU.mult, op1=ALU.add),
                  r=[B(nm), B("q_rho")], w=[B("q_G" + sfx_)])
                V(lambda src=src, dst=dst: nc.vector.tensor_tensor_scan(out=rv(dst, 32, 544), data0=rcol.to_broadcast([128, 512]), data1=rv(src, 32, 544), initial=dst[:, 0:1], op0=ALU.mult, op1=ALU.add),
                  r=[B(nm), B("q_rho")], w=[B("q_G" + sfx_)])
        yield
        hs_ = hst[si]; hsb = B(f"q_hst{si}")
        if d == 0:
            G(lambda hs_=hs_: nc.gpsimd.memset(hs_[:, :, 0:1], 0.0), w=[hsb])
        if d == 0:
            so, si_ = slice(1, 544), slice(0, 543)
        else:
            so, si_ = slice(0, 543), slice(1, 544)
        V(lambda: nc.vector.tensor_tensor(out=t1[:], in0=Gr[:], in1=cosT[:], op=ALU.mult), r=[B("q_G" + sfx_), B("q_tw" + sfx_)], w=[B("q_t1" + sfx_)])
        G(lambda: nc.gpsimd.tensor_tensor(out=t2[:], in0=Gi[:], in1=sinT[:], op=ALU.mult), r=[B("q_G" + sfx_), B("q_tw" + sfx_)], w=[B("q_t2" + sfx_)])
        V(lambda: nc.vector.tensor_tensor(out=hs_[:, 0, so], in0=t1[:, si_], in1=t2[:, si_], op=ALU.subtract), r=[B("q_t1" + sfx_), B("q_t2" + sfx_)], w=[hsb])
        if d == 1:
            V(lambda: nc.vector.tensor_tensor(out=hs_[:, 0, 543:544], in0=t1[:, 0:1], in1=t2[:, 0:1], op=ALU.subtract), r=[B("q_t1" + sfx_), B("q_t2" + sfx_)], w=[hsb])
        G(lambda: nc.gpsimd.tensor_tensor(out=xr[:], in0=Gr[:], in1=sinT[:], op=ALU.mult), r=[B("q_G" + sfx_), B("q_tw" + sfx_)], w=[B("q_xr" + sfx_)])
        V(lambda: nc.vector.tensor_tensor(out=xi[:], in0=Gi[:], in1=cosT[:], op=ALU.mult), r=[B("q_G" + sfx_), B("q_tw" + sfx_)], w=[B("q_xi" + sfx_)])
        G(lambda: nc.gpsimd.tensor_tensor(out=hs_[:, 1, so], in0=xr[:, si_], in1=xi[:, si_], op=ALU.add), r=[B("q_xr" + sfx_), B("q_xi" + sfx_)], w=[hsb])
        if d == 1:
            G(lambda: nc.gpsimd.tensor_tensor(out=hs_[:, 1, 543:544], in0=xr[:, 0:1], in1=xi[:, 0:1], op=ALU.add), r=[B("q_xr" + sfx_), B("q_xi" + sfx_)], w=[hsb])
            G(lambda: nc.gpsimd.memset(hs_[:, :, 31:32], 0.0), w=[hsb])
        s.dma("sp", k.HD[d, pr].rearrange("r p c -> p r c"), hs_[:], r=[hsb], w=[B("HD")])

    def interleave(gens):
        gens = list(gens)
        while gens:
            for g_ in list(gens):
                try:
                    next(g_)
                except StopIteration:
                    gens.remove(g_)
    its = [(d, pr) for d in range(2) for pr in range(16)]
    for i_ in range(0, len(its), 2):
        interleave([it_gen(its[i_][0], its[i_][1], 0), it_gen(its[i_ + 1][0], its[i_ + 1][1], 1)])
    s.barrier()
    mst.close()
    if getattr(k, 'stop', None) == 's5main':
        return
    s5_glu(k, l, need_ctx, WR, KD, uT, None, st)


def s5_glu(k, l, need_ctx, WR, KD, uT, Hin, st):
    nc, s, I, B = k.nc, k.s, k.I, k.B
    V = lambda fn, r=(), w=(): s.op("dve", fn, r=r, w=w)
    A = lambda fn, r=(), w=(): s.op("act", fn, r=r, w=w)
    G = lambda fn, r=(), w=(): s.op("pool", fn, r=r, w=w)
    PE = lambda fn, r=(), w=(): s.op("pe", fn, r=r, w=w)
    T_ = lambda name, shape, dt: st.enter_context(nc.sbuf_tensor(name + k.sfx, shape, dt))
    P_ = lambda name, shape, dt: st.enter_context(nc.psum_tensor(name + k.sfx, shape, dt))
    wgs = T_("g_wgs", [128, 512], F32); wg = T_("g_wg", [128, 4, 512], BF16)
    bg = T_("g_bg", [128, 4], F32); dsk = T_("g_dsk", [128, 4], F32)
    dgs = T_("g_dgs", [128, 4, 128], BF16)
    y32 = [T_(f"g_y{i}", [128, 512], F32) for i in range(2)]
    y2 = [T_(f"g_y2{i}", [128, 512], F32) for i in range(2)]
    vT = T_("g_v", [128, 4, 2048], BF16)
    gs = [T_(f"g_gs{i}", [128, 512], F32) for i in range(2)]
    o1 = T_("g_o1", [128, 512], F32)
    ob = [T_(f"g_ob{i}", [128, 512], BF16) for i in range(2)]
    Hc = T_("g_Hc", [128, 64, 256], BF16)
    py4 = P_("psY4", [128, 2048], F32)
    psG = [P_(f"psGq{i}", [128, 512], F32) for i in range(2)]
    for t in range(4):
        s.dma("sp", wgs[:], I["s5_w_glu"][l, t * 128:(t + 1) * 128, :], w=[B("g_wgs")])
        V(lambda t=t: nc.vector.tensor_copy(out=wg[:, t, :], in_=wgs[:]), r=[B("g_wgs")], w=[B("g_wg")])
    s.dma("sp", bg[:], I["s5_b_glu"][l].rearrange("(t p) -> p t", p=128), w=[B("g_bg")], allow_slow_non_contiguous=True)
    s.dma("sp", dsk[:], I["s5_d"][l].rearrange("(t p) -> p t", p=128), w=[B("g_dsk")], allow_slow_non_contiguous=True)
    for t in range(4):
        V(lambda t=t: nc.vector.tensor_scalar(out=dgs[:, t, :], in0=k.identb[:], scalar1=dsk[:, t:t + 1], scalar2=None, op0=ALU.mult), r=[B("g_dsk"), B("identb")], w=[B("g_dgs")])
    chunks = [(0, 32)] if need_ctx else []
    chunks += [(32, 256), (288, 256)]
    hcb = B("g_Hc"); vb = B("g_v"); pyb = B("psY4")
    npc = 0
    for ci, (b0, nb) in enumerate(chunks):
        t0 = 8 * b0; ntok = 8 * nb
        for dd in range(2):
            for qq in range(4):
                s.dma("sp", Hc[:, dd * 32 + qq * 8:dd * 32 + qq * 8 + 8, 0:nb], k.HD[dd, qq * 4:qq * 4 + 4, :, :, b0:b0 + nb].rearrange("q r p c -> p (q r) c"), r=[B("HD")], w=[hcb])
        for t in range(4):
            reg = lambda o, rows=slice(0, 128): py4[rows, o * 256:o * 256 + nb]
            for o in range(8):
                PE(lambda o=o, t=t: nc.tensor.matmul(reg(o), lhsT=dgs[:, t, :], rhs=uT[:, t, t0 + o:t0 + ntok:8], start=(o % 2 == 0), stop=False, skip_group_check=True),
                   r=[B("g_dgs"), B("q_uT")], w=[pyb])
            for d in range(2):
                for o in range(8):
                    srcs = range(0, o + 1) if d == 0 else range(o, 8)
                    for o2 in srcs:
                        lag = abs(o - o2)
                        PE(lambda t=t, d=d, lag=lag, o=o, o2=o2: nc.tensor.matmul(reg(o), lhsT=KD[:, d, t, lag, :], rhs=uT[:, t, t0 + o2:t0 + ntok:8], start=False, stop=False, skip_group_check=True),
                           r=[B("q_KD"), B("q_uT")], w=[pyb])
                for q in range(4):
                    pr = 4 * t + q
                    for o in range(8):
                        i_ = o if d == 0 else 7 - o
                        for ri in range(2):
                            PE(lambda d=d, pr=pr, i_=i_, ri=ri, o=o, q=q: nc.tensor.matmul(reg(o, slice(32 * q, 32 * q + 32)), lhsT=WR[:, d, pr, i_, ri, :], rhs=Hc[:, (d * 16 + pr) * 2 + ri, 0:nb],
                                                                                     start=False, stop=False, tile_position=(0, 32 * q), skip_group_check=True),
                               r=[B("q_WR"), hcb], w=[pyb])
            for pc in range(4):
                ya = y32[npc % 2]; yab = B(f"g_y{npc % 2}"); yb_ = y2[npc % 2]; ybb = B(f"g_y2{npc % 2}")
                src = py4[:, pc * 512:(pc + 1) * 512].rearrange("p (o c) -> p o c", o=2)[:, :, 0:nb]
                ya3 = ya[:, 0:2 * nb].rearrange("p (o c) -> p o c", o=2)
                yb3 = yb_[:, 0:2 * nb].rearrange("p (o c) -> p o c", o=2)
                dst = vT[:, t, 0:ntok].rearrange("p (c o) -> p o c", o=8)[:, 2 * pc:2 * pc + 2, :]
                A(lambda src=src, ya3=ya3: nc.scalar.copy(out=ya3, in_=src), w=[pyb, yab])
                G(lambda ya=ya, yb_=yb_: nc.gpsimd.tensor_tensor(out=yb_[:, 0:2 * nb], in0=ya[:, 0:2 * nb], in1=ya[:, 0:2 * nb], op=ALU.mult), r=[yab], w=[ybb])
                V(lambda yb_=yb_: nc.vector.tensor_scalar(out=yb_[:, 0:2 * nb], in0=yb_[:, 0:2 * nb], scalar1=0.044715, scalar2=1.0, op0=ALU.mult, op1=ALU.add), w=[ybb])
                V(lambda ya=ya, yb_=yb_: nc.vector.tensor_tensor(out=yb_[:, 0:2 * nb], in0=yb_[:, 0:2 * nb], in1=ya[:, 0:2 * nb], op=ALU.mult), r=[yab], w=[ybb])
                A(lambda yb_=yb_: nc.scalar.activation(out=yb_[:, 0:2 * nb], in_=yb_[:, 0:2 * nb], func=AF.Sigmoid, scale=1.5957691216), w=[ybb])
                V(lambda dst=dst, yb3=yb3, ya3=ya3: nc.vector.tensor_tensor(out=dst, in0=yb3, in1=ya3, op=ALU.mult), r=[yab, ybb], w=[vb])
                npc += 1
        for sc0 in range(0, ntok, 512):
            n_ = min(512, ntok - sc0)
            for ct in range(4):
                pg = psG[ct % 2]; pgb = B(f"psGq{ct % 2}")
                g_ = gs[ct % 2]; gb = B(f"g_gs{ct % 2}")
                s.dma("sp", g_[:, 0:n_], k.PF[32 + ct * 128:32 + (ct + 1) * 128, t0 + sc0:t0 + sc0 + n_], r=[B("PF")], w=[gb])
                for t in range(4):
                    PE(lambda pg=pg, t=t, ct=ct, sc0=sc0, n_=n_: nc.tensor.matmul(pg[:, 0:n_], lhsT=wg[:, t, ct * 128:(ct + 1) * 128], rhs=vT[:, t, sc0:sc0 + n_], start=(t == 0), stop=(t == 3)),
                       r=[B("g_wg"), vb], w=[pgb])
                A(lambda pg=pg, ct=ct, n_=n_: nc.scalar.activation(out=o1[:, 0:n_], in_=pg[:, 0:n_], func=AF.Sigmoid, bias=bg[:, ct:ct + 1]), r=[B("g_bg")], w=[pgb, B("g_o1")])
                V(lambda ct=ct, sc0=sc0, n_=n_: nc.vector.tensor_tensor(out=o1[:, 0:n_], in0=o1[:, 0:n_], in1=vT[:, ct, sc0:sc0 + n_], op=ALU.mult), r=[vb], w=[B("g_o1")])
                A(lambda g_=g_, n_=n_: nc.scalar.activation(out=g_[:, 0:n_], in_=g_[:, 0:n_], func=AF.Silu), w=[gb])
                o_ = ob[ct % 2]; obb = B(f"g_ob{ct % 2}")
                V(lambda o_=o_, g_=g_, n_=n_: nc.vector.tensor_tensor(out=o_[:, 0:n_], in0=o1[:, 0:n_], in1=g_[:, 0:n_], op=ALU.mult), r=[gb, B("g_o1")], w=[obb])
                s.dma("pool", k.MT[1024 + ct * 128:1024 + (ct + 1) * 128, t0 + sc0:t0 + sc0 + n_], o_[:, 0:n_], r=[obb], w=[B("MT")])


D = 1024; T = 4352; TC = 256; TL = 4096; NT = 34; DEPTH = 4
DIN = 4640; EPS = 1e-6

class K:
    pass

def dram(nc, name, shape, dt, kind="Internal"):
    return nc.dram_tensor(name, list(shape), dt, kind=kind).ap()

def build(nlayers=DEPTH, stop=None, dbg=()):
    nc = bass.Bass("TRN2", target_bir_lowering=False)
    s = Sched(nc)
    k = K(); k.nc = nc; k.s = s; k.dbg = dbg; k.stop = stop; k.sfx = ''
    I = {}
    def inp(name, shape):
        I[name] = dram(nc, name, shape, F32, "ExternalInput")
    inp("x", [TL, D]); inp("c", [D]); inp("ctx", [TC, D]); inp("c_ctx", [D])
    inp("w_mod", [DEPTH, D, 3 * D]); inp("b_mod", [DEPTH, 3 * D]); inp("g_pre", [DEPTH, D]); inp("g_post", [DEPTH, D])
    inp("w_in", [DEPTH, D, DIN]); inp("conv_w", [DEPTH, 9, 1536]); inp("conv_b", [DEPTH, 1536])
    inp("dt_bias", [DEPTH, 32]); inp("a_log", [DEPTH, 32]); inp("d_ssd", [DEPTH, 16]); inp("g_ssd_norm", [DEPTH, D])
    inp("s5_lambda_re", [DEPTH, 2, 32, 64]); inp("s5_lambda_im", [DEPTH, 2, 32, 64]); inp("s5_log_step", [DEPTH, 2, 32])
    inp("s5_b_re", [DEPTH, 2, 32, 64, 16]); inp("s5_b_im", [DEPTH, 2, 32, 64, 16])
    inp("s5_c_re", [DEPTH, 2, 32, 16, 64]); inp("s5_c_im", [DEPTH, 2, 32, 16, 64])
    inp("s5_d", [DEPTH, 512]); inp("s5_w_glu", [DEPTH, 512, 512]); inp("s5_b_glu", [DEPTH, 512])
    inp("fnet_w", [DEPTH, 512, 512]); inp("fnet_b", [DEPTH, 512]); inp("w_out", [DEPTH, 2048, D])
    inp("cst", [128, 2304])
    k.I = I
    k.out = dram(nc, "out", [TL, D], F32, "ExternalOutput")
    def scr(name, shape, dt):
        return dram(nc, name, shape, dt, "ExternalOutput" if name in dbg else "Internal")
    k.XS = scr("XS", [T, D], F32)
    k.ZT = scr("ZT", [T, D], F32)
    k.PF = scr("PF", [1056, T], F32)
    k.PB = scr("PB", [2560, T], BF16)
    k.XC = scr("XC", [1536, T], BF16)
    k.MT = scr("MT", [2048, T], BF16)
    k.YF = scr("YF", [T, D], F32)
    k.MODS = scr("MODS", [DEPTH, 2, 3, D], F32)
    k.CL = scr("CL", [4096, 4096], BF16)
    k.HD = scr("HD", [2, 16, 2, 128, 544], BF16)
    k.SLN = scr("SLN", [4096, 4096], BF16)
    k.bufs = {}
    def B(name):
        if name not in k.bufs:
            k.bufs[name] = Buf(name)
        return k.bufs[name]
    k.B = B
    k.cst = nc.alloc_sbuf_tensor("cst_sb", [128, 2304], F32)
    k.identb = nc.alloc_sbuf_tensor("identb", [128, 128], BF16)
    s.dma("sp", k.cst[:], I["cst"][:, :], w=[B("cst")])
    s.op("dve", lambda: nc.vector.tensor_copy(out=k.identb[:], in_=k.cst[:, 0:128]), r=[B("cst")], w=[B("identb")])
    k.negpi = nc.alloc_sbuf_tensor("negpi", [128, 1], F32)
    s.op("dve", lambda: nc.vector.memset(k.negpi[:], -3.14159265), w=[B("negpi")])
    prep_mods(k)
    gen_dft(k)
    for l in range(nlayers):
        k.sfx = f'_L{l}'
        phase_ab(k, l)
        if stop == "ab":
            break
        phase_conv(k, l)
        if stop == "conv":
            break
        phase_ssd(k, l, need_ctx=(l < DEPTH - 1))
        if stop == "ssd":
            break
        phase_s5(k, l, need_ctx=(l < DEPTH - 1))
        if stop in ("s5", "s5prep", "s5main"):
            break
        phase_fnet(k, l, need_ctx=(l < DEPTH - 1))
        if stop == "fnet":
            break
        if stop != "outonly":
            pass
        phase_out(k, l, last=(l == nlayers - 1 and nlayers == DEPTH))
        if stop == "out":
            break
    s.drain("sp")
    return nc


def prep_mods(k):
    nc, s, I, B = k.nc, k.s, k.I, k.B
    with ExitStack() as st:
        T_ = lambda name, shape, dt: st.enter_context(nc.sbuf_tensor(name + k.sfx, shape, dt))
        craw = T_("craw", [128, 8, 2], F32)
        sc = T_("sc", [128, 8, 2], F32)
        wm = [T_(f"wm{i}", [128, 3 * D], F32) for i in range(2)]
        rows = T_("mrows", [2, 3 * D], F32)
        gp = T_("gp", [2, 2, D], F32)
        res = T_("mres", [2, 3, D], F32)
        psM = [st.enter_context(nc.psum_tensor(f"psM{i}", [128, 512], F32)) for i in range(6)]
        s.dma("sp", craw[:, :, 0], I["c"].rearrange("(k p) -> p k", p=128), w=[B("craw")], allow_slow_non_contiguous=True)
        s.dma("sp", craw[:, :, 1], I["c_ctx"].rearrange("(k p) -> p k", p=128), w=[B("craw")], allow_slow_non_contiguous=True)
        s.op("act", lambda: nc.scalar.activation(out=sc[:], in_=craw[:], func=AF.Silu), r=[B("craw")], w=[B("sc")])
        for l in range(DEPTH):
            for kk in range(8):
                w_ = wm[kk % 2]; wb = B(f"wm{kk % 2}")
                s.dma("sp", w_[:], I["w_mod"][l, kk * 128:(kk + 1) * 128, :], w=[wb])
                for n in range(6):
                    s.op("pe", lambda n=n, w_=w_, kk=kk: nc.tensor.matmul(psM[n][0:2, :], lhsT=sc[:, kk, :], rhs=w_[:, n * 512:(n + 1) * 512],
                                                                start=(kk == 0), stop=(kk == 7)), r=[B("sc"), wb], w=[B(f"psM{n}")])
            s.dma("sp", rows[:], I["b_mod"][l:l + 1, :].partition_broadcast(2) if False else I["b_mod"][l, :].partition_broadcast(2), w=[B("mrows")])
            s.dma("sp", gp[:, 0, :], I["g_pre"][l, :].partition_broadcast(2), w=[B("gp")])
            s.dma("sp", gp[:, 1, :], I["g_post"][l, :].partition_broadcast(2), w=[B("gp")])
            for n in range(6):
                s.op("dve", lambda n=n: nc.vector.tensor_tensor(out=rows[:, n * 512:(n + 1) * 512], in0=rows[:, n * 512:(n + 1) * 512],
                                                            in1=psM[n][0:2, :], op=ALU.add), r=[B(f"psM{n}")], w=[B("mrows")])
            s.op("dve", lambda: nc.vector.tensor_copy(out=res[:, 0, :], in_=rows[:, 0:D]), r=[B("mrows")], w=[B("mres")])
            s.op("dve", lambda: nc.vector.scalar_tensor_tensor(out=res[:, 1, :], in0=rows[:, D:2 * D], scalar=1.0, in1=gp[:, 0, :],
                                                             op0=ALU.add, op1=ALU.mult), r=[B("mrows"), B("gp")], w=[B("mres")])
            s.op("dve", lambda: nc.vector.tensor_tensor(out=res[:, 2, :], in0=rows[:, 2 * D:3 * D], in1=gp[:, 1, :], op=ALU.mult),
                 r=[B("mrows"), B("gp")], w=[B("mres")])
            s.dma("sp", k.MODS[l], res[:], r=[B("mres")], w=[B("MODS")])
        s.barrier()


def fm_cols():
    lst = []
    for i in range(12):
        lst.append((1024 + i * 128, 128, "PB", i * 128, BF16))
    lst.append((2560, 32, "PF", 0, F32))
    for i in range(4):
        lst.append((2592 + i * 128, 128, "PB", 1536 + i * 128, BF16))
    for i in range(4):
        lst.append((3104 + i * 128, 128, "PF", 32 + i * 128, F32))
    for i in range(4):
        lst.append((3616 + i * 128, 128, "PB", 2048 + i * 128, BF16))
    for i in range(4):
        lst.append((4128 + i * 128, 128, "PF", 544 + i * 128, F32))
    return lst


def phase_ab(k, l):
    nc, s, I, B = k.nc, k.s, k.I, k.B
    with ExitStack() as st:
        T_ = lambda name, shape, dt: st.enter_context(nc.sbuf_tensor(name + k.sfx, shape, dt))
        P_ = lambda name, shape, dt: st.enter_context(nc.psum_tensor(name + k.sfx, shape, dt))
        wsb = T_("wsb", [128, 8, DIN], BF16)
        wst = [T_(f"wst{i}", [128, DIN], F32) for i in range(2)]
        bc = T_("bcab", [128, 4, D], F32)
        xt = [T_(f"xt{i}", [128, D], F32) for i in range(2)]
        junk = T_("junk", [128, D], BF16)
        h1 = T_("h1", [128, D], F32)
        hl = [T_(f"hl{i}", [128, D], BF16) for i in range(2)]
        hlT = [T_(f"hlT{i}", [128, 8, 512], BF16) for i in range(2)]
        st4 = T_("st4", [128, 8], F32)
        zt = [T_(f"zt{i}", [128, D], F32) for i in range(2)]
        ev32 = [T_(f"ev32_{i}", [128, 512], F32) for i in range(2)]
        ev16 = [T_(f"ev16_{i}", [128, 512], BF16) for i in range(2)]
        psT = P_("psT", [128, 1024], BF16)
        psZ = [P_(f"psZ{i}", [128, 512], F32) for i in range(2)]
        psF = [P_(f"psF{i}", [128, 512], F32) for i in range(3)]
        for kk in range(8):
            w_ = wst[kk % 2]; wb = B(f"wst{kk % 2}")
            s.dma("sp", w_[:], I["w_in"][l, kk * 128:(kk + 1) * 128, :], w=[wb])
            e = "act" if kk % 2 == 0 else "pool"
            if e == "act":
                s.op("act", lambda w_=w_, kk=kk: nc.scalar.copy(out=wsb[:, kk, :], in_=w_[:]), r=[wb], w=[B(f"wsb{kk}")])
            else:
                s.op("pool", lambda w_=w_, kk=kk: nc.gpsimd.tensor_copy(out=wsb[:, kk, :], in_=w_[:]), r=[wb], w=[B(f"wsb{kk}")])
        wsb_bufs = [B(f"wsb{kk}") for kk in range(8)]
        for j, (which, comp) in enumerate([(1, 0), (1, 1), (0, 0), (0, 1)]):
            s.dma("pool", bc[:, j, :], k.MODS[l, which, comp, :].partition_broadcast(128), r=[B("MODS")], w=[B("bcab")])
        cols = fm_cols()
        ngroups = 9
        ev_i = 0
        for g in range(ngroups):
            ntok = 512 if g < 8 else 256
            nt = ntok // 128
            hT = hlT[g % 2]; hTb = B(f"hlT{g % 2}")
            for tt in range(nt):
                ti = g * 4 + tt
                x_ = xt[ti % 2]; xb = B(f"xt{ti % 2}")
                if l == 0:
                    src = I["ctx"][ti * 128:(ti + 1) * 128, :] if ti < 2 else I["x"][(ti - 2) * 128:(ti - 1) * 128, :]
                    s.dma("sp", x_[:], src, w=[xb])
                else:
                    s.dma("sp", x_[:], k.XS[ti * 128:(ti + 1) * 128, :], r=[B("XS")], w=[xb])
                jb = 0 if ti < 2 else 2
                c0 = (ti % 4) * 2
                s.op("act", lambda x_=x_, c0=c0: nc.scalar.activation(out=junk[:], in_=x_[:], func=AF.Square, accum_out=st4[:, c0:c0 + 1]),
                     r=[xb], w=[B("junk"), B(f"st4_{ti % 4}")])
                s.op("dve", lambda c0=c0: nc.vector.tensor_scalar(out=st4[:, c0:c0 + 1], in0=st4[:, c0:c0 + 1], scalar1=1.0 / D, scalar2=EPS,
                                                              op0=ALU.mult, op1=ALU.add), w=[B(f"st4_{ti % 4}")])
                s.op("act", lambda c0=c0: nc.scalar.activation(out=st4[:, c0:c0 + 1], in_=st4[:, c0:c0 + 1], func=AF.Sqrt), w=[B(f"st4_{ti % 4}")])
                s.op("dve", lambda c0=c0: nc.vector.reciprocal(out=st4[:, c0 + 1:c0 + 2], in_=st4[:, c0:c0 + 1]), w=[B(f"st4_{ti % 4}")])
                s.op("dve", lambda x_=x_, c0=c0, jb=jb: nc.vector.scalar_tensor_tensor(out=h1[:], in0=x_[:], scalar=st4[:, c0 + 1:c0 + 2], in1=bc[:, jb + 1, :],
                                                                                 op0=ALU.mult, op1=ALU.mult), r=[xb, B(f"st4_{ti % 4}"), B("bcab")], w=[B("h1")])
                h_ = hl[ti % 2]; hb = B(f"hl{ti % 2}")
                s.op("dve", lambda h_=h_, jb=jb: nc.vector.tensor_tensor(out=h_[:], in0=h1[:], in1=bc[:, jb, :], op=ALU.add), r=[B("h1"), B("bcab")], w=[hb])
                for kk in range(8):
                    s.op("pe", lambda h_=h_, kk=kk: nc.tensor.transpose(out=psT[:, kk * 128:(kk + 1) * 128], in_=h_[:, kk * 128:(kk + 1) * 128], identity=k.identb[:]),
                         r=[hb, B("identb")], w=[B("psT")])
                s.op("act", lambda hT=hT, tt=tt: nc.scalar.copy(out=hT[:, :, tt * 128:(tt + 1) * 128], in_=psT[:].rearrange("p (k t) -> p k t", t=128)),
                     r=[B("psT")], w=[hTb])
                z_ = zt[ti % 2]; zb = B(f"zt{ti % 2}")
                for hh in range(2):
                    pz = psZ[hh]; pzb = B(f"psZ{hh}")
                    for kk in range(8):
                        s.op("pe", lambda pz=pz, hT=hT, kk=kk, tt=tt, hh=hh: nc.tensor.matmul(pz[:], lhsT=hT[:, kk, tt * 128:(tt + 1) * 128], rhs=wsb[:, kk, hh * 512:(hh + 1) * 512],
                                                                                        start=(kk == 0), stop=(kk == 7)), r=[hTb, wsb_bufs[kk]], w=[pzb])
                    if hh == 0:
                        s.op("act", lambda z_=z_, pz=pz: nc.scalar.copy(out=z_[:, 0:512], in_=pz[:]), r=[pzb], w=[zb])
                    else:
                        s.op("dve", lambda z_=z_, pz=pz: nc.vector.tensor_copy(out=z_[:, 512:1024], in_=pz[:]), r=[pzb], w=[zb])
                s.dma("pool", k.ZT[ti * 128:(ti + 1) * 128, :], z_[:], r=[zb], w=[B("ZT")])
            t0 = g * 512
            for ci, (c0, wd, dest, r0, dt_) in enumerate(cols):
                pf = psF[ci % 3]; pfb = B(f"psF{ci % 3}")
                for kk in range(8):
                    s.op("pe", lambda pf=pf, hT=hT, kk=kk, c0=c0, wd=wd, ntok=ntok: nc.tensor.matmul(pf[0:wd, 0:ntok], lhsT=wsb[:, kk, c0:c0 + wd], rhs=hT[:, kk, 0:ntok],
                                                                                             start=(kk == 0), stop=(kk == 7)), r=[hTb, wsb_bufs[kk]], w=[pfb])
                ev = (ev32 if dt_ == F32 else ev16)[ev_i % 2]
                evb = B(("ev32_" if dt_ == F32 else "ev16_") + str(ev_i % 2))
                if ev_i % 2 == 0:
                    s.op("act", lambda ev=ev, pf=pf, wd=wd, ntok=ntok: nc.scalar.copy(out=ev[0:wd, 0:ntok], in_=pf[0:wd, 0:ntok]), r=[pfb], w=[evb])
                else:
                    s.op("dve", lambda ev=ev, pf=pf, wd=wd, ntok=ntok: nc.vector.tensor_copy(out=ev[0:wd, 0:ntok], in_=pf[0:wd, 0:ntok]), r=[pfb], w=[evb])
                dst = getattr(k, dest)
                s.dma("pool", dst[r0:r0 + wd, t0:t0 + ntok], ev[0:wd, 0:ntok], r=[evb], w=[B(dest)])
                ev_i += 1
        s.barrier()


def _consts():
    c = np.zeros((128, 2304), np.float32)
    c[:, 0:128] = np.eye(128)
    c[:, 128:137] = np.arange(9)
    p = np.arange(128)
    c[:, 137] = (p % 32) < 16
    c[:, 138] = (p % 32) >= 16
    c[:, 139] = p
    c[:, 140:268] = 1.0
    jj, ii = np.meshgrid(p, p, indexing="ij")
    c[:, 268:396] = np.where(jj > ii, -30000.0, 0.0)
    c[:, 396:524] = np.where(jj < ii, -30000.0, 0.0)
    c[:, 524:532] = (p[:, None] // 16) == np.arange(8)[None]
    c[:, 1024:1568] = np.arange(544)[None]
    cn = np.arange(544)
    c[:, 1600:2144] = np.where(cn < 32, 31 - cn, 575 - cn)[None]
    return c


def kernel(**inputs):
    inp = {k_: np.asarray(v) for k_, v in inputs.items()}
    cst = _consts()
    shared = {}
    for name, v in inp.items():
        if name in ("x", "c", "ctx"):
            continue
        v = np.ascontiguousarray(v, dtype=np.float32)
        if name == "conv_w":
            v = np.ascontiguousarray(v.reshape(4, 9, 1536))
        elif name in ("dt_bias", "a_log"):
            v = np.ascontiguousarray(v.reshape(4, 32))
        shared[name] = v
    shared["cst"] = cst
    in_maps = []
    for core in range(8):
        b = core % 4
        m = dict(shared)
        m["x"] = np.ascontiguousarray(inp["x"][b], dtype=np.float32)
        m["c"] = np.ascontiguousarray(inp["c"][b], dtype=np.float32)
        m["ctx"] = np.ascontiguousarray(inp["ctx"][b], dtype=np.float32)
        in_maps.append(m)
    nc = build()
    res = run_bass_kernel_spmd(nc, in_maps, core_ids=list(range(8)))
    out = np.stack([np.asarray(res.results[b]["out"], dtype=np.float32) for b in range(4)], axis=0)
    return out
```

```python
import math
from contextlib import ExitStack
import numpy as np
import concourse.bass as bass
import concourse.mybir as mybir
from concourse.bass_utils import run_bass_kernel_spmd

F32 = mybir.dt.float32
BF16 = mybir.dt.bfloat16
AF = mybir.ActivationFunctionType
ALU = mybir.AluOpType
AX = mybir.AxisListType


class Buf:
    __slots__ = ("w", "r", "name")

    def __init__(self, name=""):
        self.w = None
        self.r = {}
        self.name = name


class Sched:
    NR = 8

    def __init__(self, nc):
        self.nc = nc
        self.eng = {"pe": nc.tensor, "act": nc.scalar, "dve": nc.vector,
                    "pool": nc.gpsimd, "sp": nc.sync}
        self.csem = {e: nc.alloc_semaphore("c_" + e) for e in ("pe", "act", "dve", "pool")}
        self.ccnt = {e: 0 for e in self.csem}
        self.dq = {}
        for q in ("sp", "act", "pool"):
            self.dq[q] = dict(sems=[nc.alloc_semaphore(f"d_{q}{i}") for i in range(self.NR)],
                              n=0, tk=[None] * self.NR)
        self.waited = {}
        self.ninstr = 0

    def _wait(self, e, tk):
        if tk is None:
            return
        key, sem, val = tk
        if key == "pe" and e == "pe":
            return
        if self.waited.get((e, key), 0) >= val:
            return
        self.eng[e].wait_ge(sem, val)
        self.waited[(e, key)] = val

    def _deps(self, e, r, w):
        for b in r:
            self._wait(e, b.w)
        for b in w:
            self._wait(e, b.w)
            for t in list(b.r.values()):
                self._wait(e, t)

    def _record(self, tk, r, w):
        for b in r:
            b.r[tk[0]] = tk
        for b in w:
            b.w = tk
            b.r = {}

    def op(self, e, fn, r=(), w=()):
        self._deps(e, r, w)
        ins = fn()
        self.ccnt[e] += 1
        ins.then_inc(self.csem[e], 1)
        tk = (e, self.csem[e], self.ccnt[e])
        self._record(tk, r, w)
        self.ninstr += 1
        return tk

    def dma(self, q, out, in_, r=(), w=(), **kw):
        d = self.dq[q]
        slot = d["n"] % self.NR
        self._wait(q, d["tk"][slot])
        self._deps(q, r, w)
        ins = self.eng[q].dma_start(out=out, in_=in_, **kw)
        ins.then_inc(d["sems"][slot], 16)
        val = 16 * (d["n"] // self.NR + 1)
        tk = (("d", q, slot), d["sems"][slot], val)
        d["tk"][slot] = tk
        d["n"] += 1
        self._record(tk, r, w)
        self.ninstr += 1
        return tk

    def op_cc(self, fn, r=(), w=()):
        q = "pool"
        d = self.dq[q]
        slot = d["n"] % self.NR
        self._wait(q, d["tk"][slot])
        self._deps(q, r, w)
        ins = fn()
        ins.then_inc(d["sems"][slot], 16)
        val = 16 * (d["n"] // self.NR + 1)
        tk = (("d", q, slot), d["sems"][slot], val)
        d["tk"][slot] = tk
        d["n"] += 1
        self._record(tk, r, w)
        return tk

    def all_tickets(self):
        tks = []
        for e in self.csem:
            if self.ccnt[e]:
                tks.append((e, self.csem[e], self.ccnt[e]))
        for q, d in self.dq.items():
            for t in d["tk"]:
                if t is not None:
                    tks.append(t)
        return tks

    def barrier(self, engines=("pe", "act", "dve", "pool", "sp")):
        tks = self.all_tickets()
        for e in engines:
            for t in tks:
                self._wait(e, t)

    def drain(self, e="sp"):
        for t in self.all_tickets():
            if t[0] == "pe" and e == "pe":
                continue
            self._wait(e, t)

T = 4352

def phase_conv(k, l):
    nc, s, I, B = k.nc, k.s, k.I, k.B
    with ExitStack() as st:
        T_ = lambda name, shape, dt: st.enter_context(nc.sbuf_tensor(name + k.sfx, shape, dt))
        P_ = lambda name, shape, dt: st.enter_context(nc.psum_tensor(name + k.sfx, shape, dt))
        cw9 = T_("cw9", [9, 1536], F32)
        cwT = T_("cwT", [128, 108], F32)
        cb = T_("cb", [128, 12], F32)
        dg = T_("dg", [128, 108, 128], BF16)
        xp = [T_(f"xp{i}", [128, 258 + 66 * 66], BF16) for i in range(2)]
        ev = [T_(f"cev{i}", [128, 512], BF16) for i in range(2)]
        psW = P_("psW", [128, 108], F32)
        psC = [P_(f"psC{i}", [128, 512], F32) for i in range(2)]
        s.dma("sp", cw9[:], I["conv_w"][l], w=[B("cw9")])
        s.dma("sp", cb[:], I["conv_b"][l].rearrange("(t p) -> p t", p=128), w=[B("cb")], allow_slow_non_contiguous=True)
        for t in range(12):
            s.op("pe", lambda t=t: nc.tensor.transpose(out=psW[:, t * 9:(t + 1) * 9], in_=cw9[:, t * 128:(t + 1) * 128], identity=k.cst[0:9, 0:9]),
                 r=[B("cw9"), B("cst")], w=[B("psW")])
        s.op("dve", lambda: nc.vector.tensor_copy(out=cwT[:], in_=psW[:]), r=[B("psW")], w=[B("cwT")])
        for j in range(108):
            e = "dve" if j % 2 == 0 else "pool"
            eng = nc.vector if e == "dve" else nc.gpsimd
            s.op(e, lambda j=j, eng=eng: eng.tensor_scalar(out=dg[:, j, :], in0=k.identb[:], scalar1=cwT[:, j:j + 1], scalar2=None, op0=ALU.mult),
                 r=[B("cwT"), B("identb")], w=[B(f"dg{j}")])
        for i in range(2):
            s.op("pool", lambda i=i: nc.gpsimd.memset(xp[i][:], 0.0), w=[B(f"xp{i}")])
        n = 0
        for t in range(12):
            x_ = xp[t % 2]; xb = B(f"xp{t % 2}")
            rows = k.PB[t * 128:(t + 1) * 128, :]
            s.dma("sp", x_[:, 1:257], rows[:, 0:256], r=[B("PB")], w=[xb])
            grid = x_[:, 258:258 + 4356].rearrange("p (r c) -> p r c", c=66)
            for hh in range(2):
                s.dma("sp", grid[:, 1 + hh * 32:33 + hh * 32, 1:65], rows[:, 256 + hh * 2048:256 + (hh + 1) * 2048].rearrange("p (r c) -> p r c", c=64), r=[B("PB")], w=[xb])
            for rg in range(9):
                pc = psC[n % 2]; pcb = B(f"psC{n % 2}")
                if rg < 8:
                    for tap in range(9):
                        ky, kx = tap // 3, tap % 3
                        rhs = grid[:, rg * 8 + ky:rg * 8 + ky + 8, kx:kx + 64]
                        s.op("pe", lambda pc=pc, t=t, tap=tap, rhs=rhs: nc.tensor.matmul(pc[:].rearrange("p (r c) -> p r c", c=64), lhsT=dg[:, t * 9 + tap, :], rhs=rhs,
                                                                              start=(tap == 0), stop=(tap == 8)), r=[xb, B(f"dg{t * 9 + tap}")], w=[pcb])
                    ntok = 512; t0 = 256 + rg * 512
                else:
                    for kx in range(3):
                        s.op("pe", lambda pc=pc, t=t, kx=kx: nc.tensor.matmul(pc[:, 0:256], lhsT=dg[:, t * 9 + 3 + kx, :], rhs=x_[:, kx:kx + 256],
                                                                    start=(kx == 0), stop=(kx == 2)), r=[xb, B(f"dg{t * 9 + 3 + kx}")], w=[pcb])
                    ntok = 256; t0 = 0
                e_ = ev[n % 2]; eb = B(f"cev{n % 2}")
                s.op("act", lambda e_=e_, pc=pc, t=t, ntok=ntok: nc.scalar.activation(out=e_[:, 0:ntok], in_=pc[:, 0:ntok], func=AF.Silu, bias=cb[:, t:t + 1]),
                     r=[pcb, B("cb")], w=[eb])
                s.dma("pool" if n % 2 == 0 else "sp", k.XC[t * 128:(t + 1) * 128, t0:t0 + ntok], e_[:, 0:ntok], r=[eb], w=[B("XC")])
                n += 1
        s.barrier()

T = 4352; NCH = 34; D = 1024; EPS = 1e-6
C_MF = 137; C_MB = 138; C_ONES = 140; C_NEGF = 268; C_NEGB = 396; C_BM8 = 524; C_IOTA = 1024

def phase_ssd(k, l, need_ctx=True):
    nc, s, I, B = k.nc, k.s, k.I, k.B
    cst = k.cst
    with ExitStack() as st:
        T_ = lambda name, shape, dt: st.enter_context(nc.sbuf_tensor(name + k.sfx, shape, dt))
        P_ = lambda name, shape, dt: st.enter_context(nc.psum_tensor(name + k.sfx, shape, dt))
        pst = ExitStack()
        TP_ = lambda name, shape, dt: pst.enter_context(nc.sbuf_tensor(name + k.sfx, shape, dt))
        acs = T_("s_acs", [128, T], F32)
        nacs = T_("s_nacs", [128, T], F32)
        Q = T_("s_Q", [128, T], F32)
        colp = T_("s_colp", [128, 4], F32)
        tot = T_("s_tot", [128, NCH], F32)
        DT = T_("s_DT", [32, NCH, 32], F32)
        cdall = T_("s_cd", [128, NCH, 32], F32)
        Esel2 = T_("s_Esel2", [64, 32, 128], BF16)
        hl = T_("s_hl", [64, T], BF16)
        nhl = T_("s_nhl", [64, T], BF16)
        negm = T_("s_negm", [128, 2, 512], BF16)
        dskc = T_("s_dskc", [128, 8], F32)
        dgd = T_("s_dgd", [128, 8, 128], BF16)
        gnb = T_("s_gnb", [128, D], F32)
        dt_ = TP_("s_dt", [128, T], F32)
        a_ = TP_("s_a", [128, T], F32)
        cum = TP_("s_cum", [128, T], F32)
        psCB = P_("psCB", [128, 256], F32)
        psX = P_("psX", [128, 1024], BF16)
        psB = P_("psB", [128, 256], BF16)
        psE = P_("psE", [128, 512], F32)
        psY = [P_(f"psY{i}", [128, 512], F32) for i in range(2)]
        psS1 = P_("psSst", [128, 512], F32)
        psO1 = P_("psOst", [128, 512], F32)

        V = lambda fn, r=(), w=(): s.op("dve", fn, r=r, w=w)
        A = lambda fn, r=(), w=(): s.op("act", fn, r=r, w=w)
        G = lambda fn, r=(), w=(): s.op("pool", fn, r=r, w=w)
        PE = lambda fn, r=(), w=(): s.op("pe", fn, r=r, w=w)
        for q in range(4):
            s.dma("sp", dt_[32 * q:32 * q + 32, :], k.PF[0:32, :], r=[B("PF")], w=[B("s_dt")])
            s.dma("sp", colp[32 * q:32 * q + 32, 0:1], I["dt_bias"][l].rearrange("(p o) -> p o", o=1), w=[B("s_colp")])
            s.dma("sp", colp[32 * q:32 * q + 32, 1:2], I["a_log"][l].rearrange("(p o) -> p o", o=1), w=[B("s_colp")])
        for t in range(8):
            for hh in range(2):
                s.dma("sp", dskc[64 * hh:64 * hh + 64, t:t + 1], I["d_ssd"][l, 2 * t + hh:2 * t + hh + 1].partition_broadcast(64), w=[B("s_dskc")])
        s.dma("sp", gnb[:], I["g_ssd_norm"][l].partition_broadcast(128), w=[B("s_gnb")])
        A(lambda: nc.scalar.activation(out=colp[:, 2:3], in_=colp[:, 1:2], func=AF.Exp), w=[B("s_colp")])
        V(lambda: nc.vector.tensor_scalar(out=colp[:, 3:4], in0=colp[:, 2:3], scalar1=-1.0, scalar2=None, op0=ALU.mult), w=[B("s_colp")])
        A(lambda: nc.scalar.activation(out=dt_[:], in_=dt_[:], func=AF.Exp, bias=colp[:, 0:1]), r=[B("s_colp")], w=[B("s_dt")])
        A(lambda: nc.scalar.activation(out=dt_[:], in_=dt_[:], func=AF.Ln, bias=1.0), w=[B("s_dt")])
        V(lambda: nc.vector.tensor_scalar(out=a_[:], in0=dt_[:], scalar1=colp[:, 3:4], scalar2=None, op0=ALU.mult), r=[B("s_dt"), B("s_colp")], w=[B("s_a")])
        for c in range(NCH):
            V(lambda c=c: nc.vector.tensor_tensor_scan(out=cum[:, c * 128:(c + 1) * 128], data0=cst[:, C_ONES:C_ONES + 128], data1=a_[:, c * 128:(c + 1) * 128],
                                                   initial=0.0, op0=ALU.mult, op1=ALU.add), r=[B("s_a"), B("cst")], w=[B("s_cum")])
        cum3 = cum[:].rearrange("p (c i) -> p c i", i=128)
        V(lambda: nc.vector.tensor_copy(out=tot[:], in_=cum3[:, :, 127]), r=[B("s_cum")], w=[B("s_tot")])
        totb = tot[:].unsqueeze(2).to_broadcast([128, NCH, 128])
        V(lambda: nc.vector.tensor_tensor(out=nacs[:], in0=a_[:], in1=cum[:], op=ALU.subtract), r=[B("s_a"), B("s_cum")], w=[B("s_nacs")])
        V(lambda: nc.vector.tensor_tensor(out=nacs[:].rearrange("p (c i) -> p c i", i=128), in0=nacs[:].rearrange("p (c i) -> p c i", i=128), in1=totb, op=ALU.add),
          r=[B("s_tot")], w=[B("s_nacs")])
        V(lambda: nc.vector.tensor_scalar(out=acs[:], in0=cum[:], scalar1=cst[:, C_MF:C_MF + 1], scalar2=None, op0=ALU.mult), r=[B("s_cum"), B("cst")], w=[B("s_acs")])
        V(lambda: nc.vector.scalar_tensor_tensor(out=acs[:], in0=nacs[:], scalar=cst[:, C_MB:C_MB + 1], in1=acs[:], op0=ALU.mult, op1=ALU.add), r=[B("s_nacs")], w=[B("s_acs")])
        V(lambda: nc.vector.tensor_scalar(out=nacs[:], in0=acs[:], scalar1=-1.0, scalar2=None, op0=ALU.mult), r=[B("s_acs")], w=[B("s_nacs")])
        V(lambda: nc.vector.tensor_tensor(out=a_[:].rearrange("p (c i) -> p c i", i=128), in0=nacs[:].rearrange("p (c i) -> p c i", i=128), in1=totb, op=ALU.add),
          r=[B("s_nacs"), B("s_tot")], w=[B("s_a")])
        A(lambda: nc.scalar.activation(out=a_[:], in_=a_[:], func=AF.Exp), w=[B("s_a")])
        V(lambda: nc.vector.tensor_tensor(out=a_[:], in0=a_[:], in1=dt_[:], op=ALU.mult), r=[B("s_dt")], w=[B("s_a")])
        A(lambda: nc.scalar.activation(out=cum[:], in_=acs[:], func=AF.Exp), r=[B("s_acs")], w=[B("s_cum")])
        G(lambda: nc.gpsimd.tensor_copy(out=Q[0:32, :], in_=dt_[0:32, :]), r=[B("s_dt")], w=[B("s_Q")])
        G(lambda: nc.gpsimd.tensor_copy(out=Q[32:64, :], in_=a_[32:64, :]), r=[B("s_a")], w=[B("s_Q")])
        G(lambda: nc.gpsimd.tensor_copy(out=Q[64:96, :], in_=cum[64:96, :]), r=[B("s_cum")], w=[B("s_Q")])
        G(lambda: nc.gpsimd.tensor_copy(out=Q[96:128, :], in_=nacs[96:128, :]), r=[B("s_nacs")], w=[B("s_Q")])
        V(lambda: nc.vector.tensor_copy(out=Esel2[0:32], in_=cst[0:32, 0:32].unsqueeze(2).to_broadcast([32, 32, 128])), r=[B("cst")], w=[B("s_Esel")])
        V(lambda: nc.vector.tensor_copy(out=Esel2[32:64], in_=cst[32:64, 32:64].unsqueeze(2).to_broadcast([32, 32, 128])), r=[B("cst")], w=[B("s_Esel")])
        V(lambda: nc.vector.tensor_copy(out=hl[0:32, :], in_=acs[0:32, :]), r=[B("s_acs")], w=[B("s_hl")])
        V(lambda: nc.vector.tensor_copy(out=nhl[32:64, :], in_=acs[32:64, :]), r=[B("s_acs")], w=[B("s_nhl")])
        V(lambda: nc.vector.tensor_tensor(out=hl[32:64, :], in0=acs[32:64, :], in1=nhl[32:64, :], op=ALU.subtract), r=[B("s_acs"), B("s_nhl")], w=[B("s_hl")])
        V(lambda: nc.vector.tensor_scalar(out=nhl[:], in0=hl[:], scalar1=-1.0, scalar2=None, op0=ALU.mult), r=[B("s_hl")], w=[B("s_nhl")])
        V(lambda: nc.vector.tensor_copy(out=negm[:, 0, :].rearrange("p (a i) -> p a i", i=128), in_=cst[:, C_NEGF:C_NEGF + 128].unsqueeze(1).to_broadcast([128, 4, 128])), r=[B("cst")], w=[B("s_negm")])
        V(lambda: nc.vector.tensor_copy(out=negm[:, 1, :].rearrange("p (a i) -> p a i", i=128), in_=cst[:, C_NEGB:C_NEGB + 128].unsqueeze(1).to_broadcast([128, 4, 128])), r=[B("cst")], w=[B("s_negm")])
        V(lambda: nc.vector.tensor_tensor(out=DT[:], in0=cst[0:32, 0:32].unsqueeze(1).to_broadcast([32, NCH, 32]), in1=tot[0:32, :].unsqueeze(2).to_broadcast([32, NCH, 32]), op=ALU.mult),
          r=[B("s_tot"), B("cst")], w=[B("s_DT")])
        for c0 in range(0, NCH, 16):
            n = min(16, NCH - c0)
            PE(lambda c0=c0, n=n: nc.tensor.matmul(psE[:, 0:n * 32], lhsT=cst[0:32, C_ONES:C_ONES + 128], rhs=DT[:, c0:c0 + n, :].rearrange("p c k -> p (c k)"), start=True, stop=True),
               r=[B("s_DT"), B("cst")], w=[B("psE")])
            A(lambda c0=c0, n=n: nc.scalar.activation(out=cdall[:, c0:c0 + n, :].rearrange("p c k -> p (c k)"), in_=psE[:, 0:n * 32], func=AF.Exp), r=[], w=[B("psE"), B("s_cd")])
        for t in range(8):
            V(lambda t=t: nc.vector.tensor_scalar(out=dgd[:, t, :], in0=k.identb[:], scalar1=dskc[:, t:t + 1], scalar2=None, op0=ALU.mult), r=[B("s_dskc"), B("identb")], w=[B("s_dgd")])
        s.barrier()
        pst.close()
        hst = [T_(f"s_h{d}", [128, D], F32) for d in range(2)]
        hbf = [T_(f"s_hb{d}", [128, D], BF16) for d in range(2)]
        xT = [T_(f"s_xT{i}", [128, 8, 128], BF16) for i in range(2)]
        BC = [T_(f"s_BC{i}", [128, 4, 128], BF16) for i in range(2)]
        tmq = [T_(f"s_tmq{i}", [128, 128], F32) for i in range(2)]
        xdt = [T_(f"s_xdt{i}", [128, D], BF16) for i in range(2)]
        xw = [T_(f"s_xw{i}", [128, D], BF16) for i in range(2)]
        Btok = [T_(f"s_Btok{i}", [128, 256], BF16) for i in range(2)]
        dec = [T_(f"s_dec{i}", [128, 512], F32) for i in range(2)]
        MTt = [T_(f"s_MT{i}", [128, 16, 128], BF16) for i in range(2)]
        ydg = [T_(f"s_ydg{i}", [128, D], F32) for i in range(2)]
        Sc = [T_(f"s_Sc{i}", [128, D], F32) for i in range(2)]
        tmp = T_("s_tmp", [128, 512], F32)
        yt = [T_(f"s_y{i}", [128, D], F32) for i in range(2)]
        yf = [T_(f"s_yf{i}", [128, D], F32) for i in range(2)]
        zt = [T_(f"s_z{i}", [128, D], F32) for i in range(2)]
        sz = T_("s_sz", [128, D], F32)
        junk = T_("s_junk", [128, 512], BF16)
        st2 = T_("s_st2", [128, 4], F32)
        mo = T_("s_mo", [128, D], BF16)
        mT = [T_(f"s_mT{i}", [128, 8, 128], BF16) for i in range(2)]
        def stageA(d, n_, c):
            sl = n_ % 2
            t0 = c * 128
            x_ = xT[sl]; xb = B(f"s_xT{sl}"); bc_ = BC[sl]; bcb = B(f"s_BC{sl}")
            tq = tmq[sl]; tqb = B(f"s_tmq{sl}")
            s.dma("sp", x_[:], k.XC[0:1024, t0:t0 + 128].rearrange("(t p) j -> p t j", p=128), r=[B("XC")], w=[xb])
            s.dma("sp", bc_[:], k.XC[1024:1536, t0:t0 + 128].rearrange("(t p) j -> p t j", p=128), r=[B("XC")], w=[bcb])
            if d == 1:
                s.dma("sp", zt[sl][:], k.ZT[t0:t0 + 128, :], r=[B("ZT")], w=[B(f"s_z{sl}")])
                s.dma("sp", yf[sl][:], k.YF[t0:t0 + 128, :], r=[B("YF")], w=[B(f"s_yf{sl}")])
            PE(lambda: nc.tensor.transpose(out=psE[:, 0:128], in_=Q[:, t0:t0 + 128], identity=cst[:, 0:128]), r=[B("s_Q"), B("cst")], w=[B("psE")])
            A(lambda: nc.scalar.copy(out=tq[:], in_=psE[:, 0:128]), w=[B("psE"), tqb])
            for t in range(8):
                PE(lambda t=t: nc.tensor.transpose(out=psX[:, t * 128:(t + 1) * 128], in_=x_[:, t, :], identity=k.identb[:]), r=[xb, B("identb")], w=[B("psX")])
            for g in range(2):
                PE(lambda g=g: nc.tensor.transpose(out=psB[:, g * 128:(g + 1) * 128], in_=bc_[:, g, :], identity=k.identb[:]), r=[bcb, B("identb")], w=[B("psB")])
                PE(lambda g=g: nc.tensor.matmul(psCB[:, g * 128:(g + 1) * 128], lhsT=bc_[:, g, :], rhs=bc_[:, 2 + g, :], start=True, stop=True), r=[bcb], w=[B("psCB")])
            A(lambda: nc.scalar.copy(out=Btok[sl][:], in_=psB[:]), w=[B("psB"), B(f"s_Btok{sl}")])
            yield
            psX3 = psX[:].rearrange("p (h e) -> p h e", e=64)
            V(lambda: nc.vector.tensor_tensor(out=xdt[sl][:].rearrange("p (h e) -> p h e", e=64), in0=psX3, in1=tq[:, d * 16:d * 16 + 16].unsqueeze(2).to_broadcast([128, 16, 64]), op=ALU.mult),
              r=[tqb], w=[B("psX"), B(f"s_xdt{sl}")])
            V(lambda: nc.vector.tensor_tensor(out=xw[sl][:].rearrange("p (h e) -> p h e", e=64), in0=psX3, in1=tq[:, 32 + d * 16:48 + d * 16].unsqueeze(2).to_broadcast([128, 16, 64]), op=ALU.mult),
              r=[tqb], w=[B("psX"), B(f"s_xw{sl}")])
            yield
            for hq in range(4):
                g = hq // 2
                PE(lambda: nc.tensor.matmul(psE[:], lhsT=k.identb[:], rhs=negm[:, d, :], start=True, stop=False, skip_group_check=True), r=[B("s_negm"), B("identb")], w=[B("psE")])
                for hh in range(4):
                    dh = d * 16 + hq * 4 + hh
                    o_ = psE[:, hh * 128:(hh + 1) * 128]
                    PE(lambda o_=o_, dh=dh: nc.tensor.matmul(o_, lhsT=Esel2[:, dh, :], rhs=hl[:, t0:t0 + 128], start=False, stop=False, skip_group_check=True), r=[B("s_Esel"), B("s_hl")], w=[B("psE")])
                    PE(lambda o_=o_, dh=dh, hh=hh: nc.tensor.matmul(o_, lhsT=nhl[:, t0:t0 + 128], rhs=Esel2[:, dh, :], start=False, stop=(hh == 3), skip_group_check=True), r=[B("s_Esel"), B("s_nhl")], w=[B("psE")])
                dc = dec[hq % 2]; dcb = B(f"s_dec{hq % 2}")
                A(lambda dc=dc: nc.scalar.activation(out=dc[:], in_=psE[:], func=AF.Exp), w=[B("psE"), dcb])
                V(lambda dc=dc, hq=hq, g=g: nc.vector.tensor_tensor(out=MTt[sl][:, hq * 4:hq * 4 + 4, :], in0=dc[:].rearrange("p (a i) -> p a i", i=128),
                                                               in1=psCB[:, g * 128:(g + 1) * 128].unsqueeze(1).to_broadcast([128, 4, 128]), op=ALU.mult),
                  r=[dcb], w=[B("psCB"), B(f"s_MT{sl}_{hq}")])
                yield
            if d == 0:
                for t in range(8):
                    py = psY[t // 4]
                    PE(lambda t=t, py=py: nc.tensor.matmul(py[:, (t % 4) * 128:(t % 4) * 128 + 128], lhsT=x_[:, t, :], rhs=dgd[:, t, :], start=(t % 4 == 0), stop=False, skip_group_check=True),
                       r=[xb, B("s_dgd")], w=[B(f"psY{t // 4}")])
            for h in range(16):
                py = psY[h // 8]
                PE(lambda h=h, py=py: nc.tensor.matmul(py[:, (h % 8) * 64:(h % 8) * 64 + 64], lhsT=MTt[sl][:, h, :], rhs=xdt[sl][:, h * 64:(h + 1) * 64], start=(d == 1 and h % 8 == 0), stop=(h % 8 == 7), skip_group_check=True),
                   r=[B(f"s_MT{sl}_{h // 4}"), B(f"s_xdt{sl}")], w=[B(f"psY{h // 8}")])
            yield
            A(lambda: nc.scalar.copy(out=ydg[sl][:, 0:512], in_=psY[0][:]), w=[B("psY0"), B(f"s_ydg{sl}")])
            V(lambda: nc.vector.tensor_copy(out=ydg[sl][:, 512:1024], in_=psY[1][:]), w=[B("psY1"), B(f"s_ydg{sl}")])
            if d == 1:
                G(lambda: nc.gpsimd.tensor_tensor(out=ydg[sl][:], in0=ydg[sl][:], in1=yf[sl][:], op=ALU.add), r=[B(f"s_yf{sl}")], w=[B(f"s_ydg{sl}")])
            yield
            for g in range(2):
                PE(lambda g=g: nc.tensor.matmul(psS1[:], lhsT=Btok[sl][:, g * 128:(g + 1) * 128], rhs=xw[sl][:, g * 512:(g + 1) * 512], start=True, stop=True),
                   r=[B(f"s_Btok{sl}"), B(f"s_xw{sl}")], w=[B("psSst")])
                if g == 0:
                    A(lambda: nc.scalar.copy(out=Sc[sl][:, 0:512], in_=psS1[:]), w=[B("psSst"), B(f"s_Sc{sl}")])
                else:
                    V(lambda: nc.vector.tensor_copy(out=Sc[sl][:, 512:1024], in_=psS1[:]), w=[B("psSst"), B(f"s_Sc{sl}")])
                yield

        def stageB(d, n_, c):
            sl = n_ % 2
            t0 = c * 128
            hb_ = B(f"s_h{d}"); hbb = B(f"s_hb{d}")
            bc_ = BC[sl]; bcb = B(f"s_BC{sl}")
            tq = tmq[sl]; tqb = B(f"s_tmq{sl}")
            y_ = yt[sl]; yb = B(f"s_y{sl}")
            for g in range(2):
                PE(lambda g=g: nc.tensor.matmul(psO1[:], lhsT=bc_[:, 2 + g, :], rhs=hbf[d][:, g * 512:(g + 1) * 512], start=True, stop=True), r=[bcb, hbb], w=[B("psOst")])
                V(lambda g=g: nc.vector.tensor_tensor(out=tmp[:].rearrange("p (h e) -> p h e", e=64), in0=psO1[:].rearrange("p (h e) -> p h e", e=64),
                                                   in1=tq[:, 64 + d * 16 + g * 8:64 + d * 16 + g * 8 + 8].unsqueeze(2).to_broadcast([128, 8, 64]), op=ALU.mult),
                  r=[tqb], w=[B("psOst"), B("s_tmp")])
                G(lambda g=g: nc.gpsimd.tensor_tensor(out=y_[:, g * 512:(g + 1) * 512], in0=tmp[:], in1=ydg[sl][:, g * 512:(g + 1) * 512], op=ALU.add), r=[B("s_tmp"), B(f"s_ydg{sl}")], w=[yb])
                yield
            G(lambda: nc.gpsimd.tensor_tensor(out=hst[d][:].rearrange("p (h e) -> p h e", e=64), in0=hst[d][:].rearrange("p (h e) -> p h e", e=64),
                                               in1=cdall[:, c, d * 16:d * 16 + 16].unsqueeze(2).to_broadcast([128, 16, 64]), op=ALU.mult), r=[B("s_cd")], w=[hb_])
            G(lambda: nc.gpsimd.tensor_tensor(out=hst[d][:], in0=hst[d][:], in1=Sc[sl][:], op=ALU.add), r=[B(f"s_Sc{sl}")], w=[hb_])
            A(lambda: nc.scalar.copy(out=hbf[d][:], in_=hst[d][:]), r=[hb_], w=[hbb])
            yield
            if d == 0:
                s.dma("pool", k.YF[t0:t0 + 128, :], y_[:], r=[yb], w=[B("YF")])
            elif need_ctx or c >= 2:
                z_ = zt[sl]; zb = B(f"s_z{sl}")
                A(lambda: nc.scalar.activation(out=sz[:], in_=z_[:], func=AF.Silu), r=[zb], w=[B("s_sz")])
                V(lambda: nc.vector.tensor_tensor(out=y_[:], in0=y_[:], in1=sz[:], op=ALU.mult), r=[B("s_sz")], w=[yb])
                for g in range(2):
                    A(lambda g=g: nc.scalar.activation(out=junk[:], in_=y_[:, g * 512:(g + 1) * 512], func=AF.Square, accum_out=st2[:, g:g + 1]), r=[yb], w=[B("s_junk"), B("s_st2")])
                yield
                V(lambda: nc.vector.tensor_scalar(out=st2[:, 0:2], in0=st2[:, 0:2], scalar1=1.0 / 512, scalar2=EPS, op0=ALU.mult, op1=ALU.add), w=[B("s_st2")])
                A(lambda: nc.scalar.activation(out=st2[:, 0:2], in_=st2[:, 0:2], func=AF.Sqrt), w=[B("s_st2")])
                V(lambda: nc.vector.reciprocal(out=st2[:, 2:4], in_=st2[:, 0:2]), w=[B("s_st2")])
                for g in range(2):
                    V(lambda g=g: nc.vector.scalar_tensor_tensor(out=mo[:, g * 512:(g + 1) * 512], in0=y_[:, g * 512:(g + 1) * 512], scalar=st2[:, 2 + g:3 + g], in1=gnb[:, g * 512:(g + 1) * 512],
                                                               op0=ALU.mult, op1=ALU.mult), r=[yb, B("s_st2"), B("s_gnb")], w=[B("s_mo")])
                yield
                for t in range(8):
                    PE(lambda t=t: nc.tensor.transpose(out=psX[:, t * 128:(t + 1) * 128], in_=mo[:, t * 128:(t + 1) * 128], identity=k.identb[:]), r=[B("s_mo"), B("identb")], w=[B("psX")])
                m_ = mT[sl]; mb_ = B(f"s_mT{sl}")
                A(lambda: nc.scalar.copy(out=m_[:], in_=psX[:].rearrange("p (t j) -> p t j", j=128)), w=[B("psX"), mb_])
                s.dma("pool", k.MT[0:1024, t0:t0 + 128].rearrange("(t p) j -> p t j", p=128), m_[:], r=[mb_], w=[B("MT")])
            yield

        def interleave(gens):
            gens = [g for g in gens if g is not None]
            while gens:
                for g in list(gens):
                    try:
                        next(g)
                    except StopIteration:
                        gens.remove(g)

        for d in range(2):
            hb_ = B(f"s_h{d}"); hbb = B(f"s_hb{d}")
            V(lambda d=d: nc.vector.memset(hst[d][:], 0.0), w=[hb_])
            V(lambda d=d: nc.vector.memset(hbf[d][:], 0.0), w=[hbb])
            order = list(range(NCH)) if d == 0 else [1, 0] + list(range(NCH - 1, 1, -1))
            interleave([stageA(d, 0, order[0])])
            for n_ in range(len(order)):
                ga = stageA(d, n_ + 1, order[n_ + 1]) if n_ + 1 < len(order) else None
                interleave([ga, stageB(d, n_, order[n_])])
        s.barrier()

T = 4352; D = 1024; EPS = 1e-6; DEPTH = 4
C_PIDX = 139
I32 = mybir.dt.int32

def gen_dft(k):
    nc, s, B = k.nc, k.s, k.B
    cst = k.cst
    with ExitStack() as st:
        T_ = lambda name, shape, dt: st.enter_context(nc.sbuf_tensor(name + k.sfx, shape, dt))
        kio_i = T_("kio_i", [128, 4096], I32)
        kio = T_("kio", [128, 4096], F32)
        lcol = T_("lcol", [128, 32], F32)
        pi_ = [T_(f"pi{i}", [128, 4096], I32) for i in range(2)]
        tb = [T_(f"tb{i}", [128, 4096], BF16) for i in range(4)]
        s.op("pool", lambda: nc.gpsimd.iota(kio_i[:], pattern=[[1, 4096]], base=0, channel_multiplier=0), w=[B("kio_i")])
        s.op("dve", lambda: nc.vector.tensor_copy(out=kio[:], in_=kio_i[:]), r=[B("kio_i")], w=[B("kio")])
        for lt in range(32):
            s.op("dve", lambda lt=lt: nc.vector.tensor_scalar(out=lcol[:, lt:lt + 1], in0=cst[:, C_PIDX:C_PIDX + 1], scalar1=float(lt * 128), scalar2=None, op0=ALU.add), r=[B("cst")], w=[B("lcol")])
        sc = 2.0 * math.pi / 4096.0
        for lt in range(32):
            for j, (off, dst) in enumerate([(0.0, k.SLN), (3072.0, k.CL)]):
                e = "dve" if j == 0 else "pool"
                eng = nc.vector if j == 0 else nc.gpsimd
                p_ = pi_[j]; pb = B(f"pi{j}")
                s.op(e, lambda eng=eng, p_=p_, lt=lt, off=off: eng.tensor_scalar(out=p_[:], in0=kio[:], scalar1=lcol[:, lt:lt + 1], scalar2=off, op0=ALU.mult, op1=ALU.add), r=[B("kio"), B("lcol")], w=[pb])
                s.op("dve", lambda p_=p_: nc.vector.tensor_single_scalar(out=p_[:], in_=p_[:], scalar=4095, op=ALU.bitwise_and), w=[pb])
                t_ = tb[(lt % 2) * 2 + j]; tbb = B(f"tb{(lt % 2) * 2 + j}")
                s.op("act", lambda t_=t_, p_=p_: nc.scalar.activation(out=t_[:], in_=p_[:], func=AF.Sin, scale=sc, bias=k.negpi[:, 0:1]), r=[pb, B("negpi")], w=[tbb])
                s.dma("sp", dst[lt * 128:(lt + 1) * 128, :], t_[:], r=[tbb], w=[B("DFT")])
        s.barrier()


def phase_fnet(k, l, need_ctx=True):
    nc, s, I, B = k.nc, k.s, k.I, k.B
    with ExitStack() as st:
        T_ = lambda name, shape, dt: st.enter_context(nc.sbuf_tensor(name + k.sfx, shape, dt))
        P_ = lambda name, shape, dt: st.enter_context(nc.psum_tensor(name + k.sfx, shape, dt))
        cs = T_("f_cs", [128, 256], BF16)
        fw32 = T_("f_fw32", [128, 4, 512], F32)
        fw = T_("f_fw", [128, 4, 512], BF16)
        fb = T_("f_fb", [128, 4], F32)
        fuT = [T_(f"f_fuT{i}", [128, 4, 128], BF16) for i in range(2)]
        PQ = T_("f_PQ", [128, 34, 4, 2, 128], BF16)
        tabs = [T_(f"f_tab{i}", [128, 2, 512], BF16) for i in range(4)]
        specT = [T_(f"f_spec{i}", [128, 4, 512], BF16) for i in range(2)]
        gt = [T_(f"f_g{i}", [128, 512], F32) for i in range(2)]
        ob = [T_(f"f_ob{i}", [128, 512], BF16) for i in range(2)]
        psPQ = [P_(f"psPQ{i}", [128, 512], F32) for i in range(2)]
        psS = [P_(f"psS{i}", [128, 512], F32) for i in range(4)]
        psM = [P_(f"psMx{i}", [128, 512], F32) for i in range(2)]
        rows32 = lambda tsr: bass.AP(tsr.tensor, tsr.offset, [[32 * 4096, 128], [1, 128]])
        s.dma("sp", cs[:, 0:128], rows32(k.CL), r=[B("DFT")], w=[B("f_cs")])
        s.dma("sp", cs[:, 128:256], rows32(k.SLN), r=[B("DFT")], w=[B("f_cs")])
        s.dma("sp", fw32[:], I["fnet_w"][l].rearrange("(t p) c -> p t c", p=128), w=[B("f_fw32")])
        s.dma("sp", fb[:], I["fnet_b"][l].rearrange("(t p) -> p t", p=128), w=[B("f_fb")], allow_slow_non_contiguous=True)
        s.op("dve", lambda: nc.vector.tensor_copy(out=fw[:], in_=fw32[:]), r=[B("f_fw32")], w=[B("f_fw")])
        for tt in range(34):
            if tt < 2 and not need_ctx:
                continue
            f_ = fuT[tt % 2]; fb_ = B(f"f_fuT{tt % 2}")
            s.dma("sp", f_[:], k.PB[2048:2560, tt * 128:(tt + 1) * 128].rearrange("(h p) j -> p h j", p=128), r=[B("PB")], w=[fb_])
            for hd in range(4):
                pp = psPQ[hd // 2]; ppb = B(f"psPQ{hd // 2}")
                s.op("pe", lambda pp=pp, hd=hd, f_=f_: nc.tensor.matmul(pp[:, (hd % 2) * 256:(hd % 2) * 256 + 256], lhsT=f_[:, hd, :], rhs=cs[:], start=True, stop=True), r=[fb_, B("f_cs")], w=[ppb])
            for hf in range(2):
                pp = psPQ[hf]; ppb = B(f"psPQ{hf}")
                src = pp[:].rearrange("p (h q m) -> p h q m", h=2, q=2)
                s.op("act", lambda tt=tt, hf=hf, src=src: nc.scalar.copy(out=PQ[:, tt, hf * 2:hf * 2 + 2, 0, :], in_=src[:, :, 0, :]), w=[ppb, B(f"f_PQ{tt}")])
                s.op("dve", lambda tt=tt, hf=hf, src=src: nc.vector.tensor_scalar(out=PQ[:, tt, hf * 2:hf * 2 + 2, 1, :], in0=src[:, :, 1, :], scalar1=-1.0, scalar2=None, op0=ALU.mult), w=[ppb, B(f"f_PQ{tt}")])
        nload = [0]
        nsp = [0]
        bsb = [T_(f"f_bsb{i}", [128, 256], F32) for i in range(2)]
        tab4 = [T_(f"f_tab4_{i}", [128, 2, 4, 256], BF16) for i in range(3)]
        alt = T_("f_alt", [128, 2], F32)
        altb = T_("f_altb", [128, 1], BF16)
        s.op("dve", lambda: nc.vector.tensor_single_scalar(out=alt[:, 0:1].bitcast(I32), in_=k.pidx_i[:, 0:1], scalar=1, op=ALU.bitwise_and), r=[B("pidx_i")], w=[B("f_alt")])
        s.op("dve", lambda: nc.vector.tensor_copy(out=alt[:, 1:2], in_=alt[:, 0:1].bitcast(I32)), w=[B("f_alt")])
        s.op("dve", lambda: nc.vector.tensor_scalar(out=altb[:], in0=alt[:, 1:2], scalar1=-2.0, scalar2=1.0, op0=ALU.mult, op1=ALU.add), r=[B("f_alt")], w=[B("f_altb")])

        def mix_group(sp_, spb, c0, ncol, tok0):
            for ct in range(4):
                pm = psM[ct % 2]; pmb = B(f"psMx{ct % 2}")
                g_ = gt[ct % 2]; gb = B(f"f_g{ct % 2}")
                kw = dict(allow_slow_non_contiguous=True) if ncol == 1 else {}
                s.dma("sp", g_[:, 0:ncol], k.PF[544 + ct * 128:544 + (ct + 1) * 128, tok0:tok0 + ncol], r=[B("PF")], w=[gb], **kw)
                for hd in range(4):
                    s.op("pe", lambda pm=pm, hd=hd, ct=ct: nc.tensor.matmul(pm[:, 0:ncol], lhsT=fw[:, hd, ct * 128:(ct + 1) * 128], rhs=sp_[:, hd, c0:c0 + ncol], start=(hd == 0), stop=(hd == 3)),
                         r=[B("f_fw"), spb], w=[pmb])
                s.op("act", lambda g_=g_: nc.scalar.activation(out=g_[:, 0:ncol], in_=g_[:, 0:ncol], func=AF.Silu), w=[gb])
                o_ = ob[ct % 2]; obb = B(f"f_ob{ct % 2}")
                s.op("dve", lambda o_=o_, pm=pm, ct=ct, g_=g_: nc.vector.scalar_tensor_tensor(out=o_[:, 0:ncol], in0=pm[:, 0:ncol], scalar=fb[:, ct:ct + 1], in1=g_[:, 0:ncol], op0=ALU.add, op1=ALU.mult),
                     r=[gb, B("f_fb")], w=[pmb, obb])
                s.dma("pool", k.MT[1536 + ct * 128:1536 + (ct + 1) * 128, tok0:tok0 + ncol], o_[:, 0:ncol], r=[obb], w=[B("MT")], **kw)

        if need_ctx:
            nrm = 1.0 / math.sqrt(256.0 * 128.0)
            for lt in range(2):
                tab = tabs[nload[0] % 4]; tabb = B(f"f_tab{nload[0] % 4}")
                for j, tsr in enumerate([k.CL, k.SLN]):
                    src = bass.AP(tsr.tensor, tsr.offset + (lt * 128) * 16 * 4096, [[16 * 4096, 128], [1, 256]])
                    s.dma("sp", tab[:, j, 0:256], src, r=[B("DFT")], w=[tabb], allow_slow_non_contiguous=True)
                nload[0] += 1
                for hd in range(4):
                    s.op("pe", lambda hd=hd, lt=lt, tab=tab: nc.tensor.matmul(psS[hd][:, 0:256], lhsT=PQ[:, lt, hd, 0, :], rhs=tab[:, 0, 0:256], start=(lt == 0), stop=False),
                         r=[B(f"f_PQ{lt}"), tabb], w=[B(f"psS{hd}")])
                    s.op("pe", lambda hd=hd, lt=lt, tab=tab: nc.tensor.matmul(psS[hd][:, 0:256], lhsT=PQ[:, lt, hd, 1, :], rhs=tab[:, 1, 0:256], start=False, stop=(lt == 1)),
                         r=[B(f"f_PQ{lt}"), tabb], w=[B(f"psS{hd}")])
            sp_ = specT[nsp[0] % 2]; spb = B(f"f_spec{nsp[0] % 2}"); nsp[0] += 1
            for hd in range(4):
                s.op("act", lambda hd=hd, sp_=sp_: nc.scalar.mul(out=sp_[:, hd, 0:256], in_=psS[hd][:, 0:256], mul=nrm), w=[B(f"psS{hd}"), spb])
            mix_group(sp_, spb, 0, 256, 0)
        nrm = 1.0 / math.sqrt(4096.0 * 128.0)
        for kt in range(8):
            for lg in range(8):
                tb4 = tab4[nload[0] % 3]; tb4b = B(f"f_tab4_{nload[0] % 3}")
                for j, tsr in enumerate([k.CL, k.SLN]):
                    q_ = "sp" if j == 0 else "act"
                    s.dma(q_, tb4[:, j, :, :], tsr[lg * 512:(lg + 1) * 512, kt * 256:(kt + 1) * 256].rearrange("(a p) c -> p a c", p=128), r=[B("DFT")], w=[tb4b])
                nload[0] += 1
                for a_ in range(4):
                    lt = lg * 4 + a_
                    for hd in range(4):
                        s.op("pe", lambda hd=hd, lt=lt, tb4=tb4, a_=a_: nc.tensor.matmul(psS[hd][:, 0:256], lhsT=PQ[:, 2 + lt, hd, 0, :], rhs=tb4[:, 0, a_, :], start=(lt == 0), stop=False, skip_group_check=True),
                             r=[B(f"f_PQ{2 + lt}"), tb4b], w=[B(f"psS{hd}")])
                        s.op("pe", lambda hd=hd, lt=lt, tb4=tb4, a_=a_: nc.tensor.matmul(psS[hd][:, 256:512], lhsT=PQ[:, 2 + lt, hd, 1, :], rhs=tb4[:, 1, a_, :], start=False, stop=(lt == 31), skip_group_check=True),
                             r=[B(f"f_PQ{2 + lt}"), tb4b], w=[B(f"psS{hd}")])
            sp_ = specT[nsp[0] % 2]; spb = B(f"f_spec{nsp[0] % 2}"); nsp[0] += 1
            for hd in range(4):
                b_ = bsb[hd % 2]; bb = B(f"f_bsb{hd % 2}")
                s.op("act", lambda hd=hd, b_=b_: nc.scalar.mul(out=b_[:], in_=psS[hd][:, 256:512], mul=nrm), w=[B(f"psS{hd}"), bb])
                s.op("dve", lambda hd=hd, b_=b_, sp_=sp_: nc.vector.scalar_tensor_tensor(out=sp_[:, hd, 0:256], in0=psS[hd][:, 0:256], scalar=nrm, in1=b_[:], op0=ALU.mult, op1=ALU.add),
                     r=[bb], w=[B(f"psS{hd}"), spb])
                rv = bass.AP(sp_[:].tensor, sp_[:, hd, 511:512].offset, [list(sp_[:, hd, 0:1].ap[0]), [-1, 256]])
                s.op("dve", lambda hd=hd, b_=b_, rv=rv: nc.vector.scalar_tensor_tensor(out=rv, in0=psS[hd][:, 0:256], scalar=nrm, in1=b_[:], op0=ALU.mult, op1=ALU.subtract),
                     r=[bb], w=[B(f"psS{hd}"), spb])
            mix_group(sp_, spb, 0, 256, 256 + kt * 256)
            nm = 256 if kt > 0 else 255
            mix_group(sp_, spb, 256, nm, 256 + 4096 - kt * 256 - 255)
        for lt in range(32):
            for hd in range(4):
                s.op("pe", lambda hd=hd, lt=lt: nc.tensor.matmul(psS[0][:, hd:hd + 1], lhsT=PQ[:, 2 + lt, hd, 0, :], rhs=altb[:, 0:1], start=(lt == 0 and hd == 0), stop=(lt == 31 and hd == 3), skip_group_check=True),
                     r=[B(f"f_PQ{2 + lt}"), B("f_altb")], w=[B("psS0")])
        sp_ = specT[nsp[0] % 2]; spb = B(f"f_spec{nsp[0] % 2}"); nsp[0] += 1
        s.op("act", lambda sp_=sp_: nc.scalar.mul(out=sp_[:, :, 0], in_=psS[0][:, 0:4], mul=nrm), w=[B("psS0"), spb])
        mix_group(sp_, spb, 0, 1, 256 + 2048)
        s.barrier()


def phase_out(k, l, last=False):
    nc, s, I, B = k.nc, k.s, k.I, k.B
    with ExitStack() as st:
        T_ = lambda name, shape, dt: st.enter_context(nc.sbuf_tensor(name + k.sfx, shape, dt))
        P_ = lambda name, shape, dt: st.enter_context(nc.psum_tensor(name + k.sfx, shape, dt))
        wo = T_("o_wo", [128, 16, D], BF16)
        wst = [T_(f"o_wst{i}", [128, 2, D], F32) for i in range(2)]
        bc = T_("o_bc", [128, 2, D], F32)
        mT = [T_(f"o_mT{i}", [128, 16, 128], BF16) for i in range(2)]
        xt = [T_(f"o_x{i}", [128, D], F32) for i in range(2)]
        ot = [T_(f"o_o{i}", [128, D], F32) for i in range(2)]
        junk = T_("o_junk", [128, 512], BF16)
        st4 = T_("o_st", [128, 4], F32)
        psO = [P_(f"psO{i}", [128, 512], F32) for i in range(4)]
        for c in range(8):
            w_ = wst[c % 2]; wb = B(f"o_wst{c % 2}")
            s.dma("sp", w_[:], I["w_out"][l, c * 256:(c + 1) * 256, :].rearrange("(t p) c -> p t c", p=128), w=[wb])
            if c % 2 == 0:
                s.op("act", lambda w_=w_, c=c: nc.scalar.copy(out=wo[:, 2 * c:2 * c + 2, :], in_=w_[:]), r=[wb], w=[B("o_wo")])
            else:
                s.op("pool", lambda w_=w_, c=c: nc.gpsimd.tensor_copy(out=wo[:, 2 * c:2 * c + 2, :], in_=w_[:]), r=[wb], w=[B("o_wo")])
        for j in range(2):
            s.dma("pool", bc[:, j, :], k.MODS[l, 1 - j, 2, :].partition_broadcast(128), r=[B("MODS")], w=[B("o_bc")])
        for n_, tt in enumerate(range(2 if last else 0, 34)):
            t0 = tt * 128
            m_ = mT[n_ % 2]; mb = B(f"o_mT{n_ % 2}")
            s.dma("sp", m_[:], k.MT[:, t0:t0 + 128].rearrange("(t p) j -> p t j", p=128), r=[B("MT")], w=[mb])
            x_ = xt[n_ % 2]; xb = B(f"o_x{n_ % 2}")
            if l == 0:
                src = I["ctx"][t0:t0 + 128, :] if tt < 2 else I["x"][t0 - 256:t0 - 128, :]
                s.dma("sp", x_[:], src, w=[xb])
            else:
                s.dma("sp", x_[:], k.XS[t0:t0 + 128, :], r=[B("XS")], w=[xb])
            for hf in range(2):
                po = psO[(n_ % 2) * 2 + hf]; pob = B(f"psO{(n_ % 2) * 2 + hf}")
                for ct in range(16):
                    s.op("pe", lambda po=po, m_=m_, ct=ct, hf=hf: nc.tensor.matmul(po[:], lhsT=m_[:, ct, :], rhs=wo[:, ct, hf * 512:(hf + 1) * 512], start=(ct == 0), stop=(ct == 15)),
                         r=[mb, B("o_wo")], w=[pob])
                s.op("act", lambda po=po, hf=hf: nc.scalar.activation(out=junk[:], in_=po[:], func=AF.Square, accum_out=st4[:, hf:hf + 1]), w=[pob, B("o_junk"), B("o_st")])
            s.op("dve", lambda: nc.vector.tensor_tensor(out=st4[:, 2:3], in0=st4[:, 0:1], in1=st4[:, 1:2], op=ALU.add), w=[B("o_st")])
            s.op("dve", lambda: nc.vector.tensor_scalar(out=st4[:, 2:3], in0=st4[:, 2:3], scalar1=1.0 / D, scalar2=EPS, op0=ALU.mult, op1=ALU.add), w=[B("o_st")])
            s.op("act", lambda: nc.scalar.activation(out=st4[:, 2:3], in_=st4[:, 2:3], func=AF.Sqrt), w=[B("o_st")])
            s.op("dve", lambda: nc.vector.reciprocal(out=st4[:, 3:4], in_=st4[:, 2:3]), w=[B("o_st")])
            o_ = ot[n_ % 2]; ob = B(f"o_o{n_ % 2}")
            jb = 0 if tt < 2 else 1
            for hf in range(2):
                po = psO[(n_ % 2) * 2 + hf]; pob = B(f"psO{(n_ % 2) * 2 + hf}")
                s.op("dve", lambda po=po, hf=hf, o_=o_, jb=jb: nc.vector.scalar_tensor_tensor(out=o_[:, hf * 512:(hf + 1) * 512], in0=po[:], scalar=st4[:, 3:4], in1=bc[:, jb, hf * 512:(hf + 1) * 512],
                                                                                      op0=ALU.mult, op1=ALU.mult), r=[B("o_st"), B("o_bc")], w=[pob, ob])
            s.op("pool", lambda o_=o_, x_=x_: nc.gpsimd.tensor_tensor(out=o_[:], in0=o_[:], in1=x_[:], op=ALU.add), r=[xb], w=[ob])
            if last:
                s.dma("pool", k.out[t0 - 256:t0 - 128, :], o_[:], r=[ob], w=[B("OUT")])
            else:
                s.dma("pool", k.XS[t0:t0 + 128, :], o_[:], r=[ob], w=[B("XS")])
        s.barrier()

T = 4352; D = 1024; NB = 544
C_MVEC = 128; C_BM8 = 524; C_IDXF = 1024; C_IDXB = 1600
I32 = mybir.dt.int32
TWO_PI = 2.0 * math.pi

def phase_s5(k, l, need_ctx=True):
    nc, s, I, B = k.nc, k.s, k.I, k.B
    cst = k.cst
    V = lambda fn, r=(), w=(): s.op("dve", fn, r=r, w=w)
    A = lambda fn, r=(), w=(): s.op("act", fn, r=r, w=w)
    G = lambda fn, r=(), w=(): s.op("pool", fn, r=r, w=w)
    PE = lambda fn, r=(), w=(): s.op("pe", fn, r=r, w=w)
    with ExitStack() as st:
        T_ = lambda name, shape, dt: st.enter_context(nc.sbuf_tensor(name + k.sfx, shape, dt))
        WS = T_("q_WS", [128, 2, 4, 8, 2, 128], BF16)
        WR = T_("q_WR", [128, 2, 16, 8, 2, 32], BF16)
        KD = T_("q_KD", [128, 2, 4, 8, 128], BF16)
        rho = T_("q_rho", [128, 2, 16], F32)
        th = T_("q_th", [128, 2, 16], F32)
        with ExitStack() as ps:
            TP = lambda name, shape, dt: ps.enter_context(nc.sbuf_tensor(name + k.sfx, shape, dt))
            PP = lambda name, shape, dt: ps.enter_context(nc.psum_tensor(name + k.sfx, shape, dt))
            lam16 = TP("p_lam16", [16, 2, 128], F32)
            lr = TP("p_lr", [128, 16], F32); li = TP("p_li", [128, 16], F32)
            stp = TP("p_stp", [128, 16], F32)
            lrs = TP("p_lrs", [128, 16], F32); lis = TP("p_lis", [128, 16], F32)
            a9 = TP("p_a9", [128, 16, 9], F32); a9b = TP("p_a9b", [128, 16, 9], F32)
            ki = TP("p_ki", [128, 16, 9], I32)
            mag9 = TP("p_mag9", [128, 16, 9], F32)
            Ar = TP("p_Ar", [128, 16, 9], F32); Ai = TP("p_Ai", [128, 16, 9], F32)
            t16 = [TP(f"p_t16_{i}", [128, 16], F32) for i in range(6)]
            Br = TP("p_Br", [128, 16, 16], F32); Bi = TP("p_Bi", [128, 16, 16], F32)
            Bbr = TP("p_Bbr", [128, 16, 16], F32); Bbi = TP("p_Bbi", [128, 16, 16], F32)
            tB = TP("p_tB", [128, 16, 16], F32)
            Cn = [TP(f"p_Cn{i}", [128, 128], F32) for i in range(2)]
            Cr = TP("p_Cr", [128, 16, 16], F32); Ci = TP("p_Ci", [128, 16, 16], F32)
            Zr = TP("p_Zr", [128, 16, 9, 16], F32); Zi = TP("p_Zi", [128, 16, 9, 16], F32)
            tZ = TP("p_tZ", [128, 16, 9, 16], F32)
            Pr = TP("p_Pr", [128, 16, 8, 16], F32); Pi = TP("p_Pi", [128, 16, 8, 16], F32)
            BPr = TP("p_BPr", [128, 16, 128], F32); BPi = TP("p_BPi", [128, 16, 128], F32)
            PPd = [TP(f"p_PPd{i}", [128, 16, 128], BF16) for i in range(2)]
            kdc = TP("p_kdc", [128, 8, 16], F32)
            psT1 = PP("psT1", [128, 128], F32)
            psK = PP("psK", [128, 128], F32)
            psW = [PP(f"psWs{i}", [128, 128], F32) for i in range(2)]
            G(lambda: nc.gpsimd.memset(WR[:], 0.0), w=[B("q_WR")])
            for d in range(2):
                s.dma("sp", lam16[:, 0, :], I["s5_lambda_re"][l, d].rearrange("(a b) n -> a (b n)", b=2), w=[B("p_lam16")])
                s.dma("sp", lam16[:, 1, :], I["s5_lambda_im"][l, d].rearrange("(a b) n -> a (b n)", b=2), w=[B("p_lam16")])
                for j, dst in enumerate([lr, li]):
                    PE(lambda j=j: nc.tensor.transpose(out=psT1[:, 0:16], in_=lam16[:, j, :], identity=cst[0:16, 0:16]), r=[B("p_lam16"), B("cst")], w=[B("psT1")])
                    V(lambda dst=dst: nc.vector.tensor_copy(out=dst[:], in_=psT1[:, 0:16]), w=[B("psT1"), B("p_l")])
                ls = I["s5_log_step"]
                for g2 in range(2):
                    src = bass.AP(ls.tensor, ls.offset + (l * 2 + d) * 32 + g2, [[0, 64], [2, 16]])
                    s.dma("sp", stp[64 * g2:64 * g2 + 64, :], src, w=[B("p_stp")], allow_slow_non_contiguous=True)
                A(lambda: nc.scalar.activation(out=stp[:], in_=stp[:], func=AF.Exp), w=[B("p_stp")])
                V(lambda: nc.vector.tensor_tensor(out=lrs[:], in0=lr[:], in1=stp[:], op=ALU.mult), r=[B("p_l"), B("p_stp")], w=[B("p_ls")])
                V(lambda: nc.vector.tensor_tensor(out=lis[:], in0=li[:], in1=stp[:], op=ALU.mult), r=[B("p_l"), B("p_stp")], w=[B("p_ls")])
                mv = cst[:, C_MVEC:C_MVEC + 9].unsqueeze(1).to_broadcast([128, 16, 9])
                V(lambda: nc.vector.tensor_tensor(out=a9[:], in0=lrs[:].unsqueeze(2).to_broadcast([128, 16, 9]), in1=mv, op=ALU.mult), r=[B("p_ls"), B("cst")], w=[B("p_a9")])
                A(lambda: nc.scalar.activation(out=mag9[:], in_=a9[:], func=AF.Exp), r=[B("p_a9")], w=[B("p_mag9")])
                V(lambda: nc.vector.tensor_tensor(out=a9[:], in0=lis[:].unsqueeze(2).to_broadcast([128, 16, 9]), in1=mv, op=ALU.mult), r=[B("p_ls"), B("cst")], w=[B("p_a9")])
                def reduce_sin(dst, src_ap, shift, shape3):
                    V(lambda: nc.vector.tensor_scalar(out=a9b[:], in0=src_ap, scalar1=shift, scalar2=None, op0=ALU.add), r=[B("p_a9")], w=[B("p_a9b")])
                    V(lambda: nc.vector.tensor_scalar(out=ki[:], in0=a9b[:], scalar1=1.0 / TWO_PI, scalar2=None, op0=ALU.mult), r=[B("p_a9b")], w=[B("p_ki")])
                    V(lambda: nc.vector.scalar_tensor_tensor(out=a9b[:], in0=ki[:], scalar=-TWO_PI, in1=a9b[:], op0=ALU.mult, op1=ALU.add), r=[B("p_ki")], w=[B("p_a9b")])
                    A(lambda: nc.scalar.activation(out=dst[:], in_=a9b[:], func=AF.Sin), r=[B("p_a9b")], w=[B("p_sc")])
                reduce_sin(Ai, a9[:], 0.0, None)
                V(lambda d=d: nc.vector.tensor_copy(out=th[:, d, :], in_=a9b[:, :, 8]), r=[B("p_a9b")], w=[B("q_th")])
                reduce_sin(Ar, a9[:], math.pi / 2.0, None)
                V(lambda: nc.vector.tensor_tensor(out=Ar[:], in0=Ar[:], in1=mag9[:], op=ALU.mult), r=[B("p_mag9")], w=[B("p_sc")])
                V(lambda: nc.vector.tensor_tensor(out=Ai[:], in0=Ai[:], in1=mag9[:], op=ALU.mult), r=[B("p_mag9")], w=[B("p_sc")])
                V(lambda d=d: nc.vector.tensor_copy(out=rho[:, d, :], in_=mag9[:, :, 8]), r=[B("p_mag9")], w=[B("q_rho")])
                am1, den, fr, fi, u1, u2 = t16
                V(lambda: nc.vector.tensor_scalar(out=am1[:], in0=Ar[:, :, 1], scalar1=-1.0, scalar2=None, op0=ALU.add), r=[B("p_sc")], w=[B("p_t16")])
                V(lambda: nc.vector.tensor_tensor(out=den[:], in0=lr[:], in1=lr[:], op=ALU.mult), r=[B("p_l")], w=[B("p_t16")])
                V(lambda: nc.vector.tensor_tensor(out=u1[:], in0=li[:], in1=li[:], op=ALU.mult), r=[B("p_l")], w=[B("p_t16")])
                V(lambda: nc.vector.tensor_tensor(out=den[:], in0=den[:], in1=u1[:], op=ALU.add), w=[B("p_t16")])
                V(lambda: nc.vector.reciprocal(out=den[:], in_=den[:]), w=[B("p_t16")])
                V(lambda: nc.vector.tensor_tensor(out=u1[:], in0=am1[:], in1=lr[:], op=ALU.mult), w=[B("p_t16")])
                V(lambda: nc.vector.tensor_tensor(out=u2[:], in0=Ai[:, :, 1], in1=li[:], op=ALU.mult), w=[B("p_t16")])
                V(lambda: nc.vector.tensor_tensor(out=fr[:], in0=u1[:], in1=u2[:], op=ALU.add), w=[B("p_t16")])
                V(lambda: nc.vector.tensor_tensor(out=fr[:], in0=fr[:], in1=den[:], op=ALU.mult), w=[B("p_t16")])
                V(lambda: nc.vector.tensor_tensor(out=u1[:], in0=Ai[:, :, 1], in1=lr[:], op=ALU.mult), w=[B("p_t16")])
                V(lambda: nc.vector.tensor_tensor(out=u2[:], in0=am1[:], in1=li[:], op=ALU.mult), w=[B("p_t16")])
                V(lambda: nc.vector.tensor_tensor(out=fi[:], in0=u1[:], in1=u2[:], op=ALU.subtract), w=[B("p_t16")])
                V(lambda: nc.vector.tensor_tensor(out=fi[:], in0=fi[:], in1=den[:], op=ALU.mult), w=[B("p_t16")])
                for j, (dst, nm) in enumerate([(Br, "s5_b_re"), (Bi, "s5_b_im")]):
                    bt = I[nm]
                    for g2 in range(2):
                        src = bass.AP(bt.tensor, bt.offset + ((l * 2 + d) * 32 + g2) * 1024, [[16, 64], [2048, 16], [1, 16]])
                        s.dma("sp", dst[64 * g2:64 * g2 + 64, :, :], src, w=[B("p_B")])
                frb = fr[:].unsqueeze(2).to_broadcast([128, 16, 16]); fib = fi[:].unsqueeze(2).to_broadcast([128, 16, 16])
                V(lambda: nc.vector.tensor_tensor(out=Bbr[:], in0=Br[:], in1=frb, op=ALU.mult), r=[B("p_B"), B("p_t16")], w=[B("p_Bb")])
                V(lambda: nc.vector.tensor_tensor(out=tB[:], in0=Bi[:], in1=fib, op=ALU.mult), r=[B("p_B"), B("p_t16")], w=[B("p_tB")])
                V(lambda: nc.vector.tensor_tensor(out=Bbr[:], in0=Bbr[:], in1=tB[:], op=ALU.subtract), r=[B("p_tB")], w=[B("p_Bb")])
                V(lambda: nc.vector.tensor_tensor(out=Bbi[:], in0=Bi[:], in1=frb, op=ALU.mult), r=[B("p_B"), B("p_t16")], w=[B("p_Bb")])
                V(lambda: nc.vector.tensor_tensor(out=tB[:], in0=Br[:], in1=fib, op=ALU.mult), r=[B("p_B"), B("p_t16")], w=[B("p_tB")])
                V(lambda: nc.vector.tensor_tensor(out=Bbi[:], in0=Bbi[:], in1=tB[:], op=ALU.add), r=[B("p_tB")], w=[B("p_Bb")])
                for j, (dst, nm) in enumerate([(Cr, "s5_c_re"), (Ci, "s5_c_im")]):
                    ct_ = I[nm]
                    for pset in range(2):
                        cn = Cn[pset]; cnb = B(f"p_Cn{pset}")
                        for pr in range(8):
                            g0 = 2 * (8 * pset + pr)
                            src = bass.AP(ct_.tensor, ct_.offset + ((l * 2 + d) * 32 + g0) * 1024, [[64, 16], [1024, 2], [1, 64]])
                            s.dma("sp", cn[16 * pr:16 * pr + 16, :].rearrange("p (a n) -> p a n", a=2), src, w=[cnb])
                        PE(lambda cn=cn: nc.tensor.transpose(out=psT1[:], in_=cn[:], identity=cst[:, 0:128]), r=[cnb, B("cst")], w=[B("psT1")])
                        V(lambda dst=dst, pset=pset: nc.vector.tensor_copy(out=dst[:, 8 * pset:8 * pset + 8, :], in_=psT1[:].rearrange("p (a k) -> p a k", k=16)), w=[B("psT1"), B("p_C")])
                Crb = lambda X: X[:].unsqueeze(2).to_broadcast([128, 16, 9, 16])
                Ab = lambda X: X[:].unsqueeze(3).to_broadcast([128, 16, 9, 16])
                V(lambda: nc.vector.tensor_tensor(out=Zr[:], in0=Crb(Cr), in1=Ab(Ar), op=ALU.mult), r=[B("p_C"), B("p_sc")], w=[B("p_Z")])
                G(lambda: nc.gpsimd.tensor_tensor(out=tZ[:], in0=Crb(Ci), in1=Ab(Ai), op=ALU.mult), r=[B("p_C"), B("p_sc")], w=[B("p_tZ")])
                V(lambda: nc.vector.tensor_tensor(out=Zr[:], in0=Zr[:], in1=tZ[:], op=ALU.subtract), r=[B("p_tZ")], w=[B("p_Z")])
                V(lambda: nc.vector.tensor_tensor(out=Zi[:], in0=Crb(Cr), in1=Ab(Ai), op=ALU.mult), r=[B("p_C"), B("p_sc")], w=[B("p_Z")])
                G(lambda: nc.gpsimd.tensor_tensor(out=tZ[:], in0=Crb(Ci), in1=Ab(Ar), op=ALU.mult), r=[B("p_C"), B("p_sc")], w=[B("p_tZ")])
                V(lambda: nc.vector.tensor_tensor(out=Zi[:], in0=Zi[:], in1=tZ[:], op=ALU.add), r=[B("p_tZ")], w=[B("p_Z")])
                for g2 in range(2):
                    sl = slice(64 * g2, 64 * g2 + 64)
                    V(lambda sl=sl, g2=g2, d=d: nc.vector.tensor_copy(out=WR[sl, d, :, :, 0, 16 * g2:16 * g2 + 16], in_=Zr[sl, :, 1:9, :]), r=[B("p_Z")], w=[B("q_WR")])
                    V(lambda sl=sl, g2=g2, d=d: nc.vector.tensor_scalar(out=WR[sl, d, :, :, 1, 16 * g2:16 * g2 + 16], in0=Zi[sl, :, 1:9, :], scalar1=-1.0, scalar2=None, op0=ALU.mult), r=[B("p_Z")], w=[B("q_WR")])
                Bb8 = lambda X: X[:].unsqueeze(2).to_broadcast([128, 16, 8, 16])
                A8 = lambda X: X[:, :, 0:8].unsqueeze(3).to_broadcast([128, 16, 8, 16])
                tP = tZ[:, :, 0:8, :]
                V(lambda: nc.vector.tensor_tensor(out=Pr[:], in0=Bb8(Bbr), in1=A8(Ar), op=ALU.mult), r=[B("p_Bb"), B("p_sc")], w=[B("p_P")])
                G(lambda: nc.gpsimd.tensor_tensor(out=tP, in0=Bb8(Bbi), in1=A8(Ai), op=ALU.mult), r=[B("p_Bb"), B("p_sc")], w=[B("p_tZ")])
                V(lambda: nc.vector.tensor_tensor(out=Pr[:], in0=Pr[:], in1=tP, op=ALU.subtract), r=[B("p_tZ")], w=[B("p_P")])
                V(lambda: nc.vector.tensor_tensor(out=Pi[:], in0=Bb8(Bbi), in1=A8(Ar), op=ALU.mult), r=[B("p_Bb"), B("p_sc")], w=[B("p_P")])
                G(lambda: nc.gpsimd.tensor_tensor(out=tP, in0=Bb8(Bbr), in1=A8(Ai), op=ALU.mult), r=[B("p_Bb"), B("p_sc")], w=[B("p_tZ")])
                V(lambda: nc.vector.tensor_tensor(out=Pi[:], in0=Pi[:], in1=tP, op=ALU.add), r=[B("p_tZ")], w=[B("p_P")])
                G(lambda: nc.gpsimd.memset(BPr[:], 0.0), w=[B("p_BP")])
                G(lambda: nc.gpsimd.memset(BPi[:], 0.0), w=[B("p_BP")])
                for g2 in range(2):
                    sl = slice(64 * g2, 64 * g2 + 64)
                    for q in range(4):
                        c0 = 32 * q + 16 * g2
                        V(lambda sl=sl, q=q, c0=c0: nc.vector.tensor_copy(out=BPr[sl, q::4, c0:c0 + 16], in_=Bbr[sl, q::4, :]), r=[B("p_Bb")], w=[B("p_BP")])
                        V(lambda sl=sl, q=q, c0=c0: nc.vector.tensor_scalar(out=BPi[sl, q::4, c0:c0 + 16], in0=Bbi[sl, q::4, :], scalar1=-1.0, scalar2=None, op0=ALU.mult), r=[B("p_Bb")], w=[B("p_BP")])
                for t in range(4):
                    n_ = 0
                    for q in range(4):
                        pr = 4 * t + q
                        for (BP_, Z_) in ((BPr, Zr), (BPi, Zi)):
                            PE(lambda pr=pr, BP_=BP_, Z_=Z_, n_=n_: nc.tensor.matmul(psK[:], lhsT=BP_[:, pr, :], rhs=Z_[:, pr, 0:8, :].rearrange("p a k -> p (a k)"), start=(n_ == 0), stop=(n_ == 7)),
                               r=[B("p_BP"), B("p_Z")], w=[B("psK")])
                            n_ += 1
                    V(lambda: nc.vector.tensor_copy(out=kdc[:], in_=psK[:].rearrange("p (a k) -> p a k", k=16)), w=[B("psK"), B("p_kdc")])
                    V(lambda t=t, d=d: nc.vector.tensor_tensor(out=KD[:, d, t, :, :].rearrange("p a (g k) -> p a g k", k=16), in0=kdc[:].unsqueeze(2).to_broadcast([128, 8, 8, 16]),
                                                          in1=cst[:, C_BM8:C_BM8 + 8].unsqueeze(1).unsqueeze(3).to_broadcast([128, 8, 8, 16]), op=ALU.mult), r=[B("p_kdc"), B("cst")], w=[B("q_KD")])
                n_w = 0
                for m in range(8):
                    for ri, P_ in enumerate((Pr, Pi)):
                        pd = PPd[n_w % 2]; pdb = B(f"p_PPd{n_w % 2}")
                        G(lambda pd=pd: nc.gpsimd.memset(pd[:], 0.0), w=[pdb])
                        for g2 in range(2):
                            sl = slice(64 * g2, 64 * g2 + 64)
                            for q in range(4):
                                c0 = 32 * q + 16 * g2
                                e = V if (q % 2 == 0) else G
                                eng = nc.vector if (q % 2 == 0) else nc.gpsimd
                                e(lambda sl=sl, q=q, c0=c0, pd=pd, P_=P_, m=m, eng=eng: eng.tensor_copy(out=pd[sl, q::4, c0:c0 + 16], in_=P_[sl, q::4, m, :]), r=[B("p_P")], w=[pdb])
                        for t in range(4):
                            pw = psW[t % 2]; pwb = B(f"psWs{t % 2}")
                            for q in range(4):
                                PE(lambda pw=pw, pd=pd, t=t, q=q: nc.tensor.matmul(pw[:], lhsT=pd[:, 4 * t + q, :], rhs=k.identb[:], start=(q == 0), stop=(q == 3)), r=[pdb, B("identb")], w=[pwb])
                            if t % 2 == 0:
                                A(lambda pw=pw, t=t, m=m, ri=ri, d=d: nc.scalar.copy(out=WS[:, d, t, m, ri, :], in_=pw[:]), w=[pwb, B("q_WS")])
                            else:
                                V(lambda pw=pw, t=t, m=m, ri=ri, d=d: nc.vector.tensor_copy(out=WS[:, d, t, m, ri, :], in_=pw[:]), w=[pwb, B("q_WS")])
                        n_w += 1
            s.barrier()
        if getattr(k, 'stop', None) == 's5prep':
            return
        phase_s5_main(k, l, need_ctx, WS, WR, KD, rho, th, st)
        s.barrier()


def phase_s5_main(k, l, need_ctx, WS, WR, KD, rho, th, st):
    nc, s, I, B = k.nc, k.s, k.I, k.B
    cst = k.cst
    V = lambda fn, r=(), w=(): s.op("dve", fn, r=r, w=w)
    A = lambda fn, r=(), w=(): s.op("act", fn, r=r, w=w)
    G = lambda fn, r=(), w=(): s.op("pool", fn, r=r, w=w)
    PE = lambda fn, r=(), w=(): s.op("pe", fn, r=r, w=w)
    T_ = lambda name, shape, dt: st.enter_context(nc.sbuf_tensor(name + k.sfx, shape, dt))
    P_ = lambda name, shape, dt: st.enter_context(nc.psum_tensor(name + k.sfx, shape, dt))
    uT = st.enter_context(nc.sbuf_tensor("q_uT" + k.sfx, [128, 4, T], BF16))
    mst = ExitStack()
    T_ = lambda name, shape, dt: mst.enter_context(nc.sbuf_tensor(name + k.sfx, shape, dt))
    hst = [T_(f"q_hst{i}", [128, 2, NB], BF16) for i in range(2)]
    SL = []
    for i_ in range(2):
        o = {}
        for nm in ("Sr", "Si", "cosT", "sinT", "ang", "xr", "xi", "t1", "t2", "Gr", "Gi"):
            o[nm] = T_(f"q_{nm}{i_}", [128, NB], F32)
        o["kiT"] = T_(f"q_ki{i_}", [128, NB], I32)
        o["psS"] = [mst.enter_context(nc.psum_tensor(f"psSq{i_}_{j}" + k.sfx, [128, 1024], F32)) for j in range(2)]
        SL.append(o)
    for t in range(4):
        s.dma("sp", uT[:, t, :], k.PB[1536 + t * 128:1536 + (t + 1) * 128, :], r=[B("PB")], w=[B("q_uT")])
    pieces = [(32, 288, 0), (288, 544, 256), (0, 32, 512)]
    def it_gen(d, pr, si):
        o = SL[si]
        Sr, Si, cosT, sinT, ang, xr, xi, t1, t2, Gr, Gi, kiT, psS = (o[n] for n in ('Sr','Si','cosT','sinT','ang','xr','xi','t1','t2','Gr','Gi','kiT','psS'))
        sfx_ = str(si)
        t = pr // 4; q = pr % 4
        rows = slice(32 * q, 32 * q + 32)
        for ri in range(2):
            for (b0, b1, pc) in pieces:
                for pos in range(8):
                    m = 7 - pos if d == 0 else pos
                    rhs = uT[rows, t, 8 * b0 + pos:8 * b1:8]
                    PE(lambda ri=ri, pc=pc, b0=b0, b1=b1, m=m, rhs=rhs, pos=pos: nc.tensor.matmul(psS[ri][:, pc:pc + (b1 - b0)], lhsT=WS[rows, d, t, m, ri, :], rhs=rhs, start=(pos == 0), stop=(pos == 7),
                                                                                        tile_position=(32 * q, 0), skip_group_check=True),
                       r=[B("q_uT"), B("q_WS")], w=[B(f"psSq{ri}_" + sfx_)])
        yield
        for ri, dst in enumerate((Sr, Si)):
            A(lambda ri=ri, dst=dst: nc.scalar.copy(out=dst[:, 32:544], in_=psS[ri][:, 0:512]), w=[B(f"psSq{ri}_" + sfx_), B("q_S" + sfx_)])
            A(lambda ri=ri, dst=dst: nc.scalar.copy(out=dst[:, 0:32], in_=psS[ri][:, 512:544]), w=[B(f"psSq{ri}_" + sfx_), B("q_S" + sfx_)])
        yield
        idx = cst[:, C_IDXF:C_IDXF + NB] if d == 0 else cst[:, C_IDXB:C_IDXB + NB]
        for (dst, shift) in ((sinT, 0.0), (cosT, math.pi / 2.0)):
            V(lambda shift=shift: nc.vector.tensor_scalar(out=ang[:], in0=idx, scalar1=th[:, d, pr:pr + 1], scalar2=shift, op0=ALU.mult, op1=ALU.add), r=[B("q_th"), B("cst")], w=[B("q_ang" + sfx_)])
            V(lambda: nc.vector.tensor_scalar(out=kiT[:], in0=ang[:], scalar1=1.0 / TWO_PI, scalar2=None, op0=ALU.mult), r=[B("q_ang" + sfx_)], w=[B("q_ki" + sfx_)])
            V(lambda: nc.vector.scalar_tensor_tensor(out=ang[:], in0=kiT[:], scalar=-TWO_PI, in1=ang[:], op0=ALU.mult, op1=ALU.add), r=[B("q_ki" + sfx_)], w=[B("q_ang" + sfx_)])
            A(lambda dst=dst: nc.scalar.activation(out=dst[:], in_=ang[:], func=AF.Sin), r=[B("q_ang" + sfx_)], w=[B("q_tw" + sfx_)])
        yield
        V(lambda: nc.vector.tensor_tensor(out=xr[:], in0=Sr[:], in1=cosT[:], op=ALU.mult), r=[B("q_S" + sfx_), B("q_tw" + sfx_)], w=[B("q_xr" + sfx_)])
        G(lambda: nc.gpsimd.tensor_tensor(out=t1[:], in0=Si[:], in1=sinT[:], op=ALU.mult), r=[B("q_S" + sfx_), B("q_tw" + sfx_)], w=[B("q_t1" + sfx_)])
        V(lambda: nc.vector.tensor_tensor(out=xr[:], in0=xr[:], in1=t1[:], op=ALU.add), r=[B("q_t1" + sfx_)], w=[B("q_xr" + sfx_)])
        G(lambda: nc.gpsimd.tensor_tensor(out=xi[:], in0=Si[:], in1=cosT[:], op=ALU.mult), r=[B("q_S" + sfx_), B("q_tw" + sfx_)], w=[B("q_xi" + sfx_)])
        V(lambda: nc.vector.tensor_tensor(out=t2[:], in0=Sr[:], in1=sinT[:], op=ALU.mult), r=[B("q_S" + sfx_), B("q_tw" + sfx_)], w=[B("q_t2" + sfx_)])
        G(lambda: nc.gpsimd.tensor_tensor(out=xi[:], in0=xi[:], in1=t2[:], op=ALU.subtract), r=[B("q_t2" + sfx_)], w=[B("q_xi" + sfx_)])
        yield
        rcol = rho[:, d, pr:pr + 1]
        for (src, dst, nm) in ((xr, Gr, "q_xr"), (xi, Gi, "q_xi")):
            if d == 0:
                V(lambda src=src, dst=dst: nc.vector.tensor_tensor_scan(out=dst[:], data0=rcol.to_broadcast([128, NB]), data1=src[:], initial=0.0, op0=ALU.mult, op1=ALU.add),
                  r=[B(nm), B("q_rho")], w=[B("q_G" + sfx_)])
            else:
                rv = lambda X, a, b: bass.AP(X[:].tensor, X[:, b - 1:b].offset, [list(X[:].ap[0]), [-1, b - a]])
                V(lambda src=src, dst=dst: nc.vector.tensor_tensor_scan(out=rv(dst, 0, 32), data0=rcol.to_broadcast([128, 32]), data1=rv(src, 0, 32), initial=0.0, op0=ALU.mult, op1=ALU.add),
                  r=[B(nm), B("q_rho")], w=[B("q_G" + sfx_)])
                V(lambda src=src, dst=dst: nc.vector.tensor_tensor_scan(out=rv(dst, 32, 544), data0=rcol.to_broadcast([128, 512]), data1=rv(src, 32, 544), initial=dst[:, 0:1], op0=ALU.mult, op1=ALU.add),
                  r=[B(nm), B("q_rho")], w=[B("q_G" + sfx_)])
        yield
        hs_ = hst[si]; hsb = B(f"q_hst{si}")
        if d == 0:
            G(lambda hs_=hs_: nc.gpsimd.memset(hs_[:, :, 0:1], 0.0), w=[hsb])
        if d == 0:
            so, si_ = slice(1, 544), slice(0, 543)
        else:
            so, si_ = slice(0, 543), slice(1, 544)
        V(lambda: nc.vector.tensor_tensor(out=t1[:], in0=Gr[:], in1=cosT[:], op=ALU.mult), r=[B("q_G" + sfx_), B("q_tw" + sfx_)], w=[B("q_t1" + sfx_)])
        G(lambda: nc.gpsimd.tensor_tensor(out=t2[:], in0=Gi[:], in1=sinT[:], op=ALU.mult), r=[B("q_G" + sfx_), B("q_tw" + sfx_)], w=[B("q_t2" + sfx_)])
        V(lambda: nc.vector.tensor_tensor(out=hs_[:, 0, so], in0=t1[:, si_], in1=t2[:, si_], op=ALU.subtract), r=[B("q_t1" + sfx_), B("q_t2" + sfx_)], w=[hsb])
        if d == 1:
            V(lambda: nc.vector.tensor_tensor(out=hs_[:, 0, 543:544], in0=t1[:, 0:1], in1=t2[:, 0:1], op=ALU.subtract), r=[B("q_t1" + sfx_), B("q_t2" + sfx_)], w=[hsb])
        G(lambda: nc.gpsimd.tensor_tensor(out=xr[:], in0=Gr[:], in1=sinT[:], op=ALU.mult), r=[B("q_G" + sfx_), B("q_tw" + sfx_)], w=[B("q_xr" + sfx_)])
        V(lambda: nc.vector.tensor_tensor(out=xi[:], in0=Gi[:], in1=cosT[:], op=ALU.mult), r=[B("q_G" + sfx_), B("q_tw" + sfx_)], w=[B("q_xi" + sfx_)])
        G(lambda: nc.gpsimd.tensor_tensor(out=hs_[:, 1, so], in0=xr[:, si_], in1=xi[:, si_], op=ALU.add), r=[B("q_xr" + sfx_), B("q_xi" + sfx_)], w=[hsb])
        if d == 1:
            G(lambda: nc.gpsimd.tensor_tensor(out=hs_[:, 1, 543:544], in0=xr[:, 0:1], in1=xi[:, 0:1], op=ALU.add), r=[B("q_xr" + sfx_), B("q_xi" + sfx_)], w=[hsb])
            G(lambda: nc.gpsimd.memset(hs_[:, :, 31:32], 0.0), w=[hsb])
        s.dma("sp", k.HD[d, pr].rearrange("r p c -> p r c"), hs_[:], r=[hsb], w=[B("HD")])

    def interleave(gens):
        gens = list(gens)
        while gens:
            for g_ in list(gens):
                try:
                    next(g_)
                except StopIteration:
                    gens.remove(g_)
    its = [(d, pr) for d in range(2) for pr in range(16)]
    for i_ in range(0, len(its), 2):
        interleave([it_gen(its[i_][0], its[i_][1], 0), it_gen(its[i_ + 1][0], its[i_ + 1][1], 1)])
    s.barrier()
    mst.close()
    if getattr(k, 'stop', None) == 's5main':
        return
    s5_glu(k, l, need_ctx, WR, KD, uT, None, st)


def s5_glu(k, l, need_ctx, WR, KD, uT, Hin, st):
    nc, s, I, B = k.nc, k.s, k.I, k.B
    V = lambda fn, r=(), w=(): s.op("dve", fn, r=r, w=w)
    A = lambda fn, r=(), w=(): s.op("act", fn, r=r, w=w)
    G = lambda fn, r=(), w=(): s.op("pool", fn, r=r, w=w)
    PE = lambda fn, r=(), w=(): s.op("pe", fn, r=r, w=w)
    T_ = lambda name, shape, dt: st.enter_context(nc.sbuf_tensor(name + k.sfx, shape, dt))
    P_ = lambda name, shape, dt: st.enter_context(nc.psum_tensor(name + k.sfx, shape, dt))
    wgs = T_("g_wgs", [128, 512], F32); wg = T_("g_wg", [128, 4, 512], BF16)
    bg = T_("g_bg", [128, 4], F32); dsk = T_("g_dsk", [128, 4], F32)
    dgs = T_("g_dgs", [128, 4, 128], BF16)
    y32 = [T_(f"g_y{i}", [128, 512], F32) for i in range(2)]
    y2 = [T_(f"g_y2{i}", [128, 512], F32) for i in range(2)]
    vT = T_("g_v", [128, 4, 2048], BF16)
    gs = [T_(f"g_gs{i}", [128, 512], F32) for i in range(2)]
    o1 = T_("g_o1", [128, 512], F32)
    ob = [T_(f"g_ob{i}", [128, 512], BF16) for i in range(2)]
    Hc = T_("g_Hc", [128, 64, 256], BF16)
    py4 = P_("psY4", [128, 2048], F32)
    psG = [P_(f"psGq{i}", [128, 512], F32) for i in range(2)]
    for t in range(4):
        s.dma("sp", wgs[:], I["s5_w_glu"][l, t * 128:(t + 1) * 128, :], w=[B("g_wgs")])
        V(lambda t=t: nc.vector.tensor_copy(out=wg[:, t, :], in_=wgs[:]), r=[B("g_wgs")], w=[B("g_wg")])
    s.dma("sp", bg[:], I["s5_b_glu"][l].rearrange("(t p) -> p t", p=128), w=[B("g_bg")], allow_slow_non_contiguous=True)
    s.dma("sp", dsk[:], I["s5_d"][l].rearrange("(t p) -> p t", p=128), w=[B("g_dsk")], allow_slow_non_contiguous=True)
    for t in range(4):
        V(lambda t=t: nc.vector.tensor_scalar(out=dgs[:, t, :], in0=k.identb[:], scalar1=dsk[:, t:t + 1], scalar2=None, op0=ALU.mult), r=[B("g_dsk"), B("identb")], w=[B("g_dgs")])
    chunks = [(0, 32)] if need_ctx else []
    chunks += [(32, 256), (288, 256)]
    hcb = B("g_Hc"); vb = B("g_v"); pyb = B("psY4")
    npc = 0
    for ci, (b0, nb) in enumerate(chunks):
        t0 = 8 * b0; ntok = 8 * nb
        for dd in range(2):
            for qq in range(4):
                s.dma("sp", Hc[:, dd * 32 + qq * 8:dd * 32 + qq * 8 + 8, 0:nb], k.HD[dd, qq * 4:qq * 4 + 4, :, :, b0:b0 + nb].rearrange("q r p c -> p (q r) c"), r=[B("HD")], w=[hcb])
        for t in range(4):
            reg = lambda o, rows=slice(0, 128): py4[rows, o * 256:o * 256 + nb]
            for o in range(8):
                PE(lambda o=o, t=t: nc.tensor.matmul(reg(o), lhsT=dgs[:, t, :], rhs=uT[:, t, t0 + o:t0 + ntok:8], start=(o % 2 == 0), stop=False, skip_group_check=True),
                   r=[B("g_dgs"), B("q_uT")], w=[pyb])
            for d in range(2):
                for o in range(8):
                    srcs = range(0, o + 1) if d == 0 else range(o, 8)
                    for o2 in srcs:
                        lag = abs(o - o2)
                        PE(lambda t=t, d=d, lag=lag, o=o, o2=o2: nc.tensor.matmul(reg(o), lhsT=KD[:, d, t, lag, :], rhs=uT[:, t, t0 + o2:t0 + ntok:8], start=False, stop=False, skip_group_check=True),
                           r=[B("q_KD"), B("q_uT")], w=[pyb])
                for q in range(4):
                    pr = 4 * t + q
                    for o in range(8):
                        i_ = o if d == 0 else 7 - o
                        for ri in range(2):
                            PE(lambda d=d, pr=pr, i_=i_, ri=ri, o=o, q=q: nc.tensor.matmul(reg(o, slice(32 * q, 32 * q + 32)), lhsT=WR[:, d, pr, i_, ri, :], rhs=Hc[:, (d * 16 + pr) * 2 + ri, 0:nb],
                                                                                     start=False, stop=False, tile_position=(0, 32 * q), skip_group_check=True),
                               r=[B("q_WR"), hcb], w=[pyb])
            for pc in range(4):
                ya = y32[npc % 2]; yab = B(f"g_y{npc % 2}"); yb_ = y2[npc % 2]; ybb = B(f"g_y2{npc % 2}")
                src = py4[:, pc * 512:(pc + 1) * 512].rearrange("p (o c) -> p o c", o=2)[:, :, 0:nb]
                ya3 = ya[:, 0:2 * nb].rearrange("p (o c) -> p o c", o=2)
                yb3 = yb_[:, 0:2 * nb].rearrange("p (o c) -> p o c", o=2)
                dst = vT[:, t, 0:ntok].rearrange("p (c o) -> p o c", o=8)[:, 2 * pc:2 * pc + 2, :]
                A(lambda src=src, ya3=ya3: nc.scalar.copy(out=ya3, in_=src), w=[pyb, yab])
                G(lambda ya=ya, yb_=yb_: nc.gpsimd.tensor_tensor(out=yb_[:, 0:2 * nb], in0=ya[:, 0:2 * nb], in1=ya[:, 0:2 * nb], op=ALU.mult), r=[yab], w=[ybb])
                V(lambda yb_=yb_: nc.vector.tensor_scalar(out=yb_[:, 0:2 * nb], in0=yb_[:, 0:2 * nb], scalar1=0.044715, scalar2=1.0, op0=ALU.mult, op1=ALU.add), w=[ybb])
                V(lambda ya=ya, yb_=yb_: nc.vector.tensor_tensor(out=yb_[:, 0:2 * nb], in0=yb_[:, 0:2 * nb], in1=ya[:, 0:2 * nb], op=ALU.mult), r=[yab], w=[ybb])
                A(lambda yb_=yb_: nc.scalar.activation(out=yb_[:, 0:2 * nb], in_=yb_[:, 0:2 * nb], func=AF.Sigmoid, scale=1.5957691216), w=[ybb])
                V(lambda dst=dst, yb3=yb3, ya3=ya3: nc.vector.tensor_tensor(out=dst, in0=yb3, in1=ya3, op=ALU.mult), r=[yab, ybb], w=[vb])
                npc += 1
        for sc0 in range(0, ntok, 512):
            n_ = min(512, ntok - sc0)
            for ct in range(4):
                pg = psG[ct % 2]; pgb = B(f"psGq{ct % 2}")
                g_ = gs[ct % 2]; gb = B(f"g_gs{ct % 2}")
                s.dma("sp", g_[:, 0:n_], k.PF[32 + ct * 128:32 + (ct + 1) * 128, t0 + sc0:t0 + sc0 + n_], r=[B("PF")], w=[gb])
                for t in range(4):
                    PE(lambda pg=pg, t=t, ct=ct, sc0=sc0, n_=n_: nc.tensor.matmul(pg[:, 0:n_], lhsT=wg[:, t, ct * 128:(ct + 1) * 128], rhs=vT[:, t, sc0:sc0 + n_], start=(t == 0), stop=(t == 3)),
                       r=[B("g_wg"), vb], w=[pgb])
                A(lambda pg=pg, ct=ct, n_=n_: nc.scalar.activation(out=o1[:, 0:n_], in_=pg[:, 0:n_], func=AF.Sigmoid, bias=bg[:, ct:ct + 1]), r=[B("g_bg")], w=[pgb, B("g_o1")])
                V(lambda ct=ct, sc0=sc0, n_=n_: nc.vector.tensor_tensor(out=o1[:, 0:n_], in0=o1[:, 0:n_], in1=vT[:, ct, sc0:sc0 + n_], op=ALU.mult), r=[vb], w=[B("g_o1")])
                A(lambda g_=g_, n_=n_: nc.scalar.activation(out=g_[:, 0:n_], in_=g_[:, 0:n_], func=AF.Silu), w=[gb])
                o_ = ob[ct % 2]; obb = B(f"g_ob{ct % 2}")
                V(lambda o_=o_, g_=g_, n_=n_: nc.vector.tensor_tensor(out=o_[:, 0:n_], in0=o1[:, 0:n_], in1=g_[:, 0:n_], op=ALU.mult), r=[gb, B("g_o1")], w=[obb])
                s.dma("pool", k.MT[1024 + ct * 128:1024 + (ct + 1) * 128, t0 + sc0:t0 + sc0 + n_], o_[:, 0:n_], r=[obb], w=[B("MT")])


D = 1024; T = 4352; TC = 256; TL = 4096; NT = 34; DEPTH = 4
DIN = 4640; EPS = 1e-6

class K:
    pass

def dram(nc, name, shape, dt, kind="Internal"):
    return nc.dram_tensor(name, list(shape), dt, kind=kind).ap()

def build(nlayers=DEPTH, stop=None, dbg=()):
    nc = bass.Bass("TRN2", target_bir_lowering=False)
    s = Sched(nc)
    k = K(); k.nc = nc; k.s = s; k.dbg = dbg; k.stop = stop; k.sfx = ''
    I = {}
    def inp(name, shape):
        I[name] = dram(nc, name, shape, F32, "ExternalInput")
    inp("x", [TL, D]); inp("c", [D]); inp("ctx", [TC, D]); inp("c_ctx", [D])
    inp("w_mod", [DEPTH, D, 3 * D]); inp("b_mod", [DEPTH, 3 * D]); inp("g_pre", [DEPTH, D]); inp("g_post", [DEPTH, D])
    inp("w_in", [DEPTH, D, DIN]); inp("conv_w", [DEPTH, 9, 1536]); inp("conv_b", [DEPTH, 1536])
    inp("dt_bias", [DEPTH, 32]); inp("a_log", [DEPTH, 32]); inp("d_ssd", [DEPTH, 16]); inp("g_ssd_norm", [DEPTH, D])
    inp("s5_lambda_re", [DEPTH, 2, 32, 64]); inp("s5_lambda_im", [DEPTH, 2, 32, 64]); inp("s5_log_step", [DEPTH, 2, 32])
    inp("s5_b_re", [DEPTH, 2, 32, 64, 16]); inp("s5_b_im", [DEPTH, 2, 32, 64, 16])
    inp("s5_c_re", [DEPTH, 2, 32, 16, 64]); inp("s5_c_im", [DEPTH, 2, 32, 16, 64])
    inp("s5_d", [DEPTH, 512]); inp("s5_w_glu", [DEPTH, 512, 512]); inp("s5_b_glu", [DEPTH, 512])
    inp("fnet_w", [DEPTH, 512, 512]); inp("fnet_b", [DEPTH, 512]); inp("w_out", [DEPTH, 2048, D])
    inp("cst", [128, 2304])
    k.I = I
    k.out = dram(nc, "out", [TL, D], F32, "ExternalOutput")
    def scr(name, shape, dt):
        return dram(nc, name, shape, dt, "ExternalOutput" if name in dbg else "Internal")
    k.XS = scr("XS", [T, D], F32)
    k.ZT = scr("ZT", [T, D], F32)
    k.PF = scr("PF", [1056, T], F32)
    k.PB = scr("PB", [2560, T], BF16)
    k.XC = scr("XC", [1536, T], BF16)
    k.MT = scr("MT", [2048, T], BF16)
    k.YF = scr("YF", [T, D], F32)
    k.MODS = scr("MODS", [DEPTH, 2, 3, D], F32)
    k.CL = scr("CL", [4096, 4096], BF16)
    k.HD = scr("HD", [2, 16, 2, 128, 544], BF16)
    k.SLN = scr("SLN", [4096, 4096], BF16)
    k.bufs = {}
    def B(name):
        if name not in k.bufs:
            k.bufs[name] = Buf(name)
        return k.bufs[name]
    k.B = B
    k.cst = nc.alloc_sbuf_tensor("cst_sb", [128, 2304], F32)
    k.identb = nc.alloc_sbuf_tensor("identb", [128, 128], BF16)
    s.dma("sp", k.cst[:], I["cst"][:, :], w=[B("cst")])
    s.op("dve", lambda: nc.vector.tensor_copy(out=k.identb[:], in_=k.cst[:, 0:128]), r=[B("cst")], w=[B("identb")])
    k.negpi = nc.alloc_sbuf_tensor("negpi", [128, 1], F32)
    s.op("dve", lambda: nc.vector.memset(k.negpi[:], -3.14159265), w=[B("negpi")])
    k.pidx_i = nc.alloc_sbuf_tensor("pidx_i", [128, 1], mybir.dt.int32)
    s.op("dve", lambda: nc.vector.tensor_copy(out=k.pidx_i[:], in_=k.cst[:, 139:140]), r=[B("cst")], w=[B("pidx_i")])
    prep_mods(k)
    gen_dft(k)
    for l in range(nlayers):
        k.sfx = f'_L{l}'
        phase_ab(k, l)
        if stop == "ab":
            break
        phase_conv(k, l)
        if stop == "conv":
            break
        phase_ssd(k, l, need_ctx=(l < DEPTH - 1))
        if stop == "ssd":
            break
        phase_s5(k, l, need_ctx=(l < DEPTH - 1))
        if stop in ("s5", "s5prep", "s5main"):
            break
        phase_fnet(k, l, need_ctx=(l < DEPTH - 1))
        if stop == "fnet":
            break
        if stop != "outonly":
            pass
        phase_out(k, l, last=(l == nlayers - 1 and nlayers == DEPTH))
        if stop == "out":
            break
    s.drain("sp")
    return nc


def prep_mods(k):
    nc, s, I, B = k.nc, k.s, k.I, k.B
    with ExitStack() as st:
        T_ = lambda name, shape, dt: st.enter_context(nc.sbuf_tensor(name + k.sfx, shape, dt))
        craw = T_("craw", [128, 8, 2], F32)
        sc = T_("sc", [128, 8, 2], F32)
        wm = [T_(f"wm{i}", [128, 3 * D], F32) for i in range(2)]
        rows = T_("mrows", [2, 3 * D], F32)
        gp = T_("gp", [2, 2, D], F32)
        res = T_("mres", [2, 3, D], F32)
        psM = [st.enter_context(nc.psum_tensor(f"psM{i}", [128, 512], F32)) for i in range(6)]
        s.dma("sp", craw[:, :, 0], I["c"].rearrange("(k p) -> p k", p=128), w=[B("craw")], allow_slow_non_contiguous=True)
        s.dma("sp", craw[:, :, 1], I["c_ctx"].rearrange("(k p) -> p k", p=128), w=[B("craw")], allow_slow_non_contiguous=True)
        s.op("act", lambda: nc.scalar.activation(out=sc[:], in_=craw[:], func=AF.Silu), r=[B("craw")], w=[B("sc")])
        for l in range(DEPTH):
            for kk in range(8):
                w_ = wm[kk % 2]; wb = B(f"wm{kk % 2}")
                s.dma("sp", w_[:], I["w_mod"][l, kk * 128:(kk + 1) * 128, :], w=[wb])
                for n in range(6):
                    s.op("pe", lambda n=n, w_=w_, kk=kk: nc.tensor.matmul(psM[n][0:2, :], lhsT=sc[:, kk, :], rhs=w_[:, n * 512:(n + 1) * 512],
                                                                start=(kk == 0), stop=(kk == 7)), r=[B("sc"), wb], w=[B(f"psM{n}")])
            s.dma("sp", rows[:], I["b_mod"][l:l + 1, :].partition_broadcast(2) if False else I["b_mod"][l, :].partition_broadcast(2), w=[B("mrows")])
            s.dma("sp", gp[:, 0, :], I["g_pre"][l, :].partition_broadcast(2), w=[B("gp")])
            s.dma("sp", gp[:, 1, :], I["g_post"][l, :].partition_broadcast(2), w=[B("gp")])
            for n in range(6):
                s.op("dve", lambda n=n: nc.vector.tensor_tensor(out=rows[:, n * 512:(n + 1) * 512], in0=rows[:, n * 512:(n + 1) * 512],
                                                            in1=psM[n][0:2, :], op=ALU.add), r=[B(f"psM{n}")], w=[B("mrows")])
            s.op("dve", lambda: nc.vector.tensor_copy(out=res[:, 0, :], in_=rows[:, 0:D]), r=[B("mrows")], w=[B("mres")])
            s.op("dve", lambda: nc.vector.scalar_tensor_tensor(out=res[:, 1, :], in0=rows[:, D:2 * D], scalar=1.0, in1=gp[:, 0, :],
                                                             op0=ALU.add, op1=ALU.mult), r=[B("mrows"), B("gp")], w=[B("mres")])
            s.op("dve", lambda: nc.vector.tensor_tensor(out=res[:, 2, :], in0=rows[:, 2 * D:3 * D], in1=gp[:, 1, :], op=ALU.mult),
                 r=[B("mrows"), B("gp")], w=[B("mres")])
            s.dma("sp", k.MODS[l], res[:], r=[B("mres")], w=[B("MODS")])
        s.barrier()


def fm_cols():
    lst = []
    for i in range(12):
        lst.append((1024 + i * 128, 128, "PB", i * 128, BF16))
    lst.append((2560, 32, "PF", 0, F32))
    for i in range(4):
        lst.append((2592 + i * 128, 128, "PB", 1536 + i * 128, BF16))
    for i in range(4):
        lst.append((3104 + i * 128, 128, "PF", 32 + i * 128, F32))
    for i in range(4):
        lst.append((3616 + i * 128, 128, "PB", 2048 + i * 128, BF16))
    for i in range(4):
        lst.append((4128 + i * 128, 128, "PF", 544 + i * 128, F32))
    return lst


def phase_ab(k, l):
    nc, s, I, B = k.nc, k.s, k.I, k.B
    with ExitStack() as st:
        T_ = lambda name, shape, dt: st.enter_context(nc.sbuf_tensor(name + k.sfx, shape, dt))
        P_ = lambda name, shape, dt: st.enter_context(nc.psum_tensor(name + k.sfx, shape, dt))
        wsb = T_("wsb", [128, 8, DIN], BF16)
        wst = [T_(f"wst{i}", [128, DIN], F32) for i in range(2)]
        bc = T_("bcab", [128, 4, D], F32)
        xt = [T_(f"xt{i}", [128, D], F32) for i in range(2)]
        junk = T_("junk", [128, D], BF16)
        h1 = T_("h1", [128, D], F32)
        hl = [T_(f"hl{i}", [128, D], BF16) for i in range(2)]
        hlT = [T_(f"hlT{i}", [128, 8, 512], BF16) for i in range(2)]
        st4 = T_("st4", [128, 8], F32)
        zt = [T_(f"zt{i}", [128, D], F32) for i in range(2)]
        ev32 = [T_(f"ev32_{i}", [128, 512], F32) for i in range(2)]
        ev16 = [T_(f"ev16_{i}", [128, 512], BF16) for i in range(2)]
        psT = P_("psT", [128, 1024], BF16)
        psZ = [P_(f"psZ{i}", [128, 512], F32) for i in range(2)]
        psF = [P_(f"psF{i}", [128, 512], F32) for i in range(3)]
        for kk in range(8):
            w_ = wst[kk % 2]; wb = B(f"wst{kk % 2}")
            s.dma("sp", w_[:], I["w_in"][l, kk * 128:(kk + 1) * 128, :], w=[wb])
            e = "act" if kk % 2 == 0 else "pool"
            if e == "act":
                s.op("act", lambda w_=w_, kk=kk: nc.scalar.copy(out=wsb[:, kk, :], in_=w_[:]), r=[wb], w=[B(f"wsb{kk}")])
            else:
                s.op("pool", lambda w_=w_, kk=kk: nc.gpsimd.tensor_copy(out=wsb[:, kk, :], in_=w_[:]), r=[wb], w=[B(f"wsb{kk}")])
        wsb_bufs = [B(f"wsb{kk}") for kk in range(8)]
        for j, (which, comp) in enumerate([(1, 0), (1, 1), (0, 0), (0, 1)]):
            s.dma("pool", bc[:, j, :], k.MODS[l, which, comp, :].partition_broadcast(128), r=[B("MODS")], w=[B("bcab")])
        cols = fm_cols()
        ngroups = 9
        ev_i = 0
        for g in range(ngroups):
            ntok = 512 if g < 8 else 256
            nt = ntok // 128
            hT = hlT[g % 2]; hTb = B(f"hlT{g % 2}")
            for tt in range(nt):
                ti = g * 4 + tt
                x_ = xt[ti % 2]; xb = B(f"xt{ti % 2}")
                if l == 0:
                    src = I["ctx"][ti * 128:(ti + 1) * 128, :] if ti < 2 else I["x"][(ti - 2) * 128:(ti - 1) * 128, :]
                    s.dma("sp", x_[:], src, w=[xb])
                else:
                    s.dma("sp", x_[:], k.XS[ti * 128:(ti + 1) * 128, :], r=[B("XS")], w=[xb])
                jb = 0 if ti < 2 else 2
                c0 = (ti % 4) * 2
                s.op("act", lambda x_=x_, c0=c0: nc.scalar.activation(out=junk[:], in_=x_[:], func=AF.Square, accum_out=st4[:, c0:c0 + 1]),
                     r=[xb], w=[B("junk"), B(f"st4_{ti % 4}")])
                s.op("dve", lambda c0=c0: nc.vector.tensor_scalar(out=st4[:, c0:c0 + 1], in0=st4[:, c0:c0 + 1], scalar1=1.0 / D, scalar2=EPS,
                                                              op0=ALU.mult, op1=ALU.add), w=[B(f"st4_{ti % 4}")])
                s.op("act", lambda c0=c0: nc.scalar.activation(out=st4[:, c0:c0 + 1], in_=st4[:, c0:c0 + 1], func=AF.Sqrt), w=[B(f"st4_{ti % 4}")])
                s.op("dve", lambda c0=c0: nc.vector.reciprocal(out=st4[:, c0 + 1:c0 + 2], in_=st4[:, c0:c0 + 1]), w=[B(f"st4_{ti % 4}")])
                s.op("dve", lambda x_=x_, c0=c0, jb=jb: nc.vector.scalar_tensor_tensor(out=h1[:], in0=x_[:], scalar=st4[:, c0 + 1:c0 + 2], in1=bc[:, jb + 1, :],
                                                                                 op0=ALU.mult, op1=ALU.mult), r=[xb, B(f"st4_{ti % 4}"), B("bcab")], w=[B("h1")])
                h_ = hl[ti % 2]; hb = B(f"hl{ti % 2}")
                s.op("dve", lambda h_=h_, jb=jb: nc.vector.tensor_tensor(out=h_[:], in0=h1[:], in1=bc[:, jb, :], op=ALU.add), r=[B("h1"), B("bcab")], w=[hb])
                for kk in range(8):
                    s.op("pe", lambda h_=h_, kk=kk: nc.tensor.transpose(out=psT[:, kk * 128:(kk + 1) * 128], in_=h_[:, kk * 128:(kk + 1) * 128], identity=k.identb[:]),
                         r=[hb, B("identb")], w=[B("psT")])
                s.op("act", lambda hT=hT, tt=tt: nc.scalar.copy(out=hT[:, :, tt * 128:(tt + 1) * 128], in_=psT[:].rearrange("p (k t) -> p k t", t=128)),
                     r=[B("psT")], w=[hTb])
                z_ = zt[ti % 2]; zb = B(f"zt{ti % 2}")
                for hh in range(2):
                    pz = psZ[hh]; pzb = B(f"psZ{hh}")
                    for kk in range(8):
                        s.op("pe", lambda pz=pz, hT=hT, kk=kk, tt=tt, hh=hh: nc.tensor.matmul(pz[:], lhsT=hT[:, kk, tt * 128:(tt + 1) * 128], rhs=wsb[:, kk, hh * 512:(hh + 1) * 512],
                                                                                        start=(kk == 0), stop=(kk == 7)), r=[hTb, wsb_bufs[kk]], w=[pzb])
                    if hh == 0:
                        s.op("act", lambda z_=z_, pz=pz: nc.scalar.copy(out=z_[:, 0:512], in_=pz[:]), r=[pzb], w=[zb])
                    else:
                        s.op("dve", lambda z_=z_, pz=pz: nc.vector.tensor_copy(out=z_[:, 512:1024], in_=pz[:]), r=[pzb], w=[zb])
                s.dma("pool", k.ZT[ti * 128:(ti + 1) * 128, :], z_[:], r=[zb], w=[B("ZT")])
            t0 = g * 512
            for ci, (c0, wd, dest, r0, dt_) in enumerate(cols):
                pf = psF[ci % 3]; pfb = B(f"psF{ci % 3}")
                for kk in range(8):
                    s.op("pe", lambda pf=pf, hT=hT, kk=kk, c0=c0, wd=wd, ntok=ntok: nc.tensor.matmul(pf[0:wd, 0:ntok], lhsT=wsb[:, kk, c0:c0 + wd], rhs=hT[:, kk, 0:ntok],
                                                                                             start=(kk == 0), stop=(kk == 7)), r=[hTb, wsb_bufs[kk]], w=[pfb])
                ev = (ev32 if dt_ == F32 else ev16)[ev_i % 2]
                evb = B(("ev32_" if dt_ == F32 else "ev16_") + str(ev_i % 2))
                if ev_i % 2 == 0:
                    s.op("act", lambda ev=ev, pf=pf, wd=wd, ntok=ntok: nc.scalar.copy(out=ev[0:wd, 0:ntok], in_=pf[0:wd, 0:ntok]), r=[pfb], w=[evb])
                else:
                    s.op("dve", lambda ev=ev, pf=pf, wd=wd, ntok=ntok: nc.vector.tensor_copy(out=ev[0:wd, 0:ntok], in_=pf[0:wd, 0:ntok]), r=[pfb], w=[evb])
                dst = getattr(k, dest)
                s.dma("pool" if ev_i % 2 == 0 else "sp", dst[r0:r0 + wd, t0:t0 + ntok], ev[0:wd, 0:ntok], r=[evb], w=[B(dest)])
                ev_i += 1
        s.barrier()


def _consts():
    c = np.zeros((128, 2304), np.float32)
    c[:, 0:128] = np.eye(128)
    c[:, 128:137] = np.arange(9)
    p = np.arange(128)
    c[:, 137] = (p % 32) < 16
    c[:, 138] = (p % 32) >= 16
    c[:, 139] = p
    c[:, 140:268] = 1.0
    jj, ii = np.meshgrid(p, p, indexing="ij")
    c[:, 268:396] = np.where(jj > ii, -30000.0, 0.0)
    c[:, 396:524] = np.where(jj < ii, -30000.0, 0.0)
    c[:, 524:532] = (p[:, None] // 16) == np.arange(8)[None]
    c[:, 1024:1568] = np.arange(544)[None]
    cn = np.arange(544)
    c[:, 1600:2144] = np.where(cn < 32, 31 - cn, 575 - cn)[None]
    return c


def kernel(**inputs):
    inp = {k_: np.asarray(v) for k_, v in inputs.items()}
    cst = _consts()
    shared = {}
    for name, v in inp.items():
        if name in ("x", "c", "ctx"):
            continue
        v = np.ascontiguousarray(v, dtype=np.float32)
        if name == "conv_w":
            v = np.ascontiguousarray(v.reshape(4, 9, 1536))
        elif name in ("dt_bias", "a_log"):
            v = np.ascontiguousarray(v.reshape(4, 32))
        shared[name] = v
    shared["cst"] = cst
    in_maps = []
    for core in range(8):
        b = core % 4
        m = dict(shared)
        m["x"] = np.ascontiguousarray(inp["x"][b], dtype=np.float32)
        m["c"] = np.ascontiguousarray(inp["c"][b], dtype=np.float32)
        m["ctx"] = np.ascontiguousarray(inp["ctx"][b], dtype=np.float32)
        in_maps.append(m)
    nc = build()
    res = run_bass_kernel_spmd(nc, in_maps, core_ids=list(range(8)))
    out = np.stack([np.asarray(res.results[b]["out"], dtype=np.float32) for b in range(4)], axis=0)
    return out
```

```python
import math
from contextlib import ExitStack
import numpy as np
import concourse.bass as bass
import concourse.mybir as mybir
from concourse.bass_utils import run_bass_kernel_spmd

F32 = mybir.dt.float32
BF16 = mybir.dt.bfloat16
AF = mybir.ActivationFunctionType
ALU = mybir.AluOpType
AX = mybir.AxisListType


class Buf:
    __slots__ = ("w", "r", "name")

    def __init__(self, name=""):
        self.w = None
        self.r = {}
        self.name = name


class Sched:
    NR = 8

    def __init__(self, nc):
        self.nc = nc
        self.eng = {"pe": nc.tensor, "act": nc.scalar, "dve": nc.vector,
                    "pool": nc.gpsimd, "sp": nc.sync}
        self.csem = {e: nc.alloc_semaphore("c_" + e) for e in ("pe", "act", "dve", "pool")}
        self.ccnt = {e: 0 for e in self.csem}
        self.dq = {}
        for q in ("sp", "act", "pool"):
            self.dq[q] = dict(sems=[nc.alloc_semaphore(f"d_{q}{i}") for i in range(self.NR)],
                              n=0, tk=[None] * self.NR)
        self.waited = {}
        self.ninstr = 0

    def _wait(self, e, tk):
        if tk is None:
            return
        key, sem, val = tk
        if key == "pe" and e == "pe":
            return
        if self.waited.get((e, key), 0) >= val:
            return
        self.eng[e].wait_ge(sem, val)
        self.waited[(e, key)] = val

    def _deps(self, e, r, w):
        for b in r:
            self._wait(e, b.w)
        for b in w:
            self._wait(e, b.w)
            for t in list(b.r.values()):
                self._wait(e, t)

    def _record(self, tk, r, w):
        for b in r:
            b.r[tk[0]] = tk
        for b in w:
            b.w = tk
            b.r = {}

    def op(self, e, fn, r=(), w=()):
        self._deps(e, r, w)
        ins = fn()
        self.ccnt[e] += 1
        ins.then_inc(self.csem[e], 1)
        tk = (e, self.csem[e], self.ccnt[e])
        self._record(tk, r, w)
        self.ninstr += 1
        return tk

    def dma(self, q, out, in_, r=(), w=(), **kw):
        d = self.dq[q]
        slot = d["n"] % self.NR
        self._wait(q, d["tk"][slot])
        self._deps(q, r, w)
        ins = self.eng[q].dma_start(out=out, in_=in_, **kw)
        ins.then_inc(d["sems"][slot], 16)
        val = 16 * (d["n"] // self.NR + 1)
        tk = (("d", q, slot), d["sems"][slot], val)
        d["tk"][slot] = tk
        d["n"] += 1
        self._record(tk, r, w)
        self.ninstr += 1
        return tk

    def op_cc(self, fn, r=(), w=()):
        q = "pool"
        d = self.dq[q]
        slot = d["n"] % self.NR
        self._wait(q, d["tk"][slot])
        self._deps(q, r, w)
        ins = fn()
        ins.then_inc(d["sems"][slot], 16)
        val = 16 * (d["n"] // self.NR + 1)
        tk = (("d", q, slot), d["sems"][slot], val)
        d["tk"][slot] = tk
        d["n"] += 1
        self._record(tk, r, w)
        return tk

    def all_tickets(self):
        tks = []
        for e in self.csem:
            if self.ccnt[e]:
                tks.append((e, self.csem[e], self.ccnt[e]))
        for q, d in self.dq.items():
            for t in d["tk"]:
                if t is not None:
                    tks.append(t)
        return tks

    def barrier(self, engines=("pe", "act", "dve", "pool", "sp")):
        tks = self.all_tickets()
        for e in engines:
            for t in tks:
                self._wait(e, t)

    def drain(self, e="sp"):
        for t in self.all_tickets():
            if t[0] == "pe" and e == "pe":
                continue
            self._wait(e, t)

T = 4352

def phase_conv(k, l):
    nc, s, I, B = k.nc, k.s, k.I, k.B
    with ExitStack() as st:
        T_ = lambda name, shape, dt: st.enter_context(nc.sbuf_tensor(name + k.sfx, shape, dt))
        P_ = lambda name, shape, dt: st.enter_context(nc.psum_tensor(name + k.sfx, shape, dt))
        cw9 = T_("cw9", [9, 1536], F32)
        cwT = T_("cwT", [128, 108], F32)
        cb = T_("cb", [128, 12], F32)
        dg = T_("dg", [128, 108, 128], BF16)
        xp = [T_(f"xp{i}", [128, 258 + 66 * 66], BF16) for i in range(2)]
        ev = [T_(f"cev{i}", [128, 512], BF16) for i in range(2)]
        psW = P_("psW", [128, 108], F32)
        psC = [P_(f"psC{i}", [128, 512], F32) for i in range(2)]
        s.dma("sp", cw9[:], I["conv_w"][l], w=[B("cw9")])
        s.dma("sp", cb[:], I["conv_b"][l].rearrange("(t p) -> p t", p=128), w=[B("cb")], allow_slow_non_contiguous=True)
        for t in range(12):
            s.op("pe", lambda t=t: nc.tensor.transpose(out=psW[:, t * 9:(t + 1) * 9], in_=cw9[:, t * 128:(t + 1) * 128], identity=k.cst[0:9, 0:9]),
                 r=[B("cw9"), B("cst")], w=[B("psW")])
        s.op("dve", lambda: nc.vector.tensor_copy(out=cwT[:], in_=psW[:]), r=[B("psW")], w=[B("cwT")])
        for j in range(108):
            e = "dve" if j % 2 == 0 else "pool"
            eng = nc.vector if e == "dve" else nc.gpsimd
            s.op(e, lambda j=j, eng=eng: eng.tensor_scalar(out=dg[:, j, :], in0=k.identb[:], scalar1=cwT[:, j:j + 1], scalar2=None, op0=ALU.mult),
                 r=[B("cwT"), B("identb")], w=[B(f"dg{j}")])
        for i in range(2):
            s.op("pool", lambda i=i: nc.gpsimd.memset(xp[i][:], 0.0), w=[B(f"xp{i}")])
        n = 0
        for t in range(12):
            x_ = xp[t % 2]; xb = B(f"xp{t % 2}")
            rows = k.PB[t * 128:(t + 1) * 128, :]
            s.dma("sp", x_[:, 1:257], rows[:, 0:256], r=[B("PB")], w=[xb])
            grid = x_[:, 258:258 + 4356].rearrange("p (r c) -> p r c", c=66)
            for hh in range(2):
                s.dma("sp", grid[:, 1 + hh * 32:33 + hh * 32, 1:65], rows[:, 256 + hh * 2048:256 + (hh + 1) * 2048].rearrange("p (r c) -> p r c", c=64), r=[B("PB")], w=[xb])
            for rg in range(9):
                pc = psC[n % 2]; pcb = B(f"psC{n % 2}")
                if rg < 8:
                    for tap in range(9):
                        ky, kx = tap // 3, tap % 3
                        rhs = grid[:, rg * 8 + ky:rg * 8 + ky + 8, kx:kx + 64]
                        s.op("pe", lambda pc=pc, t=t, tap=tap, rhs=rhs: nc.tensor.matmul(pc[:].rearrange("p (r c) -> p r c", c=64), lhsT=dg[:, t * 9 + tap, :], rhs=rhs,
                                                                              start=(tap == 0), stop=(tap == 8)), r=[xb, B(f"dg{t * 9 + tap}")], w=[pcb])
                    ntok = 512; t0 = 256 + rg * 512
                else:
                    for kx in range(3):
                        s.op("pe", lambda pc=pc, t=t, kx=kx: nc.tensor.matmul(pc[:, 0:256], lhsT=dg[:, t * 9 + 3 + kx, :], rhs=x_[:, kx:kx + 256],
                                                                    start=(kx == 0), stop=(kx == 2)), r=[xb, B(f"dg{t * 9 + 3 + kx}")], w=[pcb])
                    ntok = 256; t0 = 0
                e_ = ev[n % 2]; eb = B(f"cev{n % 2}")
                s.op("act", lambda e_=e_, pc=pc, t=t, ntok=ntok: nc.scalar.activation(out=e_[:, 0:ntok], in_=pc[:, 0:ntok], func=AF.Silu, bias=cb[:, t:t + 1]),
                     r=[pcb, B("cb")], w=[eb])
                s.dma("pool" if n % 2 == 0 else "sp", k.XC[t * 128:(t + 1) * 128, t0:t0 + ntok], e_[:, 0:ntok], r=[eb], w=[B("XC")])
                n += 1
        s.barrier()

T = 4352; NCH = 34; D = 1024; EPS = 1e-6
C_MF = 137; C_MB = 138; C_ONES = 140; C_NEGF = 268; C_NEGB = 396; C_BM8 = 524; C_IOTA = 1024

def phase_ssd(k, l, need_ctx=True):
    nc, s, I, B = k.nc, k.s, k.I, k.B
    cst = k.cst
    with ExitStack() as st:
        T_ = lambda name, shape, dt: st.enter_context(nc.sbuf_tensor(name + k.sfx, shape, dt))
        P_ = lambda name, shape, dt: st.enter_context(nc.psum_tensor(name + k.sfx, shape, dt))
        pst = ExitStack()
        TP_ = lambda name, shape, dt: pst.enter_context(nc.sbuf_tensor(name + k.sfx, shape, dt))
        acs = T_("s_acs", [128, T], F32)
        nacs = T_("s_nacs", [128, T], F32)
        Q = T_("s_Q", [128, T], F32)
        colp = T_("s_colp", [128, 4], F32)
        tot = T_("s_tot", [128, NCH], F32)
        DT = T_("s_DT", [32, NCH, 32], F32)
        cdall = T_("s_cd", [128, NCH, 32], F32)
        Esel2 = T_("s_Esel2", [64, 32, 128], BF16)
        hl = T_("s_hl", [64, T], BF16)
        nhl = T_("s_nhl", [64, T], BF16)
        negm = T_("s_negm", [128, 2, 512], BF16)
        dskc = T_("s_dskc", [128, 8], F32)
        dgd = T_("s_dgd", [128, 8, 128], BF16)
        gnb = T_("s_gnb", [128, D], F32)
        dt_ = TP_("s_dt", [128, T], F32)
        a_ = TP_("s_a", [128, T], F32)
        cum = TP_("s_cum", [128, T], F32)
        psCB = P_("psCB", [128, 256], F32)
        psX = P_("psX", [128, 1024], BF16)
        psB = P_("psB", [128, 256], BF16)
        psE = P_("psE", [128, 512], F32)
        psY = [P_(f"psY{i}", [128, 512], F32) for i in range(2)]
        psS1 = P_("psSst", [128, 512], F32)
        psO1 = P_("psOst", [128, 512], F32)

        V = lambda fn, r=(), w=(): s.op("dve", fn, r=r, w=w)
        A = lambda fn, r=(), w=(): s.op("act", fn, r=r, w=w)
        G = lambda fn, r=(), w=(): s.op("pool", fn, r=r, w=w)
        PE = lambda fn, r=(), w=(): s.op("pe", fn, r=r, w=w)
        for q in range(4):
            s.dma("sp", dt_[32 * q:32 * q + 32, :], k.PF[0:32, :], r=[B("PF")], w=[B("s_dt")])
            s.dma("sp", colp[32 * q:32 * q + 32, 0:1], I["dt_bias"][l].rearrange("(p o) -> p o", o=1), w=[B("s_colp")])
            s.dma("sp", colp[32 * q:32 * q + 32, 1:2], I["a_log"][l].rearrange("(p o) -> p o", o=1), w=[B("s_colp")])
        for t in range(8):
            for hh in range(2):
                s.dma("sp", dskc[64 * hh:64 * hh + 64, t:t + 1], I["d_ssd"][l, 2 * t + hh:2 * t + hh + 1].partition_broadcast(64), w=[B("s_dskc")])
        s.dma("sp", gnb[:], I["g_ssd_norm"][l].partition_broadcast(128), w=[B("s_gnb")])
        A(lambda: nc.scalar.activation(out=colp[:, 2:3], in_=colp[:, 1:2], func=AF.Exp), w=[B("s_colp")])
        V(lambda: nc.vector.tensor_scalar(out=colp[:, 3:4], in0=colp[:, 2:3], scalar1=-1.0, scalar2=None, op0=ALU.mult), w=[B("s_colp")])
        A(lambda: nc.scalar.activation(out=dt_[:], in_=dt_[:], func=AF.Exp, bias=colp[:, 0:1]), r=[B("s_colp")], w=[B("s_dt")])
        A(lambda: nc.scalar.activation(out=dt_[:], in_=dt_[:], func=AF.Ln, bias=1.0), w=[B("s_dt")])
        V(lambda: nc.vector.tensor_scalar(out=a_[:], in0=dt_[:], scalar1=colp[:, 3:4], scalar2=None, op0=ALU.mult), r=[B("s_dt"), B("s_colp")], w=[B("s_a")])
        for c in range(NCH):
            V(lambda c=c: nc.vector.tensor_tensor_scan(out=cum[:, c * 128:(c + 1) * 128], data0=cst[:, C_ONES:C_ONES + 128], data1=a_[:, c * 128:(c + 1) * 128],
                                                   initial=0.0, op0=ALU.mult, op1=ALU.add), r=[B("s_a"), B("cst")], w=[B("s_cum")])
        cum3 = cum[:].rearrange("p (c i) -> p c i", i=128)
        V(lambda: nc.vector.tensor_copy(out=tot[:], in_=cum3[:, :, 127]), r=[B("s_cum")], w=[B("s_tot")])
        totb = tot[:].unsqueeze(2).to_broadcast([128, NCH, 128])
        V(lambda: nc.vector.tensor_tensor(out=nacs[:], in0=a_[:], in1=cum[:], op=ALU.subtract), r=[B("s_a"), B("s_cum")], w=[B("s_nacs")])
        V(lambda: nc.vector.tensor_tensor(out=nacs[:].rearrange("p (c i) -> p c i", i=128), in0=nacs[:].rearrange("p (c i) -> p c i", i=128), in1=totb, op=ALU.add),
          r=[B("s_tot")], w=[B("s_nacs")])
        V(lambda: nc.vector.tensor_scalar(out=acs[:], in0=cum[:], scalar1=cst[:, C_MF:C_MF + 1], scalar2=None, op0=ALU.mult), r=[B("s_cum"), B("cst")], w=[B("s_acs")])
        V(lambda: nc.vector.scalar_tensor_tensor(out=acs[:], in0=nacs[:], scalar=cst[:, C_MB:C_MB + 1], in1=acs[:], op0=ALU.mult, op1=ALU.add), r=[B("s_nacs")], w=[B("s_acs")])
        V(lambda: nc.vector.tensor_scalar(out=nacs[:], in0=acs[:], scalar1=-1.0, scalar2=None, op0=ALU.mult), r=[B("s_acs")], w=[B("s_nacs")])
        V(lambda: nc.vector.tensor_tensor(out=a_[:].rearrange("p (c i) -> p c i", i=128), in0=nacs[:].rearrange("p (c i) -> p c i", i=128), in1=totb, op=ALU.add),
          r=[B("s_nacs"), B("s_tot")], w=[B("s_a")])
        A(lambda: nc.scalar.activation(out=a_[:], in_=a_[:], func=AF.Exp), w=[B("s_a")])
        V(lambda: nc.vector.tensor_tensor(out=a_[:], in0=a_[:], in1=dt_[:], op=ALU.mult), r=[B("s_dt")], w=[B("s_a")])
        A(lambda: nc.scalar.activation(out=cum[:], in_=acs[:], func=AF.Exp), r=[B("s_acs")], w=[B("s_cum")])
        G(lambda: nc.gpsimd.tensor_copy(out=Q[0:32, :], in_=dt_[0:32, :]), r=[B("s_dt")], w=[B("s_Q")])
        G(lambda: nc.gpsimd.tensor_copy(out=Q[32:64, :], in_=a_[32:64, :]), r=[B("s_a")], w=[B("s_Q")])
        G(lambda: nc.gpsimd.tensor_copy(out=Q[64:96, :], in_=cum[64:96, :]), r=[B("s_cum")], w=[B("s_Q")])
        G(lambda: nc.gpsimd.tensor_copy(out=Q[96:128, :], in_=nacs[96:128, :]), r=[B("s_nacs")], w=[B("s_Q")])
        V(lambda: nc.vector.tensor_copy(out=Esel2[0:32], in_=cst[0:32, 0:32].unsqueeze(2).to_broadcast([32, 32, 128])), r=[B("cst")], w=[B("s_Esel")])
        V(lambda: nc.vector.tensor_copy(out=Esel2[32:64], in_=cst[32:64, 32:64].unsqueeze(2).to_broadcast([32, 32, 128])), r=[B("cst")], w=[B("s_Esel")])
        V(lambda: nc.vector.tensor_copy(out=hl[0:32, :], in_=acs[0:32, :]), r=[B("s_acs")], w=[B("s_hl")])
        V(lambda: nc.vector.tensor_copy(out=nhl[32:64, :], in_=acs[32:64, :]), r=[B("s_acs")], w=[B("s_nhl")])
        V(lambda: nc.vector.tensor_tensor(out=hl[32:64, :], in0=acs[32:64, :], in1=nhl[32:64, :], op=ALU.subtract), r=[B("s_acs"), B("s_nhl")], w=[B("s_hl")])
        V(lambda: nc.vector.tensor_scalar(out=nhl[:], in0=hl[:], scalar1=-1.0, scalar2=None, op0=ALU.mult), r=[B("s_hl")], w=[B("s_nhl")])
        V(lambda: nc.vector.tensor_copy(out=negm[:, 0, :].rearrange("p (a i) -> p a i", i=128), in_=cst[:, C_NEGF:C_NEGF + 128].unsqueeze(1).to_broadcast([128, 4, 128])), r=[B("cst")], w=[B("s_negm")])
        V(lambda: nc.vector.tensor_copy(out=negm[:, 1, :].rearrange("p (a i) -> p a i", i=128), in_=cst[:, C_NEGB:C_NEGB + 128].unsqueeze(1).to_broadcast([128, 4, 128])), r=[B("cst")], w=[B("s_negm")])
        V(lambda: nc.vector.tensor_tensor(out=DT[:], in0=cst[0:32, 0:32].unsqueeze(1).to_broadcast([32, NCH, 32]), in1=tot[0:32, :].unsqueeze(2).to_broadcast([32, NCH, 32]), op=ALU.mult),
          r=[B("s_tot"), B("cst")], w=[B("s_DT")])
        for c0 in range(0, NCH, 16):
            n = min(16, NCH - c0)
            PE(lambda c0=c0, n=n: nc.tensor.matmul(psE[:, 0:n * 32], lhsT=cst[0:32, C_ONES:C_ONES + 128], rhs=DT[:, c0:c0 + n, :].rearrange("p c k -> p (c k)"), start=True, stop=True),
               r=[B("s_DT"), B("cst")], w=[B("psE")])
            A(lambda c0=c0, n=n: nc.scalar.activation(out=cdall[:, c0:c0 + n, :].rearrange("p c k -> p (c k)"), in_=psE[:, 0:n * 32], func=AF.Exp), r=[], w=[B("psE"), B("s_cd")])
        for t in range(8):
            V(lambda t=t: nc.vector.tensor_scalar(out=dgd[:, t, :], in0=k.identb[:], scalar1=dskc[:, t:t + 1], scalar2=None, op0=ALU.mult), r=[B("s_dskc"), B("identb")], w=[B("s_dgd")])
        s.barrier()
        pst.close()
        hst = [T_(f"s_h{d}", [128, D], F32) for d in range(2)]
        hbf = [T_(f"s_hb{d}", [128, D], BF16) for d in range(2)]
        xT = [T_(f"s_xT{i}", [128, 8, 128], BF16) for i in range(2)]
        BC = [T_(f"s_BC{i}", [128, 4, 128], BF16) for i in range(2)]
        tmq = [T_(f"s_tmq{i}", [128, 128], F32) for i in range(2)]
        xdt = [T_(f"s_xdt{i}", [128, D], BF16) for i in range(2)]
        xw = [T_(f"s_xw{i}", [128, D], BF16) for i in range(2)]
        Btok = [T_(f"s_Btok{i}", [128, 256], BF16) for i in range(2)]
        dec = [T_(f"s_dec{i}", [128, 512], F32) for i in range(2)]
        MTt = [T_(f"s_MT{i}", [128, 16, 128], BF16) for i in range(2)]
        ydg = [T_(f"s_ydg{i}", [128, D], F32) for i in range(2)]
        Sc = [T_(f"s_Sc{i}", [128, D], F32) for i in range(2)]
        tmp = T_("s_tmp", [128, 512], F32)
        yt = [T_(f"s_y{i}", [128, D], F32) for i in range(2)]
        yf = [T_(f"s_yf{i}", [128, D], F32) for i in range(2)]
        zt = [T_(f"s_z{i}", [128, D], F32) for i in range(2)]
        sz = T_("s_sz", [128, D], F32)
        junk = T_("s_junk", [128, 512], BF16)
        st2 = T_("s_st2", [128, 4], F32)
        mo = T_("s_mo", [128, D], BF16)
        mT = [T_(f"s_mT{i}", [128, 8, 128], BF16) for i in range(2)]
        def stageA(d, n_, c):
            sl = n_ % 2
            t0 = c * 128
            x_ = xT[sl]; xb = B(f"s_xT{sl}"); bc_ = BC[sl]; bcb = B(f"s_BC{sl}")
            tq = tmq[sl]; tqb = B(f"s_tmq{sl}")
            s.dma("sp", x_[:], k.XC[0:1024, t0:t0 + 128].rearrange("(t p) j -> p t j", p=128), r=[B("XC")], w=[xb])
            s.dma("sp", bc_[:], k.XC[1024:1536, t0:t0 + 128].rearrange("(t p) j -> p t j", p=128), r=[B("XC")], w=[bcb])
            if d == 1:
                s.dma("sp", zt[sl][:], k.ZT[t0:t0 + 128, :], r=[B("ZT")], w=[B(f"s_z{sl}")])
                s.dma("sp", yf[sl][:], k.YF[t0:t0 + 128, :], r=[B("YF")], w=[B(f"s_yf{sl}")])
            PE(lambda: nc.tensor.transpose(out=psE[:, 0:128], in_=Q[:, t0:t0 + 128], identity=cst[:, 0:128]), r=[B("s_Q"), B("cst")], w=[B("psE")])
            A(lambda: nc.scalar.copy(out=tq[:], in_=psE[:, 0:128]), w=[B("psE"), tqb])
            for t in range(8):
                PE(lambda t=t: nc.tensor.transpose(out=psX[:, t * 128:(t + 1) * 128], in_=x_[:, t, :], identity=k.identb[:]), r=[xb, B("identb")], w=[B("psX")])
            for g in range(2):
                PE(lambda g=g: nc.tensor.transpose(out=psB[:, g * 128:(g + 1) * 128], in_=bc_[:, g, :], identity=k.identb[:]), r=[bcb, B("identb")], w=[B("psB")])
                PE(lambda g=g: nc.tensor.matmul(psCB[:, g * 128:(g + 1) * 128], lhsT=bc_[:, g, :], rhs=bc_[:, 2 + g, :], start=True, stop=True), r=[bcb], w=[B("psCB")])
            A(lambda: nc.scalar.copy(out=Btok[sl][:], in_=psB[:]), w=[B("psB"), B(f"s_Btok{sl}")])
            yield
            psX3 = psX[:].rearrange("p (h e) -> p h e", e=64)
            V(lambda: nc.vector.tensor_tensor(out=xdt[sl][:].rearrange("p (h e) -> p h e", e=64), in0=psX3, in1=tq[:, d * 16:d * 16 + 16].unsqueeze(2).to_broadcast([128, 16, 64]), op=ALU.mult),
              r=[tqb], w=[B("psX"), B(f"s_xdt{sl}")])
            V(lambda: nc.vector.tensor_tensor(out=xw[sl][:].rearrange("p (h e) -> p h e", e=64), in0=psX3, in1=tq[:, 32 + d * 16:48 + d * 16].unsqueeze(2).to_broadcast([128, 16, 64]), op=ALU.mult),
              r=[tqb], w=[B("psX"), B(f"s_xw{sl}")])
            yield
            for hq in range(4):
                g = hq // 2
                PE(lambda: nc.tensor.matmul(psE[:], lhsT=k.identb[:], rhs=negm[:, d, :], start=True, stop=False, skip_group_check=True), r=[B("s_negm"), B("identb")], w=[B("psE")])
                for hh in range(4):
                    dh = d * 16 + hq * 4 + hh
                    o_ = psE[:, hh * 128:(hh + 1) * 128]
                    PE(lambda o_=o_, dh=dh: nc.tensor.matmul(o_, lhsT=Esel2[:, dh, :], rhs=hl[:, t0:t0 + 128], start=False, stop=False, skip_group_check=True), r=[B("s_Esel"), B("s_hl")], w=[B("psE")])
                    PE(lambda o_=o_, dh=dh, hh=hh: nc.tensor.matmul(o_, lhsT=nhl[:, t0:t0 + 128], rhs=Esel2[:, dh, :], start=False, stop=(hh == 3), skip_group_check=True), r=[B("s_Esel"), B("s_nhl")], w=[B("psE")])
                dc = dec[hq % 2]; dcb = B(f"s_dec{hq % 2}")
                A(lambda dc=dc: nc.scalar.activation(out=dc[:], in_=psE[:], func=AF.Exp), w=[B("psE"), dcb])
                V(lambda dc=dc, hq=hq, g=g: nc.vector.tensor_tensor(out=MTt[sl][:, hq * 4:hq * 4 + 4, :], in0=dc[:].rearrange("p (a i) -> p a i", i=128),
                                                               in1=psCB[:, g * 128:(g + 1) * 128].unsqueeze(1).to_broadcast([128, 4, 128]), op=ALU.mult),
                  r=[dcb], w=[B("psCB"), B(f"s_MT{sl}_{hq}")])
                yield
            if d == 0:
                for t in range(8):
                    py = psY[t // 4]
                    PE(lambda t=t, py=py: nc.tensor.matmul(py[:, (t % 4) * 128:(t % 4) * 128 + 128], lhsT=x_[:, t, :], rhs=dgd[:, t, :], start=(t % 4 == 0), stop=False, skip_group_check=True),
                       r=[xb, B("s_dgd")], w=[B(f"psY{t // 4}")])
            for h in range(16):
                py = psY[h // 8]
                PE(lambda h=h, py=py: nc.tensor.matmul(py[:, (h % 8) * 64:(h % 8) * 64 + 64], lhsT=MTt[sl][:, h, :], rhs=xdt[sl][:, h * 64:(h + 1) * 64], start=(d == 1 and h % 8 == 0), stop=(h % 8 == 7), skip_group_check=True),
                   r=[B(f"s_MT{sl}_{h // 4}"), B(f"s_xdt{sl}")], w=[B(f"psY{h // 8}")])
            yield
            A(lambda: nc.scalar.copy(out=ydg[sl][:, 0:512], in_=psY[0][:]), w=[B("psY0"), B(f"s_ydg{sl}")])
            V(lambda: nc.vector.tensor_copy(out=ydg[sl][:, 512:1024], in_=psY[1][:]), w=[B("psY1"), B(f"s_ydg{sl}")])
            if d == 1:
                G(lambda: nc.gpsimd.tensor_tensor(out=ydg[sl][:], in0=ydg[sl][:], in1=yf[sl][:], op=ALU.add), r=[B(f"s_yf{sl}")], w=[B(f"s_ydg{sl}")])
            yield
            for g in range(2):
                PE(lambda g=g: nc.tensor.matmul(psS1[:], lhsT=Btok[sl][:, g * 128:(g + 1) * 128], rhs=xw[sl][:, g * 512:(g + 1) * 512], start=True, stop=True),
                   r=[B(f"s_Btok{sl}"), B(f"s_xw{sl}")], w=[B("psSst")])
                if g == 0:
                    A(lambda: nc.scalar.copy(out=Sc[sl][:, 0:512], in_=psS1[:]), w=[B("psSst"), B(f"s_Sc{sl}")])
                else:
                    V(lambda: nc.vector.tensor_copy(out=Sc[sl][:, 512:1024], in_=psS1[:]), w=[B("psSst"), B(f"s_Sc{sl}")])
                yield

        def stageB(d, n_, c):
            sl = n_ % 2
            t0 = c * 128
            hb_ = B(f"s_h{d}"); hbb = B(f"s_hb{d}")
            bc_ = BC[sl]; bcb = B(f"s_BC{sl}")
            tq = tmq[sl]; tqb = B(f"s_tmq{sl}")
            y_ = yt[sl]; yb = B(f"s_y{sl}")
            for g in range(2):
                PE(lambda g=g: nc.tensor.matmul(psO1[:], lhsT=bc_[:, 2 + g, :], rhs=hbf[d][:, g * 512:(g + 1) * 512], start=True, stop=True), r=[bcb, hbb], w=[B("psOst")])
                V(lambda g=g: nc.vector.tensor_tensor(out=tmp[:].rearrange("p (h e) -> p h e", e=64), in0=psO1[:].rearrange("p (h e) -> p h e", e=64),
                                                   in1=tq[:, 64 + d * 16 + g * 8:64 + d * 16 + g * 8 + 8].unsqueeze(2).to_broadcast([128, 8, 64]), op=ALU.mult),
                  r=[tqb], w=[B("psOst"), B("s_tmp")])
                G(lambda g=g: nc.gpsimd.tensor_tensor(out=y_[:, g * 512:(g + 1) * 512], in0=tmp[:], in1=ydg[sl][:, g * 512:(g + 1) * 512], op=ALU.add), r=[B("s_tmp"), B(f"s_ydg{sl}")], w=[yb])
                yield
            G(lambda: nc.gpsimd.tensor_tensor(out=hst[d][:].rearrange("p (h e) -> p h e", e=64), in0=hst[d][:].rearrange("p (h e) -> p h e", e=64),
                                               in1=cdall[:, c, d * 16:d * 16 + 16].unsqueeze(2).to_broadcast([128, 16, 64]), op=ALU.mult), r=[B("s_cd")], w=[hb_])
            G(lambda: nc.gpsimd.tensor_tensor(out=hst[d][:], in0=hst[d][:], in1=Sc[sl][:], op=ALU.add), r=[B(f"s_Sc{sl}")], w=[hb_])
            A(lambda: nc.scalar.copy(out=hbf[d][:], in_=hst[d][:]), r=[hb_], w=[hbb])
            yield
            if d == 0:
                s.dma("pool", k.YF[t0:t0 + 128, :], y_[:], r=[yb], w=[B("YF")])
            elif need_ctx or c >= 2:
                z_ = zt[sl]; zb = B(f"s_z{sl}")
                A(lambda: nc.scalar.activation(out=sz[:], in_=z_[:], func=AF.Silu), r=[zb], w=[B("s_sz")])
                V(lambda: nc.vector.tensor_tensor(out=y_[:], in0=y_[:], in1=sz[:], op=ALU.mult), r=[B("s_sz")], w=[yb])
                for g in range(2):
                    A(lambda g=g: nc.scalar.activation(out=junk[:], in_=y_[:, g * 512:(g + 1) * 512], func=AF.Square, accum_out=st2[:, g:g + 1]), r=[yb], w=[B("s_junk"), B("s_st2")])
                yield
                V(lambda: nc.vector.tensor_scalar(out=st2[:, 0:2], in0=st2[:, 0:2], scalar1=1.0 / 512, scalar2=EPS, op0=ALU.mult, op1=ALU.add), w=[B("s_st2")])
                A(lambda: nc.scalar.activation(out=st2[:, 0:2], in_=st2[:, 0:2], func=AF.Sqrt), w=[B("s_st2")])
                V(lambda: nc.vector.reciprocal(out=st2[:, 2:4], in_=st2[:, 0:2]), w=[B("s_st2")])
                for g in range(2):
                    V(lambda g=g: nc.vector.scalar_tensor_tensor(out=mo[:, g * 512:(g + 1) * 512], in0=y_[:, g * 512:(g + 1) * 512], scalar=st2[:, 2 + g:3 + g], in1=gnb[:, g * 512:(g + 1) * 512],
                                                               op0=ALU.mult, op1=ALU.mult), r=[yb, B("s_st2"), B("s_gnb")], w=[B("s_mo")])
                yield
                for t in range(8):
                    PE(lambda t=t: nc.tensor.transpose(out=psX[:, t * 128:(t + 1) * 128], in_=mo[:, t * 128:(t + 1) * 128], identity=k.identb[:]), r=[B("s_mo"), B("identb")], w=[B("psX")])
                m_ = mT[sl]; mb_ = B(f"s_mT{sl}")
                A(lambda: nc.scalar.copy(out=m_[:], in_=psX[:].rearrange("p (t j) -> p t j", j=128)), w=[B("psX"), mb_])
                s.dma("pool", k.MT[0:1024, t0:t0 + 128].rearrange("(t p) j -> p t j", p=128), m_[:], r=[mb_], w=[B("MT")])
            yield

        def interleave(gens):
            gens = [g for g in gens if g is not None]
            while gens:
                for g in list(gens):
                    try:
                        next(g)
                    except StopIteration:
                        gens.remove(g)

        for d in range(2):
            hb_ = B(f"s_h{d}"); hbb = B(f"s_hb{d}")
            V(lambda d=d: nc.vector.memset(hst[d][:], 0.0), w=[hb_])
            V(lambda d=d: nc.vector.memset(hbf[d][:], 0.0), w=[hbb])
            order = list(range(NCH)) if d == 0 else [1, 0] + list(range(NCH - 1, 1, -1))
            interleave([stageA(d, 0, order[0])])
            for n_ in range(len(order)):
                ga = stageA(d, n_ + 1, order[n_ + 1]) if n_ + 1 < len(order) else None
                interleave([ga, stageB(d, n_, order[n_])])
        s.barrier()

T = 4352; D = 1024; EPS = 1e-6; DEPTH = 4
C_PIDX = 139
I32 = mybir.dt.int32

def gen_dft(k):
    nc, s, B = k.nc, k.s, k.B
    cst = k.cst
    with ExitStack() as st:
        T_ = lambda name, shape, dt: st.enter_context(nc.sbuf_tensor(name + k.sfx, shape, dt))
        kio_i = T_("kio_i", [128, 4096], I32)
        kio = T_("kio", [128, 4096], F32)
        lcol = T_("lcol", [128, 32], F32)
        pi_ = [T_(f"pi{i}", [128, 4096], I32) for i in range(2)]
        tb = [T_(f"tb{i}", [128, 4096], BF16) for i in range(4)]
        s.op("pool", lambda: nc.gpsimd.iota(kio_i[:], pattern=[[1, 4096]], base=0, channel_multiplier=0), w=[B("kio_i")])
        s.op("dve", lambda: nc.vector.tensor_copy(out=kio[:], in_=kio_i[:]), r=[B("kio_i")], w=[B("kio")])
        for lt in range(32):
            s.op("dve", lambda lt=lt: nc.vector.tensor_scalar(out=lcol[:, lt:lt + 1], in0=cst[:, C_PIDX:C_PIDX + 1], scalar1=float(lt * 128), scalar2=None, op0=ALU.add), r=[B("cst")], w=[B("lcol")])
        sc = 2.0 * math.pi / 4096.0
        for lt in range(32):
            for j, (off, dst) in enumerate([(0.0, k.SLN), (3072.0, k.CL)]):
                e = "dve" if j == 0 else "pool"
                eng = nc.vector if j == 0 else nc.gpsimd
                p_ = pi_[j]; pb = B(f"pi{j}")
                s.op(e, lambda eng=eng, p_=p_, lt=lt, off=off: eng.tensor_scalar(out=p_[:], in0=kio[:], scalar1=lcol[:, lt:lt + 1], scalar2=off, op0=ALU.mult, op1=ALU.add), r=[B("kio"), B("lcol")], w=[pb])
                s.op("dve", lambda p_=p_: nc.vector.tensor_single_scalar(out=p_[:], in_=p_[:], scalar=4095, op=ALU.bitwise_and), w=[pb])
                t_ = tb[(lt % 2) * 2 + j]; tbb = B(f"tb{(lt % 2) * 2 + j}")
                s.op("act", lambda t_=t_, p_=p_: nc.scalar.activation(out=t_[:], in_=p_[:], func=AF.Sin, scale=sc, bias=k.negpi[:, 0:1]), r=[pb, B("negpi")], w=[tbb])
                s.dma("sp", dst[lt * 128:(lt + 1) * 128, :], t_[:], r=[tbb], w=[B("DFT")])
        s.barrier()


def phase_fnet(k, l, need_ctx=True):
    nc, s, I, B = k.nc, k.s, k.I, k.B
    with ExitStack() as st:
        T_ = lambda name, shape, dt: st.enter_context(nc.sbuf_tensor(name + k.sfx, shape, dt))
        P_ = lambda name, shape, dt: st.enter_context(nc.psum_tensor(name + k.sfx, shape, dt))
        cs = T_("f_cs", [128, 256], BF16)
        fw32 = T_("f_fw32", [128, 4, 512], F32)
        fw = T_("f_fw", [128, 4, 512], BF16)
        fb = T_("f_fb", [128, 4], F32)
        fuT = [T_(f"f_fuT{i}", [128, 4, 128], BF16) for i in range(2)]
        PQ = T_("f_PQ", [128, 34, 4, 2, 128], BF16)
        tabs = [T_(f"f_tab{i}", [128, 2, 512], BF16) for i in range(4)]
        specT = [T_(f"f_spec{i}", [128, 4, 512], BF16) for i in range(2)]
        gt = [T_(f"f_g{i}", [128, 512], F32) for i in range(2)]
        ob = [T_(f"f_ob{i}", [128, 512], BF16) for i in range(2)]
        psPQ = [P_(f"psPQ{i}", [128, 512], F32) for i in range(2)]
        psS = [P_(f"psS{i}", [128, 512], F32) for i in range(4)]
        psM = [P_(f"psMx{i}", [128, 512], F32) for i in range(2)]
        rows32 = lambda tsr: bass.AP(tsr.tensor, tsr.offset, [[32 * 4096, 128], [1, 128]])
        s.dma("sp", cs[:, 0:128], rows32(k.CL), r=[B("DFT")], w=[B("f_cs")])
        s.dma("sp", cs[:, 128:256], rows32(k.SLN), r=[B("DFT")], w=[B("f_cs")])
        s.dma("sp", fw32[:], I["fnet_w"][l].rearrange("(t p) c -> p t c", p=128), w=[B("f_fw32")])
        s.dma("sp", fb[:], I["fnet_b"][l].rearrange("(t p) -> p t", p=128), w=[B("f_fb")], allow_slow_non_contiguous=True)
        s.op("dve", lambda: nc.vector.tensor_copy(out=fw[:], in_=fw32[:]), r=[B("f_fw32")], w=[B("f_fw")])
        for tt in range(34):
            if tt < 2 and not need_ctx:
                continue
            f_ = fuT[tt % 2]; fb_ = B(f"f_fuT{tt % 2}")
            s.dma("sp", f_[:], k.PB[2048:2560, tt * 128:(tt + 1) * 128].rearrange("(h p) j -> p h j", p=128), r=[B("PB")], w=[fb_])
            for hd in range(4):
                pp = psPQ[hd // 2]; ppb = B(f"psPQ{hd // 2}")
                s.op("pe", lambda pp=pp, hd=hd, f_=f_: nc.tensor.matmul(pp[:, (hd % 2) * 256:(hd % 2) * 256 + 256], lhsT=f_[:, hd, :], rhs=cs[:], start=True, stop=True), r=[fb_, B("f_cs")], w=[ppb])
            for hf in range(2):
                pp = psPQ[hf]; ppb = B(f"psPQ{hf}")
                src = pp[:].rearrange("p (h q m) -> p h q m", h=2, q=2)
                s.op("act", lambda tt=tt, hf=hf, src=src: nc.scalar.copy(out=PQ[:, tt, hf * 2:hf * 2 + 2, 0, :], in_=src[:, :, 0, :]), w=[ppb, B(f"f_PQ{tt}")])
                s.op("dve", lambda tt=tt, hf=hf, src=src: nc.vector.tensor_scalar(out=PQ[:, tt, hf * 2:hf * 2 + 2, 1, :], in0=src[:, :, 1, :], scalar1=-1.0, scalar2=None, op0=ALU.mult), w=[ppb, B(f"f_PQ{tt}")])
        nload = [0]
        nsp = [0]
        bsb = [T_(f"f_bsb{i}", [128, 256], F32) for i in range(2)]
        tab4 = [T_(f"f_tab4_{i}", [128, 2, 4, 256], BF16) for i in range(3)]
        alt = T_("f_alt", [128, 2], F32)
        altb = T_("f_altb", [128, 1], BF16)
        s.op("dve", lambda: nc.vector.tensor_single_scalar(out=alt[:, 0:1].bitcast(I32), in_=k.pidx_i[:, 0:1], scalar=1, op=ALU.bitwise_and), r=[B("pidx_i")], w=[B("f_alt")])
        s.op("dve", lambda: nc.vector.tensor_copy(out=alt[:, 1:2], in_=alt[:, 0:1].bitcast(I32)), w=[B("f_alt")])
        s.op("dve", lambda: nc.vector.tensor_scalar(out=altb[:], in0=alt[:, 1:2], scalar1=-2.0, scalar2=1.0, op0=ALU.mult, op1=ALU.add), r=[B("f_alt")], w=[B("f_altb")])

        def mix_group(sp_, spb, c0, ncol, tok0):
            for ct in range(4):
                pm = psM[ct % 2]; pmb = B(f"psMx{ct % 2}")
                g_ = gt[ct % 2]; gb = B(f"f_g{ct % 2}")
                kw = dict(allow_slow_non_contiguous=True) if ncol == 1 else {}
                s.dma("sp", g_[:, 0:ncol], k.PF[544 + ct * 128:544 + (ct + 1) * 128, tok0:tok0 + ncol], r=[B("PF")], w=[gb], **kw)
                for hd in range(4):
                    s.op("pe", lambda pm=pm, hd=hd, ct=ct: nc.tensor.matmul(pm[:, 0:ncol], lhsT=fw[:, hd, ct * 128:(ct + 1) * 128], rhs=sp_[:, hd, c0:c0 + ncol], start=(hd == 0), stop=(hd == 3)),
                         r=[B("f_fw"), spb], w=[pmb])
                s.op("act", lambda g_=g_: nc.scalar.activation(out=g_[:, 0:ncol], in_=g_[:, 0:ncol], func=AF.Silu), w=[gb])
                o_ = ob[ct % 2]; obb = B(f"f_ob{ct % 2}")
                s.op("dve", lambda o_=o_, pm=pm, ct=ct, g_=g_: nc.vector.scalar_tensor_tensor(out=o_[:, 0:ncol], in0=pm[:, 0:ncol], scalar=fb[:, ct:ct + 1], in1=g_[:, 0:ncol], op0=ALU.add, op1=ALU.mult),
                     r=[gb, B("f_fb")], w=[pmb, obb])
                s.dma("pool", k.MT[1536 + ct * 128:1536 + (ct + 1) * 128, tok0:tok0 + ncol], o_[:, 0:ncol], r=[obb], w=[B("MT")], **kw)

        if need_ctx:
            nrm = 1.0 / math.sqrt(256.0 * 128.0)
            for lt in range(2):
                tab = tabs[nload[0] % 4]; tabb = B(f"f_tab{nload[0] % 4}")
                for j, tsr in enumerate([k.CL, k.SLN]):
                    src = bass.AP(tsr.tensor, tsr.offset + (lt * 128) * 16 * 4096, [[16 * 4096, 128], [1, 256]])
                    s.dma("sp", tab[:, j, 0:256], src, r=[B("DFT")], w=[tabb], allow_slow_non_contiguous=True)
                nload[0] += 1
                for hd in range(4):
                    s.op("pe", lambda hd=hd, lt=lt, tab=tab: nc.tensor.matmul(psS[hd][:, 0:256], lhsT=PQ[:, lt, hd, 0, :], rhs=tab[:, 0, 0:256], start=(lt == 0), stop=False),
                         r=[B(f"f_PQ{lt}"), tabb], w=[B(f"psS{hd}")])
                    s.op("pe", lambda hd=hd, lt=lt, tab=tab: nc.tensor.matmul(psS[hd][:, 0:256], lhsT=PQ[:, lt, hd, 1, :], rhs=tab[:, 1, 0:256], start=False, stop=(lt == 1)),
                         r=[B(f"f_PQ{lt}"), tabb], w=[B(f"psS{hd}")])
            sp_ = specT[nsp[0] % 2]; spb = B(f"f_spec{nsp[0] % 2}"); nsp[0] += 1
            for hd in range(4):
                s.op("act", lambda hd=hd, sp_=sp_: nc.scalar.mul(out=sp_[:, hd, 0:256], in_=psS[hd][:, 0:256], mul=nrm), w=[B(f"psS{hd}"), spb])
            mix_group(sp_, spb, 0, 256, 0)
        nrm = 1.0 / math.sqrt(4096.0 * 128.0)
        for kt in range(8):
            for lg in range(8):
                tb4 = tab4[nload[0] % 3]; tb4b = B(f"f_tab4_{nload[0] % 3}")
                for j, tsr in enumerate([k.CL, k.SLN]):
                    q_ = "sp" if j == 0 else "act"
                    s.dma(q_, tb4[:, j, :, :], tsr[lg * 512:(lg + 1) * 512, kt * 256:(kt + 1) * 256].rearrange("(a p) c -> p a c", p=128), r=[B("DFT")], w=[tb4b])
                nload[0] += 1
                for a_ in range(4):
                    lt = lg * 4 + a_
                    for hd in range(4):
                        s.op("pe", lambda hd=hd, lt=lt, tb4=tb4, a_=a_: nc.tensor.matmul(psS[hd][:, 0:256], lhsT=PQ[:, 2 + lt, hd, 0, :], rhs=tb4[:, 0, a_, :], start=(lt == 0), stop=False, skip_group_check=True),
                             r=[B(f"f_PQ{2 + lt}"), tb4b], w=[B(f"psS{hd}")])
                        s.op("pe", lambda hd=hd, lt=lt, tb4=tb4, a_=a_: nc.tensor.matmul(psS[hd][:, 256:512], lhsT=PQ[:, 2 + lt, hd, 1, :], rhs=tb4[:, 1, a_, :], start=False, stop=(lt == 31), skip_group_check=True),
                             r=[B(f"f_PQ{2 + lt}"), tb4b], w=[B(f"psS{hd}")])
            sp_ = specT[nsp[0] % 2]; spb = B(f"f_spec{nsp[0] % 2}"); nsp[0] += 1
            for hd in range(4):
                b_ = bsb[hd % 2]; bb = B(f"f_bsb{hd % 2}")
                s.op("act", lambda hd=hd, b_=b_: nc.scalar.mul(out=b_[:], in_=psS[hd][:, 256:512], mul=nrm), w=[B(f"psS{hd}"), bb])
                s.op("dve", lambda hd=hd, b_=b_, sp_=sp_: nc.vector.scalar_tensor_tensor(out=sp_[:, hd, 0:256], in0=psS[hd][:, 0:256], scalar=nrm, in1=b_[:], op0=ALU.mult, op1=ALU.add),
                     r=[bb], w=[B(f"psS{hd}"), spb])
                rv = bass.AP(sp_[:].tensor, sp_[:, hd, 511:512].offset, [list(sp_[:, hd, 0:1].ap[0]), [-1, 256]])
                s.op("dve", lambda hd=hd, b_=b_, rv=rv: nc.vector.scalar_tensor_tensor(out=rv, in0=psS[hd][:, 0:256], scalar=nrm, in1=b_[:], op0=ALU.mult, op1=ALU.subtract),
                     r=[bb], w=[B(f"psS{hd}"), spb])
            mix_group(sp_, spb, 0, 256, 256 + kt * 256)
            nm = 256 if kt > 0 else 255
            mix_group(sp_, spb, 256, nm, 256 + 4096 - kt * 256 - 255)
        for lt in range(32):
            for hd in range(4):
                s.op("pe", lambda hd=hd, lt=lt: nc.tensor.matmul(psS[0][:, hd:hd + 1], lhsT=PQ[:, 2 + lt, hd, 0, :], rhs=altb[:, 0:1], start=(lt == 0 and hd == 0), stop=(lt == 31 and hd == 3), skip_group_check=True),
                     r=[B(f"f_PQ{2 + lt}"), B("f_altb")], w=[B("psS0")])
        sp_ = specT[nsp[0] % 2]; spb = B(f"f_spec{nsp[0] % 2}"); nsp[0] += 1
        s.op("act", lambda sp_=sp_: nc.scalar.mul(out=sp_[:, :, 0], in_=psS[0][:, 0:4], mul=nrm), w=[B("psS0"), spb])
        mix_group(sp_, spb, 0, 1, 256 + 2048)
        s.barrier()


def phase_out(k, l, last=False):
    nc, s, I, B = k.nc, k.s, k.I, k.B
    with ExitStack() as st:
        T_ = lambda name, shape, dt: st.enter_context(nc.sbuf_tensor(name + k.sfx, shape, dt))
        P_ = lambda name, shape, dt: st.enter_context(nc.psum_tensor(name + k.sfx, shape, dt))
        wo = T_("o_wo", [128, 16, D], BF16)
        wst = [T_(f"o_wst{i}", [128, 2, D], F32) for i in range(2)]
        bc = T_("o_bc", [128, 2, D], F32)
        mT = [T_(f"o_mT{i}", [128, 16, 128], BF16) for i in range(2)]
        xt = [T_(f"o_x{i}", [128, D], F32) for i in range(2)]
        ot = [T_(f"o_o{i}", [128, D], F32) for i in range(2)]
        junk = T_("o_junk", [128, 512], BF16)
        st4 = T_("o_st", [128, 4], F32)
        psO = [P_(f"psO{i}", [128, 512], F32) for i in range(4)]
        for c in range(8):
            w_ = wst[c % 2]; wb = B(f"o_wst{c % 2}")
            s.dma("sp", w_[:], I["w_out"][l, c * 256:(c + 1) * 256, :].rearrange("(t p) c -> p t c", p=128), w=[wb])
            if c % 2 == 0:
                s.op("act", lambda w_=w_, c=c: nc.scalar.copy(out=wo[:, 2 * c:2 * c + 2, :], in_=w_[:]), r=[wb], w=[B("o_wo")])
            else:
                s.op("pool", lambda w_=w_, c=c: nc.gpsimd.tensor_copy(out=wo[:, 2 * c:2 * c + 2, :], in_=w_[:]), r=[wb], w=[B("o_wo")])
        for j in range(2):
            s.dma("pool", bc[:, j, :], k.MODS[l, 1 - j, 2, :].partition_broadcast(128), r=[B("MODS")], w=[B("o_bc")])
        for n_, tt in enumerate(range(2 if last else 0, 34)):
            t0 = tt * 128
            m_ = mT[n_ % 2]; mb = B(f"o_mT{n_ % 2}")
            s.dma("sp", m_[:], k.MT[:, t0:t0 + 128].rearrange("(t p) j -> p t j", p=128), r=[B("MT")], w=[mb])
            x_ = xt[n_ % 2]; xb = B(f"o_x{n_ % 2}")
            if l == 0:
                src = I["ctx"][t0:t0 + 128, :] if tt < 2 else I["x"][t0 - 256:t0 - 128, :]
                s.dma("sp", x_[:], src, w=[xb])
            else:
                s.dma("sp", x_[:], k.XS[t0:t0 + 128, :], r=[B("XS")], w=[xb])
            for hf in range(2):
                po = psO[(n_ % 2) * 2 + hf]; pob = B(f"psO{(n_ % 2) * 2 + hf}")
                for ct in range(16):
                    s.op("pe", lambda po=po, m_=m_, ct=ct, hf=hf: nc.tensor.matmul(po[:], lhsT=m_[:, ct, :], rhs=wo[:, ct, hf * 512:(hf + 1) * 512], start=(ct == 0), stop=(ct == 15)),
                         r=[mb, B("o_wo")], w=[pob])
                s.op("act", lambda po=po, hf=hf: nc.scalar.activation(out=junk[:], in_=po[:], func=AF.Square, accum_out=st4[:, hf:hf + 1]), w=[pob, B("o_junk"), B("o_st")])
            s.op("dve", lambda: nc.vector.tensor_tensor(out=st4[:, 2:3], in0=st4[:, 0:1], in1=st4[:, 1:2], op=ALU.add), w=[B("o_st")])
            s.op("dve", lambda: nc.vector.tensor_scalar(out=st4[:, 2:3], in0=st4[:, 2:3], scalar1=1.0 / D, scalar2=EPS, op0=ALU.mult, op1=ALU.add), w=[B("o_st")])
            s.op("act", lambda: nc.scalar.activation(out=st4[:, 2:3], in_=st4[:, 2:3], func=AF.Sqrt), w=[B("o_st")])
            s.op("dve", lambda: nc.vector.reciprocal(out=st4[:, 3:4], in_=st4[:, 2:3]), w=[B("o_st")])
            o_ = ot[n_ % 2]; ob = B(f"o_o{n_ % 2}")
            jb = 0 if tt < 2 else 1
            for hf in range(2):
                po = psO[(n_ % 2) * 2 + hf]; pob = B(f"psO{(n_ % 2) * 2 + hf}")
                s.op("dve", lambda po=po, hf=hf, o_=o_, jb=jb: nc.vector.scalar_tensor_tensor(out=o_[:, hf * 512:(hf + 1) * 512], in0=po[:], scalar=st4[:, 3:4], in1=bc[:, jb, hf * 512:(hf + 1) * 512],
                                                                                      op0=ALU.mult, op1=ALU.mult), r=[B("o_st"), B("o_bc")], w=[pob, ob])
            s.op("pool", lambda o_=o_, x_=x_: nc.gpsimd.tensor_tensor(out=o_[:], in0=o_[:], in1=x_[:], op=ALU.add), r=[xb], w=[ob])
            if last:
                s.dma("pool", k.out[t0 - 256:t0 - 128, :], o_[:], r=[ob], w=[B("OUT")])
            else:
                s.dma("pool", k.XS[t0:t0 + 128, :], o_[:], r=[ob], w=[B("XS")])
        s.barrier()

T = 4352; D = 1024; NB = 544
C_MVEC = 128; C_BM8 = 524; C_IDXF = 1024; C_IDXB = 1600
I32 = mybir.dt.int32
TWO_PI = 2.0 * math.pi

def phase_s5(k, l, need_ctx=True):
    nc, s, I, B = k.nc, k.s, k.I, k.B
    cst = k.cst
    V = lambda fn, r=(), w=(): s.op("dve", fn, r=r, w=w)
    A = lambda fn, r=(), w=(): s.op("act", fn, r=r, w=w)
    G = lambda fn, r=(), w=(): s.op("pool", fn, r=r, w=w)
    PE = lambda fn, r=(), w=(): s.op("pe", fn, r=r, w=w)
    with ExitStack() as st:
        T_ = lambda name, shape, dt: st.enter_context(nc.sbuf_tensor(name + k.sfx, shape, dt))
        WS = T_("q_WS", [128, 2, 4, 8, 2, 128], BF16)
        WR = T_("q_WR", [128, 2, 16, 8, 2, 32], BF16)
        KD = T_("q_KD", [128, 2, 4, 8, 128], BF16)
        rho = T_("q_rho", [128, 2, 16], F32)
        th = T_("q_th", [128, 2, 16], F32)
        with ExitStack() as ps:
            TP = lambda name, shape, dt: ps.enter_context(nc.sbuf_tensor(name + k.sfx, shape, dt))
            PP = lambda name, shape, dt: ps.enter_context(nc.psum_tensor(name + k.sfx, shape, dt))
            lam16 = TP("p_lam16", [16, 2, 128], F32)
            lr = TP("p_lr", [128, 16], F32); li = TP("p_li", [128, 16], F32)
            stp = TP("p_stp", [128, 16], F32)
            lrs = TP("p_lrs", [128, 16], F32); lis = TP("p_lis", [128, 16], F32)
            a9 = TP("p_a9", [128, 16, 9], F32); a9b = TP("p_a9b", [128, 16, 9], F32)
            ki = TP("p_ki", [128, 16, 9], I32)
            mag9 = TP("p_mag9", [128, 16, 9], F32)
            Ar = TP("p_Ar", [128, 16, 9], F32); Ai = TP("p_Ai", [128, 16, 9], F32)
            t16 = [TP(f"p_t16_{i}", [128, 16], F32) for i in range(6)]
            Br = TP("p_Br", [128, 16, 16], F32); Bi = TP("p_Bi", [128, 16, 16], F32)
            Bbr = TP("p_Bbr", [128, 16, 16], F32); Bbi = TP("p_Bbi", [128, 16, 16], F32)
            tB = TP("p_tB", [128, 16, 16], F32)
            Cn = [TP(f"p_Cn{i}", [128, 128], F32) for i in range(2)]
            Cr = TP("p_Cr", [128, 16, 16], F32); Ci = TP("p_Ci", [128, 16, 16], F32)
            Zr = TP("p_Zr", [128, 16, 9, 16], F32); Zi = TP("p_Zi", [128, 16, 9, 16], F32)
            tZ = TP("p_tZ", [128, 16, 9, 16], F32)
            Pr = TP("p_Pr", [128, 16, 8, 16], F32); Pi = TP("p_Pi", [128, 16, 8, 16], F32)
            BPr = TP("p_BPr", [128, 16, 128], F32); BPi = TP("p_BPi", [128, 16, 128], F32)
            PPd = [TP(f"p_PPd{i}", [128, 16, 128], BF16) for i in range(2)]
            kdc = TP("p_kdc", [128, 8, 16], F32)
            psT1 = PP("psT1", [128, 128], F32)
            psK = PP("psK", [128, 128], F32)
            psW = [PP(f"psWs{i}", [128, 128], F32) for i in range(2)]
            G(lambda: nc.gpsimd.memset(WR[:], 0.0), w=[B("q_WR")])
            for d in range(2):
                s.dma("sp", lam16[:, 0, :], I["s5_lambda_re"][l, d].rearrange("(a b) n -> a (b n)", b=2), w=[B("p_lam16")])
                s.dma("sp", lam16[:, 1, :], I["s5_lambda_im"][l, d].rearrange("(a b) n -> a (b n)", b=2), w=[B("p_lam16")])
                for j, dst in enumerate([lr, li]):
                    PE(lambda j=j: nc.tensor.transpose(out=psT1[:, 0:16], in_=lam16[:, j, :], identity=cst[0:16, 0:16]), r=[B("p_lam16"), B("cst")], w=[B("psT1")])
                    V(lambda dst=dst: nc.vector.tensor_copy(out=dst[:], in_=psT1[:, 0:16]), w=[B("psT1"), B("p_l")])
                ls = I["s5_log_step"]
                for g2 in range(2):
                    src = bass.AP(ls.tensor, ls.offset + (l * 2 + d) * 32 + g2, [[0, 64], [2, 16]])
                    s.dma("sp", stp[64 * g2:64 * g2 + 64, :], src, w=[B("p_stp")], allow_slow_non_contiguous=True)
                A(lambda: nc.scalar.activation(out=stp[:], in_=stp[:], func=AF.Exp), w=[B("p_stp")])
                V(lambda: nc.vector.tensor_tensor(out=lrs[:], in0=lr[:], in1=stp[:], op=ALU.mult), r=[B("p_l"), B("p_stp")], w=[B("p_ls")])
                V(lambda: nc.vector.tensor_tensor(out=lis[:], in0=li[:], in1=stp[:], op=ALU.mult), r=[B("p_l"), B("p_stp")], w=[B("p_ls")])
                mv = cst[:, C_MVEC:C_MVEC + 9].unsqueeze(1).to_broadcast([128, 16, 9])
                V(lambda: nc.vector.tensor_tensor(out=a9[:], in0=lrs[:].unsqueeze(2).to_broadcast([128, 16, 9]), in1=mv, op=ALU.mult), r=[B("p_ls"), B("cst")], w=[B("p_a9")])
                A(lambda: nc.scalar.activation(out=mag9[:], in_=a9[:], func=AF.Exp), r=[B("p_a9")], w=[B("p_mag9")])
                V(lambda: nc.vector.tensor_tensor(out=a9[:], in0=lis[:].unsqueeze(2).to_broadcast([128, 16, 9]), in1=mv, op=ALU.mult), r=[B("p_ls"), B("cst")], w=[B("p_a9")])
                def reduce_sin(dst, src_ap, shift, shape3):
                    V(lambda: nc.vector.tensor_scalar(out=a9b[:], in0=src_ap, scalar1=shift, scalar2=None, op0=ALU.add), r=[B("p_a9")], w=[B("p_a9b")])
                    V(lambda: nc.vector.tensor_scalar(out=ki[:], in0=a9b[:], scalar1=1.0 / TWO_PI, scalar2=None, op0=ALU.mult), r=[B("p_a9b")], w=[B("p_ki")])
                    V(lambda: nc.vector.scalar_tensor_tensor(out=a9b[:], in0=ki[:], scalar=-TWO_PI, in1=a9b[:], op0=ALU.mult, op1=ALU.add), r=[B("p_ki")], w=[B("p_a9b")])
                    A(lambda: nc.scalar.activation(out=dst[:], in_=a9b[:], func=AF.Sin), r=[B("p_a9b")], w=[B("p_sc")])
                reduce_sin(Ai, a9[:], 0.0, None)
                V(lambda d=d: nc.vector.tensor_copy(out=th[:, d, :], in_=a9b[:, :, 8]), r=[B("p_a9b")], w=[B("q_th")])
                reduce_sin(Ar, a9[:], math.pi / 2.0, None)
                V(lambda: nc.vector.tensor_tensor(out=Ar[:], in0=Ar[:], in1=mag9[:], op=ALU.mult), r=[B("p_mag9")], w=[B("p_sc")])
                V(lambda: nc.vector.tensor_tensor(out=Ai[:], in0=Ai[:], in1=mag9[:], op=ALU.mult), r=[B("p_mag9")], w=[B("p_sc")])
                V(lambda d=d: nc.vector.tensor_copy(out=rho[:, d, :], in_=mag9[:, :, 8]), r=[B("p_mag9")], w=[B("q_rho")])
                am1, den, fr, fi, u1, u2 = t16
                V(lambda: nc.vector.tensor_scalar(out=am1[:], in0=Ar[:, :, 1], scalar1=-1.0, scalar2=None, op0=ALU.add), r=[B("p_sc")], w=[B("p_t16")])
                V(lambda: nc.vector.tensor_tensor(out=den[:], in0=lr[:], in1=lr[:], op=ALU.mult), r=[B("p_l")], w=[B("p_t16")])
                V(lambda: nc.vector.tensor_tensor(out=u1[:], in0=li[:], in1=li[:], op=ALU.mult), r=[B("p_l")], w=[B("p_t16")])
                V(lambda: nc.vector.tensor_tensor(out=den[:], in0=den[:], in1=u1[:], op=ALU.add), w=[B("p_t16")])
                V(lambda: nc.vector.reciprocal(out=den[:], in_=den[:]), w=[B("p_t16")])
                V(lambda: nc.vector.tensor_tensor(out=u1[:], in0=am1[:], in1=lr[:], op=ALU.mult), w=[B("p_t16")])
                V(lambda: nc.vector.tensor_tensor(out=u2[:], in0=Ai[:, :, 1], in1=li[:], op=ALU.mult), w=[B("p_t16")])
                V(lambda: nc.vector.tensor_tensor(out=fr[:], in0=u1[:], in1=u2[:], op=ALU.add), w=[B("p_t16")])
                V(lambda: nc.vector.tensor_tensor(out=fr[:], in0=fr[:], in1=den[:], op=ALU.mult), w=[B("p_t16")])
                V(lambda: nc.vector.tensor_tensor(out=u1[:], in0=Ai[:, :, 1], in1=lr[:], op=ALU.mult), w=[B("p_t16")])
                V(lambda: nc.vector.tensor_tensor(out=u2[:], in0=am1[:], in1=li[:], op=ALU.mult), w=[B("p_t16")])
                V(lambda: nc.vector.tensor_tensor(out=fi[:], in0=u1[:], in1=u2[:], op=ALU.subtract), w=[B("p_t16")])
                V(lambda: nc.vector.tensor_tensor(out=fi[:], in0=fi[:], in1=den[:], op=ALU.mult), w=[B("p_t16")])
                for j, (dst, nm) in enumerate([(Br, "s5_b_re"), (Bi, "s5_b_im")]):
                    bt = I[nm]
                    for g2 in range(2):
                        src = bass.AP(bt.tensor, bt.offset + ((l * 2 + d) * 32 + g2) * 1024, [[16, 64], [2048, 16], [1, 16]])
                        s.dma("sp", dst[64 * g2:64 * g2 + 64, :, :], src, w=[B("p_B")])
                frb = fr[:].unsqueeze(2).to_broadcast([128, 16, 16]); fib = fi[:].unsqueeze(2).to_broadcast([128, 16, 16])
                V(lambda: nc.vector.tensor_tensor(out=Bbr[:], in0=Br[:], in1=frb, op=ALU.mult), r=[B("p_B"), B("p_t16")], w=[B("p_Bb")])
                V(lambda: nc.vector.tensor_tensor(out=tB[:], in0=Bi[:], in1=fib, op=ALU.mult), r=[B("p_B"), B("p_t16")], w=[B("p_tB")])
                V(lambda: nc.vector.tensor_tensor(out=Bbr[:], in0=Bbr[:], in1=tB[:], op=ALU.subtract), r=[B("p_tB")], w=[B("p_Bb")])
                V(lambda: nc.vector.tensor_tensor(out=Bbi[:], in0=Bi[:], in1=frb, op=ALU.mult), r=[B("p_B"), B("p_t16")], w=[B("p_Bb")])
                V(lambda: nc.vector.tensor_tensor(out=tB[:], in0=Br[:], in1=fib, op=ALU.mult), r=[B("p_B"), B("p_t16")], w=[B("p_tB")])
                V(lambda: nc.vector.tensor_tensor(out=Bbi[:], in0=Bbi[:], in1=tB[:], op=ALU.add), r=[B("p_tB")], w=[B("p_Bb")])
                for j, (dst, nm) in enumerate([(Cr, "s5_c_re"), (Ci, "s5_c_im")]):
                    ct_ = I[nm]
                    for pset in range(2):
                        cn = Cn[pset]; cnb = B(f"p_Cn{pset}")
                        for pr in range(8):
                            g0 = 2 * (8 * pset + pr)
                            src = bass.AP(ct_.tensor, ct_.offset + ((l * 2 + d) * 32 + g0) * 1024, [[64, 16], [1024, 2], [1, 64]])
                            s.dma("sp", cn[16 * pr:16 * pr + 16, :].rearrange("p (a n) -> p a n", a=2), src, w=[cnb])
                        PE(lambda cn=cn: nc.tensor.transpose(out=psT1[:], in_=cn[:], identity=cst[:, 0:128]), r=[cnb, B("cst")], w=[B("psT1")])
                        V(lambda dst=dst, pset=pset: nc.vector.tensor_copy(out=dst[:, 8 * pset:8 * pset + 8, :], in_=psT1[:].rearrange("p (a k) -> p a k", k=16)), w=[B("psT1"), B("p_C")])
                Crb = lambda X: X[:].unsqueeze(2).to_broadcast([128, 16, 9, 16])
                Ab = lambda X: X[:].unsqueeze(3).to_broadcast([128, 16, 9, 16])
                V(lambda: nc.vector.tensor_tensor(out=Zr[:], in0=Crb(Cr), in1=Ab(Ar), op=ALU.mult), r=[B("p_C"), B("p_sc")], w=[B("p_Z")])
                G(lambda: nc.gpsimd.tensor_tensor(out=tZ[:], in0=Crb(Ci), in1=Ab(Ai), op=ALU.mult), r=[B("p_C"), B("p_sc")], w=[B("p_tZ")])
                V(lambda: nc.vector.tensor_tensor(out=Zr[:], in0=Zr[:], in1=tZ[:], op=ALU.subtract), r=[B("p_tZ")], w=[B("p_Z")])
                V(lambda: nc.vector.tensor_tensor(out=Zi[:], in0=Crb(Cr), in1=Ab(Ai), op=ALU.mult), r=[B("p_C"), B("p_sc")], w=[B("p_Z")])
                G(lambda: nc.gpsimd.tensor_tensor(out=tZ[:], in0=Crb(Ci), in1=Ab(Ar), op=ALU.mult), r=[B("p_C"), B("p_sc")], w=[B("p_tZ")])
                V(lambda: nc.vector.tensor_tensor(out=Zi[:], in0=Zi[:], in1=tZ[:], op=ALU.add), r=[B("p_tZ")], w=[B("p_Z")])
                for g2 in range(2):
                    sl = slice(64 * g2, 64 * g2 + 64)
                    V(lambda sl=sl, g2=g2, d=d: nc.vector.tensor_copy(out=WR[sl, d, :, :, 0, 16 * g2:16 * g2 + 16], in_=Zr[sl, :, 1:9, :]), r=[B("p_Z")], w=[B("q_WR")])
                    V(lambda sl=sl, g2=g2, d=d: nc.vector.tensor_scalar(out=WR[sl, d, :, :, 1, 16 * g2:16 * g2 + 16], in0=Zi[sl, :, 1:9, :], scalar1=-1.0, scalar2=None, op0=ALU.mult), r=[B("p_Z")], w=[B("q_WR")])
                Bb8 = lambda X: X[:].unsqueeze(2).to_broadcast([128, 16, 8, 16])
                A8 = lambda X: X[:, :, 0:8].unsqueeze(3).to_broadcast([128, 16, 8, 16])
                tP = tZ[:, :, 0:8, :]
                V(lambda: nc.vector.tensor_tensor(out=Pr[:], in0=Bb8(Bbr), in1=A8(Ar), op=ALU.mult), r=[B("p_Bb"), B("p_sc")], w=[B("p_P")])
                G(lambda: nc.gpsimd.tensor_tensor(out=tP, in0=Bb8(Bbi), in1=A8(Ai), op=ALU.mult), r=[B("p_Bb"), B("p_sc")], w=[B("p_tZ")])
                V(lambda: nc.vector.tensor_tensor(out=Pr[:], in0=Pr[:], in1=tP, op=ALU.subtract), r=[B("p_tZ")], w=[B("p_P")])
                V(lambda: nc.vector.tensor_tensor(out=Pi[:], in0=Bb8(Bbi), in1=A8(Ar), op=ALU.mult), r=[B("p_Bb"), B("p_sc")], w=[B("p_P")])
                G(lambda: nc.gpsimd.tensor_tensor(out=tP, in0=Bb8(Bbr), in1=A8(Ai), op=ALU.mult), r=[B("p_Bb"), B("p_sc")], w=[B("p_tZ")])
                V(lambda: nc.vector.tensor_tensor(out=Pi[:], in0=Pi[:], in1=tP, op=ALU.add), r=[B("p_tZ")], w=[B("p_P")])
                G(lambda: nc.gpsimd.memset(BPr[:], 0.0), w=[B("p_BP")])
                G(lambda: nc.gpsimd.memset(BPi[:], 0.0), w=[B("p_BP")])
                for g2 in range(2):
                    sl = slice(64 * g2, 64 * g2 + 64)
                    for q in range(4):
                        c0 = 32 * q + 16 * g2
                        V(lambda sl=sl, q=q, c0=c0: nc.vector.tensor_copy(out=BPr[sl, q::4, c0:c0 + 16], in_=Bbr[sl, q::4, :]), r=[B("p_Bb")], w=[B("p_BP")])
                        V(lambda sl=sl, q=q, c0=c0: nc.vector.tensor_scalar(out=BPi[sl, q::4, c0:c0 + 16], in0=Bbi[sl, q::4, :], scalar1=-1.0, scalar2=None, op0=ALU.mult), r=[B("p_Bb")], w=[B("p_BP")])
                for t in range(4):
                    n_ = 0
                    for q in range(4):
                        pr = 4 * t + q
                        for (BP_, Z_) in ((BPr, Zr), (BPi, Zi)):
                            PE(lambda pr=pr, BP_=BP_, Z_=Z_, n_=n_: nc.tensor.matmul(psK[:], lhsT=BP_[:, pr, :], rhs=Z_[:, pr, 0:8, :].rearrange("p a k -> p (a k)"), start=(n_ == 0), stop=(n_ == 7)),
                               r=[B("p_BP"), B("p_Z")], w=[B("psK")])
                            n_ += 1
                    V(lambda: nc.vector.tensor_copy(out=kdc[:], in_=psK[:].rearrange("p (a k) -> p a k", k=16)), w=[B("psK"), B("p_kdc")])
                    V(lambda t=t, d=d: nc.vector.tensor_tensor(out=KD[:, d, t, :, :].rearrange("p a (g k) -> p a g k", k=16), in0=kdc[:].unsqueeze(2).to_broadcast([128, 8, 8, 16]),
                                                          in1=cst[:, C_BM8:C_BM8 + 8].unsqueeze(1).unsqueeze(3).to_broadcast([128, 8, 8, 16]), op=ALU.mult), r=[B("p_kdc"), B("cst")], w=[B("q_KD")])
                n_w = 0
                for m in range(8):
                    for ri, P_ in enumerate((Pr, Pi)):
                        pd = PPd[n_w % 2]; pdb = B(f"p_PPd{n_w % 2}")
                        G(lambda pd=pd: nc.gpsimd.memset(pd[:], 0.0), w=[pdb])
                        for g2 in range(2):
                            sl = slice(64 * g2, 64 * g2 + 64)
                            for q in range(4):
                                c0 = 32 * q + 16 * g2
                                e = V if (q % 2 == 0) else G
                                eng = nc.vector if (q % 2 == 0) else nc.gpsimd
                                e(lambda sl=sl, q=q, c0=c0, pd=pd, P_=P_, m=m, eng=eng: eng.tensor_copy(out=pd[sl, q::4, c0:c0 + 16], in_=P_[sl, q::4, m, :]), r=[B("p_P")], w=[pdb])
                        for t in range(4):
                            pw = psW[t % 2]; pwb = B(f"psWs{t % 2}")
                            for q in range(4):
                                PE(lambda pw=pw, pd=pd, t=t, q=q: nc.tensor.matmul(pw[:], lhsT=pd[:, 4 * t + q, :], rhs=k.identb[:], start=(q == 0), stop=(q == 3)), r=[pdb, B("identb")], w=[pwb])
                            if t % 2 == 0:
                                A(lambda pw=pw, t=t, m=m, ri=ri, d=d: nc.scalar.copy(out=WS[:, d, t, m, ri, :], in_=pw[:]), w=[pwb, B("q_WS")])
                            else:
                                V(lambda pw=pw, t=t, m=m, ri=ri, d=d: nc.vector.tensor_copy(out=WS[:, d, t, m, ri, :], in_=pw[:]), w=[pwb, B("q_WS")])
                        n_w += 1
            s.barrier()
        if getattr(k, 'stop', None) == 's5prep':
            return
        phase_s5_main(k, l, need_ctx, WS, WR, KD, rho, th, st)
        s.barrier()


def phase_s5_main(k, l, need_ctx, WS, WR, KD, rho, th, st):
    nc, s, I, B = k.nc, k.s, k.I, k.B
    cst = k.cst
    V = lambda fn, r=(), w=(): s.op("dve", fn, r=r, w=w)
    A = lambda fn, r=(), w=(): s.op("act", fn, r=r, w=w)
    G = lambda fn, r=(), w=(): s.op("pool", fn, r=r, w=w)
    PE = lambda fn, r=(), w=(): s.op("pe", fn, r=r, w=w)
    T_ = lambda name, shape, dt: st.enter_context(nc.sbuf_tensor(name + k.sfx, shape, dt))
    P_ = lambda name, shape, dt: st.enter_context(nc.psum_tensor(name + k.sfx, shape, dt))
    uT = st.enter_context(nc.sbuf_tensor("q_uT" + k.sfx, [128, 4, T], BF16))
    mst = ExitStack()
    T_ = lambda name, shape, dt: mst.enter_context(nc.sbuf_tensor(name + k.sfx, shape, dt))
    hst = [T_(f"q_hst{i}", [128, 2, NB], BF16) for i in range(2)]
    SL = []
    for i_ in range(2):
        o = {}
        for nm in ("Sr", "Si", "cosT", "sinT", "ang", "xr", "xi", "t1", "t2", "Gr", "Gi"):
            o[nm] = T_(f"q_{nm}{i_}", [128, NB], F32)
        o["kiT"] = T_(f"q_ki{i_}", [128, NB], I32)
        o["psS"] = [mst.enter_context(nc.psum_tensor(f"psSq{i_}_{j}" + k.sfx, [128, 1024], F32)) for j in range(2)]
        SL.append(o)
    for t in range(4):
        s.dma("sp", uT[:, t, :], k.PB[1536 + t * 128:1536 + (t + 1) * 128, :], r=[B("PB")], w=[B("q_uT")])
    pieces = [(32, 288, 0), (288, 544, 256), (0, 32, 512)]
    def it_gen(d, pr, si):
        o = SL[si]
        Sr, Si, cosT, sinT, ang, xr, xi, t1, t2, Gr, Gi, kiT, psS = (o[n] for n in ('Sr','Si','cosT','sinT','ang','xr','xi','t1','t2','Gr','Gi','kiT','psS'))
        sfx_ = str(si)
        t = pr // 4; q = pr % 4
        rows = slice(32 * q, 32 * q + 32)
        for ri in range(2):
            for (b0, b1, pc) in pieces:
                for pos in range(8):
                    m = 7 - pos if d == 0 else pos
                    rhs = uT[rows, t, 8 * b0 + pos:8 * b1:8]
                    PE(lambda ri=ri, pc=pc, b0=b0, b1=b1, m=m, rhs=rhs, pos=pos: nc.tensor.matmul(psS[ri][:, pc:pc + (b1 - b0)], lhsT=WS[rows, d, t, m, ri, :], rhs=rhs, start=(pos == 0), stop=(pos == 7),
                                                                                        tile_position=(32 * q, 0), skip_group_check=True),
                       r=[B("q_uT"), B("q_WS")], w=[B(f"psSq{ri}_" + sfx_)])
        yield
        for ri, dst in enumerate((Sr, Si)):
            A(lambda ri=ri, dst=dst: nc.scalar.copy(out=dst[:, 32:544], in_=psS[ri][:, 0:512]), w=[B(f"psSq{ri}_" + sfx_), B("q_S" + sfx_)])
            A(lambda ri=ri, dst=dst: nc.scalar.copy(out=dst[:, 0:32], in_=psS[ri][:, 512:544]), w=[B(f"psSq{ri}_" + sfx_), B("q_S" + sfx_)])
        yield
        idx = cst[:, C_IDXF:C_IDXF + NB] if d == 0 else cst[:, C_IDXB:C_IDXB + NB]
        for (dst, shift) in ((sinT, 0.0), (cosT, math.pi / 2.0)):
            V(lambda shift=shift: nc.vector.tensor_scalar(out=ang[:], in0=idx, scalar1=th[:, d, pr:pr + 1], scalar2=shift, op0=ALU.mult, op1=ALU.add), r=[B("q_th"), B("cst")], w=[B("q_ang" + sfx_)])
            V(lambda: nc.vector.tensor_scalar(out=kiT[:], in0=ang[:], scalar1=1.0 / TWO_PI, scalar2=None, op0=ALU.mult), r=[B("q_ang" + sfx_)], w=[B("q_ki" + sfx_)])
            V(lambda: nc.vector.scalar_tensor_tensor(out=ang[:], in0=kiT[:], scalar=-TWO_PI, in1=ang[:], op0=ALU.mult, op1=ALU.add), r=[B("q_ki" + sfx_)], w=[B("q_ang" + sfx_)])
            A(lambda dst=dst: nc.scalar.activation(out=dst[:], in_=ang[:], func=AF.Sin), r=[B("q_ang" + sfx_)], w=[B("q_tw" + sfx_)])
        yield
        V(lambda: nc.vector.tensor_tensor(out=xr[:], in0=Sr[:], in1=cosT[:], op=ALU.mult), r=[B("q_S" + sfx_), B("q_tw" + sfx_)], w=[B("q_xr" + sfx_)])
        G(lambda: nc.gpsimd.tensor_tensor(out=t1[:], in0=Si[:], in1=sinT[:], op=ALU.mult), r=[B("q_S" + sfx_), B("q_tw" + sfx_)], w=[B("q_t1" + sfx_)])
        V(lambda: nc.vector.tensor_tensor(out=xr[:], in0=xr[:], in1=t1[:], op=ALU.add), r=[B("q_t1" + sfx_)], w=[B("q_xr" + sfx_)])
        G(lambda: nc.gpsimd.tensor_tensor(out=xi[:], in0=Si[:], in1=cosT[:], op=ALU.mult), r=[B("q_S" + sfx_), B("q_tw" + sfx_)], w=[B("q_xi" + sfx_)])
        V(lambda: nc.vector.tensor_tensor(out=t2[:], in0=Sr[:], in1=sinT[:], op=ALU.mult), r=[B("q_S" + sfx_), B("q_tw" + sfx_)], w=[B("q_t2" + sfx_)])
        G(lambda: nc.gpsimd.tensor_tensor(out=xi[:], in0=xi[:], in1=t2[:], op=ALU.subtract), r=[B("q_t2" + sfx_)], w=[B("q_xi" + sfx_)])
        yield
        rcol = rho[:, d, pr:pr + 1]
        for (src, dst, nm) in ((xr, Gr, "q_xr"), (xi, Gi, "q_xi")):
            if d == 0:
                V(lambda src=src, dst=dst: nc.vector.tensor_tensor_scan(out=dst[:], data0=rcol.to_broadcast([128, NB]), data1=src[:], initial=0.0, op0=ALU.mult, op1=ALU.add),
                  r=[B(nm + sfx_), B("q_rho")], w=[B("q_G" + sfx_)])
            else:
                rv = lambda X, a, b: bass.AP(X[:].tensor, X[:, b - 1:b].offset, [list(X[:].ap[0]), [-1, b - a]])
                V(lambda src=src, dst=dst: nc.vector.tensor_tensor_scan(out=rv(dst, 0, 32), data0=rcol.to_broadcast([128, 32]), data1=rv(src, 0, 32), initial=0.0, op0=ALU.mult, op1=ALU.add),
                  r=[B(nm + sfx_), B("q_rho")], w=[B("q_G" + sfx_)])
                V(lambda src=src, dst=dst: nc.vector.tensor_tensor_scan(out=rv(dst, 32, 544), data0=rcol.to_broadcast([128, 512]), data1=rv(src, 32, 544), initial=dst[:, 0:1], op0=ALU.mult, op1=ALU.add),
                  r=[B(nm + sfx_), B("q_rho")], w=[B("q_G" + sfx_)])
        yield
        hs_ = hst[si]; hsb = B(f"q_hst{si}")
        if d == 0:
            G(lambda hs_=hs_: nc.gpsimd.memset(hs_[:, :, 0:1], 0.0), w=[hsb])
        if d == 0:
            so, si_ = slice(1, 544), slice(0, 543)
        else:
            so, si_ = slice(0, 543), slice(1, 544)
        V(lambda: nc.vector.tensor_tensor(out=t1[:], in0=Gr[:], in1=cosT[:], op=ALU.mult), r=[B("q_G" + sfx_), B("q_tw" + sfx_)], w=[B("q_t1" + sfx_)])
        G(lambda: nc.gpsimd.tensor_tensor(out=t2[:], in0=Gi[:], in1=sinT[:], op=ALU.mult), r=[B("q_G" + sfx_), B("q_tw" + sfx_)], w=[B("q_t2" + sfx_)])
        V(lambda: nc.vector.tensor_tensor(out=hs_[:, 0, so], in0=t1[:, si_], in1=t2[:, si_], op=ALU.subtract), r=[B("q_t1" + sfx_), B("q_t2" + sfx_)], w=[hsb])
        if d == 1:
            V(lambda: nc.vector.tensor_tensor(out=hs_[:, 0, 543:544], in0=t1[:, 0:1], in1=t2[:, 0:1], op=ALU.subtract), r=[B("q_t1" + sfx_), B("q_t2" + sfx_)], w=[hsb])
        G(lambda: nc.gpsimd.tensor_tensor(out=xr[:], in0=Gr[:], in1=sinT[:], op=ALU.mult), r=[B("q_G" + sfx_), B("q_tw" + sfx_)], w=[B("q_xr" + sfx_)])
        V(lambda: nc.vector.tensor_tensor(out=xi[:], in0=Gi[:], in1=cosT[:], op=ALU.mult), r=[B("q_G" + sfx_), B("q_tw" + sfx_)], w=[B("q_xi" + sfx_)])
        G(lambda: nc.gpsimd.tensor_tensor(out=hs_[:, 1, so], in0=xr[:, si_], in1=xi[:, si_], op=ALU.add), r=[B("q_xr" + sfx_), B("q_xi" + sfx_)], w=[hsb])
        if d == 1:
            G(lambda: nc.gpsimd.tensor_tensor(out=hs_[:, 1, 543:544], in0=xr[:, 0:1], in1=xi[:, 0:1], op=ALU.add), r=[B("q_xr" + sfx_), B("q_xi" + sfx_)], w=[hsb])
            G(lambda: nc.gpsimd.memset(hs_[:, :, 31:32], 0.0), w=[hsb])
        s.dma("sp", k.HD[d, pr].rearrange("r p c -> p r c"), hs_[:], r=[hsb], w=[B("HD")])

    def interleave(gens):
        gens = list(gens)
        while gens:
            for g_ in list(gens):
                try:
                    next(g_)
                except StopIteration:
                    gens.remove(g_)
    its = [(d, pr) for d in range(2) for pr in range(16)]
    for i_ in range(0, len(its), 2):
        interleave([it_gen(its[i_][0], its[i_][1], 0), it_gen(its[i_ + 1][0], its[i_ + 1][1], 1)])
    s.barrier()
    mst.close()
    if getattr(k, 'stop', None) == 's5main':
        return
    s5_glu(k, l, need_ctx, WR, KD, uT, None, st)


def s5_glu(k, l, need_ctx, WR, KD, uT, Hin, st):
    nc, s, I, B = k.nc, k.s, k.I, k.B
    V = lambda fn, r=(), w=(): s.op("dve", fn, r=r, w=w)
    A = lambda fn, r=(), w=(): s.op("act", fn, r=r, w=w)
    G = lambda fn, r=(), w=(): s.op("pool", fn, r=r, w=w)
    PE = lambda fn, r=(), w=(): s.op("pe", fn, r=r, w=w)
    T_ = lambda name, shape, dt: st.enter_context(nc.sbuf_tensor(name + k.sfx, shape, dt))
    P_ = lambda name, shape, dt: st.enter_context(nc.psum_tensor(name + k.sfx, shape, dt))
    wgs = T_("g_wgs", [128, 512], F32); wg = T_("g_wg", [128, 4, 512], BF16)
    bg = T_("g_bg", [128, 4], F32); dsk = T_("g_dsk", [128, 4], F32)
    dgs = T_("g_dgs", [128, 4, 128], BF16)
    y32 = [T_(f"g_y{i}", [128, 512], F32) for i in range(2)]
    y2 = [T_(f"g_y2{i}", [128, 512], F32) for i in range(2)]
    vT = T_("g_v", [128, 4, 2048], BF16)
    gs = [T_(f"g_gs{i}", [128, 512], F32) for i in range(2)]
    o1 = T_("g_o1", [128, 512], F32)
    ob = [T_(f"g_ob{i}", [128, 512], BF16) for i in range(2)]
    Hc = T_("g_Hc", [128, 64, 256], BF16)
    py4 = P_("psY4", [128, 2048], F32)
    psG = [P_(f"psGq{i}", [128, 512], F32) for i in range(2)]
    for t in range(4):
        s.dma("sp", wgs[:], I["s5_w_glu"][l, t * 128:(t + 1) * 128, :], w=[B("g_wgs")])
        V(lambda t=t: nc.vector.tensor_copy(out=wg[:, t, :], in_=wgs[:]), r=[B("g_wgs")], w=[B("g_wg")])
    s.dma("sp", bg[:], I["s5_b_glu"][l].rearrange("(t p) -> p t", p=128), w=[B("g_bg")], allow_slow_non_contiguous=True)
    s.dma("sp", dsk[:], I["s5_d"][l].rearrange("(t p) -> p t", p=128), w=[B("g_dsk")], allow_slow_non_contiguous=True)
    for t in range(4):
        V(lambda t=t: nc.vector.tensor_scalar(out=dgs[:, t, :], in0=k.identb[:], scalar1=dsk[:, t:t + 1], scalar2=None, op0=ALU.mult), r=[B("g_dsk"), B("identb")], w=[B("g_dgs")])
    chunks = [(0, 32)] if need_ctx else []
    chunks += [(32, 256), (288, 256)]
    hcb = B("g_Hc"); vb = B("g_v"); pyb = B("psY4")
    npc = 0
    for ci, (b0, nb) in enumerate(chunks):
        t0 = 8 * b0; ntok = 8 * nb
        for dd in range(2):
            for qq in range(4):
                s.dma("sp", Hc[:, dd * 32 + qq * 8:dd * 32 + qq * 8 + 8, 0:nb], k.HD[dd, qq * 4:qq * 4 + 4, :, :, b0:b0 + nb].rearrange("q r p c -> p (q r) c"), r=[B("HD")], w=[hcb])
        for t in range(4):
            reg = lambda o, rows=slice(0, 128): py4[rows, o * 256:o * 256 + nb]
            for o in range(8):
                PE(lambda o=o, t=t: nc.tensor.matmul(reg(o), lhsT=dgs[:, t, :], rhs=uT[:, t, t0 + o:t0 + ntok:8], start=(o % 2 == 0), stop=False, skip_group_check=True),
                   r=[B("g_dgs"), B("q_uT")], w=[pyb])
            for d in range(2):
                for o in range(8):
                    srcs = range(0, o + 1) if d == 0 else range(o, 8)
                    for o2 in srcs:
                        lag = abs(o - o2)
                        PE(lambda t=t, d=d, lag=lag, o=o, o2=o2: nc.tensor.matmul(reg(o), lhsT=KD[:, d, t, lag, :], rhs=uT[:, t, t0 + o2:t0 + ntok:8], start=False, stop=False, skip_group_check=True),
                           r=[B("q_KD"), B("q_uT")], w=[pyb])
                for q in range(4):
                    pr = 4 * t + q
                    for o in range(8):
                        i_ = o if d == 0 else 7 - o
                        for ri in range(2):
                            PE(lambda d=d, pr=pr, i_=i_, ri=ri, o=o, q=q: nc.tensor.matmul(reg(o, slice(32 * q, 32 * q + 32)), lhsT=WR[:, d, pr, i_, ri, :], rhs=Hc[:, (d * 16 + pr) * 2 + ri, 0:nb],
                                                                                     start=False, stop=False, tile_position=(0, 32 * q), skip_group_check=True),
                               r=[B("q_WR"), hcb], w=[pyb])
            for pc in range(4):
                ya = y32[npc % 2]; yab = B(f"g_y{npc % 2}"); yb_ = y2[npc % 2]; ybb = B(f"g_y2{npc % 2}")
                src = py4[:, pc * 512:(pc + 1) * 512].rearrange("p (o c) -> p o c", o=2)[:, :, 0:nb]
                ya3 = ya[:, 0:2 * nb].rearrange("p (o c) -> p o c", o=2)
                yb3 = yb_[:, 0:2 * nb].rearrange("p (o c) -> p o c", o=2)
                dst = vT[:, t, 0:ntok].rearrange("p (c o) -> p o c", o=8)[:, 2 * pc:2 * pc + 2, :]
                A(lambda src=src, ya3=ya3: nc.scalar.copy(out=ya3, in_=src), w=[pyb, yab])
                G(lambda ya=ya, yb_=yb_: nc.gpsimd.tensor_tensor(out=yb_[:, 0:2 * nb], in0=ya[:, 0:2 * nb], in1=ya[:, 0:2 * nb], op=ALU.mult), r=[yab], w=[ybb])
                V(lambda yb_=yb_: nc.vector.tensor_scalar(out=yb_[:, 0:2 * nb], in0=yb_[:, 0:2 * nb], scalar1=0.044715, scalar2=1.0, op0=ALU.mult, op1=ALU.add), w=[ybb])
                V(lambda ya=ya, yb_=yb_: nc.vector.tensor_tensor(out=yb_[:, 0:2 * nb], in0=yb_[:, 0:2 * nb], in1=ya[:, 0:2 * nb], op=ALU.mult), r=[yab], w=[ybb])
                A(lambda yb_=yb_: nc.scalar.activation(out=yb_[:, 0:2 * nb], in_=yb_[:, 0:2 * nb], func=AF.Sigmoid, scale=1.5957691216), w=[ybb])
                V(lambda dst=dst, yb3=yb3, ya3=ya3: nc.vector.tensor_tensor(out=dst, in0=yb3, in1=ya3, op=ALU.mult), r=[yab, ybb], w=[vb])
                npc += 1
        for sc0 in range(0, ntok, 512):
            n_ = min(512, ntok - sc0)
            for ct in range(4):
                pg = psG[ct % 2]; pgb = B(f"psGq{ct % 2}")
                g_ = gs[ct % 2]; gb = B(f"g_gs{ct % 2}")
                s.dma("sp", g_[:, 0:n_], k.PF[32 + ct * 128:32 + (ct + 1) * 128, t0 + sc0:t0 + sc0 + n_], r=[B("PF")], w=[gb])
                for t in range(4):
                    PE(lambda pg=pg, t=t, ct=ct, sc0=sc0, n_=n_: nc.tensor.matmul(pg[:, 0:n_], lhsT=wg[:, t, ct * 128:(ct + 1) * 128], rhs=vT[:, t, sc0:sc0 + n_], start=(t == 0), stop=(t == 3)),
                       r=[B("g_wg"), vb], w=[pgb])
                A(lambda pg=pg, ct=ct, n_=n_: nc.scalar.activation(out=o1[:, 0:n_], in_=pg[:, 0:n_], func=AF.Sigmoid, bias=bg[:, ct:ct + 1]), r=[B("g_bg")], w=[pgb, B("g_o1")])
                V(lambda ct=ct, sc0=sc0, n_=n_: nc.vector.tensor_tensor(out=o1[:, 0:n_], in0=o1[:, 0:n_], in1=vT[:, ct, sc0:sc0 + n_], op=ALU.mult), r=[vb], w=[B("g_o1")])
                A(lambda g_=g_, n_=n_: nc.scalar.activation(out=g_[:, 0:n_], in_=g_[:, 0:n_], func=AF.Silu), w=[gb])
                o_ = ob[ct % 2]; obb = B(f"g_ob{ct % 2}")
                V(lambda o_=o_, g_=g_, n_=n_: nc.vector.tensor_tensor(out=o_[:, 0:n_], in0=o1[:, 0:n_], in1=g_[:, 0:n_], op=ALU.mult), r=[gb, B("g_o1")], w=[obb])
                s.dma("pool", k.MT[1024 + ct * 128:1024 + (ct + 1) * 128, t0 + sc0:t0 + sc0 + n_], o_[:, 0:n_], r=[obb], w=[B("MT")])


D = 1024; T = 4352; TC = 256; TL = 4096; NT = 34; DEPTH = 4
DIN = 4640; EPS = 1e-6

class K:
    pass

def dram(nc, name, shape, dt, kind="Internal"):
    return nc.dram_tensor(name, list(shape), dt, kind=kind).ap()

def build(nlayers=DEPTH, stop=None, dbg=()):
    nc = bass.Bass("TRN2", target_bir_lowering=False)
    s = Sched(nc)
    k = K(); k.nc = nc; k.s = s; k.dbg = dbg; k.stop = stop; k.sfx = ''
    I = {}
    def inp(name, shape):
        I[name] = dram(nc, name, shape, F32, "ExternalInput")
    inp("x", [TL, D]); inp("c", [D]); inp("ctx", [TC, D]); inp("c_ctx", [D])
    inp("w_mod", [DEPTH, D, 3 * D]); inp("b_mod", [DEPTH, 3 * D]); inp("g_pre", [DEPTH, D]); inp("g_post", [DEPTH, D])
    inp("w_in", [DEPTH, D, DIN]); inp("conv_w", [DEPTH, 9, 1536]); inp("conv_b", [DEPTH, 1536])
    inp("dt_bias", [DEPTH, 32]); inp("a_log", [DEPTH, 32]); inp("d_ssd", [DEPTH, 16]); inp("g_ssd_norm", [DEPTH, D])
    inp("s5_lambda_re", [DEPTH, 2, 32, 64]); inp("s5_lambda_im", [DEPTH, 2, 32, 64]); inp("s5_log_step", [DEPTH, 2, 32])
    inp("s5_b_re", [DEPTH, 2, 32, 64, 16]); inp("s5_b_im", [DEPTH, 2, 32, 64, 16])
    inp("s5_c_re", [DEPTH, 2, 32, 16, 64]); inp("s5_c_im", [DEPTH, 2, 32, 16, 64])
    inp("s5_d", [DEPTH, 512]); inp("s5_w_glu", [DEPTH, 512, 512]); inp("s5_b_glu", [DEPTH, 512])
    inp("fnet_w", [DEPTH, 512, 512]); inp("fnet_b", [DEPTH, 512]); inp("w_out", [DEPTH, 2048, D])
    inp("cst", [128, 2304])
    k.I = I
    k.out = dram(nc, "out", [TL, D], F32, "ExternalOutput")
    def scr(name, shape, dt):
        return dram(nc, name, shape, dt, "ExternalOutput" if name in dbg else "Internal")
    k.XS = scr("XS", [T, D], F32)
    k.ZT = scr("ZT", [T, D], F32)
    k.PF = scr("PF", [1056, T], F32)
    k.PB = scr("PB", [2560, T], BF16)
    k.XC = scr("XC", [1536, T], BF16)
    k.MT = scr("MT", [2048, T], BF16)
    k.YF = scr("YF", [T, D], F32)
    k.MODS = scr("MODS", [DEPTH, 2, 3, D], F32)
    k.CL = scr("CL", [4096, 4096], BF16)
    k.HD = scr("HD", [2, 16, 2, 128, 544], BF16)
    k.SLN = scr("SLN", [4096, 4096], BF16)
    k.bufs = {}
    def B(name):
        if name not in k.bufs:
            k.bufs[name] = Buf(name)
        return k.bufs[name]
    k.B = B
    k.cst = nc.alloc_sbuf_tensor("cst_sb", [128, 2304], F32)
    k.identb = nc.alloc_sbuf_tensor("identb", [128, 128], BF16)
    s.dma("sp", k.cst[:], I["cst"][:, :], w=[B("cst")])
    s.op("dve", lambda: nc.vector.tensor_copy(out=k.identb[:], in_=k.cst[:, 0:128]), r=[B("cst")], w=[B("identb")])
    k.negpi = nc.alloc_sbuf_tensor("negpi", [128, 1], F32)
    s.op("dve", lambda: nc.vector.memset(k.negpi[:], -3.14159265), w=[B("negpi")])
    k.pidx_i = nc.alloc_sbuf_tensor("pidx_i", [128, 1], mybir.dt.int32)
    s.op("dve", lambda: nc.vector.tensor_copy(out=k.pidx_i[:], in_=k.cst[:, 139:140]), r=[B("cst")], w=[B("pidx_i")])
    prep_mods(k)
    gen_dft(k)
    for l in range(nlayers):
        k.sfx = f'_L{l}'
        phase_ab(k, l)
        if stop == "ab":
            break
        phase_conv(k, l)
        if stop == "conv":
            break
        phase_ssd(k, l, need_ctx=(l < DEPTH - 1))
        if stop == "ssd":
            break
        phase_s5(k, l, need_ctx=(l < DEPTH - 1))
        if stop in ("s5", "s5prep", "s5main"):
            break
        phase_fnet(k, l, need_ctx=(l < DEPTH - 1))
        if stop == "fnet":
            break
        if stop != "outonly":
            pass
        phase_out(k, l, last=(l == nlayers - 1 and nlayers == DEPTH))
        if stop == "out":
            break
    s.drain("sp")
    return nc


def prep_mods(k):
    nc, s, I, B = k.nc, k.s, k.I, k.B
    with ExitStack() as st:
        T_ = lambda name, shape, dt: st.enter_context(nc.sbuf_tensor(name + k.sfx, shape, dt))
        craw = T_("craw", [128, 8, 2], F32)
        sc = T_("sc", [128, 8, 2], F32)
        wm = [T_(f"wm{i}", [128, 3 * D], F32) for i in range(2)]
        rows = T_("mrows", [2, 3 * D], F32)
        gp = T_("gp", [2, 2, D], F32)
        res = T_("mres", [2, 3, D], F32)
        psM = [st.enter_context(nc.psum_tensor(f"psM{i}", [128, 512], F32)) for i in range(6)]
        s.dma("sp", craw[:, :, 0], I["c"].rearrange("(k p) -> p k", p=128), w=[B("craw")], allow_slow_non_contiguous=True)
        s.dma("sp", craw[:, :, 1], I["c_ctx"].rearrange("(k p) -> p k", p=128), w=[B("craw")], allow_slow_non_contiguous=True)
        s.op("act", lambda: nc.scalar.activation(out=sc[:], in_=craw[:], func=AF.Silu), r=[B("craw")], w=[B("sc")])
        for l in range(DEPTH):
            for kk in range(8):
                w_ = wm[kk % 2]; wb = B(f"wm{kk % 2}")
                s.dma("sp", w_[:], I["w_mod"][l, kk * 128:(kk + 1) * 128, :], w=[wb])
                for n in range(6):
                    s.op("pe", lambda n=n, w_=w_, kk=kk: nc.tensor.matmul(psM[n][0:2, :], lhsT=sc[:, kk, :], rhs=w_[:, n * 512:(n + 1) * 512],
                                                                start=(kk == 0), stop=(kk == 7)), r=[B("sc"), wb], w=[B(f"psM{n}")])
            s.dma("sp", rows[:], I["b_mod"][l:l + 1, :].partition_broadcast(2) if False else I["b_mod"][l, :].partition_broadcast(2), w=[B("mrows")])
            s.dma("sp", gp[:, 0, :], I["g_pre"][l, :].partition_broadcast(2), w=[B("gp")])
            s.dma("sp", gp[:, 1, :], I["g_post"][l, :].partition_broadcast(2), w=[B("gp")])
            for n in range(6):
                s.op("dve", lambda n=n: nc.vector.tensor_tensor(out=rows[:, n * 512:(n + 1) * 512], in0=rows[:, n * 512:(n + 1) * 512],
                                                            in1=psM[n][0:2, :], op=ALU.add), r=[B(f"psM{n}")], w=[B("mrows")])
            s.op("dve", lambda: nc.vector.tensor_copy(out=res[:, 0, :], in_=rows[:, 0:D]), r=[B("mrows")], w=[B("mres")])
            s.op("dve", lambda: nc.vector.scalar_tensor_tensor(out=res[:, 1, :], in0=rows[:, D:2 * D], scalar=1.0, in1=gp[:, 0, :],
                                                             op0=ALU.add, op1=ALU.mult), r=[B("mrows"), B("gp")], w=[B("mres")])
            s.op("dve", lambda: nc.vector.tensor_tensor(out=res[:, 2, :], in0=rows[:, 2 * D:3 * D], in1=gp[:, 1, :], op=ALU.mult),
                 r=[B("mrows"), B("gp")], w=[B("mres")])
            s.dma("sp", k.MODS[l], res[:], r=[B("mres")], w=[B("MODS")])
        s.barrier()


def fm_cols():
    lst = []
    for i in range(12):
        lst.append((1024 + i * 128, 128, "PB", i * 128, BF16))
    lst.append((2560, 32, "PF", 0, F32))
    for i in range(4):
        lst.append((2592 + i * 128, 128, "PB", 1536 + i * 128, BF16))
    for i in range(4):
        lst.append((3104 + i * 128, 128, "PF", 32 + i * 128, F32))
    for i in range(4):
        lst.append((3616 + i * 128, 128, "PB", 2048 + i * 128, BF16))
    for i in range(4):
        lst.append((4128 + i * 128, 128, "PF", 544 + i * 128, F32))
    return lst


def phase_ab(k, l):
    nc, s, I, B = k.nc, k.s, k.I, k.B
    with ExitStack() as st:
        T_ = lambda name, shape, dt: st.enter_context(nc.sbuf_tensor(name + k.sfx, shape, dt))
        P_ = lambda name, shape, dt: st.enter_context(nc.psum_tensor(name + k.sfx, shape, dt))
        wsb = T_("wsb", [128, 8, DIN], BF16)
        wst = [T_(f"wst{i}", [128, DIN], F32) for i in range(2)]
        bc = T_("bcab", [128, 4, D], F32)
        xt = [T_(f"xt{i}", [128, D], F32) for i in range(2)]
        junk = T_("junk", [128, D], BF16)
        h1 = T_("h1", [128, D], F32)
        hl = [T_(f"hl{i}", [128, D], BF16) for i in range(2)]
        hlT = [T_(f"hlT{i}", [128, 8, 512], BF16) for i in range(2)]
        st4 = T_("st4", [128, 8], F32)
        zt = [T_(f"zt{i}", [128, D], F32) for i in range(2)]
        ev32 = [T_(f"ev32_{i}", [128, 512], F32) for i in range(2)]
        ev16 = [T_(f"ev16_{i}", [128, 512], BF16) for i in range(2)]
        psT = P_("psT", [128, 1024], BF16)
        psZ = [P_(f"psZ{i}", [128, 512], F32) for i in range(2)]
        psF = [P_(f"psF{i}", [128, 512], F32) for i in range(3)]
        for kk in range(8):
            w_ = wst[kk % 2]; wb = B(f"wst{kk % 2}")
            s.dma("sp", w_[:], I["w_in"][l, kk * 128:(kk + 1) * 128, :], w=[wb])
            e = "act" if kk % 2 == 0 else "pool"
            if e == "act":
                s.op("act", lambda w_=w_, kk=kk: nc.scalar.copy(out=wsb[:, kk, :], in_=w_[:]), r=[wb], w=[B(f"wsb{kk}")])
            else:
                s.op("pool", lambda w_=w_, kk=kk: nc.gpsimd.tensor_copy(out=wsb[:, kk, :], in_=w_[:]), r=[wb], w=[B(f"wsb{kk}")])
        wsb_bufs = [B(f"wsb{kk}") for kk in range(8)]
        for j, (which, comp) in enumerate([(1, 0), (1, 1), (0, 0), (0, 1)]):
            s.dma("pool", bc[:, j, :], k.MODS[l, which, comp, :].partition_broadcast(128), r=[B("MODS")], w=[B("bcab")])
        cols = fm_cols()
        ngroups = 9
        ev_i = 0
        for g in range(ngroups):
            ntok = 512 if g < 8 else 256
            nt = ntok // 128
            hT = hlT[g % 2]; hTb = B(f"hlT{g % 2}")
            for tt in range(nt):
                ti = g * 4 + tt
                x_ = xt[ti % 2]; xb = B(f"xt{ti % 2}")
                if l == 0:
                    src = I["ctx"][ti * 128:(ti + 1) * 128, :] if ti < 2 else I["x"][(ti - 2) * 128:(ti - 1) * 128, :]
                    s.dma("sp", x_[:], src, w=[xb])
                else:
                    s.dma("sp", x_[:], k.XS[ti * 128:(ti + 1) * 128, :], r=[B("XS")], w=[xb])
                jb = 0 if ti < 2 else 2
                c0 = (ti % 4) * 2
                s.op("act", lambda x_=x_, c0=c0: nc.scalar.activation(out=junk[:], in_=x_[:], func=AF.Square, accum_out=st4[:, c0:c0 + 1]),
                     r=[xb], w=[B("junk"), B(f"st4_{ti % 4}")])
                s.op("dve", lambda c0=c0: nc.vector.tensor_scalar(out=st4[:, c0:c0 + 1], in0=st4[:, c0:c0 + 1], scalar1=1.0 / D, scalar2=EPS,
                                                              op0=ALU.mult, op1=ALU.add), w=[B(f"st4_{ti % 4}")])
                s.op("act", lambda c0=c0: nc.scalar.activation(out=st4[:, c0:c0 + 1], in_=st4[:, c0:c0 + 1], func=AF.Sqrt), w=[B(f"st4_{ti % 4}")])
                s.op("dve", lambda c0=c0: nc.vector.reciprocal(out=st4[:, c0 + 1:c0 + 2], in_=st4[:, c0:c0 + 1]), w=[B(f"st4_{ti % 4}")])
                s.op("dve", lambda x_=x_, c0=c0, jb=jb: nc.vector.scalar_tensor_tensor(out=h1[:], in0=x_[:], scalar=st4[:, c0 + 1:c0 + 2], in1=bc[:, jb + 1, :],
                                                                                 op0=ALU.mult, op1=ALU.mult), r=[xb, B(f"st4_{ti % 4}"), B("bcab")], w=[B("h1")])
                h_ = hl[ti % 2]; hb = B(f"hl{ti % 2}")
                s.op("dve", lambda h_=h_, jb=jb: nc.vector.tensor_tensor(out=h_[:], in0=h1[:], in1=bc[:, jb, :], op=ALU.add), r=[B("h1"), B("bcab")], w=[hb])
                for kk in range(8):
                    s.op("pe", lambda h_=h_, kk=kk: nc.tensor.transpose(out=psT[:, kk * 128:(kk + 1) * 128], in_=h_[:, kk * 128:(kk + 1) * 128], identity=k.identb[:]),
                         r=[hb, B("identb")], w=[B("psT")])
                s.op("act", lambda hT=hT, tt=tt: nc.scalar.copy(out=hT[:, :, tt * 128:(tt + 1) * 128], in_=psT[:].rearrange("p (k t) -> p k t", t=128)),
                     r=[B("psT")], w=[hTb])
                z_ = zt[ti % 2]; zb = B(f"zt{ti % 2}")
                for hh in range(2):
                    pz = psZ[hh]; pzb = B(f"psZ{hh}")
                    for kk in range(8):
                        s.op("pe", lambda pz=pz, hT=hT, kk=kk, tt=tt, hh=hh: nc.tensor.matmul(pz[:], lhsT=hT[:, kk, tt * 128:(tt + 1) * 128], rhs=wsb[:, kk, hh * 512:(hh + 1) * 512],
                                                                                        start=(kk == 0), stop=(kk == 7)), r=[hTb, wsb_bufs[kk]], w=[pzb])
                    if hh == 0:
                        s.op("act", lambda z_=z_, pz=pz: nc.scalar.copy(out=z_[:, 0:512], in_=pz[:]), r=[pzb], w=[zb])
                    else:
                        s.op("dve", lambda z_=z_, pz=pz: nc.vector.tensor_copy(out=z_[:, 512:1024], in_=pz[:]), r=[pzb], w=[zb])
                s.dma("pool", k.ZT[ti * 128:(ti + 1) * 128, :], z_[:], r=[zb], w=[B("ZT")])
            t0 = g * 512
            for ci, (c0, wd, dest, r0, dt_) in enumerate(cols):
                pf = psF[ci % 3]; pfb = B(f"psF{ci % 3}")
                for kk in range(8):
                    s.op("pe", lambda pf=pf, hT=hT, kk=kk, c0=c0, wd=wd, ntok=ntok: nc.tensor.matmul(pf[0:wd, 0:ntok], lhsT=wsb[:, kk, c0:c0 + wd], rhs=hT[:, kk, 0:ntok],
                                                                                             start=(kk == 0), stop=(kk == 7)), r=[hTb, wsb_bufs[kk]], w=[pfb])
                ev = (ev32 if dt_ == F32 else ev16)[ev_i % 2]
                evb = B(("ev32_" if dt_ == F32 else "ev16_") + str(ev_i % 2))
                if ev_i % 2 == 0:
                    s.op("act", lambda ev=ev, pf=pf, wd=wd, ntok=ntok: nc.scalar.copy(out=ev[0:wd, 0:ntok], in_=pf[0:wd, 0:ntok]), r=[pfb], w=[evb])
                else:
                    s.op("dve", lambda ev=ev, pf=pf, wd=wd, ntok=ntok: nc.vector.tensor_copy(out=ev[0:wd, 0:ntok], in_=pf[0:wd, 0:ntok]), r=[pfb], w=[evb])
                dst = getattr(k, dest)
                s.dma("pool" if ev_i % 2 == 0 else "sp", dst[r0:r0 + wd, t0:t0 + ntok], ev[0:wd, 0:ntok], r=[evb], w=[B(dest)])
                ev_i += 1
        s.barrier()


def _consts():
    c = np.zeros((128, 2304), np.float32)
    c[:, 0:128] = np.eye(128)
    c[:, 128:137] = np.arange(9)
    p = np.arange(128)
    c[:, 137] = (p % 32) < 16
    c[:, 138] = (p % 32) >= 16
    c[:, 139] = p
    c[:, 140:268] = 1.0
    jj, ii = np.meshgrid(p, p, indexing="ij")
    c[:, 268:396] = np.where(jj > ii, -30000.0, 0.0)
    c[:, 396:524] = np.where(jj < ii, -30000.0, 0.0)
    c[:, 524:532] = (p[:, None] // 16) == np.arange(8)[None]
    c[:, 1024:1568] = np.arange(544)[None]
    cn = np.arange(544)
    c[:, 1600:2144] = np.where(cn < 32, 31 - cn, 575 - cn)[None]
    return c


def kernel(**inputs):
    inp = {k_: np.asarray(v) for k_, v in inputs.items()}
    cst = _consts()
    shared = {}
    for name, v in inp.items():
        if name in ("x", "c", "ctx"):
            continue
        v = np.ascontiguousarray(v, dtype=np.float32)
        if name == "conv_w":
            v = np.ascontiguousarray(v.reshape(4, 9, 1536))
        elif name in ("dt_bias", "a_log"):
            v = np.ascontiguousarray(v.reshape(4, 32))
        shared[name] = v
    shared["cst"] = cst
    in_maps = []
    for core in range(8):
        b = core % 4
        m = dict(shared)
        m["x"] = np.ascontiguousarray(inp["x"][b], dtype=np.float32)
        m["c"] = np.ascontiguousarray(inp["c"][b], dtype=np.float32)
        m["ctx"] = np.ascontiguousarray(inp["ctx"][b], dtype=np.float32)
        in_maps.append(m)
    nc = build()
    res = run_bass_kernel_spmd(nc, in_maps, core_ids=list(range(8)))
    out = np.stack([np.asarray(res.results[b]["out"], dtype=np.float32) for b in range(4)], axis=0)
    return out
```

```python
import math
from contextlib import ExitStack
import numpy as np
import concourse.bass as bass
import concourse.mybir as mybir
from concourse.bass_utils import run_bass_kernel_spmd

F32 = mybir.dt.float32
BF16 = mybir.dt.bfloat16
AF = mybir.ActivationFunctionType
ALU = mybir.AluOpType
AX = mybir.AxisListType


class Buf:
    __slots__ = ("w", "r", "name")

    def __init__(self, name=""):
        self.w = None
        self.r = {}
        self.name = name


class Sched:
    NR = 8

    def __init__(self, nc):
        self.nc = nc
        self.eng = {"pe": nc.tensor, "act": nc.scalar, "dve": nc.vector,
                    "pool": nc.gpsimd, "sp": nc.sync}
        self.csem = {e: nc.alloc_semaphore("c_" + e) for e in ("pe", "act", "dve", "pool")}
        self.ccnt = {e: 0 for e in self.csem}
        self.dq = {}
        for q in ("sp", "act", "pool"):
            self.dq[q] = dict(sems=[nc.alloc_semaphore(f"d_{q}{i}") for i in range(self.NR)],
                              n=0, tk=[None] * self.NR)
        self.waited = {}
        self.ninstr = 0

    def _wait(self, e, tk):
        if tk is None:
            return
        key, sem, val = tk
        if key == "pe" and e == "pe":
            return
        if self.waited.get((e, key), 0) >= val:
            return
        self.eng[e].wait_ge(sem, val)
        self.waited[(e, key)] = val

    def _deps(self, e, r, w):
        for b in r:
            self._wait(e, b.w)
        for b in w:
            self._wait(e, b.w)
            for t in list(b.r.values()):
                self._wait(e, t)

    def _record(self, tk, r, w):
        for b in r:
            b.r[tk[0]] = tk
        for b in w:
            b.w = tk
            b.r = {}

    def op(self, e, fn, r=(), w=()):
        self._deps(e, r, w)
        ins = fn()
        self.ccnt[e] += 1
        ins.then_inc(self.csem[e], 1)
        tk = (e, self.csem[e], self.ccnt[e])
        self._record(tk, r, w)
        self.ninstr += 1
        return tk

    def dma(self, q, out, in_, r=(), w=(), **kw):
        d = self.dq[q]
        slot = d["n"] % self.NR
        self._wait(q, d["tk"][slot])
        self._deps(q, r, w)
        ins = self.eng[q].dma_start(out=out, in_=in_, **kw)
        ins.then_inc(d["sems"][slot], 16)
        val = 16 * (d["n"] // self.NR + 1)
        tk = (("d", q, slot), d["sems"][slot], val)
        d["tk"][slot] = tk
        d["n"] += 1
        self._record(tk, r, w)
        self.ninstr += 1
        return tk

    def op_cc(self, fn, r=(), w=()):
        q = "pool"
        d = self.dq[q]
        slot = d["n"] % self.NR
        self._wait(q, d["tk"][slot])
        self._deps(q, r, w)
        ins = fn()
        ins.then_inc(d["sems"][slot], 16)
        val = 16 * (d["n"] // self.NR + 1)
        tk = (("d", q, slot), d["sems"][slot], val)
        d["tk"][slot] = tk
        d["n"] += 1
        self._record(tk, r, w)
        return tk

    def all_tickets(self):
        tks = []
        for e in self.csem:
            if self.ccnt[e]:
                tks.append((e, self.csem[e], self.ccnt[e]))
        for q, d in self.dq.items():
            for t in d["tk"]:
                if t is not None:
                    tks.append(t)
        return tks

    def barrier(self, engines=("pe", "act", "dve", "pool", "sp")):
        tks = self.all_tickets()
        for e in engines:
            for t in tks:
                self._wait(e, t)

    def drain(self, e="sp"):
        for t in self.all_tickets():
            if t[0] == "pe" and e == "pe":
                continue
            self._wait(e, t)

T = 4352

def phase_conv(k, l):
    nc, s, I, B = k.nc, k.s, k.I, k.B
    with ExitStack() as st:
        T_ = lambda name, shape, dt: st.enter_context(nc.sbuf_tensor(name + k.sfx, shape, dt))
        P_ = lambda name, shape, dt: st.enter_context(nc.psum_tensor(name + k.sfx, shape, dt))
        cw9 = T_("cw9", [9, 1536], F32)
        cwT = T_("cwT", [128, 108], F32)
        cb = T_("cb", [128, 12], F32)
        dg = T_("dg", [128, 108, 128], BF16)
        xp = [T_(f"xp{i}", [128, 258 + 66 * 66], BF16) for i in range(2)]
        ev = [T_(f"cev{i}", [128, 512], BF16) for i in range(2)]
        psW = P_("psW", [128, 108], F32)
        psC = [P_(f"psC{i}", [128, 512], F32) for i in range(2)]
        s.dma("sp", cw9[:], I["conv_w"][l], w=[B("cw9")])
        s.dma("sp", cb[:], I["conv_b"][l].rearrange("(t p) -> p t", p=128), w=[B("cb")], allow_slow_non_contiguous=True)
        for t in range(12):
            s.op("pe", lambda t=t: nc.tensor.transpose(out=psW[:, t * 9:(t + 1) * 9], in_=cw9[:, t * 128:(t + 1) * 128], identity=k.cst[0:9, 0:9]),
                 r=[B("cw9"), B("cst")], w=[B("psW")])
        s.op("dve", lambda: nc.vector.tensor_copy(out=cwT[:], in_=psW[:]), r=[B("psW")], w=[B("cwT")])
        for j in range(108):
            e = "dve" if j % 2 == 0 else "pool"
            eng = nc.vector if e == "dve" else nc.gpsimd
            s.op(e, lambda j=j, eng=eng: eng.tensor_scalar(out=dg[:, j, :], in0=k.identb[:], scalar1=cwT[:, j:j + 1], scalar2=None, op0=ALU.mult),
                 r=[B("cwT"), B("identb")], w=[B(f"dg{j}")])
        for i in range(2):
            s.op("pool", lambda i=i: nc.gpsimd.memset(xp[i][:], 0.0), w=[B(f"xp{i}")])
        n = 0
        for t in range(12):
            x_ = xp[t % 2]; xb = B(f"xp{t % 2}")
            rows = k.PB[t * 128:(t + 1) * 128, :]
            s.dma("sp", x_[:, 1:257], rows[:, 0:256], r=[B("PB")], w=[xb])
            grid = x_[:, 258:258 + 4356].rearrange("p (r c) -> p r c", c=66)
            for hh in range(2):
                s.dma("sp", grid[:, 1 + hh * 32:33 + hh * 32, 1:65], rows[:, 256 + hh * 2048:256 + (hh + 1) * 2048].rearrange("p (r c) -> p r c", c=64), r=[B("PB")], w=[xb])
            for rg in range(9):
                pc = psC[n % 2]; pcb = B(f"psC{n % 2}")
                if rg < 8:
                    for tap in range(9):
                        ky, kx = tap // 3, tap % 3
                        rhs = grid[:, rg * 8 + ky:rg * 8 + ky + 8, kx:kx + 64]
                        s.op("pe", lambda pc=pc, t=t, tap=tap, rhs=rhs: nc.tensor.matmul(pc[:].rearrange("p (r c) -> p r c", c=64), lhsT=dg[:, t * 9 + tap, :], rhs=rhs,
                                                                              start=(tap == 0), stop=(tap == 8)), r=[xb, B(f"dg{t * 9 + tap}")], w=[pcb])
                    ntok = 512; t0 = 256 + rg * 512
                else:
                    for kx in range(3):
                        s.op("pe", lambda pc=pc, t=t, kx=kx: nc.tensor.matmul(pc[:, 0:256], lhsT=dg[:, t * 9 + 3 + kx, :], rhs=x_[:, kx:kx + 256],
                                                                    start=(kx == 0), stop=(kx == 2)), r=[xb, B(f"dg{t * 9 + 3 + kx}")], w=[pcb])
                    ntok = 256; t0 = 0
                e_ = ev[n % 2]; eb = B(f"cev{n % 2}")
                s.op("act", lambda e_=e_, pc=pc, t=t, ntok=ntok: nc.scalar.activation(out=e_[:, 0:ntok], in_=pc[:, 0:ntok], func=AF.Silu, bias=cb[:, t:t + 1]),
                     r=[pcb, B("cb")], w=[eb])
                s.dma("pool" if n % 2 == 0 else "sp", k.XC[t * 128:(t + 1) * 128, t0:t0 + ntok], e_[:, 0:ntok], r=[eb], w=[B("XC")])
                n += 1
        s.barrier()

T = 4352; NCH = 34; D = 1024; EPS = 1e-6
C_MF = 137; C_MB = 138; C_ONES = 140; C_NEGF = 268; C_NEGB = 396; C_BM8 = 524; C_IOTA = 1024

def phase_ssd(k, l, need_ctx=True):
    nc, s, I, B = k.nc, k.s, k.I, k.B
    cst = k.cst
    with ExitStack() as st:
        T_ = lambda name, shape, dt: st.enter_context(nc.sbuf_tensor(name + k.sfx, shape, dt))
        P_ = lambda name, shape, dt: st.enter_context(nc.psum_tensor(name + k.sfx, shape, dt))
        pst = ExitStack()
        TP_ = lambda name, shape, dt: pst.enter_context(nc.sbuf_tensor(name + k.sfx, shape, dt))
        acs = T_("s_acs", [128, T], F32)
        nacs = T_("s_nacs", [128, T], F32)
        Q = T_("s_Q", [128, T], F32)
        colp = T_("s_colp", [128, 4], F32)
        tot = T_("s_tot", [128, NCH], F32)
        DT = T_("s_DT", [32, NCH, 32], F32)
        cdall = T_("s_cd", [128, NCH, 32], F32)
        Esel2 = T_("s_Esel2", [64, 32, 128], BF16)
        hl = T_("s_hl", [64, T], BF16)
        nhl = T_("s_nhl", [64, T], BF16)
        negm = T_("s_negm", [128, 2, 512], BF16)
        dskrow = T_("s_dskrow", [128, 16], F32)
        gnb = T_("s_gnb", [128, D], F32)
        dt_ = TP_("s_dt", [128, T], F32)
        a_ = TP_("s_a", [128, T], F32)
        cum = TP_("s_cum", [128, T], F32)
        psCB = P_("psCB", [128, 256], F32)
        psX = P_("psX", [128, 1024], BF16)
        psB = P_("psB", [128, 256], BF16)
        psE = P_("psE", [128, 512], F32)
        psY = [P_(f"psY{i}", [128, 512], F32) for i in range(2)]
        psS1 = P_("psSst", [128, 512], F32)
        psO1 = P_("psOst", [128, 512], F32)

        V = lambda fn, r=(), w=(): s.op("dve", fn, r=r, w=w)
        A = lambda fn, r=(), w=(): s.op("act", fn, r=r, w=w)
        G = lambda fn, r=(), w=(): s.op("pool", fn, r=r, w=w)
        PE = lambda fn, r=(), w=(): s.op("pe", fn, r=r, w=w)
        for q in range(4):
            s.dma("sp", dt_[32 * q:32 * q + 32, :], k.PF[0:32, :], r=[B("PF")], w=[B("s_dt")])
            s.dma("sp", colp[32 * q:32 * q + 32, 0:1], I["dt_bias"][l].rearrange("(p o) -> p o", o=1), w=[B("s_colp")])
            s.dma("sp", colp[32 * q:32 * q + 32, 1:2], I["a_log"][l].rearrange("(p o) -> p o", o=1), w=[B("s_colp")])
        s.dma("sp", dskrow[:], I["d_ssd"][l].partition_broadcast(128), w=[B("s_dskrow")])
        s.dma("sp", gnb[:], I["g_ssd_norm"][l].partition_broadcast(128), w=[B("s_gnb")])
        A(lambda: nc.scalar.activation(out=colp[:, 2:3], in_=colp[:, 1:2], func=AF.Exp), w=[B("s_colp")])
        V(lambda: nc.vector.tensor_scalar(out=colp[:, 3:4], in0=colp[:, 2:3], scalar1=-1.0, scalar2=None, op0=ALU.mult), w=[B("s_colp")])
        A(lambda: nc.scalar.activation(out=dt_[:], in_=dt_[:], func=AF.Exp, bias=colp[:, 0:1]), r=[B("s_colp")], w=[B("s_dt")])
        A(lambda: nc.scalar.activation(out=dt_[:], in_=dt_[:], func=AF.Ln, bias=1.0), w=[B("s_dt")])
        V(lambda: nc.vector.tensor_scalar(out=a_[:], in0=dt_[:], scalar1=colp[:, 3:4], scalar2=None, op0=ALU.mult), r=[B("s_dt"), B("s_colp")], w=[B("s_a")])
        for c in range(NCH):
            V(lambda c=c: nc.vector.tensor_tensor_scan(out=cum[:, c * 128:(c + 1) * 128], data0=cst[:, C_ONES:C_ONES + 128], data1=a_[:, c * 128:(c + 1) * 128],
                                                   initial=0.0, op0=ALU.mult, op1=ALU.add), r=[B("s_a"), B("cst")], w=[B("s_cum")])
        cum3 = cum[:].rearrange("p (c i) -> p c i", i=128)
        V(lambda: nc.vector.tensor_copy(out=tot[:], in_=cum3[:, :, 127]), r=[B("s_cum")], w=[B("s_tot")])
        totb = tot[:].unsqueeze(2).to_broadcast([128, NCH, 128])
        V(lambda: nc.vector.tensor_tensor(out=nacs[:], in0=a_[:], in1=cum[:], op=ALU.subtract), r=[B("s_a"), B("s_cum")], w=[B("s_nacs")])
        V(lambda: nc.vector.tensor_tensor(out=nacs[:].rearrange("p (c i) -> p c i", i=128), in0=nacs[:].rearrange("p (c i) -> p c i", i=128), in1=totb, op=ALU.add),
          r=[B("s_tot")], w=[B("s_nacs")])
        V(lambda: nc.vector.tensor_scalar(out=acs[:], in0=cum[:], scalar1=cst[:, C_MF:C_MF + 1], scalar2=None, op0=ALU.mult), r=[B("s_cum"), B("cst")], w=[B("s_acs")])
        V(lambda: nc.vector.scalar_tensor_tensor(out=acs[:], in0=nacs[:], scalar=cst[:, C_MB:C_MB + 1], in1=acs[:], op0=ALU.mult, op1=ALU.add), r=[B("s_nacs")], w=[B("s_acs")])
        V(lambda: nc.vector.tensor_scalar(out=nacs[:], in0=acs[:], scalar1=-1.0, scalar2=None, op0=ALU.mult), r=[B("s_acs")], w=[B("s_nacs")])
        V(lambda: nc.vector.tensor_tensor(out=a_[:].rearrange("p (c i) -> p c i", i=128), in0=nacs[:].rearrange("p (c i) -> p c i", i=128), in1=totb, op=ALU.add),
          r=[B("s_nacs"), B("s_tot")], w=[B("s_a")])
        A(lambda: nc.scalar.activation(out=a_[:], in_=a_[:], func=AF.Exp), w=[B("s_a")])
        V(lambda: nc.vector.tensor_tensor(out=a_[:], in0=a_[:], in1=dt_[:], op=ALU.mult), r=[B("s_dt")], w=[B("s_a")])
        A(lambda: nc.scalar.activation(out=cum[:], in_=acs[:], func=AF.Exp), r=[B("s_acs")], w=[B("s_cum")])
        G(lambda: nc.gpsimd.tensor_copy(out=Q[0:32, :], in_=dt_[0:32, :]), r=[B("s_dt")], w=[B("s_Q")])
        G(lambda: nc.gpsimd.tensor_copy(out=Q[32:64, :], in_=a_[32:64, :]), r=[B("s_a")], w=[B("s_Q")])
        G(lambda: nc.gpsimd.tensor_copy(out=Q[64:96, :], in_=cum[64:96, :]), r=[B("s_cum")], w=[B("s_Q")])
        G(lambda: nc.gpsimd.tensor_copy(out=Q[96:128, :], in_=nacs[96:128, :]), r=[B("s_nacs")], w=[B("s_Q")])
        V(lambda: nc.vector.tensor_copy(out=Esel2[0:32], in_=cst[0:32, 0:32].unsqueeze(2).to_broadcast([32, 32, 128])), r=[B("cst")], w=[B("s_Esel")])
        V(lambda: nc.vector.tensor_copy(out=Esel2[32:64], in_=cst[32:64, 32:64].unsqueeze(2).to_broadcast([32, 32, 128])), r=[B("cst")], w=[B("s_Esel")])
        V(lambda: nc.vector.tensor_copy(out=hl[0:32, :], in_=acs[0:32, :]), r=[B("s_acs")], w=[B("s_hl")])
        V(lambda: nc.vector.tensor_copy(out=nhl[32:64, :], in_=acs[32:64, :]), r=[B("s_acs")], w=[B("s_nhl")])
        V(lambda: nc.vector.tensor_tensor(out=hl[32:64, :], in0=acs[32:64, :], in1=nhl[32:64, :], op=ALU.subtract), r=[B("s_acs"), B("s_nhl")], w=[B("s_hl")])
        V(lambda: nc.vector.tensor_scalar(out=nhl[:], in0=hl[:], scalar1=-1.0, scalar2=None, op0=ALU.mult), r=[B("s_hl")], w=[B("s_nhl")])
        V(lambda: nc.vector.tensor_copy(out=negm[:, 0, :].rearrange("p (a i) -> p a i", i=128), in_=cst[:, C_NEGF:C_NEGF + 128].unsqueeze(1).to_broadcast([128, 4, 128])), r=[B("cst")], w=[B("s_negm")])
        V(lambda: nc.vector.tensor_copy(out=negm[:, 1, :].rearrange("p (a i) -> p a i", i=128), in_=cst[:, C_NEGB:C_NEGB + 128].unsqueeze(1).to_broadcast([128, 4, 128])), r=[B("cst")], w=[B("s_negm")])
        V(lambda: nc.vector.tensor_tensor(out=DT[:], in0=cst[0:32, 0:32].unsqueeze(1).to_broadcast([32, NCH, 32]), in1=tot[0:32, :].unsqueeze(2).to_broadcast([32, NCH, 32]), op=ALU.mult),
          r=[B("s_tot"), B("cst")], w=[B("s_DT")])
        for c0 in range(0, NCH, 16):
            n = min(16, NCH - c0)
            PE(lambda c0=c0, n=n: nc.tensor.matmul(psE[:, 0:n * 32], lhsT=cst[0:32, C_ONES:C_ONES + 128], rhs=DT[:, c0:c0 + n, :].rearrange("p c k -> p (c k)"), start=True, stop=True),
               r=[B("s_DT"), B("cst")], w=[B("psE")])
            A(lambda c0=c0, n=n: nc.scalar.activation(out=cdall[:, c0:c0 + n, :].rearrange("p c k -> p (c k)"), in_=psE[:, 0:n * 32], func=AF.Exp), r=[], w=[B("psE"), B("s_cd")])
        s.barrier()
        pst.close()
        hst = [T_(f"s_h{d}", [128, D], F32) for d in range(2)]
        hbf = [T_(f"s_hb{d}", [128, D], BF16) for d in range(2)]
        xT = [T_(f"s_xT{i}", [128, 8, 128], BF16) for i in range(2)]
        BC = [T_(f"s_BC{i}", [128, 4, 128], BF16) for i in range(2)]
        tmq = [T_(f"s_tmq{i}", [128, 128], F32) for i in range(2)]
        xdt = [T_(f"s_xdt{i}", [128, D], BF16) for i in range(2)]
        xw = [T_(f"s_xw{i}", [128, D], BF16) for i in range(2)]
        Btok = [T_(f"s_Btok{i}", [128, 256], BF16) for i in range(2)]
        dec = [T_(f"s_dec{i}", [128, 512], F32) for i in range(2)]
        MTt = [T_(f"s_MT{i}", [128, 16, 128], BF16) for i in range(2)]
        ydg = [T_(f"s_ydg{i}", [128, D], F32) for i in range(2)]
        Sc = [T_(f"s_Sc{i}", [128, D], F32) for i in range(2)]
        xds = [T_(f"s_xds{i}", [128, D], F32) for i in range(2)]
        tmp = T_("s_tmp", [128, 512], F32)
        yt = [T_(f"s_y{i}", [128, D], F32) for i in range(2)]
        yf = [T_(f"s_yf{i}", [128, D], F32) for i in range(2)]
        zt = [T_(f"s_z{i}", [128, D], F32) for i in range(2)]
        sz = T_("s_sz", [128, D], F32)
        junk = T_("s_junk", [128, 512], BF16)
        st2 = T_("s_st2", [128, 4], F32)
        mo = T_("s_mo", [128, D], BF16)
        mT = [T_(f"s_mT{i}", [128, 8, 128], BF16) for i in range(2)]
        def stageA(d, n_, c):
            sl = n_ % 2
            t0 = c * 128
            x_ = xT[sl]; xb = B(f"s_xT{sl}"); bc_ = BC[sl]; bcb = B(f"s_BC{sl}")
            tq = tmq[sl]; tqb = B(f"s_tmq{sl}")
            s.dma("sp", x_[:], k.XC[0:1024, t0:t0 + 128].rearrange("(t p) j -> p t j", p=128), r=[B("XC")], w=[xb])
            s.dma("sp", bc_[:], k.XC[1024:1536, t0:t0 + 128].rearrange("(t p) j -> p t j", p=128), r=[B("XC")], w=[bcb])
            if d == 1:
                s.dma("sp", zt[sl][:], k.ZT[t0:t0 + 128, :], r=[B("ZT")], w=[B(f"s_z{sl}")])
                s.dma("sp", yf[sl][:], k.YF[t0:t0 + 128, :], r=[B("YF")], w=[B(f"s_yf{sl}")])
            PE(lambda: nc.tensor.transpose(out=psE[:, 0:128], in_=Q[:, t0:t0 + 128], identity=cst[:, 0:128]), r=[B("s_Q"), B("cst")], w=[B("psE")])
            A(lambda: nc.scalar.copy(out=tq[:], in_=psE[:, 0:128]), w=[B("psE"), tqb])
            for t in range(8):
                PE(lambda t=t: nc.tensor.transpose(out=psX[:, t * 128:(t + 1) * 128], in_=x_[:, t, :], identity=k.identb[:]), r=[xb, B("identb")], w=[B("psX")])
            for g in range(2):
                PE(lambda g=g: nc.tensor.transpose(out=psB[:, g * 128:(g + 1) * 128], in_=bc_[:, g, :], identity=k.identb[:]), r=[bcb, B("identb")], w=[B("psB")])
                PE(lambda g=g: nc.tensor.matmul(psCB[:, g * 128:(g + 1) * 128], lhsT=bc_[:, g, :], rhs=bc_[:, 2 + g, :], start=True, stop=True), r=[bcb], w=[B("psCB")])
            A(lambda: nc.scalar.copy(out=Btok[sl][:], in_=psB[:]), w=[B("psB"), B(f"s_Btok{sl}")])
            yield
            psX3 = psX[:].rearrange("p (h e) -> p h e", e=64)
            V(lambda: nc.vector.tensor_tensor(out=xdt[sl][:].rearrange("p (h e) -> p h e", e=64), in0=psX3, in1=tq[:, d * 16:d * 16 + 16].unsqueeze(2).to_broadcast([128, 16, 64]), op=ALU.mult),
              r=[tqb], w=[B("psX"), B(f"s_xdt{sl}")])
            V(lambda: nc.vector.tensor_tensor(out=xw[sl][:].rearrange("p (h e) -> p h e", e=64), in0=psX3, in1=tq[:, 32 + d * 16:48 + d * 16].unsqueeze(2).to_broadcast([128, 16, 64]), op=ALU.mult),
              r=[tqb], w=[B("psX"), B(f"s_xw{sl}")])
            if d == 0:
                V(lambda: nc.vector.tensor_tensor(out=xds[sl][:].rearrange("p (h e) -> p h e", e=64), in0=psX3, in1=dskrow[:].unsqueeze(2).to_broadcast([128, 16, 64]), op=ALU.mult),
                  r=[B("s_dskrow")], w=[B("psX"), B(f"s_xds{sl}")])
            yield
            for hq in range(4):
                g = hq // 2
                PE(lambda: nc.tensor.matmul(psE[:], lhsT=k.identb[:], rhs=negm[:, d, :], start=True, stop=False, skip_group_check=True), r=[B("s_negm"), B("identb")], w=[B("psE")])
                for hh in range(4):
                    dh = d * 16 + hq * 4 + hh
                    o_ = psE[:, hh * 128:(hh + 1) * 128]
                    PE(lambda o_=o_, dh=dh: nc.tensor.matmul(o_, lhsT=Esel2[:, dh, :], rhs=hl[:, t0:t0 + 128], start=False, stop=False, skip_group_check=True), r=[B("s_Esel"), B("s_hl")], w=[B("psE")])
                    PE(lambda o_=o_, dh=dh, hh=hh: nc.tensor.matmul(o_, lhsT=nhl[:, t0:t0 + 128], rhs=Esel2[:, dh, :], start=False, stop=(hh == 3), skip_group_check=True), r=[B("s_Esel"), B("s_nhl")], w=[B("psE")])
                dc = dec[hq % 2]; dcb = B(f"s_dec{hq % 2}")
                A(lambda dc=dc: nc.scalar.activation(out=dc[:], in_=psE[:], func=AF.Exp), w=[B("psE"), dcb])
                V(lambda dc=dc, hq=hq, g=g: nc.vector.tensor_tensor(out=MTt[sl][:, hq * 4:hq * 4 + 4, :], in0=dc[:].rearrange("p (a i) -> p a i", i=128),
                                                               in1=psCB[:, g * 128:(g + 1) * 128].unsqueeze(1).to_broadcast([128, 4, 128]), op=ALU.mult),
                  r=[dcb], w=[B("psCB"), B(f"s_MT{sl}_{hq}")])
                yield
            for h in range(16):
                py = psY[h // 8]
                PE(lambda h=h, py=py: nc.tensor.matmul(py[:, (h % 8) * 64:(h % 8) * 64 + 64], lhsT=MTt[sl][:, h, :], rhs=xdt[sl][:, h * 64:(h + 1) * 64], start=(h % 8 == 0), stop=(h % 8 == 7), skip_group_check=True),
                   r=[B(f"s_MT{sl}_{h // 4}"), B(f"s_xdt{sl}")], w=[B(f"psY{h // 8}")])
            yield
            A(lambda: nc.scalar.copy(out=ydg[sl][:, 0:512], in_=psY[0][:]), w=[B("psY0"), B(f"s_ydg{sl}")])
            V(lambda: nc.vector.tensor_copy(out=ydg[sl][:, 512:1024], in_=psY[1][:]), w=[B("psY1"), B(f"s_ydg{sl}")])
            if d == 1:
                G(lambda: nc.gpsimd.tensor_tensor(out=ydg[sl][:], in0=ydg[sl][:], in1=yf[sl][:], op=ALU.add), r=[B(f"s_yf{sl}")], w=[B(f"s_ydg{sl}")])
            else:
                G(lambda: nc.gpsimd.tensor_tensor(out=ydg[sl][:], in0=ydg[sl][:], in1=xds[sl][:], op=ALU.add), r=[B(f"s_xds{sl}")], w=[B(f"s_ydg{sl}")])
            yield
            for g in range(2):
                PE(lambda g=g: nc.tensor.matmul(psS1[:], lhsT=Btok[sl][:, g * 128:(g + 1) * 128], rhs=xw[sl][:, g * 512:(g + 1) * 512], start=True, stop=True),
                   r=[B(f"s_Btok{sl}"), B(f"s_xw{sl}")], w=[B("psSst")])
                if g == 0:
                    A(lambda: nc.scalar.copy(out=Sc[sl][:, 0:512], in_=psS1[:]), w=[B("psSst"), B(f"s_Sc{sl}")])
                else:
                    V(lambda: nc.vector.tensor_copy(out=Sc[sl][:, 512:1024], in_=psS1[:]), w=[B("psSst"), B(f"s_Sc{sl}")])
                yield

        def stageB(d, n_, c):
            sl = n_ % 2
            t0 = c * 128
            hb_ = B(f"s_h{d}"); hbb = B(f"s_hb{d}")
            bc_ = BC[sl]; bcb = B(f"s_BC{sl}")
            tq = tmq[sl]; tqb = B(f"s_tmq{sl}")
            y_ = yt[sl]; yb = B(f"s_y{sl}")
            for g in range(2):
                PE(lambda g=g: nc.tensor.matmul(psO1[:], lhsT=bc_[:, 2 + g, :], rhs=hbf[d][:, g * 512:(g + 1) * 512], start=True, stop=True), r=[bcb, hbb], w=[B("psOst")])
                V(lambda g=g: nc.vector.tensor_tensor(out=tmp[:].rearrange("p (h e) -> p h e", e=64), in0=psO1[:].rearrange("p (h e) -> p h e", e=64),
                                                   in1=tq[:, 64 + d * 16 + g * 8:64 + d * 16 + g * 8 + 8].unsqueeze(2).to_broadcast([128, 8, 64]), op=ALU.mult),
                  r=[tqb], w=[B("psOst"), B("s_tmp")])
                G(lambda g=g: nc.gpsimd.tensor_tensor(out=y_[:, g * 512:(g + 1) * 512], in0=tmp[:], in1=ydg[sl][:, g * 512:(g + 1) * 512], op=ALU.add), r=[B("s_tmp"), B(f"s_ydg{sl}")], w=[yb])
                yield
            G(lambda: nc.gpsimd.tensor_tensor(out=hst[d][:].rearrange("p (h e) -> p h e", e=64), in0=hst[d][:].rearrange("p (h e) -> p h e", e=64),
                                               in1=cdall[:, c, d * 16:d * 16 + 16].unsqueeze(2).to_broadcast([128, 16, 64]), op=ALU.mult), r=[B("s_cd")], w=[hb_])
            G(lambda: nc.gpsimd.tensor_tensor(out=hst[d][:], in0=hst[d][:], in1=Sc[sl][:], op=ALU.add), r=[B(f"s_Sc{sl}")], w=[hb_])
            A(lambda: nc.scalar.copy(out=hbf[d][:], in_=hst[d][:]), r=[hb_], w=[hbb])
            yield
            if d == 0:
                s.dma("pool", k.YF[t0:t0 + 128, :], y_[:], r=[yb], w=[B("YF")])
            elif need_ctx or c >= 2:
                z_ = zt[sl]; zb = B(f"s_z{sl}")
                A(lambda: nc.scalar.activation(out=sz[:], in_=z_[:], func=AF.Silu), r=[zb], w=[B("s_sz")])
                V(lambda: nc.vector.tensor_tensor(out=y_[:], in0=y_[:], in1=sz[:], op=ALU.mult), r=[B("s_sz")], w=[yb])
                for g in range(2):
                    A(lambda g=g: nc.scalar.activation(out=junk[:], in_=y_[:, g * 512:(g + 1) * 512], func=AF.Square, accum_out=st2[:, g:g + 1]), r=[yb], w=[B("s_junk"), B("s_st2")])
                yield
                V(lambda: nc.vector.tensor_scalar(out=st2[:, 0:2], in0=st2[:, 0:2], scalar1=1.0 / 512, scalar2=EPS, op0=ALU.mult, op1=ALU.add), w=[B("s_st2")])
                A(lambda: nc.scalar.activation(out=st2[:, 0:2], in_=st2[:, 0:2], func=AF.Sqrt), w=[B("s_st2")])
                V(lambda: nc.vector.reciprocal(out=st2[:, 2:4], in_=st2[:, 0:2]), w=[B("s_st2")])
                for g in range(2):
                    V(lambda g=g: nc.vector.scalar_tensor_tensor(out=mo[:, g * 512:(g + 1) * 512], in0=y_[:, g * 512:(g + 1) * 512], scalar=st2[:, 2 + g:3 + g], in1=gnb[:, g * 512:(g + 1) * 512],
                                                               op0=ALU.mult, op1=ALU.mult), r=[yb, B("s_st2"), B("s_gnb")], w=[B("s_mo")])
                yield
                for t in range(8):
                    PE(lambda t=t: nc.tensor.transpose(out=psX[:, t * 128:(t + 1) * 128], in_=mo[:, t * 128:(t + 1) * 128], identity=k.identb[:]), r=[B("s_mo"), B("identb")], w=[B("psX")])
                m_ = mT[sl]; mb_ = B(f"s_mT{sl}")
                A(lambda: nc.scalar.copy(out=m_[:], in_=psX[:].rearrange("p (t j) -> p t j", j=128)), w=[B("psX"), mb_])
                s.dma("pool", k.MT[0:1024, t0:t0 + 128].rearrange("(t p) j -> p t j", p=128), m_[:], r=[mb_], w=[B("MT")])
            yield

        def interleave(gens):
            gens = [g for g in gens if g is not None]
            while gens:
                for g in list(gens):
                    try:
                        next(g)
                    except StopIteration:
                        gens.remove(g)

        for d in range(2):
            hb_ = B(f"s_h{d}"); hbb = B(f"s_hb{d}")
            V(lambda d=d: nc.vector.memset(hst[d][:], 0.0), w=[hb_])
            V(lambda d=d: nc.vector.memset(hbf[d][:], 0.0), w=[hbb])
            order = list(range(NCH)) if d == 0 else [1, 0] + list(range(NCH - 1, 1, -1))
            interleave([stageA(d, 0, order[0])])
            for n_ in range(len(order)):
                ga = stageA(d, n_ + 1, order[n_ + 1]) if n_ + 1 < len(order) else None
                interleave([ga, stageB(d, n_, order[n_])])
        s.barrier()

T = 4352; D = 1024; EPS = 1e-6; DEPTH = 4
C_PIDX = 139
I32 = mybir.dt.int32

def gen_dft(k):
    nc, s, B = k.nc, k.s, k.B
    cst = k.cst
    with ExitStack() as st:
        T_ = lambda name, shape, dt: st.enter_context(nc.sbuf_tensor(name + k.sfx, shape, dt))
        kio_i = T_("kio_i", [128, 4096], I32)
        kio = T_("kio", [128, 4096], F32)
        lcol = T_("lcol", [128, 32], F32)
        pi_ = [T_(f"pi{i}", [128, 4096], I32) for i in range(2)]
        tb = [T_(f"tb{i}", [128, 4096], BF16) for i in range(4)]
        s.op("pool", lambda: nc.gpsimd.iota(kio_i[:], pattern=[[1, 4096]], base=0, channel_multiplier=0), w=[B("kio_i")])
        s.op("dve", lambda: nc.vector.tensor_copy(out=kio[:], in_=kio_i[:]), r=[B("kio_i")], w=[B("kio")])
        for lt in range(32):
            s.op("dve", lambda lt=lt: nc.vector.tensor_scalar(out=lcol[:, lt:lt + 1], in0=cst[:, C_PIDX:C_PIDX + 1], scalar1=float(lt * 128), scalar2=None, op0=ALU.add), r=[B("cst")], w=[B("lcol")])
        sc = 2.0 * math.pi / 4096.0
        for lt in range(32):
            for j, (off, dst) in enumerate([(0.0, k.SLN), (3072.0, k.CL)]):
                e = "dve" if j == 0 else "pool"
                eng = nc.vector if j == 0 else nc.gpsimd
                p_ = pi_[j]; pb = B(f"pi{j}")
                s.op(e, lambda eng=eng, p_=p_, lt=lt, off=off: eng.tensor_scalar(out=p_[:], in0=kio[:], scalar1=lcol[:, lt:lt + 1], scalar2=off, op0=ALU.mult, op1=ALU.add), r=[B("kio"), B("lcol")], w=[pb])
                s.op("dve", lambda p_=p_: nc.vector.tensor_single_scalar(out=p_[:], in_=p_[:], scalar=4095, op=ALU.bitwise_and), w=[pb])
                t_ = tb[(lt % 2) * 2 + j]; tbb = B(f"tb{(lt % 2) * 2 + j}")
                s.op("act", lambda t_=t_, p_=p_: nc.scalar.activation(out=t_[:], in_=p_[:], func=AF.Sin, scale=sc, bias=k.negpi[:, 0:1]), r=[pb, B("negpi")], w=[tbb])
                s.dma("sp", dst[lt * 128:(lt + 1) * 128, :], t_[:], r=[tbb], w=[B("DFT")])
        s.barrier()


def phase_fnet(k, l, need_ctx=True):
    nc, s, I, B = k.nc, k.s, k.I, k.B
    with ExitStack() as st:
        T_ = lambda name, shape, dt: st.enter_context(nc.sbuf_tensor(name + k.sfx, shape, dt))
        P_ = lambda name, shape, dt: st.enter_context(nc.psum_tensor(name + k.sfx, shape, dt))
        cs = T_("f_cs", [128, 256], BF16)
        fw32 = T_("f_fw32", [128, 4, 512], F32)
        fw = T_("f_fw", [128, 4, 512], BF16)
        fb = T_("f_fb", [128, 4], F32)
        fuT = [T_(f"f_fuT{i}", [128, 4, 128], BF16) for i in range(2)]
        PQ = T_("f_PQ", [128, 34, 4, 2, 128], BF16)
        tabs = [T_(f"f_tab{i}", [128, 2, 512], BF16) for i in range(4)]
        specT = [T_(f"f_spec{i}", [128, 4, 512], BF16) for i in range(2)]
        gt = [T_(f"f_g{i}", [128, 512], F32) for i in range(2)]
        ob = [T_(f"f_ob{i}", [128, 512], BF16) for i in range(2)]
        psPQ = [P_(f"psPQ{i}", [128, 512], F32) for i in range(2)]
        psS = [P_(f"psS{i}", [128, 512], F32) for i in range(4)]
        psM = [P_(f"psMx{i}", [128, 512], F32) for i in range(2)]
        rows32 = lambda tsr: bass.AP(tsr.tensor, tsr.offset, [[32 * 4096, 128], [1, 128]])
        s.dma("sp", cs[:, 0:128], rows32(k.CL), r=[B("DFT")], w=[B("f_cs")])
        s.dma("sp", cs[:, 128:256], rows32(k.SLN), r=[B("DFT")], w=[B("f_cs")])
        s.dma("sp", fw32[:], I["fnet_w"][l].rearrange("(t p) c -> p t c", p=128), w=[B("f_fw32")])
        s.dma("sp", fb[:], I["fnet_b"][l].rearrange("(t p) -> p t", p=128), w=[B("f_fb")], allow_slow_non_contiguous=True)
        s.op("dve", lambda: nc.vector.tensor_copy(out=fw[:], in_=fw32[:]), r=[B("f_fw32")], w=[B("f_fw")])
        for tt in range(34):
            if tt < 2 and not need_ctx:
                continue
            f_ = fuT[tt % 2]; fb_ = B(f"f_fuT{tt % 2}")
            s.dma("sp", f_[:], k.PB[2048:2560, tt * 128:(tt + 1) * 128].rearrange("(h p) j -> p h j", p=128), r=[B("PB")], w=[fb_])
            for hd in range(4):
                pp = psPQ[hd // 2]; ppb = B(f"psPQ{hd // 2}")
                s.op("pe", lambda pp=pp, hd=hd, f_=f_: nc.tensor.matmul(pp[:, (hd % 2) * 256:(hd % 2) * 256 + 256], lhsT=f_[:, hd, :], rhs=cs[:], start=True, stop=True), r=[fb_, B("f_cs")], w=[ppb])
            for hf in range(2):
                pp = psPQ[hf]; ppb = B(f"psPQ{hf}")
                src = pp[:].rearrange("p (h q m) -> p h q m", h=2, q=2)
                s.op("act", lambda tt=tt, hf=hf, src=src: nc.scalar.copy(out=PQ[:, tt, hf * 2:hf * 2 + 2, 0, :], in_=src[:, :, 0, :]), w=[ppb, B(f"f_PQ{tt}")])
                s.op("dve", lambda tt=tt, hf=hf, src=src: nc.vector.tensor_scalar(out=PQ[:, tt, hf * 2:hf * 2 + 2, 1, :], in0=src[:, :, 1, :], scalar1=-1.0, scalar2=None, op0=ALU.mult), w=[ppb, B(f"f_PQ{tt}")])
        nload = [0]
        nsp = [0]
        bsb = [T_(f"f_bsb{i}", [128, 256], F32) for i in range(2)]
        tab4 = [T_(f"f_tab4_{i}", [128, 2, 4, 256], BF16) for i in range(3)]
        alt = T_("f_alt", [128, 2], F32)
        altb = T_("f_altb", [128, 1], BF16)
        s.op("dve", lambda: nc.vector.tensor_single_scalar(out=alt[:, 0:1].bitcast(I32), in_=k.pidx_i[:, 0:1], scalar=1, op=ALU.bitwise_and), r=[B("pidx_i")], w=[B("f_alt")])
        s.op("dve", lambda: nc.vector.tensor_copy(out=alt[:, 1:2], in_=alt[:, 0:1].bitcast(I32)), w=[B("f_alt")])
        s.op("dve", lambda: nc.vector.tensor_scalar(out=altb[:], in0=alt[:, 1:2], scalar1=-2.0, scalar2=1.0, op0=ALU.mult, op1=ALU.add), r=[B("f_alt")], w=[B("f_altb")])

        def mix_group(sp_, spb, c0, ncol, tok0):
            for ct in range(4):
                pm = psM[ct % 2]; pmb = B(f"psMx{ct % 2}")
                g_ = gt[ct % 2]; gb = B(f"f_g{ct % 2}")
                kw = dict(allow_slow_non_contiguous=True) if ncol == 1 else {}
                s.dma("sp", g_[:, 0:ncol], k.PF[544 + ct * 128:544 + (ct + 1) * 128, tok0:tok0 + ncol], r=[B("PF")], w=[gb], **kw)
                for hd in range(4):
                    s.op("pe", lambda pm=pm, hd=hd, ct=ct: nc.tensor.matmul(pm[:, 0:ncol], lhsT=fw[:, hd, ct * 128:(ct + 1) * 128], rhs=sp_[:, hd, c0:c0 + ncol], start=(hd == 0), stop=(hd == 3)),
                         r=[B("f_fw"), spb], w=[pmb])
                s.op("act", lambda g_=g_: nc.scalar.activation(out=g_[:, 0:ncol], in_=g_[:, 0:ncol], func=AF.Silu), w=[gb])
                o_ = ob[ct % 2]; obb = B(f"f_ob{ct % 2}")
                s.op("dve", lambda o_=o_, pm=pm, ct=ct, g_=g_: nc.vector.scalar_tensor_tensor(out=o_[:, 0:ncol], in0=pm[:, 0:ncol], scalar=fb[:, ct:ct + 1], in1=g_[:, 0:ncol], op0=ALU.add, op1=ALU.mult),
                     r=[gb, B("f_fb")], w=[pmb, obb])
                s.dma("pool", k.MT[1536 + ct * 128:1536 + (ct + 1) * 128, tok0:tok0 + ncol], o_[:, 0:ncol], r=[obb], w=[B("MT")], **kw)

        if need_ctx:
            nrm = 1.0 / math.sqrt(256.0 * 128.0)
            for lt in range(2):
                tab = tabs[nload[0] % 4]; tabb = B(f"f_tab{nload[0] % 4}")
                for j, tsr in enumerate([k.CL, k.SLN]):
                    src = bass.AP(tsr.tensor, tsr.offset + (lt * 128) * 16 * 4096, [[16 * 4096, 128], [1, 256]])
                    s.dma("sp", tab[:, j, 0:256], src, r=[B("DFT")], w=[tabb], allow_slow_non_contiguous=True)
                nload[0] += 1
                for hd in range(4):
                    s.op("pe", lambda hd=hd, lt=lt, tab=tab: nc.tensor.matmul(psS[hd][:, 0:256], lhsT=PQ[:, lt, hd, 0, :], rhs=tab[:, 0, 0:256], start=(lt == 0), stop=False),
                         r=[B(f"f_PQ{lt}"), tabb], w=[B(f"psS{hd}")])
                    s.op("pe", lambda hd=hd, lt=lt, tab=tab: nc.tensor.matmul(psS[hd][:, 0:256], lhsT=PQ[:, lt, hd, 1, :], rhs=tab[:, 1, 0:256], start=False, stop=(lt == 1)),
                         r=[B(f"f_PQ{lt}"), tabb], w=[B(f"psS{hd}")])
            sp_ = specT[nsp[0] % 2]; spb = B(f"f_spec{nsp[0] % 2}"); nsp[0] += 1
            for hd in range(4):
                s.op("act", lambda hd=hd, sp_=sp_: nc.scalar.mul(out=sp_[:, hd, 0:256], in_=psS[hd][:, 0:256], mul=nrm), w=[B(f"psS{hd}"), spb])
            mix_group(sp_, spb, 0, 256, 0)
        nrm = 1.0 / math.sqrt(4096.0 * 128.0)
        for kt in range(8):
            for lg in range(8):
                tb4 = tab4[nload[0] % 3]; tb4b = B(f"f_tab4_{nload[0] % 3}")
                for j, tsr in enumerate([k.CL, k.SLN]):
                    q_ = "sp" if j == 0 else "act"
                    s.dma(q_, tb4[:, j, :, :], tsr[lg * 512:(lg + 1) * 512, kt * 256:(kt + 1) * 256].rearrange("(a p) c -> p a c", p=128), r=[B("DFT")], w=[tb4b])
                nload[0] += 1
                for a_ in range(4):
                    lt = lg * 4 + a_
                    for hd in range(4):
                        s.op("pe", lambda hd=hd, lt=lt, tb4=tb4, a_=a_: nc.tensor.matmul(psS[hd][:, 0:256], lhsT=PQ[:, 2 + lt, hd, 0, :], rhs=tb4[:, 0, a_, :], start=(lt == 0), stop=False, skip_group_check=True),
                             r=[B(f"f_PQ{2 + lt}"), tb4b], w=[B(f"psS{hd}")])
                        s.op("pe", lambda hd=hd, lt=lt, tb4=tb4, a_=a_: nc.tensor.matmul(psS[hd][:, 256:512], lhsT=PQ[:, 2 + lt, hd, 1, :], rhs=tb4[:, 1, a_, :], start=False, stop=(lt == 31), skip_group_check=True),
                             r=[B(f"f_PQ{2 + lt}"), tb4b], w=[B(f"psS{hd}")])
            sp_ = specT[nsp[0] % 2]; spb = B(f"f_spec{nsp[0] % 2}"); nsp[0] += 1
            for hd in range(4):
                b_ = bsb[hd % 2]; bb = B(f"f_bsb{hd % 2}")
                s.op("act", lambda hd=hd, b_=b_: nc.scalar.mul(out=b_[:], in_=psS[hd][:, 256:512], mul=nrm), w=[B(f"psS{hd}"), bb])
                s.op("dve", lambda hd=hd, b_=b_, sp_=sp_: nc.vector.scalar_tensor_tensor(out=sp_[:, hd, 0:256], in0=psS[hd][:, 0:256], scalar=nrm, in1=b_[:], op0=ALU.mult, op1=ALU.add),
                     r=[bb], w=[B(f"psS{hd}"), spb])
                rv = bass.AP(sp_[:].tensor, sp_[:, hd, 511:512].offset, [list(sp_[:, hd, 0:1].ap[0]), [-1, 256]])
                s.op("dve", lambda hd=hd, b_=b_, rv=rv: nc.vector.scalar_tensor_tensor(out=rv, in0=psS[hd][:, 0:256], scalar=nrm, in1=b_[:], op0=ALU.mult, op1=ALU.subtract),
                     r=[bb], w=[B(f"psS{hd}"), spb])
            mix_group(sp_, spb, 0, 256, 256 + kt * 256)
            nm = 256 if kt > 0 else 255
            mix_group(sp_, spb, 256, nm, 256 + 4096 - kt * 256 - 255)
        for lt in range(32):
            for hd in range(4):
                s.op("pe", lambda hd=hd, lt=lt: nc.tensor.matmul(psS[0][:, hd:hd + 1], lhsT=PQ[:, 2 + lt, hd, 0, :], rhs=altb[:, 0:1], start=(lt == 0 and hd == 0), stop=(lt == 31 and hd == 3), skip_group_check=True),
                     r=[B(f"f_PQ{2 + lt}"), B("f_altb")], w=[B("psS0")])
        sp_ = specT[nsp[0] % 2]; spb = B(f"f_spec{nsp[0] % 2}"); nsp[0] += 1
        s.op("act", lambda sp_=sp_: nc.scalar.mul(out=sp_[:, :, 0], in_=psS[0][:, 0:4], mul=nrm), w=[B("psS0"), spb])
        mix_group(sp_, spb, 0, 1, 256 + 2048)
        s.barrier()


def phase_out(k, l, last=False):
    nc, s, I, B = k.nc, k.s, k.I, k.B
    with ExitStack() as st:
        T_ = lambda name, shape, dt: st.enter_context(nc.sbuf_tensor(name + k.sfx, shape, dt))
        P_ = lambda name, shape, dt: st.enter_context(nc.psum_tensor(name + k.sfx, shape, dt))
        wo = T_("o_wo", [128, 16, D], BF16)
        wst = [T_(f"o_wst{i}", [128, 2, D], F32) for i in range(2)]
        bc = T_("o_bc", [128, 2, D], F32)
        mT = [T_(f"o_mT{i}", [128, 16, 128], BF16) for i in range(2)]
        xt = [T_(f"o_x{i}", [128, D], F32) for i in range(2)]
        ot = [T_(f"o_o{i}", [128, D], F32) for i in range(2)]
        junk = T_("o_junk", [128, 512], BF16)
        st4 = T_("o_st", [128, 4], F32)
        psO = [P_(f"psO{i}", [128, 512], F32) for i in range(4)]
        for c in range(8):
            w_ = wst[c % 2]; wb = B(f"o_wst{c % 2}")
            s.dma("sp", w_[:], I["w_out"][l, c * 256:(c + 1) * 256, :].rearrange("(t p) c -> p t c", p=128), w=[wb])
            if c % 2 == 0:
                s.op("act", lambda w_=w_, c=c: nc.scalar.copy(out=wo[:, 2 * c:2 * c + 2, :], in_=w_[:]), r=[wb], w=[B("o_wo")])
            else:
                s.op("pool", lambda w_=w_, c=c: nc.gpsimd.tensor_copy(out=wo[:, 2 * c:2 * c + 2, :], in_=w_[:]), r=[wb], w=[B("o_wo")])
        for j in range(2):
            s.dma("pool", bc[:, j, :], k.MODS[l, 1 - j, 2, :].partition_broadcast(128), r=[B("MODS")], w=[B("o_bc")])
        for n_, tt in enumerate(range(2 if last else 0, 34)):
            t0 = tt * 128
            m_ = mT[n_ % 2]; mb = B(f"o_mT{n_ % 2}")
            s.dma("sp", m_[:], k.MT[:, t0:t0 + 128].rearrange("(t p) j -> p t j", p=128), r=[B("MT")], w=[mb])
            x_ = xt[n_ % 2]; xb = B(f"o_x{n_ % 2}")
            if l == 0:
                src = I["ctx"][t0:t0 + 128, :] if tt < 2 else I["x"][t0 - 256:t0 - 128, :]
                s.dma("sp", x_[:], src, w=[xb])
            else:
                s.dma("sp", x_[:], k.XS[t0:t0 + 128, :], r=[B("XS")], w=[xb])
            for hf in range(2):
                po = psO[(n_ % 2) * 2 + hf]; pob = B(f"psO{(n_ % 2) * 2 + hf}")
                for ct in range(16):
                    s.op("pe", lambda po=po, m_=m_, ct=ct, hf=hf: nc.tensor.matmul(po[:], lhsT=m_[:, ct, :], rhs=wo[:, ct, hf * 512:(hf + 1) * 512], start=(ct == 0), stop=(ct == 15)),
                         r=[mb, B("o_wo")], w=[pob])
                s.op("act", lambda po=po, hf=hf: nc.scalar.activation(out=junk[:], in_=po[:], func=AF.Square, accum_out=st4[:, hf:hf + 1]), w=[pob, B("o_junk"), B("o_st")])
            s.op("dve", lambda: nc.vector.tensor_tensor(out=st4[:, 2:3], in0=st4[:, 0:1], in1=st4[:, 1:2], op=ALU.add), w=[B("o_st")])
            s.op("dve", lambda: nc.vector.tensor_scalar(out=st4[:, 2:3], in0=st4[:, 2:3], scalar1=1.0 / D, scalar2=EPS, op0=ALU.mult, op1=ALU.add), w=[B("o_st")])
            s.op("act", lambda: nc.scalar.activation(out=st4[:, 2:3], in_=st4[:, 2:3], func=AF.Sqrt), w=[B("o_st")])
            s.op("dve", lambda: nc.vector.reciprocal(out=st4[:, 3:4], in_=st4[:, 2:3]), w=[B("o_st")])
            o_ = ot[n_ % 2]; ob = B(f"o_o{n_ % 2}")
            jb = 0 if tt < 2 else 1
            for hf in range(2):
                po = psO[(n_ % 2) * 2 + hf]; pob = B(f"psO{(n_ % 2) * 2 + hf}")
                s.op("dve", lambda po=po, hf=hf, o_=o_, jb=jb: nc.vector.scalar_tensor_tensor(out=o_[:, hf * 512:(hf + 1) * 512], in0=po[:], scalar=st4[:, 3:4], in1=bc[:, jb, hf * 512:(hf + 1) * 512],
                                                                                      op0=ALU.mult, op1=ALU.mult), r=[B("o_st"), B("o_bc")], w=[pob, ob])
            s.op("pool", lambda o_=o_, x_=x_: nc.gpsimd.tensor_tensor(out=o_[:], in0=o_[:], in1=x_[:], op=ALU.add), r=[xb], w=[ob])
            if last:
                s.dma("pool", k.out[t0 - 256:t0 - 128, :], o_[:], r=[ob], w=[B("OUT")])
            else:
                s.dma("pool", k.XS[t0:t0 + 128, :], o_[:], r=[ob], w=[B("XS")])
        s.barrier()

T = 4352; D = 1024; NB = 544
C_MVEC = 128; C_BM8 = 524; C_IDXF = 1024; C_IDXB = 1600
I32 = mybir.dt.int32
TWO_PI = 2.0 * math.pi

def phase_s5(k, l, need_ctx=True):
    nc, s, I, B = k.nc, k.s, k.I, k.B
    cst = k.cst
    V = lambda fn, r=(), w=(): s.op("dve", fn, r=r, w=w)
    A = lambda fn, r=(), w=(): s.op("act", fn, r=r, w=w)
    G = lambda fn, r=(), w=(): s.op("pool", fn, r=r, w=w)
    PE = lambda fn, r=(), w=(): s.op("pe", fn, r=r, w=w)
    with ExitStack() as st:
        T_ = lambda name, shape, dt: st.enter_context(nc.sbuf_tensor(name + k.sfx, shape, dt))
        WS = T_("q_WS", [128, 2, 4, 8, 2, 128], BF16)
        WR = T_("q_WR", [128, 2, 16, 8, 2, 32], BF16)
        KD = T_("q_KD", [128, 2, 4, 8, 128], BF16)
        rho = T_("q_rho", [128, 2, 16], F32)
        th = T_("q_th", [128, 2, 16], F32)
        with ExitStack() as ps:
            TP = lambda name, shape, dt: ps.enter_context(nc.sbuf_tensor(name + k.sfx, shape, dt))
            PP = lambda name, shape, dt: ps.enter_context(nc.psum_tensor(name + k.sfx, shape, dt))
            lam16 = TP("p_lam16", [16, 2, 128], F32)
            lr = TP("p_lr", [128, 16], F32); li = TP("p_li", [128, 16], F32)
            stp = TP("p_stp", [128, 16], F32)
            lrs = TP("p_lrs", [128, 16], F32); lis = TP("p_lis", [128, 16], F32)
            a9 = TP("p_a9", [128, 16, 9], F32); a9b = TP("p_a9b", [128, 16, 9], F32)
            ki = TP("p_ki", [128, 16, 9], I32)
            mag9 = TP("p_mag9", [128, 16, 9], F32)
            Ar = TP("p_Ar", [128, 16, 9], F32); Ai = TP("p_Ai", [128, 16, 9], F32)
            t16 = [TP(f"p_t16_{i}", [128, 16], F32) for i in range(6)]
            Br = TP("p_Br", [128, 16, 16], F32); Bi = TP("p_Bi", [128, 16, 16], F32)
            Bbr = TP("p_Bbr", [128, 16, 16], F32); Bbi = TP("p_Bbi", [128, 16, 16], F32)
            tB = TP("p_tB", [128, 16, 16], F32)
            Cn = [TP(f"p_Cn{i}", [128, 128], F32) for i in range(2)]
            Cr = TP("p_Cr", [128, 16, 16], F32); Ci = TP("p_Ci", [128, 16, 16], F32)
            Zr = TP("p_Zr", [128, 16, 9, 16], F32); Zi = TP("p_Zi", [128, 16, 9, 16], F32)
            tZ = TP("p_tZ", [128, 16, 9, 16], F32)
            Pr = TP("p_Pr", [128, 16, 8, 16], F32); Pi = TP("p_Pi", [128, 16, 8, 16], F32)
            BPr = TP("p_BPr", [128, 16, 128], F32); BPi = TP("p_BPi", [128, 16, 128], F32)
            PPd = [TP(f"p_PPd{i}", [128, 16, 128], BF16) for i in range(2)]
            kdc = TP("p_kdc", [128, 8, 16], F32)
            psT1 = PP("psT1", [128, 128], F32)
            psK = PP("psK", [128, 128], F32)
            psW = [PP(f"psWs{i}", [128, 128], F32) for i in range(2)]
            G(lambda: nc.gpsimd.memset(WR[:], 0.0), w=[B("q_WR")])
            for d in range(2):
                s.dma("sp", lam16[:, 0, :], I["s5_lambda_re"][l, d].rearrange("(a b) n -> a (b n)", b=2), w=[B("p_lam16")])
                s.dma("sp", lam16[:, 1, :], I["s5_lambda_im"][l, d].rearrange("(a b) n -> a (b n)", b=2), w=[B("p_lam16")])
                for j, dst in enumerate([lr, li]):
                    PE(lambda j=j: nc.tensor.transpose(out=psT1[:, 0:16], in_=lam16[:, j, :], identity=cst[0:16, 0:16]), r=[B("p_lam16"), B("cst")], w=[B("psT1")])
                    V(lambda dst=dst: nc.vector.tensor_copy(out=dst[:], in_=psT1[:, 0:16]), w=[B("psT1"), B("p_l")])
                ls = I["s5_log_step"]
                for g2 in range(2):
                    src = bass.AP(ls.tensor, ls.offset + (l * 2 + d) * 32 + g2, [[0, 64], [2, 16]])
                    s.dma("sp", stp[64 * g2:64 * g2 + 64, :], src, w=[B("p_stp")], allow_slow_non_contiguous=True)
                A(lambda: nc.scalar.activation(out=stp[:], in_=stp[:], func=AF.Exp), w=[B("p_stp")])
                V(lambda: nc.vector.tensor_tensor(out=lrs[:], in0=lr[:], in1=stp[:], op=ALU.mult), r=[B("p_l"), B("p_stp")], w=[B("p_ls")])
                V(lambda: nc.vector.tensor_tensor(out=lis[:], in0=li[:], in1=stp[:], op=ALU.mult), r=[B("p_l"), B("p_stp")], w=[B("p_ls")])
                mv = cst[:, C_MVEC:C_MVEC + 9].unsqueeze(1).to_broadcast([128, 16, 9])
                V(lambda: nc.vector.tensor_tensor(out=a9[:], in0=lrs[:].unsqueeze(2).to_broadcast([128, 16, 9]), in1=mv, op=ALU.mult), r=[B("p_ls"), B("cst")], w=[B("p_a9")])
                A(lambda: nc.scalar.activation(out=mag9[:], in_=a9[:], func=AF.Exp), r=[B("p_a9")], w=[B("p_mag9")])
                V(lambda: nc.vector.tensor_tensor(out=a9[:], in0=lis[:].unsqueeze(2).to_broadcast([128, 16, 9]), in1=mv, op=ALU.mult), r=[B("p_ls"), B("cst")], w=[B("p_a9")])
                def reduce_sin(dst, src_ap, shift, shape3):
                    V(lambda: nc.vector.tensor_scalar(out=a9b[:], in0=src_ap, scalar1=shift, scalar2=None, op0=ALU.add), r=[B("p_a9")], w=[B("p_a9b")])
                    V(lambda: nc.vector.tensor_scalar(out=ki[:], in0=a9b[:], scalar1=1.0 / TWO_PI, scalar2=None, op0=ALU.mult), r=[B("p_a9b")], w=[B("p_ki")])
                    V(lambda: nc.vector.scalar_tensor_tensor(out=a9b[:], in0=ki[:], scalar=-TWO_PI, in1=a9b[:], op0=ALU.mult, op1=ALU.add), r=[B("p_ki")], w=[B("p_a9b")])
                    A(lambda: nc.scalar.activation(out=dst[:], in_=a9b[:], func=AF.Sin), r=[B("p_a9b")], w=[B("p_sc")])
                reduce_sin(Ai, a9[:], 0.0, None)
                V(lambda d=d: nc.vector.tensor_copy(out=th[:, d, :], in_=a9b[:, :, 8]), r=[B("p_a9b")], w=[B("q_th")])
                reduce_sin(Ar, a9[:], math.pi / 2.0, None)
                V(lambda: nc.vector.tensor_tensor(out=Ar[:], in0=Ar[:], in1=mag9[:], op=ALU.mult), r=[B("p_mag9")], w=[B("p_sc")])
                V(lambda: nc.vector.tensor_tensor(out=Ai[:], in0=Ai[:], in1=mag9[:], op=ALU.mult), r=[B("p_mag9")], w=[B("p_sc")])
                V(lambda d=d: nc.vector.tensor_copy(out=rho[:, d, :], in_=mag9[:, :, 8]), r=[B("p_mag9")], w=[B("q_rho")])
                am1, den, fr, fi, u1, u2 = t16
                V(lambda: nc.vector.tensor_scalar(out=am1[:], in0=Ar[:, :, 1], scalar1=-1.0, scalar2=None, op0=ALU.add), r=[B("p_sc")], w=[B("p_t16")])
                V(lambda: nc.vector.tensor_tensor(out=den[:], in0=lr[:], in1=lr[:], op=ALU.mult), r=[B("p_l")], w=[B("p_t16")])
                V(lambda: nc.vector.tensor_tensor(out=u1[:], in0=li[:], in1=li[:], op=ALU.mult), r=[B("p_l")], w=[B("p_t16")])
                V(lambda: nc.vector.tensor_tensor(out=den[:], in0=den[:], in1=u1[:], op=ALU.add), w=[B("p_t16")])
                V(lambda: nc.vector.reciprocal(out=den[:], in_=den[:]), w=[B("p_t16")])
                V(lambda: nc.vector.tensor_tensor(out=u1[:], in0=am1[:], in1=lr[:], op=ALU.mult), w=[B("p_t16")])
                V(lambda: nc.vector.tensor_tensor(out=u2[:], in0=Ai[:, :, 1], in1=li[:], op=ALU.mult), w=[B("p_t16")])
                V(lambda: nc.vector.tensor_tensor(out=fr[:], in0=u1[:], in1=u2[:], op=ALU.add), w=[B("p_t16")])
                V(lambda: nc.vector.tensor_tensor(out=fr[:], in0=fr[:], in1=den[:], op=ALU.mult), w=[B("p_t16")])
                V(lambda: nc.vector.tensor_tensor(out=u1[:], in0=Ai[:, :, 1], in1=lr[:], op=ALU.mult), w=[B("p_t16")])
                V(lambda: nc.vector.tensor_tensor(out=u2[:], in0=am1[:], in1=li[:], op=ALU.mult), w=[B("p_t16")])
                V(lambda: nc.vector.tensor_tensor(out=fi[:], in0=u1[:], in1=u2[:], op=ALU.subtract), w=[B("p_t16")])
                V(lambda: nc.vector.tensor_tensor(out=fi[:], in0=fi[:], in1=den[:], op=ALU.mult), w=[B("p_t16")])
                for j, (dst, nm) in enumerate([(Br, "s5_b_re"), (Bi, "s5_b_im")]):
                    bt = I[nm]
                    for g2 in range(2):
                        src = bass.AP(bt.tensor, bt.offset + ((l * 2 + d) * 32 + g2) * 1024, [[16, 64], [2048, 16], [1, 16]])
                        s.dma("sp", dst[64 * g2:64 * g2 + 64, :, :], src, w=[B("p_B")])
                frb = fr[:].unsqueeze(2).to_broadcast([128, 16, 16]); fib = fi[:].unsqueeze(2).to_broadcast([128, 16, 16])
                V(lambda: nc.vector.tensor_tensor(out=Bbr[:], in0=Br[:], in1=frb, op=ALU.mult), r=[B("p_B"), B("p_t16")], w=[B("p_Bb")])
                V(lambda: nc.vector.tensor_tensor(out=tB[:], in0=Bi[:], in1=fib, op=ALU.mult), r=[B("p_B"), B("p_t16")], w=[B("p_tB")])
                V(lambda: nc.vector.tensor_tensor(out=Bbr[:], in0=Bbr[:], in1=tB[:], op=ALU.subtract), r=[B("p_tB")], w=[B("p_Bb")])
                V(lambda: nc.vector.tensor_tensor(out=Bbi[:], in0=Bi[:], in1=frb, op=ALU.mult), r=[B("p_B"), B("p_t16")], w=[B("p_Bb")])
                V(lambda: nc.vector.tensor_tensor(out=tB[:], in0=Br[:], in1=fib, op=ALU.mult), r=[B("p_B"), B("p_t16")], w=[B("p_tB")])
                V(lambda: nc.vector.tensor_tensor(out=Bbi[:], in0=Bbi[:], in1=tB[:], op=ALU.add), r=[B("p_tB")], w=[B("p_Bb")])
                for j, (dst, nm) in enumerate([(Cr, "s5_c_re"), (Ci, "s5_c_im")]):
                    ct_ = I[nm]
                    for pset in range(2):
                        cn = Cn[pset]; cnb = B(f"p_Cn{pset}")
                        for pr in range(8):
                            g0 = 2 * (8 * pset + pr)
                            src = bass.AP(ct_.tensor, ct_.offset + ((l * 2 + d) * 32 + g0) * 1024, [[64, 16], [1024, 2], [1, 64]])
                            s.dma("sp", cn[16 * pr:16 * pr + 16, :].rearrange("p (a n) -> p a n", a=2), src, w=[cnb])
                        PE(lambda cn=cn: nc.tensor.transpose(out=psT1[:], in_=cn[:], identity=cst[:, 0:128]), r=[cnb, B("cst")], w=[B("psT1")])
                        V(lambda dst=dst, pset=pset: nc.vector.tensor_copy(out=dst[:, 8 * pset:8 * pset + 8, :], in_=psT1[:].rearrange("p (a k) -> p a k", k=16)), w=[B("psT1"), B("p_C")])
                Crb = lambda X: X[:].unsqueeze(2).to_broadcast([128, 16, 9, 16])
                Ab = lambda X: X[:].unsqueeze(3).to_broadcast([128, 16, 9, 16])
                V(lambda: nc.vector.tensor_tensor(out=Zr[:], in0=Crb(Cr), in1=Ab(Ar), op=ALU.mult), r=[B("p_C"), B("p_sc")], w=[B("p_Z")])
                G(lambda: nc.gpsimd.tensor_tensor(out=tZ[:], in0=Crb(Ci), in1=Ab(Ai), op=ALU.mult), r=[B("p_C"), B("p_sc")], w=[B("p_tZ")])
                V(lambda: nc.vector.tensor_tensor(out=Zr[:], in0=Zr[:], in1=tZ[:], op=ALU.subtract), r=[B("p_tZ")], w=[B("p_Z")])
                V(lambda: nc.vector.tensor_tensor(out=Zi[:], in0=Crb(Cr), in1=Ab(Ai), op=ALU.mult), r=[B("p_C"), B("p_sc")], w=[B("p_Z")])
                G(lambda: nc.gpsimd.tensor_tensor(out=tZ[:], in0=Crb(Ci), in1=Ab(Ar), op=ALU.mult), r=[B("p_C"), B("p_sc")], w=[B("p_tZ")])
                V(lambda: nc.vector.tensor_tensor(out=Zi[:], in0=Zi[:], in1=tZ[:], op=ALU.add), r=[B("p_tZ")], w=[B("p_Z")])
                for g2 in range(2):
                    sl = slice(64 * g2, 64 * g2 + 64)
                    V(lambda sl=sl, g2=g2, d=d: nc.vector.tensor_copy(out=WR[sl, d, :, :, 0, 16 * g2:16 * g2 + 16], in_=Zr[sl, :, 1:9, :]), r=[B("p_Z")], w=[B("q_WR")])
                    V(lambda sl=sl, g2=g2, d=d: nc.vector.tensor_scalar(out=WR[sl, d, :, :, 1, 16 * g2:16 * g2 + 16], in0=Zi[sl, :, 1:9, :], scalar1=-1.0, scalar2=None, op0=ALU.mult), r=[B("p_Z")], w=[B("q_WR")])
                Bb8 = lambda X: X[:].unsqueeze(2).to_broadcast([128, 16, 8, 16])
                A8 = lambda X: X[:, :, 0:8].unsqueeze(3).to_broadcast([128, 16, 8, 16])
                tP = tZ[:, :, 0:8, :]
                V(lambda: nc.vector.tensor_tensor(out=Pr[:], in0=Bb8(Bbr), in1=A8(Ar), op=ALU.mult), r=[B("p_Bb"), B("p_sc")], w=[B("p_P")])
                G(lambda: nc.gpsimd.tensor_tensor(out=tP, in0=Bb8(Bbi), in1=A8(Ai), op=ALU.mult), r=[B("p_Bb"), B("p_sc")], w=[B("p_tZ")])
                V(lambda: nc.vector.tensor_tensor(out=Pr[:], in0=Pr[:], in1=tP, op=ALU.subtract), r=[B("p_tZ")], w=[B("p_P")])
                V(lambda: nc.vector.tensor_tensor(out=Pi[:], in0=Bb8(Bbi), in1=A8(Ar), op=ALU.mult), r=[B("p_Bb"), B("p_sc")], w=[B("p_P")])
                G(lambda: nc.gpsimd.tensor_tensor(out=tP, in0=Bb8(Bbr), in1=A8(Ai), op=ALU.mult), r=[B("p_Bb"), B("p_sc")], w=[B("p_tZ")])
                V(lambda: nc.vector.tensor_tensor(out=Pi[:], in0=Pi[:], in1=tP, op=ALU.add), r=[B("p_tZ")], w=[B("p_P")])
                G(lambda: nc.gpsimd.memset(BPr[:], 0.0), w=[B("p_BP")])
                G(lambda: nc.gpsimd.memset(BPi[:], 0.0), w=[B("p_BP")])
                for g2 in range(2):
                    sl = slice(64 * g2, 64 * g2 + 64)
                    for q in range(4):
                        c0 = 32 * q + 16 * g2
                        V(lambda sl=sl, q=q, c0=c0: nc.vector.tensor_copy(out=BPr[sl, q::4, c0:c0 + 16], in_=Bbr[sl, q::4, :]), r=[B("p_Bb")], w=[B("p_BP")])
                        V(lambda sl=sl, q=q, c0=c0: nc.vector.tensor_scalar(out=BPi[sl, q::4, c0:c0 + 16], in0=Bbi[sl, q::4, :], scalar1=-1.0, scalar2=None, op0=ALU.mult), r=[B("p_Bb")], w=[B("p_BP")])
                for t in range(4):
                    n_ = 0
                    for q in range(4):
                        pr = 4 * t + q
                        for (BP_, Z_) in ((BPr, Zr), (BPi, Zi)):
                            PE(lambda pr=pr, BP_=BP_, Z_=Z_, n_=n_: nc.tensor.matmul(psK[:], lhsT=BP_[:, pr, :], rhs=Z_[:, pr, 0:8, :].rearrange("p a k -> p (a k)"), start=(n_ == 0), stop=(n_ == 7)),
                               r=[B("p_BP"), B("p_Z")], w=[B("psK")])
                            n_ += 1
                    V(lambda: nc.vector.tensor_copy(out=kdc[:], in_=psK[:].rearrange("p (a k) -> p a k", k=16)), w=[B("psK"), B("p_kdc")])
                    V(lambda t=t, d=d: nc.vector.tensor_tensor(out=KD[:, d, t, :, :].rearrange("p a (g k) -> p a g k", k=16), in0=kdc[:].unsqueeze(2).to_broadcast([128, 8, 8, 16]),
                                                          in1=cst[:, C_BM8:C_BM8 + 8].unsqueeze(1).unsqueeze(3).to_broadcast([128, 8, 8, 16]), op=ALU.mult), r=[B("p_kdc"), B("cst")], w=[B("q_KD")])
                n_w = 0
                for m in range(8):
                    for ri, P_ in enumerate((Pr, Pi)):
                        pd = PPd[n_w % 2]; pdb = B(f"p_PPd{n_w % 2}")
                        G(lambda pd=pd: nc.gpsimd.memset(pd[:], 0.0), w=[pdb])
                        for g2 in range(2):
                            sl = slice(64 * g2, 64 * g2 + 64)
                            for q in range(4):
                                c0 = 32 * q + 16 * g2
                                e = V if (q % 2 == 0) else G
                                eng = nc.vector if (q % 2 == 0) else nc.gpsimd
                                e(lambda sl=sl, q=q, c0=c0, pd=pd, P_=P_, m=m, eng=eng: eng.tensor_copy(out=pd[sl, q::4, c0:c0 + 16], in_=P_[sl, q::4, m, :]), r=[B("p_P")], w=[pdb])
                        for t in range(4):
                            pw = psW[t % 2]; pwb = B(f"psWs{t % 2}")
                            for q in range(4):
                                PE(lambda pw=pw, pd=pd, t=t, q=q: nc.tensor.matmul(pw[:], lhsT=pd[:, 4 * t + q, :], rhs=k.identb[:], start=(q == 0), stop=(q == 3)), r=[pdb, B("identb")], w=[pwb])
                            if t % 2 == 0:
                                A(lambda pw=pw, t=t, m=m, ri=ri, d=d: nc.scalar.copy(out=WS[:, d, t, m, ri, :], in_=pw[:]), w=[pwb, B("q_WS")])
                            else:
                                V(lambda pw=pw, t=t, m=m, ri=ri, d=d: nc.vector.tensor_copy(out=WS[:, d, t, m, ri, :], in_=pw[:]), w=[pwb, B("q_WS")])
                        n_w += 1
            s.barrier()
        if getattr(k, 'stop', None) == 's5prep':
            return
        phase_s5_main(k, l, need_ctx, WS, WR, KD, rho, th, st)
        s.barrier()


def phase_s5_main(k, l, need_ctx, WS, WR, KD, rho, th, st):
    nc, s, I, B = k.nc, k.s, k.I, k.B
    cst = k.cst
    V = lambda fn, r=(), w=(): s.op("dve", fn, r=r, w=w)
    A = lambda fn, r=(), w=(): s.op("act", fn, r=r, w=w)
    G = lambda fn, r=(), w=(): s.op("pool", fn, r=r, w=w)
    PE = lambda fn, r=(), w=(): s.op("pe", fn, r=r, w=w)
    T_ = lambda name, shape, dt: st.enter_context(nc.sbuf_tensor(name + k.sfx, shape, dt))
    P_ = lambda name, shape, dt: st.enter_context(nc.psum_tensor(name + k.sfx, shape, dt))
    uT = st.enter_context(nc.sbuf_tensor("q_uT" + k.sfx, [128, 4, T], BF16))
    mst = ExitStack()
    T_ = lambda name, shape, dt: mst.enter_context(nc.sbuf_tensor(name + k.sfx, shape, dt))
    hst = [T_(f"q_hst{i}", [128, 2, NB], BF16) for i in range(2)]
    SL = []
    for i_ in range(2):
        o = {}
        for nm in ("Sr", "Si", "cosT", "sinT", "ang", "xr", "xi", "t1", "t2", "Gr", "Gi"):
            o[nm] = T_(f"q_{nm}{i_}", [128, NB], F32)
        o["kiT"] = T_(f"q_ki{i_}", [128, NB], I32)
        o["psS"] = [mst.enter_context(nc.psum_tensor(f"psSq{i_}_{j}" + k.sfx, [128, 1024], F32)) for j in range(2)]
        SL.append(o)
    for t in range(4):
        s.dma("sp", uT[:, t, :], k.PB[1536 + t * 128:1536 + (t + 1) * 128, :], r=[B("PB")], w=[B("q_uT")])
    pieces = [(32, 288, 0), (288, 544, 256), (0, 32, 512)]
    def it_gen(d, pr, si):
        o = SL[si]
        Sr, Si, cosT, sinT, ang, xr, xi, t1, t2, Gr, Gi, kiT, psS = (o[n] for n in ('Sr','Si','cosT','sinT','ang','xr','xi','t1','t2','Gr','Gi','kiT','psS'))
        sfx_ = str(si)
        t = pr // 4; q = pr % 4
        rows = slice(32 * q, 32 * q + 32)
        for ri in range(2):
            for (b0, b1, pc) in pieces:
                for pos in range(8):
                    m = 7 - pos if d == 0 else pos
                    rhs = uT[rows, t, 8 * b0 + pos:8 * b1:8]
                    PE(lambda ri=ri, pc=pc, b0=b0, b1=b1, m=m, rhs=rhs, pos=pos: nc.tensor.matmul(psS[ri][:, pc:pc + (b1 - b0)], lhsT=WS[rows, d, t, m, ri, :], rhs=rhs, start=(pos == 0), stop=(pos == 7),
                                                                                        tile_position=(32 * q, 0), skip_group_check=True),
                       r=[B("q_uT"), B("q_WS")], w=[B(f"psSq{ri}_" + sfx_)])
        yield
        for ri, dst in enumerate((Sr, Si)):
            A(lambda ri=ri, dst=dst: nc.scalar.copy(out=dst[:, 32:544], in_=psS[ri][:, 0:512]), w=[B(f"psSq{ri}_" + sfx_), B("q_S" + sfx_)])
            A(lambda ri=ri, dst=dst: nc.scalar.copy(out=dst[:, 0:32], in_=psS[ri][:, 512:544]), w=[B(f"psSq{ri}_" + sfx_), B("q_S" + sfx_)])
        yield
        idx = cst[:, C_IDXF:C_IDXF + NB] if d == 0 else cst[:, C_IDXB:C_IDXB + NB]
        for (dst, shift) in ((sinT, 0.0), (cosT, math.pi / 2.0)):
            V(lambda shift=shift: nc.vector.tensor_scalar(out=ang[:], in0=idx, scalar1=th[:, d, pr:pr + 1], scalar2=shift, op0=ALU.mult, op1=ALU.add), r=[B("q_th"), B("cst")], w=[B("q_ang" + sfx_)])
            V(lambda: nc.vector.tensor_scalar(out=kiT[:], in0=ang[:], scalar1=1.0 / TWO_PI, scalar2=None, op0=ALU.mult), r=[B("q_ang" + sfx_)], w=[B("q_ki" + sfx_)])
            V(lambda: nc.vector.scalar_tensor_tensor(out=ang[:], in0=kiT[:], scalar=-TWO_PI, in1=ang[:], op0=ALU.mult, op1=ALU.add), r=[B("q_ki" + sfx_)], w=[B("q_ang" + sfx_)])
            A(lambda dst=dst: nc.scalar.activation(out=dst[:], in_=ang[:], func=AF.Sin), r=[B("q_ang" + sfx_)], w=[B("q_tw" + sfx_)])
        yield
        V(lambda: nc.vector.tensor_tensor(out=xr[:], in0=Sr[:], in1=cosT[:], op=ALU.mult), r=[B("q_S" + sfx_), B("q_tw" + sfx_)], w=[B("q_xr" + sfx_)])
        G(lambda: nc.gpsimd.tensor_tensor(out=t1[:], in0=Si[:], in1=sinT[:], op=ALU.mult), r=[B("q_S" + sfx_), B("q_tw" + sfx_)], w=[B("q_t1" + sfx_)])
        V(lambda: nc.vector.tensor_tensor(out=xr[:], in0=xr[:], in1=t1[:], op=ALU.add), r=[B("q_t1" + sfx_)], w=[B("q_xr" + sfx_)])
        G(lambda: nc.gpsimd.tensor_tensor(out=xi[:], in0=Si[:], in1=cosT[:], op=ALU.mult), r=[B("q_S" + sfx_), B("q_tw" + sfx_)], w=[B("q_xi" + sfx_)])
        V(lambda: nc.vector.tensor_tensor(out=t2[:], in0=Sr[:], in1=sinT[:], op=ALU.mult), r=[B("q_S" + sfx_), B("q_tw" + sfx_)], w=[B("q_t2" + sfx_)])
        G(lambda: nc.gpsimd.tensor_tensor(out=xi[:], in0=xi[:], in1=t2[:], op=ALU.subtract), r=[B("q_t2" + sfx_)], w=[B("q_xi" + sfx_)])
        yield
        rcol = rho[:, d, pr:pr + 1]
        for (src, dst, nm) in ((xr, Gr, "q_xr"), (xi, Gi, "q_xi")):
            if d == 0:
                V(lambda src=src, dst=dst: nc.vector.tensor_tensor_scan(out=dst[:], data0=rcol.to_broadcast([128, NB]), data1=src[:], initial=0.0, op0=ALU.mult, op1=ALU.add),
                  r=[B(nm + sfx_), B("q_rho")], w=[B("q_G" + sfx_)])
            else:
                rv = lambda X, a, b: bass.AP(X[:].tensor, X[:, b - 1:b].offset, [list(X[:].ap[0]), [-1, b - a]])
                V(lambda src=src, dst=dst: nc.vector.tensor_tensor_scan(out=rv(dst, 0, 32), data0=rcol.to_broadcast([128, 32]), data1=rv(src, 0, 32), initial=0.0, op0=ALU.mult, op1=ALU.add),
                  r=[B(nm + sfx_), B("q_rho")], w=[B("q_G" + sfx_)])
                V(lambda src=src, dst=dst: nc.vector.tensor_tensor_scan(out=rv(dst, 32, 544), data0=rcol.to_broadcast([128, 512]), data1=rv(src, 32, 544), initial=dst[:, 0:1], op0=ALU.mult, op1=ALU.add),
                  r=[B(nm + sfx_), B("q_rho")], w=[B("q_G" + sfx_)])
        yield
        hs_ = hst[si]; hsb = B(f"q_hst{si}")
        if d == 0:
            G(lambda hs_=hs_: nc.gpsimd.memset(hs_[:, :, 0:1], 0.0), w=[hsb])
        if d == 0:
            so, si_ = slice(1, 544), slice(0, 543)
        else:
            so, si_ = slice(0, 543), slice(1, 544)
        V(lambda: nc.vector.tensor_tensor(out=t1[:], in0=Gr[:], in1=cosT[:], op=ALU.mult), r=[B("q_G" + sfx_), B("q_tw" + sfx_)], w=[B("q_t1" + sfx_)])
        G(lambda: nc.gpsimd.tensor_tensor(out=t2[:], in0=Gi[:], in1=sinT[:], op=ALU.mult), r=[B("q_G" + sfx_), B("q_tw" + sfx_)], w=[B("q_t2" + sfx_)])
        V(lambda: nc.vector.tensor_tensor(out=hs_[:, 0, so], in0=t1[:, si_], in1=t2[:, si_], op=ALU.subtract), r=[B("q_t1" + sfx_), B("q_t2" + sfx_)], w=[hsb])
        if d == 1:
            V(lambda: nc.vector.tensor_tensor(out=hs_[:, 0, 543:544], in0=t1[:, 0:1], in1=t2[:, 0:1], op=ALU.subtract), r=[B("q_t1" + sfx_), B("q_t2" + sfx_)], w=[hsb])
        G(lambda: nc.gpsimd.tensor_tensor(out=xr[:], in0=Gr[:], in1=sinT[:], op=ALU.mult), r=[B("q_G" + sfx_), B("q_tw" + sfx_)], w=[B("q_xr" + sfx_)])
        V(lambda: nc.vector.tensor_tensor(out=xi[:], in0=Gi[:], in1=cosT[:], op=ALU.mult), r=[B("q_G" + sfx_), B("q_tw" + sfx_)], w=[B("q_xi" + sfx_)])
        G(lambda: nc.gpsimd.tensor_tensor(out=hs_[:, 1, so], in0=xr[:, si_], in1=xi[:, si_], op=ALU.add), r=[B("q_xr" + sfx_), B("q_xi" + sfx_)], w=[hsb])
        if d == 1:
            G(lambda: nc.gpsimd.tensor_tensor(out=hs_[:, 1, 543:544], in0=xr[:, 0:1], in1=xi[:, 0:1], op=ALU.add), r=[B("q_xr" + sfx_), B("q_xi" + sfx_)], w=[hsb])
            G(lambda: nc.gpsimd.memset(hs_[:, :, 31:32], 0.0), w=[hsb])
        s.dma("sp", k.HD[d, pr].rearrange("r p c -> p r c"), hs_[:], r=[hsb], w=[B("HD")])

    def interleave(gens):
        gens = list(gens)
        while gens:
            for g_ in list(gens):
                try:
                    next(g_)
                except StopIteration:
                    gens.remove(g_)
    its = [(d, pr) for d in range(2) for pr in range(16)]
    for i_ in range(0, len(its), 2):
        interleave([it_gen(its[i_][0], its[i_][1], 0), it_gen(its[i_ + 1][0], its[i_ + 1][1], 1)])
    s.barrier()
    mst.close()
    if getattr(k, 'stop', None) == 's5main':
        return
    s5_glu(k, l, need_ctx, WR, KD, uT, None, st)


def s5_glu(k, l, need_ctx, WR, KD, uT, Hin, st):
    nc, s, I, B = k.nc, k.s, k.I, k.B
    V = lambda fn, r=(), w=(): s.op("dve", fn, r=r, w=w)
    A = lambda fn, r=(), w=(): s.op("act", fn, r=r, w=w)
    G = lambda fn, r=(), w=(): s.op("pool", fn, r=r, w=w)
    PE = lambda fn, r=(), w=(): s.op("pe", fn, r=r, w=w)
    T_ = lambda name, shape, dt: st.enter_context(nc.sbuf_tensor(name + k.sfx, shape, dt))
    P_ = lambda name, shape, dt: st.enter_context(nc.psum_tensor(name + k.sfx, shape, dt))
    wgs = T_("g_wgs", [128, 512], F32); wg = T_("g_wg", [128, 4, 512], BF16)
    bg = T_("g_bg", [128, 4], F32); dsk = T_("g_dsk", [128, 4], F32)
    dgs = T_("g_dgs", [128, 4, 128], BF16)
    y32 = [T_(f"g_y{i}", [128, 512], F32) for i in range(2)]
    y2 = [T_(f"g_y2{i}", [128, 512], F32) for i in range(2)]
    vT = T_("g_v", [128, 4, 2048], BF16)
    gs = [T_(f"g_gs{i}", [128, 512], F32) for i in range(2)]
    o1 = T_("g_o1", [128, 512], F32)
    ob = [T_(f"g_ob{i}", [128, 512], BF16) for i in range(2)]
    Hc = T_("g_Hc", [128, 64, 256], BF16)
    py4 = P_("psY4", [128, 2048], F32)
    psG = [P_(f"psGq{i}", [128, 512], F32) for i in range(2)]
    for t in range(4):
        s.dma("sp", wgs[:], I["s5_w_glu"][l, t * 128:(t + 1) * 128, :], w=[B("g_wgs")])
        V(lambda t=t: nc.vector.tensor_copy(out=wg[:, t, :], in_=wgs[:]), r=[B("g_wgs")], w=[B("g_wg")])
    s.dma("sp", bg[:], I["s5_b_glu"][l].rearrange("(t p) -> p t", p=128), w=[B("g_bg")], allow_slow_non_contiguous=True)
    s.dma("sp", dsk[:], I["s5_d"][l].rearrange("(t p) -> p t", p=128), w=[B("g_dsk")], allow_slow_non_contiguous=True)
    for t in range(4):
        V(lambda t=t: nc.vector.tensor_scalar(out=dgs[:, t, :], in0=k.identb[:], scalar1=dsk[:, t:t + 1], scalar2=None, op0=ALU.mult), r=[B("g_dsk"), B("identb")], w=[B("g_dgs")])
    chunks = [(0, 32)] if need_ctx else []
    chunks += [(32, 256), (288, 256)]
    hcb = B("g_Hc"); vb = B("g_v"); pyb = B("psY4")
    npc = 0
    for ci, (b0, nb) in enumerate(chunks):
        t0 = 8 * b0; ntok = 8 * nb
        for dd in range(2):
            for qq in range(4):
                s.dma("sp", Hc[:, dd * 32 + qq * 8:dd * 32 + qq * 8 + 8, 0:nb], k.HD[dd, qq * 4:qq * 4 + 4, :, :, b0:b0 + nb].rearrange("q r p c -> p (q r) c"), r=[B("HD")], w=[hcb])
        for t in range(4):
            reg = lambda o, rows=slice(0, 128): py4[rows, o * 256:o * 256 + nb]
            for o in range(8):
                PE(lambda o=o, t=t: nc.tensor.matmul(reg(o), lhsT=dgs[:, t, :], rhs=uT[:, t, t0 + o:t0 + ntok:8], start=(o % 2 == 0), stop=False, skip_group_check=True),
                   r=[B("g_dgs"), B("q_uT")], w=[pyb])
            for d in range(2):
                for o in range(8):
                    srcs = range(0, o + 1) if d == 0 else range(o, 8)
                    for o2 in srcs:
                        lag = abs(o - o2)
                        PE(lambda t=t, d=d, lag=lag, o=o, o2=o2: nc.tensor.matmul(reg(o), lhsT=KD[:, d, t, lag, :], rhs=uT[:, t, t0 + o2:t0 + ntok:8], start=False, stop=False, skip_group_check=True),
                           r=[B("q_KD"), B("q_uT")], w=[pyb])
                for o in range(8):
                    i_ = o if d == 0 else 7 - o
                    for ri in range(2):
                        for q in range(4):
                            pr = 4 * t + q
                            PE(lambda d=d, pr=pr, i_=i_, ri=ri, o=o, q=q: nc.tensor.matmul(reg(o, slice(32 * q, 32 * q + 32)), lhsT=WR[:, d, pr, i_, ri, :], rhs=Hc[:, (d * 16 + pr) * 2 + ri, 0:nb],
                                                                                     start=False, stop=False, tile_position=(0, 32 * q), skip_group_check=True),
                               r=[B("q_WR"), hcb], w=[pyb])
            for pc in range(4):
                ya = y32[npc % 2]; yab = B(f"g_y{npc % 2}"); yb_ = y2[npc % 2]; ybb = B(f"g_y2{npc % 2}")
                src = py4[:, pc * 512:(pc + 1) * 512].rearrange("p (o c) -> p o c", o=2)[:, :, 0:nb]
                ya3 = ya[:, 0:2 * nb].rearrange("p (o c) -> p o c", o=2)
                yb3 = yb_[:, 0:2 * nb].rearrange("p (o c) -> p o c", o=2)
                dst = vT[:, t, 0:ntok].rearrange("p (c o) -> p o c", o=8)[:, 2 * pc:2 * pc + 2, :]
                A(lambda src=src, ya3=ya3: nc.scalar.copy(out=ya3, in_=src), w=[pyb, yab])
                G(lambda ya=ya, yb_=yb_: nc.gpsimd.tensor_tensor(out=yb_[:, 0:2 * nb], in0=ya[:, 0:2 * nb], in1=ya[:, 0:2 * nb], op=ALU.mult), r=[yab], w=[ybb])
                V(lambda yb_=yb_: nc.vector.tensor_scalar(out=yb_[:, 0:2 * nb], in0=yb_[:, 0:2 * nb], scalar1=0.044715, scalar2=1.0, op0=ALU.mult, op1=ALU.add), w=[ybb])
                V(lambda ya=ya, yb_=yb_: nc.vector.tensor_tensor(out=yb_[:, 0:2 * nb], in0=yb_[:, 0:2 * nb], in1=ya[:, 0:2 * nb], op=ALU.mult), r=[yab], w=[ybb])
                A(lambda yb_=yb_: nc.scalar.activation(out=yb_[:, 0:2 * nb], in_=yb_[:, 0:2 * nb], func=AF.Sigmoid, scale=1.5957691216), w=[ybb])
                V(lambda dst=dst, yb3=yb3, ya3=ya3: nc.vector.tensor_tensor(out=dst, in0=yb3, in1=ya3, op=ALU.mult), r=[yab, ybb], w=[vb])
                npc += 1
        for sc0 in range(0, ntok, 512):
            n_ = min(512, ntok - sc0)
            for ct in range(4):
                pg = psG[ct % 2]; pgb = B(f"psGq{ct % 2}")
                g_ = gs[ct % 2]; gb = B(f"g_gs{ct % 2}")
                s.dma("sp", g_[:, 0:n_], k.PF[32 + ct * 128:32 + (ct + 1) * 128, t0 + sc0:t0 + sc0 + n_], r=[B("PF")], w=[gb])
                for t in range(4):
                    PE(lambda pg=pg, t=t, ct=ct, sc0=sc0, n_=n_: nc.tensor.matmul(pg[:, 0:n_], lhsT=wg[:, t, ct * 128:(ct + 1) * 128], rhs=vT[:, t, sc0:sc0 + n_], start=(t == 0), stop=(t == 3)),
                       r=[B("g_wg"), vb], w=[pgb])
                A(lambda pg=pg, ct=ct, n_=n_: nc.scalar.activation(out=o1[:, 0:n_], in_=pg[:, 0:n_], func=AF.Sigmoid, bias=bg[:, ct:ct + 1]), r=[B("g_bg")], w=[pgb, B("g_o1")])
                V(lambda ct=ct, sc0=sc0, n_=n_: nc.vector.tensor_tensor(out=o1[:, 0:n_], in0=o1[:, 0:n_], in1=vT[:, ct, sc0:sc0 + n_], op=ALU.mult), r=[vb], w=[B("g_o1")])
                A(lambda g_=g_, n_=n_: nc.scalar.activation(out=g_[:, 0:n_], in_=g_[:, 0:n_], func=AF.Silu), w=[gb])
                o_ = ob[ct % 2]; obb = B(f"g_ob{ct % 2}")
                V(lambda o_=o_, g_=g_, n_=n_: nc.vector.tensor_tensor(out=o_[:, 0:n_], in0=o1[:, 0:n_], in1=g_[:, 0:n_], op=ALU.mult), r=[gb, B("g_o1")], w=[obb])
                s.dma("pool", k.MT[1024 + ct * 128:1024 + (ct + 1) * 128, t0 + sc0:t0 + sc0 + n_], o_[:, 0:n_], r=[obb], w=[B("MT")])


D = 1024; T = 4352; TC = 256; TL = 4096; NT = 34; DEPTH = 4
DIN = 4640; EPS = 1e-6

class K:
    pass

def dram(nc, name, shape, dt, kind="Internal"):
    return nc.dram_tensor(name, list(shape), dt, kind=kind).ap()

def build(nlayers=DEPTH, stop=None, dbg=()):
    nc = bass.Bass("TRN2", target_bir_lowering=False)
    s = Sched(nc)
    k = K(); k.nc = nc; k.s = s; k.dbg = dbg; k.stop = stop; k.sfx = ''
    I = {}
    def inp(name, shape):
        I[name] = dram(nc, name, shape, F32, "ExternalInput")
    inp("x", [TL, D]); inp("c", [D]); inp("ctx", [TC, D]); inp("c_ctx", [D])
    inp("w_mod", [DEPTH, D, 3 * D]); inp("b_mod", [DEPTH, 3 * D]); inp("g_pre", [DEPTH, D]); inp("g_post", [DEPTH, D])
    inp("w_in", [DEPTH, D, DIN]); inp("conv_w", [DEPTH, 9, 1536]); inp("conv_b", [DEPTH, 1536])
    inp("dt_bias", [DEPTH, 32]); inp("a_log", [DEPTH, 32]); inp("d_ssd", [DEPTH, 16]); inp("g_ssd_norm", [DEPTH, D])
    inp("s5_lambda_re", [DEPTH, 2, 32, 64]); inp("s5_lambda_im", [DEPTH, 2, 32, 64]); inp("s5_log_step", [DEPTH, 2, 32])
    inp("s5_b_re", [DEPTH, 2, 32, 64, 16]); inp("s5_b_im", [DEPTH, 2, 32, 64, 16])
    inp("s5_c_re", [DEPTH, 2, 32, 16, 64]); inp("s5_c_im", [DEPTH, 2, 32, 16, 64])
    inp("s5_d", [DEPTH, 512]); inp("s5_w_glu", [DEPTH, 512, 512]); inp("s5_b_glu", [DEPTH, 512])
    inp("fnet_w", [DEPTH, 512, 512]); inp("fnet_b", [DEPTH, 512]); inp("w_out", [DEPTH, 2048, D])
    inp("cst", [128, 2304])
    k.I = I
    k.out = dram(nc, "out", [TL, D], F32, "ExternalOutput")
    def scr(name, shape, dt):
        return dram(nc, name, shape, dt, "ExternalOutput" if name in dbg else "Internal")
    k.XS = scr("XS", [T, D], F32)
    k.ZT = scr("ZT", [T, D], F32)
    k.PF = scr("PF", [1056, T], F32)
    k.PB = scr("PB", [2560, T], BF16)
    k.XC = scr("XC", [1536, T], BF16)
    k.MT = scr("MT", [2048, T], BF16)
    k.YF = scr("YF", [T, D], F32)
    k.MODS = scr("MODS", [DEPTH, 2, 3, D], F32)
    k.CL = scr("CL", [4096, 4096], BF16)
    k.HD = scr("HD", [2, 16, 2, 128, 544], BF16)
    k.SLN = scr("SLN", [4096, 4096], BF16)
    k.bufs = {}
    def B(name):
        if name not in k.bufs:
            k.bufs[name] = Buf(name)
        return k.bufs[name]
    k.B = B
    k.cst = nc.alloc_sbuf_tensor("cst_sb", [128, 2304], F32)
    k.identb = nc.alloc_sbuf_tensor("identb", [128, 128], BF16)
    s.dma("sp", k.cst[:], I["cst"][:, :], w=[B("cst")])
    s.op("dve", lambda: nc.vector.tensor_copy(out=k.identb[:], in_=k.cst[:, 0:128]), r=[B("cst")], w=[B("identb")])
    k.negpi = nc.alloc_sbuf_tensor("negpi", [128, 1], F32)
    s.op("dve", lambda: nc.vector.memset(k.negpi[:], -3.14159265), w=[B("negpi")])
    k.pidx_i = nc.alloc_sbuf_tensor("pidx_i", [128, 1], mybir.dt.int32)
    s.op("dve", lambda: nc.vector.tensor_copy(out=k.pidx_i[:], in_=k.cst[:, 139:140]), r=[B("cst")], w=[B("pidx_i")])
    prep_mods(k)
    gen_dft(k)
    for l in range(nlayers):
        k.sfx = f'_L{l}'
        phase_ab(k, l)
        if stop == "ab":
            break
        phase_conv(k, l)
        if stop == "conv":
            break
        phase_ssd(k, l, need_ctx=(l < DEPTH - 1))
        if stop == "ssd":
            break
        phase_s5(k, l, need_ctx=(l < DEPTH - 1))
        if stop in ("s5", "s5prep", "s5main"):
            break
        phase_fnet(k, l, need_ctx=(l < DEPTH - 1))
        if stop == "fnet":
            break
        if stop != "outonly":
            pass
        phase_out(k, l, last=(l == nlayers - 1 and nlayers == DEPTH))
        if stop == "out":
            break
    s.drain("sp")
    return nc


def prep_mods(k):
    nc, s, I, B = k.nc, k.s, k.I, k.B
    with ExitStack() as st:
        T_ = lambda name, shape, dt: st.enter_context(nc.sbuf_tensor(name + k.sfx, shape, dt))
        craw = T_("craw", [128, 8, 2], F32)
        sc = T_("sc", [128, 8, 2], F32)
        wm = [T_(f"wm{i}", [128, 3 * D], F32) for i in range(2)]
        rows = T_("mrows", [2, 3 * D], F32)
        gp = T_("gp", [2, 2, D], F32)
        res = T_("mres", [2, 3, D], F32)
        psM = [st.enter_context(nc.psum_tensor(f"psM{i}", [128, 512], F32)) for i in range(6)]
        s.dma("sp", craw[:, :, 0], I["c"].rearrange("(k p) -> p k", p=128), w=[B("craw")], allow_slow_non_contiguous=True)
        s.dma("sp", craw[:, :, 1], I["c_ctx"].rearrange("(k p) -> p k", p=128), w=[B("craw")], allow_slow_non_contiguous=True)
        s.op("act", lambda: nc.scalar.activation(out=sc[:], in_=craw[:], func=AF.Silu), r=[B("craw")], w=[B("sc")])
        for l in range(DEPTH):
            for kk in range(8):
                w_ = wm[kk % 2]; wb = B(f"wm{kk % 2}")
                s.dma("sp", w_[:], I["w_mod"][l, kk * 128:(kk + 1) * 128, :], w=[wb])
                for n in range(6):
                    s.op("pe", lambda n=n, w_=w_, kk=kk: nc.tensor.matmul(psM[n][0:2, :], lhsT=sc[:, kk, :], rhs=w_[:, n * 512:(n + 1) * 512],
                                                                start=(kk == 0), stop=(kk == 7)), r=[B("sc"), wb], w=[B(f"psM{n}")])
            s.dma("sp", rows[:], I["b_mod"][l:l + 1, :].partition_broadcast(2) if False else I["b_mod"][l, :].partition_broadcast(2), w=[B("mrows")])
            s.dma("sp", gp[:, 0, :], I["g_pre"][l, :].partition_broadcast(2), w=[B("gp")])
            s.dma("sp", gp[:, 1, :], I["g_post"][l, :].partition_broadcast(2), w=[B("gp")])
            for n in range(6):
                s.op("dve", lambda n=n: nc.vector.tensor_tensor(out=rows[:, n * 512:(n + 1) * 512], in0=rows[:, n * 512:(n + 1) * 512],
                                                            in1=psM[n][0:2, :], op=ALU.add), r=[B(f"psM{n}")], w=[B("mrows")])
            s.op("dve", lambda: nc.vector.tensor_copy(out=res[:, 0, :], in_=rows[:, 0:D]), r=[B("mrows")], w=[B("mres")])
            s.op("dve", lambda: nc.vector.scalar_tensor_tensor(out=res[:, 1, :], in0=rows[:, D:2 * D], scalar=1.0, in1=gp[:, 0, :],
                                                             op0=ALU.add, op1=ALU.mult), r=[B("mrows"), B("gp")], w=[B("mres")])
            s.op("dve", lambda: nc.vector.tensor_tensor(out=res[:, 2, :], in0=rows[:, 2 * D:3 * D], in1=gp[:, 1, :], op=ALU.mult),
                 r=[B("mrows"), B("gp")], w=[B("mres")])
            s.dma("sp", k.MODS[l], res[:], r=[B("mres")], w=[B("MODS")])
        s.barrier()


def fm_cols():
    lst = []
    for i in range(12):
        lst.append((1024 + i * 128, 128, "PB", i * 128, BF16))
    lst.append((2560, 32, "PF", 0, F32))
    for i in range(4):
        lst.append((2592 + i * 128, 128, "PB", 1536 + i * 128, BF16))
    for i in range(4):
        lst.append((3104 + i * 128, 128, "PF", 32 + i * 128, F32))
    for i in range(4):
        lst.append((3616 + i * 128, 128, "PB", 2048 + i * 128, BF16))
    for i in range(4):
        lst.append((4128 + i * 128, 128, "PF", 544 + i * 128, F32))
    return lst


def phase_ab(k, l):
    nc, s, I, B = k.nc, k.s, k.I, k.B
    with ExitStack() as st:
        T_ = lambda name, shape, dt: st.enter_context(nc.sbuf_tensor(name + k.sfx, shape, dt))
        P_ = lambda name, shape, dt: st.enter_context(nc.psum_tensor(name + k.sfx, shape, dt))
        wsb = T_("wsb", [128, 8, DIN], BF16)
        wst = [T_(f"wst{i}", [128, DIN], F32) for i in range(2)]
        bc = T_("bcab", [128, 4, D], F32)
        xt = [T_(f"xt{i}", [128, D], F32) for i in range(2)]
        junk = T_("junk", [128, D], BF16)
        h1 = T_("h1", [128, D], F32)
        hl = [T_(f"hl{i}", [128, D], BF16) for i in range(2)]
        hlT = [T_(f"hlT{i}", [128, 8, 512], BF16) for i in range(2)]
        st4 = T_("st4", [128, 8], F32)
        zt = [T_(f"zt{i}", [128, D], F32) for i in range(2)]
        ev32 = [T_(f"ev32_{i}", [128, 512], F32) for i in range(2)]
        ev16 = [T_(f"ev16_{i}", [128, 512], BF16) for i in range(2)]
        psT = P_("psT", [128, 1024], BF16)
        psZ = [P_(f"psZ{i}", [128, 512], F32) for i in range(2)]
        psF = [P_(f"psF{i}", [128, 512], F32) for i in range(3)]
        for kk in range(8):
            w_ = wst[kk % 2]; wb = B(f"wst{kk % 2}")
            s.dma("sp", w_[:], I["w_in"][l, kk * 128:(kk + 1) * 128, :], w=[wb])
            e = "act" if kk % 2 == 0 else "pool"
            if e == "act":
                s.op("act", lambda w_=w_, kk=kk: nc.scalar.copy(out=wsb[:, kk, :], in_=w_[:]), r=[wb], w=[B(f"wsb{kk}")])
            else:
                s.op("pool", lambda w_=w_, kk=kk: nc.gpsimd.tensor_copy(out=wsb[:, kk, :], in_=w_[:]), r=[wb], w=[B(f"wsb{kk}")])
        wsb_bufs = [B(f"wsb{kk}") for kk in range(8)]
        for j, (which, comp) in enumerate([(1, 0), (1, 1), (0, 0), (0, 1)]):
            s.dma("pool", bc[:, j, :], k.MODS[l, which, comp, :].partition_broadcast(128), r=[B("MODS")], w=[B("bcab")])
        cols = fm_cols()
        ngroups = 9
        ev_i = 0
        for g in range(ngroups):
            ntok = 512 if g < 8 else 256
            nt = ntok // 128
            hT = hlT[g % 2]; hTb = B(f"hlT{g % 2}")
            for tt in range(nt):
                ti = g * 4 + tt
                x_ = xt[ti % 2]; xb = B(f"xt{ti % 2}")
                if l == 0:
                    src = I["ctx"][ti * 128:(ti + 1) * 128, :] if ti < 2 else I["x"][(ti - 2) * 128:(ti - 1) * 128, :]
                    s.dma("sp", x_[:], src, w=[xb])
                else:
                    s.dma("sp", x_[:], k.XS[ti * 128:(ti + 1) * 128, :], r=[B("XS")], w=[xb])
                jb = 0 if ti < 2 else 2
                c0 = (ti % 4) * 2
                s.op("act", lambda x_=x_, c0=c0: nc.scalar.activation(out=junk[:], in_=x_[:], func=AF.Square, accum_out=st4[:, c0:c0 + 1]),
                     r=[xb], w=[B("junk"), B(f"st4_{ti % 4}")])
                s.op("dve", lambda c0=c0: nc.vector.tensor_scalar(out=st4[:, c0:c0 + 1], in0=st4[:, c0:c0 + 1], scalar1=1.0 / D, scalar2=EPS,
                                                              op0=ALU.mult, op1=ALU.add), w=[B(f"st4_{ti % 4}")])
                s.op("act", lambda c0=c0: nc.scalar.activation(out=st4[:, c0:c0 + 1], in_=st4[:, c0:c0 + 1], func=AF.Sqrt), w=[B(f"st4_{ti % 4}")])
                s.op("dve", lambda c0=c0: nc.vector.reciprocal(out=st4[:, c0 + 1:c0 + 2], in_=st4[:, c0:c0 + 1]), w=[B(f"st4_{ti % 4}")])
                s.op("dve", lambda x_=x_, c0=c0, jb=jb: nc.vector.scalar_tensor_tensor(out=h1[:], in0=x_[:], scalar=st4[:, c0 + 1:c0 + 2], in1=bc[:, jb + 1, :],
                                                                                 op0=ALU.mult, op1=ALU.mult), r=[xb, B(f"st4_{ti % 4}"), B("bcab")], w=[B("h1")])
                h_ = hl[ti % 2]; hb = B(f"hl{ti % 2}")
                s.op("dve", lambda h_=h_, jb=jb: nc.vector.tensor_tensor(out=h_[:], in0=h1[:], in1=bc[:, jb, :], op=ALU.add), r=[B("h1"), B("bcab")], w=[hb])
                for kk in range(8):
                    s.op("pe", lambda h_=h_, kk=kk: nc.tensor.transpose(out=psT[:, kk * 128:(kk + 1) * 128], in_=h_[:, kk * 128:(kk + 1) * 128], identity=k.identb[:]),
                         r=[hb, B("identb")], w=[B("psT")])
                s.op("act", lambda hT=hT, tt=tt: nc.scalar.copy(out=hT[:, :, tt * 128:(tt + 1) * 128], in_=psT[:].rearrange("p (k t) -> p k t", t=128)),
                     r=[B("psT")], w=[hTb])
                z_ = zt[ti % 2]; zb = B(f"zt{ti % 2}")
                for hh in range(2):
                    pz = psZ[hh]; pzb = B(f"psZ{hh}")
                    for kk in range(8):
                        s.op("pe", lambda pz=pz, hT=hT, kk=kk, tt=tt, hh=hh: nc.tensor.matmul(pz[:], lhsT=hT[:, kk, tt * 128:(tt + 1) * 128], rhs=wsb[:, kk, hh * 512:(hh + 1) * 512],
                                                                                        start=(kk == 0), stop=(kk == 7)), r=[hTb, wsb_bufs[kk]], w=[pzb])
                    if hh == 0:
                        s.op("act", lambda z_=z_, pz=pz: nc.scalar.copy(out=z_[:, 0:512], in_=pz[:]), r=[pzb], w=[zb])
                    else:
                        s.op("dve", lambda z_=z_, pz=pz: nc.vector.tensor_copy(out=z_[:, 512:1024], in_=pz[:]), r=[pzb], w=[zb])
                s.dma("pool", k.ZT[ti * 128:(ti + 1) * 128, :], z_[:], r=[zb], w=[B("ZT")])
            t0 = g * 512
            for ci, (c0, wd, dest, r0, dt_) in enumerate(cols):
                pf = psF[ci % 3]; pfb = B(f"psF{ci % 3}")
                for kk in range(8):
                    s.op("pe", lambda pf=pf, hT=hT, kk=kk, c0=c0, wd=wd, ntok=ntok: nc.tensor.matmul(pf[0:wd, 0:ntok], lhsT=wsb[:, kk, c0:c0 + wd], rhs=hT[:, kk, 0:ntok],
                                                                                             start=(kk == 0), stop=(kk == 7)), r=[hTb, wsb_bufs[kk]], w=[pfb])
                ev = (ev32 if dt_ == F32 else ev16)[ev_i % 2]
                evb = B(("ev32_" if dt_ == F32 else "ev16_") + str(ev_i % 2))
                if ev_i % 2 == 0:
                    s.op("act", lambda ev=ev, pf=pf, wd=wd, ntok=ntok: nc.scalar.copy(out=ev[0:wd, 0:ntok], in_=pf[0:wd, 0:ntok]), r=[pfb], w=[evb])
                else:
                    s.op("dve", lambda ev=ev, pf=pf, wd=wd, ntok=ntok: nc.vector.tensor_copy(out=ev[0:wd, 0:ntok], in_=pf[0:wd, 0:ntok]), r=[pfb], w=[evb])
                dst = getattr(k, dest)
                s.dma("pool" if ev_i % 2 == 0 else "sp", dst[r0:r0 + wd, t0:t0 + ntok], ev[0:wd, 0:ntok], r=[evb], w=[B(dest)])
                ev_i += 1
        s.barrier()


def _consts():
    c = np.zeros((128, 2304), np.float32)
    c[:, 0:128] = np.eye(128)
    c[:, 128:137] = np.arange(9)
    p = np.arange(128)
    c[:, 137] = (p % 32) < 16
    c[:, 138] = (p % 32) >= 16
    c[:, 139] = p
    c[:, 140:268] = 1.0
    jj, ii = np.meshgrid(p, p, indexing="ij")
    c[:, 268:396] = np.where(jj > ii, -30000.0, 0.0)
    c[:, 396:524] = np.where(jj < ii, -30000.0, 0.0)
    c[:, 524:532] = (p[:, None] // 16) == np.arange(8)[None]
    c[:, 1024:1568] = np.arange(544)[None]
    cn = np.arange(544)
    c[:, 1600:2144] = np.where(cn < 32, 31 - cn, 575 - cn)[None]
    return c


def kernel(**inputs):
    inp = {k_: np.asarray(v) for k_, v in inputs.items()}
    cst = _consts()
    shared = {}
    for name, v in inp.items():
        if name in ("x", "c", "ctx"):
            continue
        v = np.ascontiguousarray(v, dtype=np.float32)
        if name == "conv_w":
            v = np.ascontiguousarray(v.reshape(4, 9, 1536))
        elif name in ("dt_bias", "a_log"):
            v = np.ascontiguousarray(v.reshape(4, 32))
        shared[name] = v
    shared["cst"] = cst
    in_maps = []
    for core in range(8):
        b = core % 4
        m = dict(shared)
        m["x"] = np.ascontiguousarray(inp["x"][b], dtype=np.float32)
        m["c"] = np.ascontiguousarray(inp["c"][b], dtype=np.float32)
        m["ctx"] = np.ascontiguousarray(inp["ctx"][b], dtype=np.float32)
        in_maps.append(m)
    nc = build()
    res = run_bass_kernel_spmd(nc, in_maps, core_ids=list(range(8)))
    out = np.stack([np.asarray(res.results[b]["out"], dtype=np.float32) for b in range(4)], axis=0)
    return out
```
